# Optimizing a Trainium2 kernel written in Bass

```python
import jax, jax.numpy as jnp
from jax import lax
import numpy as np

D_MODEL = 2048
BATCH = 2
SEQ = 8192
DEPTH = 4

CTX_LEN = 256
GRID_W = 64

MIX_WIDTH = D_MODEL
CONV_WIDTH = MIX_WIDTH // 4
CONV_K = 3
GLA_HEADS = 4
GLA_DV = MIX_WIDTH // 4 // GLA_HEADS
GLA_DK = GLA_DV // 2
GLA_LOWRANK = 16
GLA_TAU = 16.0
GLA_CHUNK = 64
SWA_HEADS = 8
SWA_KV_HEADS = 2
SWA_GROUP = SWA_HEADS // SWA_KV_HEADS
SWA_HEAD_DIM = MIX_WIDTH // 2 // SWA_HEADS
SWA_WINDOW = 128
SWA_BLOCK = 128
ROPE_THETA = 10000.0
NEG_INF = -1e30
PEER_HEADS = 8
PEER_NKEYS = 128
PEER_EXPERTS = PEER_NKEYS * PEER_NKEYS
PEER_DKEY = 256
PEER_TOPK = 16
PEER_TOKEN_BLOCK = 128
DEEPNORM_ALPHA = (2.0 * DEPTH) ** 0.25
DEEPNORM_BETA = (8.0 * DEPTH) ** -0.25
LN_EPS = 1e-6

IN_SPLITS = (
    CONV_WIDTH, CONV_WIDTH, CONV_WIDTH,
    GLA_HEADS * GLA_DK, GLA_HEADS * GLA_DK,
    GLA_HEADS * GLA_DV, GLA_HEADS * GLA_DV,
    GLA_LOWRANK, GLA_LOWRANK,
    SWA_HEADS * SWA_HEAD_DIM,
    SWA_KV_HEADS * SWA_HEAD_DIM, SWA_KV_HEADS * SWA_HEAD_DIM,
)
IN_WIDTH = sum(IN_SPLITS)

kernel_name = "hymba_conv_gla_swa_peer_deepnorm_dit"


def layer_norm(x, g, b):
    xf = x.astype(jnp.float32)
    mu = jnp.mean(xf, axis=-1, keepdims=True)
    var = jnp.mean(jnp.square(xf - mu), axis=-1, keepdims=True)
    return ((xf - mu) * lax.rsqrt(var + LN_EPS) * g + b).astype(x.dtype)


def split_cols(z):
    points, acc = [], 0
    for width in IN_SPLITS[:-1]:
        acc += width
        points.append(acc)
    return jnp.split(z, points, axis=-1)


def axial_rope_tables(n_tokens, dtype):
    rows = n_tokens // GRID_W
    r, col = jnp.meshgrid(jnp.arange(rows, dtype=jnp.float32), jnp.arange(GRID_W, dtype=jnp.float32), indexing='ij')
    axis_dim = SWA_HEAD_DIM // 2
    inv_freq = ROPE_THETA ** (-jnp.arange(0, axis_dim, 2, dtype=jnp.float32) / axis_dim)
    ang_r = r.reshape(-1, 1) * inv_freq
    ang_c = col.reshape(-1, 1) * inv_freq
    cos_r = jnp.cos(ang_r).astype(dtype)[None, :, None, :]
    sin_r = jnp.sin(ang_r).astype(dtype)[None, :, None, :]
    cos_c = jnp.cos(ang_c).astype(dtype)[None, :, None, :]
    sin_c = jnp.sin(ang_c).astype(dtype)[None, :, None, :]
    return (cos_r, sin_r, cos_c, sin_c)


def rotate_half(x, cos, sin):
    x1, x2 = jnp.split(x, 2, axis=-1)
    return jnp.concatenate([x1 * cos - x2 * sin, x2 * cos + x1 * sin], axis=-1)


def apply_axial_rope(x, tables):
    cos_r, sin_r, cos_c, sin_c = tables
    x_row, x_col = jnp.split(x, 2, axis=-1)
    return jnp.concatenate([rotate_half(x_row, cos_r, sin_r), rotate_half(x_col, cos_c, sin_c)], axis=-1)


def short_conv(z, w):
    return lax.conv_general_dilated(z, w[:, None, :], window_strides=(1,), padding=((CONV_K // 2, CONV_K // 2),),
                                    dimension_numbers=('NWC', 'WIO', 'NWC'), feature_group_count=z.shape[-1])


def conv_mixer(x_in, gate_b, gate_c, w):
    return gate_b * short_conv(gate_c * x_in, w)


def gla_heads(parts, w2, b2):
    _, _, _, q, k, v, g, lr_f, lr_b, _, _, _ = parts
    bsz, n = q.shape[:2]
    q = q.reshape(bsz, n, GLA_HEADS, GLA_DK) * GLA_DK ** -0.5
    k = k.reshape(bsz, n, GLA_HEADS, GLA_DK)
    v = v.reshape(bsz, n, GLA_HEADS, GLA_DV)
    lg_f = gla_log_decay(lr_f, w2[0], b2[0])
    lg_b = gla_log_decay(lr_b, w2[1], b2[1])
    return q, k, v, lg_f, lg_b, g


def gla_log_decay(lr, w2, b2):
    z = (lr @ w2 + b2).astype(jnp.float32)
    return (jax.nn.log_sigmoid(z) / GLA_TAU).reshape(lr.shape[0], lr.shape[1], GLA_HEADS, GLA_DK)


def gla_chunk_states(k, v, lg, s0):
    bsz, n_tok, h, dk = k.shape
    dv = v.shape[-1]
    n = n_tok // GLA_CHUNK
    kc = k.reshape(bsz, n, GLA_CHUNK, h, dk)
    vc = v.reshape(bsz, n, GLA_CHUNK, h, dv)
    b = jnp.cumsum(lg.reshape(bsz, n, GLA_CHUNK, h, dk), axis=2)
    b_last = b[:, :, -1]
    k_end = kc * jnp.exp(b_last[:, :, None] - b).astype(k.dtype)
    delta = jnp.einsum('bnchk,bnchv->nbhkv', k_end, vc)
    decay = jnp.exp(b_last).astype(k.dtype).transpose(1, 0, 2, 3)

    def step(s, inp):
        dec, dlt = inp
        return dec[..., None] * s + dlt, s

    s_final, s_starts = lax.scan(step, s0, (decay, delta))
    return s_starts, s_final, b


def gla_chunk_outputs(q, k, v, b, s_starts):
    bsz, n_tok, h, dk = q.shape
    dv = v.shape[-1]
    n = n_tok // GLA_CHUNK
    qe = q.reshape(bsz, n, GLA_CHUNK, h, dk) * jnp.exp(b).astype(q.dtype)
    ke = k.reshape(bsz, n, GLA_CHUNK, h, dk) * jnp.exp(-b).astype(k.dtype)
    vc = v.reshape(bsz, n, GLA_CHUNK, h, dv)
    att = jnp.einsum('bnihk,bnjhk->bnhij', qe, ke)
    lower_tri = jnp.tril(jnp.ones((GLA_CHUNK, GLA_CHUNK), dtype=bool))
    att = jnp.where(lower_tri, att, jnp.zeros_like(att))
    o = jnp.einsum('bnhij,bnjhv->bnihv', att, vc) + jnp.einsum('bnihk,nbhkv->bnihv', qe, s_starts)
    return o.reshape(bsz, n_tok, h, dv)


def gla_bidir(q, k, v, lg_f, lg_b, s0_f, s0_b):
    st_f, fin_f, b_f = gla_chunk_states(k, v, lg_f, s0_f)
    o_f = gla_chunk_outputs(q, k, v, b_f, st_f)
    qr, kr, vr = jnp.flip(q, axis=1), jnp.flip(k, axis=1), jnp.flip(v, axis=1)
    st_b, fin_b, b_b = gla_chunk_states(kr, vr, jnp.flip(lg_b, axis=1), s0_b)
    o_b = jnp.flip(gla_chunk_outputs(qr, kr, vr, b_b, st_b), axis=1)
    return o_f + o_b, fin_f, fin_b


def gla_final_states(k, v, lg_f, lg_b, s0):
    _, fin_f, _ = gla_chunk_states(k, v, lg_f, s0)
    _, fin_b, _ = gla_chunk_states(jnp.flip(k, axis=1), jnp.flip(v, axis=1), jnp.flip(lg_b, axis=1), s0)
    return fin_f, fin_b


def gla_finish(o, g, norm_w):
    of = o.astype(jnp.float32)
    of = of * lax.rsqrt(jnp.mean(of * of, axis=-1, keepdims=True) + LN_EPS) * norm_w
    bsz, n = o.shape[:2]
    return of.astype(o.dtype).reshape(bsz, n, GLA_HEADS * GLA_DV) * jax.nn.silu(g)


def swa_heads(parts):
    sq, sk, sv = parts[9], parts[10], parts[11]
    bsz, n = sq.shape[:2]
    return (sq.reshape(bsz, n, SWA_HEADS, SWA_HEAD_DIM),
            sk.reshape(bsz, n, SWA_KV_HEADS, SWA_HEAD_DIM),
            sv.reshape(bsz, n, SWA_KV_HEADS, SWA_HEAD_DIM))


def sink_column(sink, lead_shape):
    col = sink.astype(jnp.float32).reshape(SWA_KV_HEADS, SWA_GROUP)[:, :, None, None]
    return jnp.broadcast_to(col, lead_shape + (1,))


def swa_latent(q, k, v, k_ctx, v_ctx, sink):
    bsz, n = q.shape[:2]
    nb = n // SWA_BLOCK
    band = 3 * SWA_BLOCK
    qb = q.reshape(bsz, nb, SWA_BLOCK, SWA_KV_HEADS, SWA_GROUP, SWA_HEAD_DIM) * SWA_HEAD_DIM ** -0.5
    pad = ((0, 0), (SWA_BLOCK, SWA_BLOCK), (0, 0), (0, 0))
    band_idx = jnp.arange(nb)[:, None] * SWA_BLOCK + jnp.arange(band)[None, :]
    kb = jnp.pad(k, pad)[:, band_idx]
    vb = jnp.pad(v, pad)[:, band_idx]
    q_pos = jnp.arange(nb)[:, None] * SWA_BLOCK + jnp.arange(SWA_BLOCK)[None, :]
    k_pos = band_idx - SWA_BLOCK
    valid = ((jnp.abs(q_pos[:, :, None] - k_pos[:, None, :]) <= SWA_WINDOW)
             & (k_pos[:, None, :] >= 0) & (k_pos[:, None, :] < n))
    s_loc = jnp.einsum('bnqhgd,bnkhd->bnhgqk', qb, kb).astype(jnp.float32)
    s_loc = jnp.where(valid[None, :, None, None], s_loc, NEG_INF)
    s_ctx = jnp.einsum('bnqhgd,bchd->bnhgqc', qb, k_ctx).astype(jnp.float32)
    logits = jnp.concatenate([s_loc, s_ctx, sink_column(sink, s_loc.shape[:-1])], axis=-1)
    p = jax.nn.softmax(logits, axis=-1).astype(v.dtype)
    n_ctx = k_ctx.shape[1]
    o = (jnp.einsum('bnhgqk,bnkhd->bnqhgd', p[..., :band], vb)
         + jnp.einsum('bnhgqc,bchd->bnqhgd', p[..., band:band + n_ctx], v_ctx))
    return o.reshape(bsz, n, SWA_HEADS * SWA_HEAD_DIM)


def swa_context(q, k, v, sink):
    bsz, n = q.shape[:2]
    qg = q.reshape(bsz, n, SWA_KV_HEADS, SWA_GROUP, SWA_HEAD_DIM) * SWA_HEAD_DIM ** -0.5
    s = jnp.einsum('bqhgd,bkhd->bhgqk', qg, k).astype(jnp.float32)
    logits = jnp.concatenate([s, sink_column(sink, s.shape[:-1])], axis=-1)
    p = jax.nn.softmax(logits, axis=-1)[..., :n].astype(v.dtype)
    return jnp.einsum('bhgqk,bkhd->bqhgd', p, v).reshape(bsz, n, SWA_HEADS * SWA_HEAD_DIM)


def token_mixer(hl, hc, rope, w_in, conv_w, gla_w2, gla_b2, gla_norm, swa_sink, w_out, with_ctx_out):
    pl = split_cols(hl @ w_in)
    pc = split_cols(hc @ w_in)
    q_c, k_c, v_c, lgf_c, lgb_c, g_c = gla_heads(pc, gla_w2, gla_b2)
    sq_c, sk_c, sv_c = swa_heads(pc)
    s0 = jnp.zeros((hc.shape[0], GLA_HEADS, GLA_DK, GLA_DV), hc.dtype)
    if with_ctx_out:
        o_c, s_ctx_f, s_ctx_b = gla_bidir(q_c, k_c, v_c, lgf_c, lgb_c, s0, s0)
    else:
        s_ctx_f, s_ctx_b = gla_final_states(k_c, v_c, lgf_c, lgb_c, s0)
    q_l, k_l, v_l, lgf_l, lgb_l, g_l = gla_heads(pl, gla_w2, gla_b2)
    o_l, _, _ = gla_bidir(q_l, k_l, v_l, lgf_l, lgb_l, s_ctx_f, s_ctx_b)
    sq_l, sk_l, sv_l = swa_heads(pl)
    swa_l = swa_latent(apply_axial_rope(sq_l, rope), apply_axial_rope(sk_l, rope), sv_l, sk_c, sv_c, swa_sink)
    y_l = jnp.concatenate([conv_mixer(pl[0], pl[1], pl[2], conv_w), gla_finish(o_l, g_l, gla_norm), swa_l],
                          axis=-1) @ w_out
    if with_ctx_out:
        y_c = jnp.concatenate([conv_mixer(pc[0], pc[1], pc[2], conv_w), gla_finish(o_c, g_c, gla_norm),
                               swa_context(sq_c, sk_c, sv_c, swa_sink)], axis=-1) @ w_out
        return y_l, y_c
    return y_l, None


def peer_ffn(h, wq, subkeys, u_tab, v_tab):
    n_tok, d = h.shape
    q = (h @ wq).reshape(n_tok, PEER_HEADS, 2, PEER_DKEY // 2)
    s = jnp.einsum('thpd,pnd->thpn', q, subkeys).astype(jnp.float32)
    s_top, i_top = lax.top_k(s, PEER_TOPK)
    cand = (s_top[:, :, 0, :, None] + s_top[:, :, 1, None, :]).reshape(n_tok, PEER_HEADS, PEER_TOPK * PEER_TOPK)
    best, flat = lax.top_k(cand, PEER_TOPK)
    i1 = jnp.take_along_axis(i_top[:, :, 0], flat // PEER_TOPK, axis=-1)
    i2 = jnp.take_along_axis(i_top[:, :, 1], flat % PEER_TOPK, axis=-1)
    experts = i1 * PEER_NKEYS + i2
    gates = jax.nn.softmax(best, axis=-1).astype(h.dtype)
    nblk = n_tok // PEER_TOKEN_BLOCK

    def block(args):
        hb, eb, gb = args
        act = jax.nn.gelu(jnp.einsum('thkd,td->thk', u_tab[eb], hb), approximate=False)
        return jnp.einsum('thk,thkd->td', act * gb, v_tab[eb])

    y = lax.map(block, (h.reshape(nblk, PEER_TOKEN_BLOCK, d),
                        experts.reshape(nblk, PEER_TOKEN_BLOCK, PEER_HEADS, PEER_TOPK),
                        gates.reshape(nblk, PEER_TOKEN_BLOCK, PEER_HEADS, PEER_TOPK)))
    return y.reshape(n_tok, d)


def setup_inputs(seed: int = 0) -> dict:
    key = jax.random.key(seed)
    ks = jax.random.split(key, 19)
    f32 = jnp.float32

    def nrm(k, shape, scale):
        return jax.random.normal(k, shape, f32) * scale

    return {
        "x": nrm(ks[0], (BATCH, SEQ, D_MODEL), 1.0),
        "c": nrm(ks[1], (BATCH, D_MODEL), 1.0),
        "ctx": nrm(ks[2], (BATCH, CTX_LEN, D_MODEL), 1.0),
        "c_ctx": nrm(ks[3], (D_MODEL,), 1.0),
        "w_ada": nrm(ks[4], (DEPTH, D_MODEL, 6 * D_MODEL), D_MODEL ** -0.5),
        "b_ada": nrm(ks[5], (DEPTH, 6 * D_MODEL), 0.02),
        "w_in": nrm(ks[6], (DEPTH, D_MODEL, IN_WIDTH), D_MODEL ** -0.5),
        "conv_w": nrm(ks[7], (DEPTH, CONV_K, CONV_WIDTH), CONV_K ** -0.5),
        "gla_w2": nrm(ks[8], (DEPTH, 2, GLA_LOWRANK, GLA_HEADS * GLA_DK), GLA_LOWRANK ** -0.5),
        "gla_b2": nrm(ks[9], (DEPTH, 2, GLA_HEADS * GLA_DK), 0.1),
        "gla_norm": 1.0 + nrm(ks[10], (DEPTH, GLA_DV), 0.1),
        "swa_sink": nrm(ks[11], (DEPTH, SWA_HEADS), 0.5),
        "w_out": nrm(ks[12], (DEPTH, MIX_WIDTH, D_MODEL), MIX_WIDTH ** -0.5 * DEEPNORM_BETA),
        "ln_g": 1.0 + nrm(ks[13], (DEPTH, 2, D_MODEL), 0.1),
        "ln_b": nrm(ks[14], (DEPTH, 2, D_MODEL), 0.02),
        "peer_wq": nrm(ks[15], (DEPTH, D_MODEL, PEER_HEADS * PEER_DKEY), D_MODEL ** -0.5),
        "peer_subkeys": nrm(ks[16], (DEPTH, 2, PEER_NKEYS, PEER_DKEY // 2), (PEER_DKEY // 2) ** -0.5),
        "peer_u": nrm(ks[17], (DEPTH, PEER_EXPERTS, D_MODEL), D_MODEL ** -0.5),
        "peer_v": nrm(ks[18], (DEPTH, PEER_EXPERTS, D_MODEL), PEER_HEADS ** -0.5 * DEEPNORM_BETA),
    }


def reference(x, c, ctx, c_ctx, w_ada, b_ada, w_in, conv_w, gla_w2, gla_b2, gla_norm, swa_sink, w_out,
              ln_g, ln_b, peer_wq, peer_subkeys, peer_u, peer_v):
    xl, xc = x, ctx
    rope = axial_rope_tables(x.shape[1], x.dtype)
    for layer in range(DEPTH):
        last = layer == DEPTH - 1
        sh1, sc1, g1, sh2, sc2, g2 = jnp.split((jax.nn.silu(c) @ w_ada[layer] + b_ada[layer])[:, None, :], 6, axis=-1)
        csh1, csc1, cg1, csh2, csc2, cg2 = jnp.split(jax.nn.silu(c_ctx) @ w_ada[layer] + b_ada[layer], 6, axis=-1)
        ml, mc = token_mixer(xl * (1 + sc1) + sh1, xc * (1 + csc1) + csh1, rope, w_in[layer], conv_w[layer],
                             gla_w2[layer], gla_b2[layer], gla_norm[layer], swa_sink[layer], w_out[layer],
                             not last)
        xl = layer_norm(DEEPNORM_ALPHA * xl + g1 * ml, ln_g[layer, 0], ln_b[layer, 0])
        hl = (xl * (1 + sc2) + sh2).reshape(-1, xl.shape[-1])
        fl = peer_ffn(hl, peer_wq[layer], peer_subkeys[layer], peer_u[layer], peer_v[layer]).reshape(xl.shape)
        xl = layer_norm(DEEPNORM_ALPHA * xl + g2 * fl, ln_g[layer, 1], ln_b[layer, 1])
        if not last:
            xc = layer_norm(DEEPNORM_ALPHA * xc + cg1 * mc, ln_g[layer, 0], ln_b[layer, 0])
            hc = (xc * (1 + csc2) + csh2).reshape(-1, xc.shape[-1])
            fc = peer_ffn(hc, peer_wq[layer], peer_subkeys[layer], peer_u[layer], peer_v[layer]).reshape(xc.shape)
            xc = layer_norm(DEEPNORM_ALPHA * xc + cg2 * fc, ln_g[layer, 1], ln_b[layer, 1])
    return xl
```

```python
import numpy as np
import ml_dtypes
from contextlib import ExitStack
import concourse.bass as bass
import concourse.mybir as mybir
from concourse.bass_utils import run_bass_kernel_spmd

F32 = mybir.dt.float32; BF16 = mybir.dt.bfloat16; U32 = mybir.dt.uint32
ALU = mybir.AluOpType; AF = mybir.ActivationFunctionType; AX = mybir.AxisListType

D = 2048
ALPHA = (2.0 * 4) ** 0.25
LN_EPS = 1e-6
GRID_W = 64


class Cfg:
    def __init__(self, L=8192, LC=256, DEPTH=4, NK=128, dbg=False):
        self.L = L; self.LC = LC; self.DEPTH = DEPTH; self.NK = NK; self.dbg = dbg
        self.NC = LC // 128; self.NL = L // 128; self.TT = self.NC + self.NL
        self.TOK = L + LC
        self.NE = NK * NK


class Sched:
    ENG = ['pe', 'dve', 'act', 'pool', 'sp']
    CE = ['pe', 'dve', 'act', 'pool']
    EPOCH = 30000
    NEP = {'pe': 26, 'dve': 5, 'act': 4, 'pool': 3}

    def __init__(self, nc, es, ndma=48):
        self.nc = nc
        self.esem = {e: [es.enter_context(nc.semaphore('s_%s%d' % (e, i))) for i in range(self.NEP[e])] for e in self.CE}
        self.dsem = [es.enter_context(nc.semaphore('d%d' % i)) for i in range(ndma)]
        self.dcnt = [0] * ndma; self.dnext = 0
        self.cnt = {e: 0 for e in self.CE}; self.ep = {e: 0 for e in self.CE}
        self.prog = {e: [] for e in self.ENG}
        self.seen = {e: {} for e in self.ENG}
        self.res = {}
        self.ninst = 0

    def semh(self, sid):
        return self.esem[sid[1]][sid[2]] if sid[0] == 'e' else self.dsem[sid[1]]

    def _need(self, eng, sid, val, waits):
        s = self.seen[eng]
        if sid[0] == 'e':
            k = ('e', sid[1]); cur = s.get(k, (-1, 0)); new = (sid[2], val)
            if cur >= new:
                return
            s[k] = new
        else:
            if s.get(sid, 0) >= val:
                return
            s[sid] = val
        waits.append((sid, val))

    def _deps(self, eng, reads, writes):
        waits = []
        pe = eng == 'pe'
        for k in reads:
            st = self.res.get(k)
            if st:
                for sid, v in st[0].items():
                    if not (pe and sid[0] == 'e' and sid[1] == 'pe'):
                        self._need(eng, sid, v, waits)
        for k in writes:
            st = self.res.get(k)
            if st:
                for d in st:
                    for sid, v in d.items():
                        if not (pe and sid[0] == 'e' and sid[1] == 'pe'):
                            self._need(eng, sid, v, waits)
        return waits

    def _mark(self, ev, reads, writes):
        sid, v = ev
        for k in reads:
            self.res.setdefault(k, ({}, {}))[1][sid] = v
        for k in writes:
            self.res.setdefault(k, ({}, {}))[0][sid] = v

    def op(self, eng, fn, reads=(), writes=()):
        waits = self._deps(eng, reads, writes)
        if self.cnt[eng] >= self.EPOCH:
            self.ep[eng] += 1; self.cnt[eng] = 0
        self.cnt[eng] += 1; ev = (('e', eng, self.ep[eng]), self.cnt[eng])
        self.prog[eng].append((waits, fn, ev)); self._mark(ev, reads, writes); self.ninst += 1

    def dma(self, eng, fn, reads=(), writes=()):
        slot = self.dnext; self.dnext = (self.dnext + 1) % len(self.dsem)
        waits = self._deps(eng, reads, writes)
        if self.dcnt[slot] > 0:
            self._need(eng, ('d', slot), self.dcnt[slot], waits)
        self.dcnt[slot] += 16; ev = (('d', slot), self.dcnt[slot])
        self.prog[eng].append((waits, fn, ev)); self._mark(ev, reads, writes); self.ninst += 1

    def barrier(self):
        for e in self.ENG:
            waits = []
            for i, c in enumerate(self.dcnt):
                if c:
                    self._need(e, ('d', i), c, waits)
            for o in self.CE:
                if o != e and (self.cnt[o] or self.ep[o]):
                    self._need(e, ('e', o, self.ep[o]), self.cnt[o], waits)
            if waits:
                self.prog[e].append((waits, None, None))
        self.res = {}

    def emit(self, block):
        engmap = {'pe': 'tensor', 'dve': 'vector', 'act': 'scalar', 'pool': 'gpsimd', 'sp': 'sync'}
        for e in self.ENG:
            prog = self.prog[e]

            def body(engine, prog=prog):
                for waits, fn, ev in prog:
                    for sid, v in waits:
                        engine.wait_ge(self.semh(sid), v)
                    if fn is None:
                        continue
                    ins = fn(engine)
                    sid, v = ev
                    ins.then_inc(self.semh(sid), 16 if sid[0] == 'd' else 1)
            getattr(block, engmap[e])(body)


def _blk_stationary(w):
    n = w.shape[1] // 128
    return np.ascontiguousarray(w.reshape(16, 128, n, 128).transpose(2, 1, 0, 3))


def _blk_moving(w, width=512):
    n = w.shape[1] // width
    return np.ascontiguousarray(w.reshape(16, 128, n, width).transpose(2, 1, 0, 3))


def rope_perm():
    d = np.arange(128)
    partner = np.where((d % 64) < 32, d + 32, d - 32)
    sign = np.where((d % 64) < 32, -1.0, 1.0).astype(np.float32)
    return partner, sign


def const_tables(cfg):
    cf = np.zeros((128, 912), np.float32)
    cf[:, 0:128] = np.eye(128, dtype=np.float32)
    cf[:, 128:256] = np.arange(128, dtype=np.float32)[None, :]
    s = -1.0 / 16.0
    j = np.arange(64)[:, None]; i = np.arange(64)[None, :]
    cf[:64, 256:320] = np.where(j <= i, s, 0.0)
    cf[:64, 320:384] = np.where(j >= i, s, 0.0)
    cf[:64, 384:448] = np.where(j > i, s, 0.0)
    cf[:64, 448:512] = np.where(j < i, s, 0.0)
    cf[:64, 512:576] = np.where(j <= i, 1.0, 0.0)
    cf[:64, 576:640] = np.where(j >= i, 1.0, 0.0)
    cf[:, 640:656] = np.arange(16, dtype=np.float32)[None, :]
    jj = np.arange(128)[:, None]; ii = np.arange(128)[None, :]
    cf[:, 656:784] = np.where(jj >= ii, 1.0, 0.0)
    cf[:, 784:912] = np.where(jj <= ii, 1.0, 0.0)
    t = np.arange(cfg.L)
    r = (t // GRID_W).astype(np.float32); c = (t % GRID_W).astype(np.float32)
    inv = (10000.0 ** (-np.arange(0, 64, 2, dtype=np.float32) / 64.0)).astype(np.float32)
    ang_r = r[None, :] * inv[:, None]; ang_c = c[None, :] * inv[:, None]
    d = np.arange(128)
    ang = np.where((d < 64)[:, None], ang_r[d % 32], ang_c[d % 32]).astype(np.float32)
    _, sign = rope_perm()
    cos = np.cos(ang).astype(np.float32); sins = (np.sin(ang).astype(np.float32) * sign[:, None])
    return cf, np.ascontiguousarray(cos), np.ascontiguousarray(sins)


def prep_inputs(inp, cfg):
    f = lambda a: np.ascontiguousarray(np.asarray(a, dtype=np.float32))
    x = f(inp['x']); c = f(inp['c']); ctx = f(inp['ctx']); c_ctx = f(inp['c_ctx'])
    w_ada = f(inp['w_ada']); b_ada = f(inp['b_ada']); w_in = f(inp['w_in'])
    NL_ = cfg.DEPTH
    cf, cos, sins = const_tables(cfg)
    partner, _ = rope_perm()
    wa = np.stack([_blk_moving(w_ada[l]) for l in range(NL_)])
    ba = np.ascontiguousarray(b_ada[:NL_].reshape(NL_, 24, 1, 512))
    wf_l, wt_l = [], []
    for l in range(NL_):
        w = w_in[l]
        sq = w[:, 3104:4128].reshape(D, 8, 128); sk = w[:, 4128:4384].reshape(D, 2, 128)
        lr = np.zeros((D, 128), np.float32); lr[:, :32] = w[:, 3072:3104]
        fm = np.concatenate([w[:, 0:1536], w[:, 1536:2048], sq.reshape(D, 1024), sq[:, :, partner].reshape(D, 1024),
                             sk.reshape(D, 256), sk[:, :, partner].reshape(D, 256), lr], axis=1)
        wf_l.append(_blk_stationary(fm))
        tm = np.concatenate([w[:, 1792:2048], w[:, 2048:2560], w[:, 4384:4640], w[:, 2560:3072]], axis=1)
        wt_l.append(_blk_moving(tm))
    wf = np.stack(wf_l); wt = np.stack(wt_l)
    conv_w = f(inp['conv_w'])[:NL_]
    cw = np.ascontiguousarray(conv_w.reshape(NL_, 3, 4, 128).transpose(0, 3, 2, 1))
    w2 = f(inp['gla_w2'])[:NL_]; b2 = f(inp['gla_b2'])[:NL_]
    w2a = np.zeros((NL_, 33, 512), np.float32)
    w2a[:, 0:16, 0:256] = w2[:, 0]; w2a[:, 16:32, 256:512] = w2[:, 1]
    w2a[:, 32, 0:256] = b2[:, 0]; w2a[:, 32, 256:512] = b2[:, 1]
    gnorm = np.ascontiguousarray(np.broadcast_to(f(inp['gla_norm'])[:NL_, None, :], (NL_, 128, 128)))
    sink = np.ascontiguousarray(np.broadcast_to(f(inp['swa_sink'])[:NL_, None, :], (NL_, 128, 8)))
    w_out = f(inp['w_out'])[:NL_]
    wo = np.ascontiguousarray(w_out.reshape(NL_, 16, 128, D).transpose(0, 2, 1, 3))
    lng = np.ascontiguousarray(np.broadcast_to(f(inp['ln_g'])[:NL_, :, None, :], (NL_, 2, 128, D)))
    lnb = np.ascontiguousarray(np.broadcast_to(f(inp['ln_b'])[:NL_, :, None, :], (NL_, 2, 128, D)))
    wq = np.stack([_blk_stationary(f(inp['peer_wq'])[l]) for l in range(NL_)])
    sk_ = f(inp['peer_subkeys'])[:NL_]
    skt = np.ascontiguousarray(sk_.transpose(0, 3, 1, 2))
    NK = cfg.NK
    pu = f(inp['peer_u'])[:NL_]
    ut = np.ascontiguousarray(pu.reshape(NL_, NK, NK, 16, 128).transpose(0, 1, 4, 3, 2))
    pv = f(inp['peer_v'])[:NL_]
    maps = []
    for b in range(x.shape[0]):
        cc = np.stack([c[b].reshape(16, 128).T, c_ctx.reshape(16, 128).T])
        xs = np.concatenate([ctx[b], x[b]], axis=0).reshape(cfg.TT, 128, D)
        maps.append(dict(xs_in=np.ascontiguousarray(xs), cc=np.ascontiguousarray(cc), wa=wa, ba=ba, wf=wf, wt=wt,
                         cw=cw, w2a=w2a, gnorm=gnorm, sink=sink, wo=wo, lng=lng, lnb=lnb, wq=wq, skt=skt,
                         ut=ut, pv=pv, cf=cf, cos=cos, sins=sins))
    return maps


class Prog:
    def __init__(self, cfg):
        self.cfg = cfg
        self.nc = bass.Bass("TRN2", target_bir_lowering=False)
        self.dbg_outs = []

    def din(self, name, shape, dt=F32):
        return self.nc.dram_tensor(name, list(shape), dt, kind="ExternalInput").ap()

    def dscr(self, name, shape, dt=F32, dbg=False):
        if dbg and self.cfg.dbg:
            self.dbg_outs.append(name)
            return self.nc.dram_tensor(name, list(shape), dt, kind="ExternalOutput").ap()
        return self.nc.dram_tensor(name, list(shape), dt).ap()

    def sb(self, es, name, shape, dt):
        self.uid = getattr(self, 'uid', 0) + 1
        return es.enter_context(self.nc.sbuf_tensor("%s_u%d" % (name, self.uid), list(shape), dt))

    def dma(self, q, out, in_, r=(), w=()):
        self.S.dma(q, lambda e, o=out, i=in_: e.dma_start(out=o, in_=i), reads=r, writes=w)

    def mm(self, out, lhsT, rhs, start, stop, r=(), w=()):
        self.S.op('pe', lambda e, o=out, l=lhsT, rh=rhs, s=start, t=stop: e.matmul(o, lhsT=l, rhs=rh, start=s, stop=t), reads=r, writes=w)

    def tp(self, out, in_, ident, r=(), w=()):
        self.S.op('pe', lambda e, o=out, i=in_, d=ident: e.transpose(out=o, in_=i, identity=d), reads=r, writes=w)

    def tt(self, eng, out, in0, in1, op, r=(), w=()):
        self.S.op(eng, lambda e, o=out, a=in0, b=in1, p=op: e.tensor_tensor(out=o, in0=a, in1=b, op=p), reads=r, writes=w)

    def ts(self, eng, out, in0, s1, s2, op0, op1=None, r=(), w=()):
        if op1 is None:
            self.S.op(eng, lambda e, o=out, a=in0, x=s1, p=op0: e.tensor_scalar(out=o, in0=a, scalar1=x, scalar2=None, op0=p), reads=r, writes=w)
        else:
            self.S.op(eng, lambda e, o=out, a=in0, x=s1, y=s2, p=op0, q=op1: e.tensor_scalar(out=o, in0=a, scalar1=x, scalar2=y, op0=p, op1=q), reads=r, writes=w)

    def stt(self, eng, out, in0, scalar, in1, op0, op1, r=(), w=()):
        self.S.op(eng, lambda e, o=out, a=in0, s=scalar, b=in1, p=op0, q=op1: e.scalar_tensor_tensor(out=o, in0=a, scalar=s, in1=b, op0=p, op1=q), reads=r, writes=w)

    def act(self, out, in_, func, r=(), w=(), bias=None, scale=None, accum=None):
        kw = {}
        if bias is not None: kw['bias'] = bias
        if scale is not None: kw['scale'] = scale
        if accum is not None: kw['accum_out'] = accum
        self.S.op('act', lambda e, o=out, i=in_, f=func, kw=kw: e.activation(out=o, in_=i, func=f, **kw), reads=r, writes=w)

    def cp(self, eng, out, in_, r=(), w=()):
        if eng == 'act':
            self.S.op('act', lambda e, o=out, i=in_: e.copy(out=o, in_=i), reads=r, writes=w)
        else:
            self.S.op(eng, lambda e, o=out, i=in_: e.tensor_copy(out=o, in_=i), reads=r, writes=w)

    def memset(self, eng, ap, val, w=()):
        self.S.op(eng, lambda e, a=ap, v=val: e.memset(a, v), writes=w)

    def build(self):
        cfg = self.cfg; nc = self.nc
        TT, NLY, NK = cfg.TT, cfg.DEPTH, cfg.NK
        self.xs_in = self.din("xs_in", [TT, 128, D])
        self.cc = self.din("cc", [2, 128, 16])
        self.wa = self.din("wa", [NLY, 24, 128, 16, 512]); self.ba = self.din("ba", [NLY, 24, 1, 512])
        self.wf = self.din("wf", [NLY, 37, 128, 16, 128]); self.wt = self.din("wt", [NLY, 3, 128, 16, 512])
        self.cw = self.din("cw", [NLY, 128, 4, 3]); self.w2a = self.din("w2a", [NLY, 33, 512])
        self.gnorm = self.din("gnorm", [NLY, 128, 128]); self.sink = self.din("sink", [NLY, 128, 8])
        self.wo = self.din("wo", [NLY, 128, 16, D])
        self.lng = self.din("lng", [NLY, 2, 128, D]); self.lnb = self.din("lnb", [NLY, 2, 128, D])
        self.wq = self.din("wq", [NLY, 16, 128, 16, 128]); self.skt = self.din("skt", [NLY, 128, 2, NK])
        self.ut = self.din("ut", [NLY, NK, 128, 16, NK]); self.pv = self.din("pv", [NLY, cfg.NE, D])
        self.cf = self.din("cf", [128, 912]); self.cos = self.din("cos", [128, cfg.L]); self.sins = self.din("sins", [128, cfg.L])
        self.out = nc.dram_tensor("out", [cfg.NL, 128, D], F32, kind="ExternalOutput").ap()
        self.XS = self.dscr("XS", [TT, 128, D], F32, dbg=True)
        self.MOD = self.dscr("MOD", [NLY, 2, 128, 6 * D], F32, dbg=True)
        self.HTB = self.dscr("HTB", [TT, 128, 16, 128], BF16)
        self.ZT = self.dscr("ZT", [4736, cfg.TOK], F32, dbg=True)
        self.ZV = self.dscr("ZV", [TT, 128, 1536], F32, dbg=True)
        self.QKT = self.dscr("QKT", [1280, cfg.TOK], BF16)
        self.MIXT = self.dscr("MIXT", [D, cfg.TOK], BF16, dbg=True)
        self.OF = self.dscr("OF", [cfg.TOK // 64, 64, 512], F32)
        self.UTB = self.dscr("UTB", [NLY, NK, 128, 16, NK], BF16)
        self.VB = self.dscr("VB", [NLY, cfg.NE, D], BF16)
        es = ExitStack()
        with es:
            self.S = Sched(nc, es)
            self.ps = [es.enter_context(nc.psum_tensor("ps%d" % i, [128, 512], F32)) for i in range(8)]
            self.consts(es)
            self.stage_prep()
            self.stage_adaln()
            for l in range(NLY):
                last = (l == NLY - 1)
                self.stage_modT(l, 0, range(TT))
                self.stage_inproj(l)
                self.stage_rope(l)
                self.stage_conv(l)
                self.stage_gla(l)
                self.stage_swa(l, last)
                tiles = range(cfg.NC, TT) if last else range(TT)
                self.stage_outproj(l, tiles)
                self.stage_modT(l, 1, tiles)
                self.stage_peer(l, tiles, last)
            self.S.barrier()
            with nc.Block() as block:
                self.S.emit(block)
        return nc

    def consts(self, es):
        self.cft = self.sb(es, "cft", [128, 912], F32)
        self.identb = self.sb(es, "identb", [128, 128], BF16)
        self.onesb = self.sb(es, "onesb", [128, 128], BF16)
        self.dma('sp', self.cft[:], self.cf, w=['cft'])
        self.cp('dve', self.identb[:], self.cft[:, 0:128], r=['cft'], w=['identb'])
        self.memset('dve', self.onesb[:], 1.0, w=['onesb'])
        c = self.cft
        self.identf = c[:, 0:128]; self.iota128 = c[:, 128:256]
        self.triT = [c[0:64, 256:320], c[0:64, 320:384]]
        self.amT = [c[0:64, 384:448], c[0:64, 448:512]]
        self.maskT = [c[0:64, 512:576], c[0:64, 576:640]]
        self.iota16 = c[:, 640:656]
        self.maskP = c[:, 656:784]; self.maskN = c[:, 784:912]
        self.S.barrier()

    def psb(self, i):
        return self.ps[i][:].bitcast(BF16)

    def stage_prep(self):
        cfg = self.cfg; NK = cfg.NK
        for l in range(cfg.DEPTH):
            src = self.ut[l].rearrange("i p k e -> (i p) (k e)"); dst = self.UTB[l].rearrange("i p k e -> (i p) (k e)")
            rows = NK * 128
            for r0 in range(0, rows, 1024):
                r1 = min(rows, r0 + 1024)
                self.dma('pool', dst[r0:r1, :], src[r0:r1, :])
            for r0 in range(0, cfg.NE, 1024):
                r1 = min(cfg.NE, r0 + 1024)
                self.dma('pool', self.VB[l][r0:r1, :], self.pv[l][r0:r1, :])
        self.S.barrier()

    def stage_adaln(self):
        cfg = self.cfg
        with ExitStack() as es:
            ccs = self.sb(es, "ccs", [128, 2, 16], F32); cs = self.sb(es, "cs", [128, 2, 16], F32)
            rep = self.sb(es, "rep", [128, 2, 16, 128], BF16)
            wat = [self.sb(es, "wat%d" % i, [128, 16, 512], BF16) for i in range(2)]
            bat = [self.sb(es, "bat%d" % i, [1, 512], BF16) for i in range(2)]
            mo = [self.sb(es, "mo%d" % i, [128, 512], F32) for i in range(4)]
            self.dma('sp', ccs[:], self.cc.rearrange("w p k -> p w k"), w=['ccs'])
            self.act(cs[:], ccs[:], AF.Silu, r=['ccs'], w=['cs'])
            for w in range(2):
                self.cp('dve', rep[:, w], cs[:, w, :].unsqueeze(2).broadcast_to([128, 16, 128]), r=['cs'], w=['rep'])
            n = 0
            for l in range(cfg.DEPTH):
                for cb in range(24):
                    b = (l * 24 + cb) % 2
                    self.dma('pool', wat[b][:], self.wa[l, cb], w=['wat%d' % b])
                    self.dma('pool', bat[b][:], self.ba[l, cb], w=['bat%d' % b])
                    for w in range(2):
                        pi = n % 8; m = n % 4; n += 1
                        for k in range(16):
                            self.mm(self.ps[pi][:], rep[:, w, k, :], wat[b][:, k, :], k == 0, False, r=['rep', 'wat%d' % b], w=['ps%d' % pi])
                        self.mm(self.ps[pi][:], self.onesb[0:1, :], bat[b][0:1, :], False, True, r=['onesb', 'bat%d' % b], w=['ps%d' % pi])
                        if cb // 4 in (1, 4):
                            self.ts('dve', mo[m][:], self.ps[pi][:], 1.0, None, ALU.add, r=['ps%d' % pi], w=['mo%d' % m])
                        else:
                            self.cp('act', mo[m][:], self.ps[pi][:], r=['ps%d' % pi], w=['mo%d' % m])
                        self.dma('sp', self.MOD[l, w, :, cb * 512:(cb + 1) * 512], mo[m][:], r=['mo%d' % m])
            self.S.barrier()

    def stage_modT(self, l, which, tiles):
        cfg = self.cfg
        with ExitStack() as es:
            xs = [self.sb(es, "mx%d" % i, [128, D], F32) for i in range(2)]
            tmp = self.sb(es, "mtmp", [128, D], F32)
            hb = [self.sb(es, "mhb%d" % i, [128, D], BF16) for i in range(2)]
            hts = [self.sb(es, "mhts%d" % i, [128, 16, 128], BF16) for i in range(2)]
            sct = [self.sb(es, "msc%d" % i, [128, D], F32) for i in range(2)]
            sht = [self.sb(es, "msh%d" % i, [128, D], F32) for i in range(2)]
            o_sh = 0 if which == 0 else 3 * D
            for w in range(2):
                self.dma('sp', sht[w][:], self.MOD[l, w, :, o_sh:o_sh + D], w=['msh%d' % w])
                self.dma('sp', sct[w][:], self.MOD[l, w, :, o_sh + D:o_sh + 2 * D], w=['msc%d' % w])
            src = self.xs_in if (l == 0 and which == 0) else self.XS
            for n, tt in enumerate(tiles):
                b = n % 2; w = 1 if tt < cfg.NC else 0
                self.dma('sp', xs[b][:], src[tt], w=['mx%d' % b])
                self.tt('dve', tmp[:], xs[b][:], sct[w][:], ALU.mult, r=['mx%d' % b, 'msc%d' % w], w=['mtmp'])
                self.tt('pool', hb[b][:], tmp[:], sht[w][:], ALU.add, r=['mtmp', 'msh%d' % w], w=['mhb%d' % b])
                for half in range(2):
                    pi = (n * 2 + half) % 8
                    pb = self.psb(pi)
                    for j in range(8):
                        k = half * 8 + j
                        self.tp(pb[:, j * 128:(j + 1) * 128], hb[b][:, k * 128:(k + 1) * 128], self.identb[:], r=['mhb%d' % b, 'identb'], w=['ps%d' % pi])
                    self.cp('act', hts[b][:, half * 8:(half + 1) * 8, :], pb.rearrange("p (k t) -> p k t", k=8), r=['ps%d' % pi], w=['mhts%d' % b])
                self.dma('pool', self.HTB[tt], hts[b][:], r=['mhts%d' % b])
            self.S.barrier()

    def stage_inproj(self, l):
        cfg = self.cfg
        with ExitStack() as es:
            hT = [self.sb(es, "ihT%d" % i, [128, 4, 16, 128], BF16) for i in range(2)]
            wtm = [self.sb(es, "iwtm%d" % i, [128, 16, 512], BF16) for i in range(3)]
            wst = [self.sb(es, "iwst%d" % i, [128, 16, 128], BF16) for i in range(3)]
            zo = [self.sb(es, "izo%d" % i, [128, 512], F32) for i in range(4)]
            for tb in range(3):
                self.dma('pool', wtm[tb][:], self.wt[l, tb], w=['iwtm%d' % tb])
            n = 0
            for mi, t0 in enumerate(range(0, cfg.TT, 4)):
                nt = min(4, cfg.TT - t0); hb = mi % 2
                for j in range(nt):
                    self.dma('sp', hT[hb][:, j], self.HTB[t0 + j], w=['ihT%d' % hb])
                for nb in range(37):
                    wb = nb % 3
                    self.dma('pool', wst[wb][:], self.wf[l, nb], w=['iwst%d' % wb])
                    pi = n % 8; zb = n % 4; n += 1
                    for k in range(16):
                        self.mm(self.ps[pi][:, 0:nt * 128].rearrange("p (j t) -> p j t", j=nt), wst[wb][:, k, :], hT[hb][:, 0:nt, k, :],
                                k == 0, k == 15, r=['iwst%d' % wb, 'ihT%d' % hb], w=['ps%d' % pi])
                    self.cp('act' if n % 2 else 'dve', zo[zb][:, 0:nt * 128], self.ps[pi][:, 0:nt * 128], r=['ps%d' % pi], w=['izo%d' % zb])
                    self.dma('sp', self.ZT[nb * 128:(nb + 1) * 128, t0 * 128:(t0 + nt) * 128], zo[zb][:, 0:nt * 128], r=['izo%d' % zb])
                for j in range(nt):
                    for tb in range(3):
                        pi = n % 8; zb = n % 4; n += 1
                        for k in range(16):
                            self.mm(self.ps[pi][:], hT[hb][:, j, k, :], wtm[tb][:, k, :], k == 0, k == 15, r=['ihT%d' % hb, 'iwtm%d' % tb], w=['ps%d' % pi])
                        self.cp('act' if n % 2 else 'dve', zo[zb][:], self.ps[pi][:], r=['ps%d' % pi], w=['izo%d' % zb])
                        self.dma('sp', self.ZV[t0 + j][:, tb * 512:(tb + 1) * 512], zo[zb][:], r=['izo%d' % zb])
            self.S.barrier()

    def stage_rope(self, l):
        cfg = self.cfg; LC = cfg.LC
        rows = [(2048 + h * 128, 3072 + h * 128, h * 128) for h in range(8)] + [(4096 + g * 128, 4352 + g * 128, 1024 + g * 128) for g in range(2)]
        with ExitStack() as es:
            ta = [self.sb(es, "rta%d" % i, [128, 512], F32) for i in range(2)]
            tb = [self.sb(es, "rtb%d" % i, [128, 512], F32) for i in range(2)]
            ob = [self.sb(es, "rob%d" % i, [128, 512], BF16) for i in range(2)]
            ct = self.sb(es, "rct", [128, 512], F32); st = self.sb(es, "rst", [128, 512], F32)
            n = 0
            for c0 in range(0, LC, 512):
                cn = min(512, LC - c0)
                for (ra, rp, ro) in rows:
                    b = n % 2; n += 1
                    self.dma('sp', ta[b][:, 0:cn], self.ZT[ra:ra + 128, c0:c0 + cn], w=['rta%d' % b])
                    self.cp('act', ob[b][:, 0:cn], ta[b][:, 0:cn], r=['rta%d' % b], w=['rob%d' % b])
                    self.dma('pool', self.QKT[ro:ro + 128, c0:c0 + cn], ob[b][:, 0:cn], r=['rob%d' % b])
            for s0 in range(0, cfg.L, 512):
                self.dma('sp', ct[:], self.cos[:, s0:s0 + 512], w=['rct'])
                self.dma('sp', st[:], self.sins[:, s0:s0 + 512], w=['rst'])
                for (ra, rp, ro) in rows:
                    b = n % 2; n += 1
                    self.dma('sp', ta[b][:], self.ZT[ra:ra + 128, LC + s0:LC + s0 + 512], w=['rta%d' % b])
                    self.dma('sp', tb[b][:], self.ZT[rp:rp + 128, LC + s0:LC + s0 + 512], w=['rtb%d' % b])
                    self.tt('dve', ta[b][:], ta[b][:], ct[:], ALU.mult, r=['rta%d' % b, 'rct'], w=['rta%d' % b])
                    self.tt('pool', tb[b][:], tb[b][:], st[:], ALU.mult, r=['rtb%d' % b, 'rst'], w=['rtb%d' % b])
                    self.tt('dve', ob[b][:], ta[b][:], tb[b][:], ALU.add, r=['rta%d' % b, 'rtb%d' % b], w=['rob%d' % b])
                    self.dma('pool', self.QKT[ro:ro + 128, LC + s0:LC + s0 + 512], ob[b][:], r=['rob%d' % b])
            self.S.barrier()

    def stage_conv(self, l):
        cfg = self.cfg; SEG = 1024
        with ExitStack() as es:
            xi = [self.sb(es, "cxi%d" % i, [128, SEG + 2], F32) for i in range(2)]
            cg = [self.sb(es, "ccg%d" % i, [128, SEG + 2], F32) for i in range(2)]
            bb = [self.sb(es, "cbb%d" % i, [128, SEG], F32) for i in range(2)]
            u = [self.sb(es, "cu%d" % i, [128, SEG + 2], F32) for i in range(2)]
            acc = [self.sb(es, "cacc%d" % i, [128, SEG], F32) for i in range(2)]
            ob = [self.sb(es, "cob%d" % i, [128, SEG], BF16) for i in range(2)]
            cwt = self.sb(es, "ccw", [128, 4, 3], F32)
            self.dma('sp', cwt[:], self.cw[l], w=['ccw'])
            n = 0
            for (q0, q1) in ((0, cfg.LC), (cfg.LC, cfg.TOK)):
                for s0 in range(q0, q1, SEG):
                    sn = min(SEG, q1 - s0)
                    lo = max(q0, s0 - 1); hi = min(q1, s0 + sn + 1); off = lo - (s0 - 1); ln = hi - lo
                    for cc in range(4):
                        b = n % 2; n += 1
                        kx, kc, kb, ku, ka, ko = 'cxi%d' % b, 'ccg%d' % b, 'cbb%d' % b, 'cu%d' % b, 'cacc%d' % b, 'cob%d' % b
                        self.dma('sp', xi[b][:, off:off + ln], self.ZT[cc * 128:(cc + 1) * 128, lo:hi], w=[kx])
                        self.dma('sp', cg[b][:, off:off + ln], self.ZT[1024 + cc * 128:1024 + (cc + 1) * 128, lo:hi], w=[kc])
                        self.dma('sp', bb[b][:, 0:sn], self.ZT[512 + cc * 128:512 + (cc + 1) * 128, s0:s0 + sn], w=[kb])
                        if off > 0:
                            self.memset('pool', u[b][:, 0:1], 0.0, w=[ku])
                        if off + ln < sn + 2:
                            self.memset('pool', u[b][:, sn + 1:sn + 2], 0.0, w=[ku])
                        self.tt('pool', u[b][:, off:off + ln], cg[b][:, off:off + ln], xi[b][:, off:off + ln], ALU.mult, r=[kx, kc], w=[ku])
                        self.ts('dve', acc[b][:, 0:sn], u[b][:, 0:sn], cwt[:, cc, 0:1], None, ALU.mult, r=[ku, 'ccw'], w=[ka])
                        self.stt('dve', acc[b][:, 0:sn], u[b][:, 1:sn + 1], cwt[:, cc, 1:2], acc[b][:, 0:sn], ALU.mult, ALU.add, r=[ku, 'ccw', ka], w=[ka])
                        self.stt('dve', acc[b][:, 0:sn], u[b][:, 2:sn + 2], cwt[:, cc, 2:3], acc[b][:, 0:sn], ALU.mult, ALU.add, r=[ku, 'ccw', ka], w=[ka])
                        self.tt('pool', ob[b][:, 0:sn], acc[b][:, 0:sn], bb[b][:, 0:sn], ALU.mult, r=[ka, kb], w=[ko])
                        self.dma('pool', self.MIXT[cc * 128:(cc + 1) * 128, s0:s0 + sn], ob[b][:, 0:sn], r=[ko])
            self.S.barrier()

    def stage_gla(self, l):
        cfg = self.cfg
        NCHC = cfg.LC // 64; NCH = cfg.TOK // 64
        ps = self.ps
        with ExitStack() as es:
            sb = lambda n, s, d: self.sb(es, n, s, d)
            St = sb("gS", [64, 512], F32); Sbf = sb("gSbf", [64, 512], BF16)
            w2t = sb("gw2", [33, 512], F32); NW = sb("gNW", [64, 128], F32)
            B2 = range(2)
            qT = [sb("gqT%d" % i, [64, 4, 64], F32) for i in B2]; kT = [sb("gkT%d" % i, [64, 4, 64], F32) for i in B2]
            lrT = [sb("glr%d" % i, [33, 64], F32) for i in B2]
            kTok = [sb("gkk%d" % i, [64, 256], F32) for i in B2]; vTok = [sb("gvv%d" % i, [64, 512], F32) for i in B2]
            gTok = [sb("ggg%d" % i, [64, 512], F32) for i in B2]; ofl = [sb("gof%d" % i, [64, 512], F32) for i in B2]
            vbf = [sb("gvb%d" % i, [64, 512], BF16) for i in B2]
            lsb = [sb("gls%d" % i, [64, 256], F32) for i in B2]
            Eb = [sb("gEb%d" % i, [64, 256], F32) for i in B2]; Enb = [sb("gEn%d" % i, [64, 256], F32) for i in B2]
            Ek = [sb("gEk%d" % i, [64, 256], F32) for i in B2]
            qe = [sb("gqe%d" % i, [64, 4, 64], BF16) for i in B2]; ke = [sb("gke%d" % i, [64, 4, 64], BF16) for i in B2]
            kend = [sb("gkd%d" % i, [64, 256], BF16) for i in B2]; attT = [sb("gat%d" % i, [64, 4, 64], BF16) for i in B2]
            sq = sb("gsq", [64, 4, 128], F32); ss = sb("gss", [64, 4], F32); rstd = sb("grs", [64, 4], F32)
            on = sb("gon", [64, 512], F32); sg = sb("gsg", [64, 512], F32); res = sb("gres", [64, 512], BF16)
            mo = [sb("gmo%d" % i, [128, 4, 64], BF16) for i in B2]
            self.dma('sp', w2t[:], self.w2a[l], w=['gw2'])
            self.dma('sp', NW[:], self.gnorm[l][0:64, :], w=['gNW'])
            for i in B2:
                self.memset('pool', lrT[i][:], 1.0, w=['glr%d' % i])
            orders = [list(range(NCH)), list(range(NCHC - 1, -1, -1)) + list(range(NCH - 1, NCHC - 1, -1))]
            for d in range(2):
                self.memset('dve', St[:], 0.0, w=['gS'])
                self.memset('dve', Sbf[:], 0.0, w=['gSbf'])
                for n, ci in enumerate(orders[d]):
                    b = n % 2; c0 = ci * 64; tile = ci // 2; hf = ci % 2
                    K = lambda s: s + str(b)
                    self.dma('sp', qT[b][:], self.ZT[1536:1792, c0:c0 + 64].rearrange("(h k) t -> k h t", h=4), w=[K('gqT')])
                    self.dma('sp', kT[b][:], self.ZT[1792:2048, c0:c0 + 64].rearrange("(h k) t -> k h t", h=4), w=[K('gkT')])
                    self.dma('sp', lrT[b][0:32, :], self.ZT[4608:4640, c0:c0 + 64], w=[K('glr')])
                    zv = self.ZV[tile]
                    self.dma('sp', kTok[b][:], zv[hf * 64:(hf + 1) * 64, 0:256], w=[K('gkk')])
                    self.dma('sp', vTok[b][:], zv[hf * 64:(hf + 1) * 64, 256:768], w=[K('gvv')])
                    if d == 1:
                        self.dma('sp', gTok[b][:], zv[hf * 64:(hf + 1) * 64, 1024:1536], w=[K('ggg')])
                        self.dma('sp', ofl[b][:], self.OF[ci], w=[K('gof')])
                    self.mm(ps[0][0:64, 0:256], lrT[b][0:33, :], w2t[0:33, d * 256:(d + 1) * 256], True, True, r=[K('glr'), 'gw2'], w=['ps0'])
                    self.act(lsb[b][:], ps[0][0:64, 0:256], AF.Exp, scale=-1.0, r=['ps0'], w=[K('gls')])
                    self.act(lsb[b][:], lsb[b][:], AF.Ln, bias=1.0, r=[K('gls')], w=[K('gls')])
                    for h in range(4):
                        self.mm(ps[1][0:64, h * 64:(h + 1) * 64], lsb[b][:, h * 64:(h + 1) * 64], self.triT[d], True, True, r=[K('gls'), 'cft'], w=['ps1'])
                    self.mm(ps[2][0:64, 0:256], self.amT[d], lsb[b][:, 0:256], True, True, r=[K('gls'), 'cft'], w=['ps2'])
                    self.act(Eb[b][:], ps[1][0:64, 0:256], AF.Exp, r=['ps1'], w=[K('gEb')])
                    self.act(Enb[b][:], ps[1][0:64, 0:256], AF.Exp, scale=-1.0, r=['ps1'], w=[K('gEn')])
                    self.act(Ek[b][:], ps[2][0:64, 0:256], AF.Exp, r=['ps2'], w=[K('gEk')])
                    self.stt('dve', qe[b][:], qT[b][:], 0.125, Eb[b][:].rearrange("p (h t) -> p h t", h=4), ALU.mult, ALU.mult, r=[K('gqT'), K('gEb')], w=[K('gqe')])
                    self.tt('dve', ke[b][:], kT[b][:], Enb[b][:].rearrange("p (h t) -> p h t", h=4), ALU.mult, r=[K('gkT'), K('gEn')], w=[K('gke')])
                    self.tt('pool', kend[b][:], kTok[b][:], Ek[b][:], ALU.mult, r=[K('gkk'), K('gEk')], w=[K('gkd')])
                    self.cp('pool', vbf[b][:], vTok[b][:], r=[K('gvv')], w=[K('gvb')])
                    for h in range(4):
                        self.mm(ps[3][0:64, h * 64:(h + 1) * 64], ke[b][:, h, :], qe[b][:, h, :], True, True, r=[K('gke'), K('gqe')], w=['ps3'])
                    self.tt('dve', attT[b][:], ps[3][0:64, 0:256].rearrange("p (h t) -> p h t", h=4),
                            self.maskT[d].unsqueeze(1).broadcast_to([64, 4, 64]), ALU.mult, r=['ps3', 'cft'], w=[K('gat')])
                    for h in range(4):
                        self.mm(ps[4][0:64, h * 128:(h + 1) * 128], attT[b][:, h, :], vbf[b][:, h * 128:(h + 1) * 128], True, False, r=[K('gat'), K('gvb')], w=['ps4'])
                        self.mm(ps[4][0:64, h * 128:(h + 1) * 128], qe[b][:, h, :], Sbf[:, h * 128:(h + 1) * 128], False, True, r=[K('gqe'), 'gSbf'], w=['ps4'])
                    for h in range(4):
                        self.mm(ps[5][0:64, h * 128:(h + 1) * 128], kend[b][:, h * 64:(h + 1) * 64], vbf[b][:, h * 128:(h + 1) * 128], True, True, r=[K('gkd'), K('gvb')], w=['ps5'])
                    col = 63 if d == 0 else 0
                    for h in range(4):
                        self.stt('dve', St[:, h * 128:(h + 1) * 128], St[:, h * 128:(h + 1) * 128], Eb[b][:, h * 64 + col:h * 64 + col + 1],
                                 ps[5][0:64, h * 128:(h + 1) * 128], ALU.mult, ALU.add, r=['gS', K('gEb'), 'ps5'], w=['gS'])
                    self.cp('pool', Sbf[:], St[:], r=['gS'], w=['gSbf'])
                    if d == 0:
                        self.cp('act', ofl[b][:], ps[4][0:64, :], r=['ps4'], w=[K('gof')])
                        self.dma('pool', self.OF[ci], ofl[b][:], r=[K('gof')])
                    else:
                        self.tt('dve', ofl[b][:], ps[4][0:64, :], ofl[b][:], ALU.add, r=['ps4', K('gof')], w=[K('gof')])
                        o3 = ofl[b][:].rearrange("p (h v) -> p h v", h=4)
                        self.tt('dve', sq[:], o3, o3, ALU.mult, r=[K('gof')], w=['gsq'])
                        self.S.op('dve', lambda e, o=ss[:], i=sq[:]: e.tensor_reduce(out=o, in_=i, axis=AX.X, op=ALU.add), reads=['gsq'], writes=['gss'])
                        self.ts('dve', ss[:], ss[:], 1.0 / 128.0, LN_EPS, ALU.mult, ALU.add, r=['gss'], w=['gss'])
                        self.act(rstd[:], ss[:], AF.Ln, r=['gss'], w=['grs'])
                        self.act(rstd[:], rstd[:], AF.Exp, scale=-0.5, r=['grs'], w=['grs'])
                        for h in range(4):
                            self.stt('dve', on[:, h * 128:(h + 1) * 128], ofl[b][:, h * 128:(h + 1) * 128], rstd[:, h:h + 1], NW[:], ALU.mult, ALU.mult,
                                     r=[K('gof'), 'grs', 'gNW'], w=['gon'])
                        self.act(sg[:], gTok[b][:], AF.Silu, r=[K('ggg')], w=['gsg'])
                        self.tt('pool', res[:], on[:], sg[:], ALU.mult, r=['gon', 'gsg'], w=['gres'])
                        pb = self.psb(6)
                        for h in range(4):
                            self.tp(pb[:, h * 64:(h + 1) * 64], res[:, h * 128:(h + 1) * 128], self.identb[0:64, 0:64], r=['gres', 'identb'], w=['ps6'])
                        self.cp('act', mo[b][:], pb[:, 0:256].rearrange("p (h t) -> p h t", h=4), r=['ps6'], w=[K('gmo')])
                        self.dma('pool', self.MIXT[512:1024, c0:c0 + 64].rearrange("(h v) t -> v h t", h=4), mo[b][:], r=[K('gmo')])
                self.S.barrier()

    def stage_swa(self, l, last):
        cfg = self.cfg; NC, TT = cfg.NC, cfg.TT
        ps = self.ps
        scale = 128.0 ** -0.5
        with ExitStack() as es:
            sb = lambda n, s, d: self.sb(es, n, s, d)
            KT = sb("sKT", [128, cfg.TOK], BF16); VT = sb("sVT", [128, TT, 128], BF16)
            sk = sb("ssk", [128, 8], F32); sinkE = sb("ssinkE", [128, 8], F32)
            q4 = [sb("sq4%d" % i, [128, 4, 128], BF16) for i in range(2)]
            pT = [sb("spT%d" % i, [128, 4, 128], BF16) for i in range(3)]
            dn = sb("sdn", [128, 4, 128], F32); ob = [sb("sob%d" % i, [128, 4, 128], BF16) for i in range(2)]
            self.dma('sp', sk[:], self.sink[l], w=['ssk'])
            self.act(sinkE[:], sk[:], AF.Exp, r=['ssk'], w=['ssinkE'])
            n = 0; m = 0
            for g in range(2):
                self.dma('sp', KT[:], self.QKT[1024 + g * 128:1024 + (g + 1) * 128, :], w=['sKT'])
                for t0 in range(0, TT, 16):
                    t1 = min(TT, t0 + 16)
                    self.dma('pool', VT[:, t0:t1, :], self.ZV[t0:t1, :, 768 + g * 128:768 + (g + 1) * 128].rearrange("t p c -> p t c"), w=['sVT'])
                for qt in (range(NC, TT) if last else range(TT)):
                    if qt < NC:
                        keys = [(kt, None) for kt in range(NC)]
                    else:
                        keys = []
                        if qt - 1 >= NC: keys.append((qt - 1, self.maskP))
                        keys.append((qt, None))
                        if qt + 1 < TT: keys.append((qt + 1, self.maskN))
                        keys += [(kt, None) for kt in range(NC)]
                    b = n % 2; n += 1
                    po = 3 + 2 * b; pd = 4 + 2 * b
                    self.dma('sp', q4[b][:], self.QKT[g * 512:(g + 1) * 512, qt * 128:(qt + 1) * 128].rearrange("(j d) t -> d j t", j=4), w=['sq4%d' % b])
                    for ki, (kt, mask) in enumerate(keys):
                        pi = m % 3; m += 1
                        first = ki == 0; lastk = ki == len(keys) - 1
                        self.mm(ps[pi][:].rearrange("p (j t) -> p j t", j=4), KT[:, kt * 128:(kt + 1) * 128], q4[b][:], True, True, r=['sKT', 'sq4%d' % b], w=['ps%d' % pi])
                        self.act(pT[pi][:], ps[pi][:].rearrange("p (j t) -> p j t", j=4), AF.Exp, scale=scale, r=['ps%d' % pi], w=['spT%d' % pi])
                        if mask is not None:
                            self.tt('pool', pT[pi][:], pT[pi][:], mask.unsqueeze(1).broadcast_to([128, 4, 128]), ALU.mult, r=['spT%d' % pi, 'cft'], w=['spT%d' % pi])
                        self.mm(ps[po][:].rearrange("p (j t) -> p j t", j=4), VT[:, kt, :], pT[pi][:], first, lastk, r=['sVT', 'spT%d' % pi], w=['ps%d' % po])
                        self.mm(ps[pd][:].rearrange("p (j t) -> p j t", j=4), self.onesb[:], pT[pi][:], first, lastk, r=['onesb', 'spT%d' % pi], w=['ps%d' % pd])
                    self.tt('dve', dn[:], ps[pd][:].rearrange("p (j t) -> p j t", j=4), sinkE[:, g * 4:(g + 1) * 4].unsqueeze(2).broadcast_to([128, 4, 128]),
                            ALU.add, r=['ps%d' % pd, 'ssinkE'], w=['sdn'])
                    self.S.op('dve', lambda e, o=dn[:]: e.reciprocal(out=o, in_=o), reads=['sdn'], writes=['sdn'])
                    self.tt('dve', ob[b][:], ps[po][:].rearrange("p (j t) -> p j t", j=4), dn[:], ALU.mult, r=['ps%d' % po, 'sdn'], w=['sob%d' % b])
                    self.dma('pool', self.MIXT[1024 + g * 512:1024 + (g + 1) * 512, qt * 128:(qt + 1) * 128].rearrange("(j d) t -> d j t", j=4), ob[b][:], r=['sob%d' % b])
            self.S.barrier()

    def resid_ln(self, bufs, xsrc, banks, Gt, gkey, lngt, lnbt, dsts):
        xt, tq, st, mv, rs = bufs
        ps = self.ps
        self.dma('sp', xt[:], xsrc, w=['lx'])
        for nb in range(4):
            self.tt('dve', tq[:, nb * 512:(nb + 1) * 512], ps[banks[nb]][:], Gt[:, nb * 512:(nb + 1) * 512], ALU.mult, r=['ps%d' % banks[nb], gkey], w=['lt'])
        self.stt('dve', tq[:], xt[:], ALPHA, tq[:], ALU.mult, ALU.add, r=['lx', 'lt'], w=['lt'])
        for nb in range(4):
            self.S.op('dve', lambda e, o=st[:, nb, :], i=tq[:, nb * 512:(nb + 1) * 512]: e.bn_stats(out=o, in_=i), reads=['lt'], writes=['lst'])
        self.S.op('dve', lambda e, o=mv[:], i=st[:]: e.bn_aggr(out=o, in_=i), reads=['lst'], writes=['lmv'])
        self.act(rs[:], mv[:, 1:2], AF.Ln, bias=LN_EPS, r=['lmv'], w=['lrs'])
        self.act(rs[:], rs[:], AF.Exp, scale=-0.5, r=['lrs'], w=['lrs'])
        self.ts('dve', xt[:], tq[:], mv[:, 0:1], rs[:, 0:1], ALU.subtract, ALU.mult, r=['lt', 'lmv', 'lrs'], w=['lx'])
        self.tt('pool', xt[:], xt[:], lngt[:], ALU.mult, r=['lx', 'lng'], w=['lx'])
        self.tt('dve', xt[:], xt[:], lnbt[:], ALU.add, r=['lx', 'lnb'], w=['lx'])
        for dq, dst in zip(('sp', 'pool'), dsts):
            self.dma(dq, dst, xt[:], r=['lx'])

    def ln_bufs(self, es):
        sb = lambda n, s, d: self.sb(es, n, s, d)
        return (sb("lx", [128, D], F32), sb("lt", [128, D], F32), sb("lst", [128, 4, 6], F32), sb("lmv", [128, 2], F32), sb("lrs", [128, 1], F32))

    def stage_outproj(self, l, tiles):
        cfg = self.cfg
        with ExitStack() as es:
            sb = lambda n, s, d: self.sb(es, n, s, d)
            wot = sb("owo", [128, 16, D], BF16)
            lngt = sb("lng", [128, D], F32); lnbt = sb("lnb", [128, D], F32)
            Gt = [sb("oG%d" % i, [128, D], F32) for i in range(2)]
            mixT = [sb("omx%d" % i, [128, 16, 128], BF16) for i in range(2)]
            bufs = self.ln_bufs(es)
            for nb in range(4):
                self.dma('pool', wot[:, :, nb * 512:(nb + 1) * 512], self.wo[l][:, :, nb * 512:(nb + 1) * 512], w=['owo'])
            self.dma('sp', lngt[:], self.lng[l, 0], w=['lng']); self.dma('sp', lnbt[:], self.lnb[l, 0], w=['lnb'])
            for w in range(2):
                self.dma('sp', Gt[w][:], self.MOD[l, w, :, 2 * D:3 * D], w=['oG%d' % w])
            src = self.xs_in if l == 0 else self.XS
            for n, tt in enumerate(tiles):
                b = n % 2; w = 1 if tt < cfg.NC else 0
                self.dma('sp', mixT[b][:], self.MIXT[:, tt * 128:(tt + 1) * 128].rearrange("(k p) t -> p k t", p=128), w=['omx%d' % b])
                banks = [4 * b + i for i in range(4)]
                for nb in range(4):
                    for k in range(16):
                        self.mm(self.ps[banks[nb]][:], mixT[b][:, k, :], wot[:, k, nb * 512:(nb + 1) * 512], k == 0, k == 15, r=['omx%d' % b, 'owo'], w=['ps%d' % banks[nb]])
                self.resid_ln(bufs, src[tt], banks, Gt[w], 'oG%d' % w, lngt, lnbt, [self.XS[tt]])
            self.S.barrier()

    def stage_peer(self, l, tiles, last):
        cfg = self.cfg; NK = cfg.NK; NC = cfg.NC
        ps = self.ps
        tiles = list(tiles)
        pairs = [tiles[i:i + 2] for i in range(0, len(tiles), 2)]
        if not hasattr(self, 'RT'):
            self.RT = self.dscr("RT", [cfg.TT, 128, 3, 128], F32, dbg=True)
        with ExitStack() as es:
            sb = lambda n, s, d: self.sb(es, n, s, d)
            h2T = [sb("ph2T%d" % i, [128, 2, 16, 128], BF16) for i in range(2)]
            qTb = sb("pqTb", [128, 16, 256], BF16)
            wqs = [sb("pwq%d" % i, [128, 16, 128], BF16) for i in range(3)]
            sktt = sb("pskt", [128, 2, NK], BF16)
            ssb = sb("pssb", [128, 16, NK], F32); wk = sb("pwk", [128, 16, NK], F32)
            m8 = sb("pm8", [128, 16, 16], F32); i8 = sb("pi8", [128, 16, 16], U32); tif = sb("ptif", [128, 16, 16], F32)
            cand = sb("pcand", [128, 8, 256], F32); wk2 = sb("pwk2", [128, 8, 256], F32)
            b8 = sb("pb8", [128, 8, 16], F32); f8 = sb("pf8", [128, 8, 16], U32); ff = sb("pff", [128, 8, 16], F32)
            fa = sb("pfa", [128, 8, 16], F32); fb = sb("pfb", [128, 8, 16], F32)
            fau = sb("pfau", [128, 8, 16], U32); fbu = sb("pfbu", [128, 8, 16], U32)
            eq = sb("peq", [128, 8, 16, 16], F32)
            rt = [sb("prt%d" % i, [128, 3, 128], F32) for i in range(2)]
            ez = sb("pez", [128, 8, 16], F32); zz = sb("pzz", [128, 8], F32)
            self.dma('pool', sktt[:], self.skt[l], w=['pskt'])
            n = 0
            for pi_, pr in enumerate(pairs):
                hb = pi_ % 2
                for j, tt in enumerate(pr):
                    self.dma('sp', h2T[hb][:, j], self.HTB[tt], w=['ph2T%d' % hb])
                for blk in range(16):
                    wb = blk % 3
                    self.dma('pool', wqs[wb][:], self.wq[l, blk], w=['pwq%d' % wb])
                    pi = n % 8; n += 1
                    for k in range(16):
                        self.mm(ps[pi][:, 0:256].rearrange("p (j t) -> p j t", j=2), wqs[wb][:, k, :], h2T[hb][:, :, k, :], k == 0, k == 15,
                                r=['pwq%d' % wb, 'ph2T%d' % hb], w=['ps%d' % pi])
                    self.cp('act' if blk % 2 else 'dve', qTb[:, blk, :], ps[pi][:, 0:256], r=['ps%d' % pi], w=['pqTb'])
                for j, tt in enumerate(pr):
                    rb = j
                    for hp in range(16):
                        bank = hp // 4
                        self.mm(ps[bank][:, (hp % 4) * NK:(hp % 4 + 1) * NK], qTb[:, hp, j * 128:(j + 1) * 128], sktt[:, hp % 2, :], True, True,
                                r=['pqTb', 'pskt'], w=['ps%d' % bank])
                    for bank in range(4):
                        self.cp('act', ssb[:, bank * 4:(bank + 1) * 4, :], ps[bank][:, 0:4 * NK].rearrange("p (a n) -> p a n", a=4), r=['ps%d' % bank], w=['pssb'])
                    V = self.S.op
                    for hp in range(16):
                        V('dve', lambda e, o=m8[:, hp, 0:8], i=ssb[:, hp, :]: e.max(out=o, in_=i), reads=['pssb'], writes=['pm8'])
                        V('dve', lambda e, o=i8[:, hp, 0:8], a=m8[:, hp, 0:8], i=ssb[:, hp, :]: e.max_index(out=o, in_max=a, in_values=i), reads=['pssb', 'pm8'], writes=['pi8'])
                        V('dve', lambda e, o=wk[:, hp, :], a=m8[:, hp, 0:8], i=ssb[:, hp, :]: e.match_replace(out=o, in_to_replace=a, in_values=i, imm_value=-1e30), reads=['pssb', 'pm8'], writes=['pwk'])
                        V('dve', lambda e, o=m8[:, hp, 8:16], i=wk[:, hp, :]: e.max(out=o, in_=i), reads=['pwk'], writes=['pm8'])
                        V('dve', lambda e, o=i8[:, hp, 8:16], a=m8[:, hp, 8:16], i=wk[:, hp, :]: e.max_index(out=o, in_max=a, in_values=i), reads=['pwk', 'pm8'], writes=['pi8'])
                    self.cp('dve', tif[:], i8[:], r=['pi8'], w=['ptif'])
                    tv4 = m8[:].rearrange("q (h p) k -> q h p k", p=2); ti4 = tif[:].rearrange("q (h p) k -> q h p k", p=2)
                    c4 = cand[:].rearrange("q h (a b) -> q h a b", a=16)
                    self.tt('dve', c4, tv4[:, :, 0, :].unsqueeze(3).broadcast_to([128, 8, 16, 16]), tv4[:, :, 1, :].unsqueeze(2).broadcast_to([128, 8, 16, 16]),
                            ALU.add, r=['pm8'], w=['pcand'])
                    for h in range(8):
                        V('dve', lambda e, o=b8[:, h, 0:8], i=cand[:, h, :]: e.max(out=o, in_=i), reads=['pcand'], writes=['pb8'])
                        V('dve', lambda e, o=f8[:, h, 0:8], a=b8[:, h, 0:8], i=cand[:, h, :]: e.max_index(out=o, in_max=a, in_values=i), reads=['pcand', 'pb8'], writes=['pf8'])
                        V('dve', lambda e, o=wk2[:, h, :], a=b8[:, h, 0:8], i=cand[:, h, :]: e.match_replace(out=o, in_to_replace=a, in_values=i, imm_value=-1e30), reads=['pcand', 'pb8'], writes=['pwk2'])
                        V('dve', lambda e, o=b8[:, h, 8:16], i=wk2[:, h, :]: e.max(out=o, in_=i), reads=['pwk2'], writes=['pb8'])
                        V('dve', lambda e, o=f8[:, h, 8:16], a=b8[:, h, 8:16], i=wk2[:, h, :]: e.max_index(out=o, in_max=a, in_values=i), reads=['pwk2', 'pb8'], writes=['pf8'])
                    self.ts('dve', fau[:], f8[:], 4, None, ALU.logical_shift_right, r=['pf8'], w=['pfau'])
                    self.ts('dve', fbu[:], f8[:], 15, None, ALU.bitwise_and, r=['pf8'], w=['pfbu'])
                    self.cp('dve', fa[:], fau[:], r=['pfau'], w=['pfa'])
                    self.cp('dve', fb[:], fbu[:], r=['pfbu'], w=['pfb'])
                    io4 = self.iota16.unsqueeze(1).unsqueeze(1).broadcast_to([128, 8, 16, 16])
                    for which, (fsel, pp) in enumerate(((fa, 0), (fb, 1))):
                        self.tt('dve', eq[:], fsel[:].unsqueeze(3).broadcast_to([128, 8, 16, 16]), io4, ALU.is_equal, r=['pfa', 'pfb', 'cft'], w=['peq'])
                        self.tt('pool', eq[:], eq[:], ti4[:, :, pp, :].unsqueeze(2).broadcast_to([128, 8, 16, 16]), ALU.mult, r=['peq', 'ptif'], w=['peq'])
                        V('dve', lambda e, o=rt[rb][:, which, :].rearrange("q (h k) -> q h k", h=8), i=eq[:]: e.tensor_reduce(out=o, in_=i, axis=AX.X, op=ALU.add),
                          reads=['peq'], writes=['prt%d' % rb])
                    self.tt('dve', ez[:], b8[:], b8[:, :, 0:1].broadcast_to([128, 8, 16]), ALU.subtract, r=['pb8'], w=['pez'])
                    self.act(ez[:], ez[:], AF.Exp, r=['pez'], w=['pez'])
                    V('dve', lambda e, o=zz[:], i=ez[:]: e.tensor_reduce(out=o, in_=i, axis=AX.X, op=ALU.add), reads=['pez'], writes=['pzz'])
                    V('dve', lambda e, o=zz[:]: e.reciprocal(out=o, in_=o), reads=['pzz'], writes=['pzz'])
                    self.tt('dve', rt[rb][:, 2, :].rearrange("q (h k) -> q h k", h=8), ez[:], zz[:].unsqueeze(2).broadcast_to([128, 8, 16]), ALU.mult,
                            r=['pez', 'pzz'], w=['prt%d' % rb])
                    self.dma('pool', self.RT[tt], rt[rb][:], r=['prt%d' % rb])
            self.S.barrier()
        G = 16
        with ExitStack() as es:
            sb = lambda n, s, d: self.sb(es, n, s, d)
            WT = sb("dWT", [128, NK, 256], BF16)
            h2T = sb("dh2T", [128, 2, 16, 128], BF16)
            utc = [sb("dut%d" % i, [128, 16, NK], BF16) for i in range(3)]
            vch = [sb("dvc%d" % i, [128, D], BF16) for i in range(3)]
            rtt = sb("drt", [128, 3, 128], F32); rT = sb("drT", [128, 3, 128], F32)
            oh2 = [sb("doh2%d" % i, [128, G, NK], BF16) for i in range(2)]
            oh1 = [sb("doh1%d" % i, [128, G, NK], BF16) for i in range(2)]
            ga = [sb("dga%d" % i, [128, 256], F32) for i in range(2)]
            lngt = sb("lng", [128, D], F32); lnbt = sb("lnb", [128, D], F32)
            Gt = sb("dG", [128, D], F32)
            bufs = self.ln_bufs(es)
            self.dma('sp', lngt[:], self.lng[l, 1], w=['lng']); self.dma('sp', lnbt[:], self.lnb[l, 1], w=['lnb'])
            io3 = self.iota128[:, 0:NK].unsqueeze(1).broadcast_to([128, G, NK])
            curw = None; n = 0; m = 0
            for pr in pairs:
                w = 1 if pr[0] < NC else 0
                if w != curw:
                    self.dma('sp', Gt[:], self.MOD[l, w, :, 5 * D:6 * D], w=['dG']); curw = w
                for j, tt in enumerate(pr):
                    self.dma('sp', h2T[:, j], self.HTB[tt], w=['dh2T'])
                for j, tt in enumerate(pr):
                    self.dma('sp', rtt[:], self.RT[tt], w=['drt'])
                    for a in range(3):
                        self.tp(ps[a][:, 0:128], rtt[:, a, :], self.identf, r=['drt', 'cft'], w=['ps%d' % a])
                        self.cp('act', rT[:, a, :], ps[a][:, 0:128], r=['ps%d' % a], w=['drT'])
                    for t0 in range(0, 128, G):
                        ob = m % 2; m += 1
                        self.tt('dve', oh2[ob][:], io3, rT[:, 1, t0:t0 + G].unsqueeze(2).broadcast_to([128, G, NK]), ALU.is_equal, r=['cft', 'drT'], w=['doh2%d' % ob])
                        self.tt('dve', oh1[ob][:], io3, rT[:, 0, t0:t0 + G].unsqueeze(2).broadcast_to([128, G, NK]), ALU.is_equal, r=['cft', 'drT'], w=['doh1%d' % ob])
                        self.tt('pool', oh1[ob][:], oh1[ob][:], rT[:, 2, t0:t0 + G].unsqueeze(2).broadcast_to([128, G, NK]), ALU.mult, r=['doh1%d' % ob, 'drT'], w=['doh1%d' % ob])
                        for q0 in range(0, G, 4):
                            pi = 3 + (n % 5); n += 1
                            for q in range(4):
                                self.mm(ps[pi][0:NK, q * NK:(q + 1) * NK], oh2[ob][:, q0 + q, :], oh1[ob][:, q0 + q, :], True, True,
                                        r=['doh2%d' % ob, 'doh1%d' % ob], w=['ps%d' % pi])
                            tb = j * 128 + t0 + q0
                            self.cp('act', WT[0:NK, :, tb:tb + 4].rearrange("p i t -> p t i"), ps[pi][0:NK, 0:4 * NK].rearrange("p (t i) -> p t i", t=4),
                                    r=['ps%d' % pi], w=['dWT'])
                for i1 in range(NK):
                    ub = i1 % 3; pi = i1 % 8; gb = i1 % 2
                    self.dma('sp', utc[ub][:], self.UTB[l, i1], w=['dut%d' % ub])
                    for k in range(16):
                        self.mm(ps[pi][0:NK, 0:256].rearrange("p (j t) -> p j t", j=2), utc[ub][:, k, :], h2T[:, :, k, :], k == 0, k == 15,
                                r=['dut%d' % ub, 'dh2T'], w=['ps%d' % pi])
                    self.act(ga[gb][0:NK, :], ps[pi][0:NK, 0:256], AF.Gelu, r=['ps%d' % pi], w=['dga%d' % gb])
                    self.tt('dve' if i1 % 2 else 'pool', WT[0:NK, i1, :], WT[0:NK, i1, :], ga[gb][0:NK, :], ALU.mult, r=['dWT', 'dga%d' % gb], w=['dWT'])
                for i1 in range(NK):
                    vb = i1 % 3
                    self.dma('sp', vch[vb][0:NK, :], self.VB[l, i1 * NK:(i1 + 1) * NK, :], w=['dvc%d' % vb])
                    for j in range(len(pr)):
                        for nb in range(4):
                            self.mm(ps[j * 4 + nb][:], WT[0:NK, i1, j * 128:(j + 1) * 128], vch[vb][0:NK, nb * 512:(nb + 1) * 512], i1 == 0, i1 == NK - 1,
                                    r=['dWT', 'dvc%d' % vb], w=['ps%d' % (j * 4 + nb)])
                for j, tt in enumerate(pr):
                    dsts = [self.XS[tt]]
                    if last:
                        dsts = [self.out[tt - NC]]
                    self.resid_ln(bufs, self.XS[tt], [j * 4 + i for i in range(4)], Gt, 'dG', lngt, lnbt, dsts)
            self.S.barrier()


_CACHE = {}


def run(inputs, cfg, core_ids=None):
    maps = prep_inputs(inputs, cfg)
    key = (cfg.L, cfg.LC, cfg.DEPTH, cfg.NK, cfg.dbg)
    if key not in _CACHE:
        p = Prog(cfg); p.build(); _CACHE[key] = p
    p = _CACHE[key]
    res = run_bass_kernel_spmd(p.nc, maps, core_ids=list(range(len(maps))))
    outs = [r["out"].reshape(cfg.L, D) for r in res.results]
    return np.stack(outs).astype(np.float32), res, p


def kernel(**inputs):
    cfg = Cfg()
    out, _, _ = run(inputs, cfg)
    return out
```

```python
import numpy as np
import ml_dtypes
from contextlib import ExitStack
import concourse.bass as bass
import concourse.mybir as mybir
from concourse.bass_utils import run_bass_kernel_spmd

F32 = mybir.dt.float32; BF16 = mybir.dt.bfloat16; U32 = mybir.dt.uint32
ALU = mybir.AluOpType; AF = mybir.ActivationFunctionType; AX = mybir.AxisListType

D = 2048
ALPHA = (2.0 * 4) ** 0.25
LN_EPS = 1e-6
GRID_W = 64


class Cfg:
    def __init__(self, L=8192, LC=256, DEPTH=4, NK=128, dbg=False):
        self.L = L; self.LC = LC; self.DEPTH = DEPTH; self.NK = NK; self.dbg = dbg
        self.NC = LC // 128; self.NL = L // 128; self.TT = self.NC + self.NL
        self.TOK = L + LC
        self.NE = NK * NK


class Sched:
    ENG = ['pe', 'dve', 'act', 'pool', 'sp']
    CE = ['pe', 'dve', 'act', 'pool']
    EPOCH = 30000
    NEP = {'pe': 26, 'dve': 5, 'act': 4, 'pool': 3}

    def __init__(self, nc, es, ndma=48):
        self.nc = nc
        self.esem = {e: [es.enter_context(nc.semaphore('s_%s%d' % (e, i))) for i in range(self.NEP[e])] for e in self.CE}
        self.dsem = [es.enter_context(nc.semaphore('d%d' % i)) for i in range(ndma)]
        self.dcnt = [0] * ndma; self.dnext = 0
        self.cnt = {e: 0 for e in self.CE}; self.ep = {e: 0 for e in self.CE}
        self.prog = {e: [] for e in self.ENG}
        self.seen = {e: {} for e in self.ENG}
        self.res = {}
        self.ninst = 0

    def semh(self, sid):
        return self.esem[sid[1]][sid[2]] if sid[0] == 'e' else self.dsem[sid[1]]

    def _need(self, eng, sid, val, waits):
        s = self.seen[eng]
        if sid[0] == 'e':
            k = ('e', sid[1]); cur = s.get(k, (-1, 0)); new = (sid[2], val)
            if cur >= new:
                return
            s[k] = new
        else:
            if s.get(sid, 0) >= val:
                return
            s[sid] = val
        waits.append((sid, val))

    def _deps(self, eng, reads, writes):
        waits = []
        pe = eng == 'pe'
        for k in reads:
            st = self.res.get(k)
            if st:
                for sid, v in st[0].items():
                    if not (pe and sid[0] == 'e' and sid[1] == 'pe'):
                        self._need(eng, sid, v, waits)
        for k in writes:
            st = self.res.get(k)
            if st:
                for d in st:
                    for sid, v in d.items():
                        if not (pe and sid[0] == 'e' and sid[1] == 'pe'):
                            self._need(eng, sid, v, waits)
        return waits

    def _mark(self, ev, reads, writes):
        sid, v = ev
        for k in reads:
            self.res.setdefault(k, ({}, {}))[1][sid] = v
        for k in writes:
            self.res.setdefault(k, ({}, {}))[0][sid] = v

    def op(self, eng, fn, reads=(), writes=()):
        waits = self._deps(eng, reads, writes)
        if self.cnt[eng] >= self.EPOCH:
            self.ep[eng] += 1; self.cnt[eng] = 0
        self.cnt[eng] += 1; ev = (('e', eng, self.ep[eng]), self.cnt[eng])
        self.prog[eng].append((waits, fn, ev)); self._mark(ev, reads, writes); self.ninst += 1

    def dma(self, eng, fn, reads=(), writes=()):
        slot = self.dnext; self.dnext = (self.dnext + 1) % len(self.dsem)
        waits = self._deps(eng, reads, writes)
        if self.dcnt[slot] > 0:
            self._need(eng, ('d', slot), self.dcnt[slot], waits)
        self.dcnt[slot] += 16; ev = (('d', slot), self.dcnt[slot])
        self.prog[eng].append((waits, fn, ev)); self._mark(ev, reads, writes); self.ninst += 1

    def barrier(self):
        for e in self.ENG:
            waits = []
            for i, c in enumerate(self.dcnt):
                if c:
                    self._need(e, ('d', i), c, waits)
            for o in self.CE:
                if o != e and (self.cnt[o] or self.ep[o]):
                    self._need(e, ('e', o, self.ep[o]), self.cnt[o], waits)
            if waits:
                self.prog[e].append((waits, None, None))
        self.res = {}

    def emit(self, block):
        engmap = {'pe': 'tensor', 'dve': 'vector', 'act': 'scalar', 'pool': 'gpsimd', 'sp': 'sync'}
        for e in self.ENG:
            prog = self.prog[e]

            def body(engine, prog=prog):
                for waits, fn, ev in prog:
                    for sid, v in waits:
                        engine.wait_ge(self.semh(sid), v)
                    if fn is None:
                        continue
                    ins = fn(engine)
                    sid, v = ev
                    ins.then_inc(self.semh(sid), 16 if sid[0] == 'd' else 1)
            getattr(block, engmap[e])(body)


def _blk_stationary(w):
    n = w.shape[1] // 128
    return np.ascontiguousarray(w.reshape(16, 128, n, 128).transpose(2, 1, 0, 3))


def _blk_moving(w, width=512):
    n = w.shape[1] // width
    return np.ascontiguousarray(w.reshape(16, 128, n, width).transpose(2, 1, 0, 3))


def rope_perm():
    d = np.arange(128)
    partner = np.where((d % 64) < 32, d + 32, d - 32)
    sign = np.where((d % 64) < 32, -1.0, 1.0).astype(np.float32)
    return partner, sign


def const_tables(cfg):
    cf = np.zeros((128, 912), np.float32)
    cf[:, 0:128] = np.eye(128, dtype=np.float32)
    cf[:, 128:256] = np.arange(128, dtype=np.float32)[None, :]
    s = -1.0 / 16.0
    j = np.arange(64)[:, None]; i = np.arange(64)[None, :]
    cf[:64, 256:320] = np.where(j <= i, s, 0.0)
    cf[:64, 320:384] = np.where(j >= i, s, 0.0)
    cf[:64, 384:448] = np.where(j > i, s, 0.0)
    cf[:64, 448:512] = np.where(j < i, s, 0.0)
    cf[:64, 512:576] = np.where(j <= i, 1.0, 0.0)
    cf[:64, 576:640] = np.where(j >= i, 1.0, 0.0)
    cf[:, 640:656] = np.arange(16, dtype=np.float32)[None, :]
    jj = np.arange(128)[:, None]; ii = np.arange(128)[None, :]
    cf[:, 656:784] = np.where(jj >= ii, 1.0, 0.0)
    cf[:, 784:912] = np.where(jj <= ii, 1.0, 0.0)
    t = np.arange(cfg.L)
    r = (t // GRID_W).astype(np.float32); c = (t % GRID_W).astype(np.float32)
    inv = (10000.0 ** (-np.arange(0, 64, 2, dtype=np.float32) / 64.0)).astype(np.float32)
    ang_r = r[None, :] * inv[:, None]; ang_c = c[None, :] * inv[:, None]
    d = np.arange(128)
    ang = np.where((d < 64)[:, None], ang_r[d % 32], ang_c[d % 32]).astype(np.float32)
    _, sign = rope_perm()
    cos = np.cos(ang).astype(np.float32); sins = (np.sin(ang).astype(np.float32) * sign[:, None])
    return cf, np.ascontiguousarray(cos), np.ascontiguousarray(sins)


def prep_inputs(inp, cfg):
    f = lambda a: np.ascontiguousarray(np.asarray(a, dtype=np.float32))
    x = f(inp['x']); c = f(inp['c']); ctx = f(inp['ctx']); c_ctx = f(inp['c_ctx'])
    w_ada = f(inp['w_ada']); b_ada = f(inp['b_ada']); w_in = f(inp['w_in'])
    NL_ = cfg.DEPTH
    cf, cos, sins = const_tables(cfg)
    partner, _ = rope_perm()
    wa = np.stack([_blk_moving(w_ada[l]) for l in range(NL_)])
    ba = np.ascontiguousarray(b_ada[:NL_].reshape(NL_, 24, 1, 512))
    wf_l, wt_l = [], []
    for l in range(NL_):
        w = w_in[l]
        sq = w[:, 3104:4128].reshape(D, 8, 128); sk = w[:, 4128:4384].reshape(D, 2, 128)
        lr = np.zeros((D, 128), np.float32); lr[:, :32] = w[:, 3072:3104]
        fm = np.concatenate([w[:, 0:1536], w[:, 1536:2048], sq.reshape(D, 1024), sq[:, :, partner].reshape(D, 1024),
                             sk.reshape(D, 256), sk[:, :, partner].reshape(D, 256), lr], axis=1)
        wf_l.append(_blk_stationary(fm))
        tm = np.concatenate([w[:, 1792:2048], w[:, 2048:2560], w[:, 4384:4640], w[:, 2560:3072]], axis=1)
        wt_l.append(_blk_moving(tm))
    wf = np.stack(wf_l); wt = np.stack(wt_l)
    conv_w = f(inp['conv_w'])[:NL_]
    cw = np.ascontiguousarray(conv_w.reshape(NL_, 3, 4, 128).transpose(0, 3, 2, 1))
    w2 = f(inp['gla_w2'])[:NL_]; b2 = f(inp['gla_b2'])[:NL_]
    w2a = np.zeros((NL_, 33, 512), np.float32)
    w2a[:, 0:16, 0:256] = w2[:, 0]; w2a[:, 16:32, 256:512] = w2[:, 1]
    w2a[:, 32, 0:256] = b2[:, 0]; w2a[:, 32, 256:512] = b2[:, 1]
    gnorm = np.ascontiguousarray(np.broadcast_to(f(inp['gla_norm'])[:NL_, None, :], (NL_, 128, 128)))
    sink = np.ascontiguousarray(np.broadcast_to(f(inp['swa_sink'])[:NL_, None, :], (NL_, 128, 8)))
    w_out = f(inp['w_out'])[:NL_]
    wo = np.ascontiguousarray(w_out.reshape(NL_, 16, 128, D).transpose(0, 2, 1, 3))
    lng = np.ascontiguousarray(np.broadcast_to(f(inp['ln_g'])[:NL_, :, None, :], (NL_, 2, 128, D)))
    lnb = np.ascontiguousarray(np.broadcast_to(f(inp['ln_b'])[:NL_, :, None, :], (NL_, 2, 128, D)))
    wq = np.stack([_blk_stationary(f(inp['peer_wq'])[l]) for l in range(NL_)])
    sk_ = f(inp['peer_subkeys'])[:NL_]
    skt = np.ascontiguousarray(sk_.transpose(0, 3, 1, 2))
    NK = cfg.NK
    pu = f(inp['peer_u'])[:NL_]
    ut = np.ascontiguousarray(pu.reshape(NL_, NK, NK, 16, 128).transpose(0, 1, 4, 3, 2))
    pv = f(inp['peer_v'])[:NL_]
    maps = []
    for b in range(x.shape[0]):
        cc = np.stack([c[b].reshape(16, 128).T, c_ctx.reshape(16, 128).T])
        xs = np.concatenate([ctx[b], x[b]], axis=0).reshape(cfg.TT, 128, D)
        maps.append(dict(xs_in=np.ascontiguousarray(xs), cc=np.ascontiguousarray(cc), wa=wa, ba=ba, wf=wf, wt=wt,
                         cw=cw, w2a=w2a, gnorm=gnorm, sink=sink, wo=wo, lng=lng, lnb=lnb, wq=wq, skt=skt,
                         ut=ut, pv=pv, cf=cf, cos=cos, sins=sins))
    return maps


class Prog:
    def __init__(self, cfg):
        self.cfg = cfg
        self.nc = bass.Bass("TRN2", target_bir_lowering=False)
        self.dbg_outs = []

    def din(self, name, shape, dt=F32):
        return self.nc.dram_tensor(name, list(shape), dt, kind="ExternalInput").ap()

    def dscr(self, name, shape, dt=F32, dbg=False):
        if dbg and self.cfg.dbg:
            self.dbg_outs.append(name)
            return self.nc.dram_tensor(name, list(shape), dt, kind="ExternalOutput").ap()
        return self.nc.dram_tensor(name, list(shape), dt).ap()

    def sb(self, es, name, shape, dt):
        self.uid = getattr(self, 'uid', 0) + 1
        return es.enter_context(self.nc.sbuf_tensor("%s_u%d" % (name, self.uid), list(shape), dt))

    def dma(self, q, out, in_, r=(), w=()):
        self.S.dma(q, lambda e, o=out, i=in_: e.dma_start(out=o, in_=i), reads=r, writes=w)

    def mm(self, out, lhsT, rhs, start, stop, r=(), w=()):
        self.S.op('pe', lambda e, o=out, l=lhsT, rh=rhs, s=start, t=stop: e.matmul(o, lhsT=l, rhs=rh, start=s, stop=t), reads=r, writes=w)

    def tp(self, out, in_, ident, r=(), w=()):
        self.S.op('pe', lambda e, o=out, i=in_, d=ident: e.transpose(out=o, in_=i, identity=d), reads=r, writes=w)

    def tt(self, eng, out, in0, in1, op, r=(), w=()):
        self.S.op(eng, lambda e, o=out, a=in0, b=in1, p=op: e.tensor_tensor(out=o, in0=a, in1=b, op=p), reads=r, writes=w)

    def ts(self, eng, out, in0, s1, s2, op0, op1=None, r=(), w=()):
        if op1 is None:
            self.S.op(eng, lambda e, o=out, a=in0, x=s1, p=op0: e.tensor_scalar(out=o, in0=a, scalar1=x, scalar2=None, op0=p), reads=r, writes=w)
        else:
            self.S.op(eng, lambda e, o=out, a=in0, x=s1, y=s2, p=op0, q=op1: e.tensor_scalar(out=o, in0=a, scalar1=x, scalar2=y, op0=p, op1=q), reads=r, writes=w)

    def stt(self, eng, out, in0, scalar, in1, op0, op1, r=(), w=()):
        self.S.op(eng, lambda e, o=out, a=in0, s=scalar, b=in1, p=op0, q=op1: e.scalar_tensor_tensor(out=o, in0=a, scalar=s, in1=b, op0=p, op1=q), reads=r, writes=w)

    def act(self, out, in_, func, r=(), w=(), bias=None, scale=None, accum=None):
        kw = {}
        if bias is not None: kw['bias'] = bias
        if scale is not None: kw['scale'] = scale
        if accum is not None: kw['accum_out'] = accum
        self.S.op('act', lambda e, o=out, i=in_, f=func, kw=kw: e.activation(out=o, in_=i, func=f, **kw), reads=r, writes=w)

    def cp(self, eng, out, in_, r=(), w=()):
        if eng == 'act':
            self.S.op('act', lambda e, o=out, i=in_: e.copy(out=o, in_=i), reads=r, writes=w)
        else:
            self.S.op(eng, lambda e, o=out, i=in_: e.tensor_copy(out=o, in_=i), reads=r, writes=w)

    def memset(self, eng, ap, val, w=()):
        self.S.op(eng, lambda e, a=ap, v=val: e.memset(a, v), writes=w)

    def build(self):
        cfg = self.cfg; nc = self.nc
        TT, NLY, NK = cfg.TT, cfg.DEPTH, cfg.NK
        self.xs_in = self.din("xs_in", [TT, 128, D])
        self.cc = self.din("cc", [2, 128, 16])
        self.wa = self.din("wa", [NLY, 24, 128, 16, 512]); self.ba = self.din("ba", [NLY, 24, 1, 512])
        self.wf = self.din("wf", [NLY, 37, 128, 16, 128]); self.wt = self.din("wt", [NLY, 3, 128, 16, 512])
        self.cw = self.din("cw", [NLY, 128, 4, 3]); self.w2a = self.din("w2a", [NLY, 33, 512])
        self.gnorm = self.din("gnorm", [NLY, 128, 128]); self.sink = self.din("sink", [NLY, 128, 8])
        self.wo = self.din("wo", [NLY, 128, 16, D])
        self.lng = self.din("lng", [NLY, 2, 128, D]); self.lnb = self.din("lnb", [NLY, 2, 128, D])
        self.wq = self.din("wq", [NLY, 16, 128, 16, 128]); self.skt = self.din("skt", [NLY, 128, 2, NK])
        self.ut = self.din("ut", [NLY, NK, 128, 16, NK]); self.pv = self.din("pv", [NLY, cfg.NE, D])
        self.cf = self.din("cf", [128, 912]); self.cos = self.din("cos", [128, cfg.L]); self.sins = self.din("sins", [128, cfg.L])
        self.out = nc.dram_tensor("out", [cfg.NL, 128, D], F32, kind="ExternalOutput").ap()
        self.XS = self.dscr("XS", [TT, 128, D], F32, dbg=True)
        self.MOD = self.dscr("MOD", [NLY, 2, 128, 6 * D], F32, dbg=True)
        self.HTB = self.dscr("HTB", [TT, 128, 16, 128], BF16)
        self.ZT = self.dscr("ZT", [4736, cfg.TOK], F32, dbg=True)
        self.ZV = self.dscr("ZV", [TT, 128, 1536], F32, dbg=True)
        self.QKT = self.dscr("QKT", [1280, cfg.TOK], BF16)
        self.MIXT = self.dscr("MIXT", [D, cfg.TOK], BF16, dbg=True)
        self.OF = self.dscr("OF", [cfg.TOK // 64, 64, 512], F32)
        self.WFB = self.dscr("WFB", [NLY, 37, 128, 16, 128], BF16)
        self.UTB = self.dscr("UTB", [NLY, NK, 128, 16, NK], BF16)
        self.VB = self.dscr("VB", [NLY, cfg.NE, D], BF16)
        es = ExitStack()
        with es:
            self.S = Sched(nc, es)
            self.ps = [es.enter_context(nc.psum_tensor("ps%d" % i, [128, 512], F32)) for i in range(8)]
            self.consts(es)
            self.stage_prep()
            self.stage_adaln()
            for l in range(NLY):
                last = (l == NLY - 1)
                self.stage_modT(l, 0, range(TT))
                self.stage_inproj(l)
                self.stage_rope(l)
                self.stage_conv(l)
                self.stage_gla(l)
                self.stage_swa(l, last)
                tiles = range(cfg.NC, TT) if last else range(TT)
                self.stage_outproj(l, tiles)
                self.stage_modT(l, 1, tiles)
                self.stage_peer(l, tiles, last)
            self.S.barrier()
            with nc.Block() as block:
                self.S.emit(block)
        return nc

    def consts(self, es):
        self.cft = self.sb(es, "cft", [128, 912], F32)
        self.identb = self.sb(es, "identb", [128, 128], BF16)
        self.onesb = self.sb(es, "onesb", [128, 128], BF16)
        self.dma('sp', self.cft[:], self.cf, w=['cft'])
        self.cp('dve', self.identb[:], self.cft[:, 0:128], r=['cft'], w=['identb'])
        self.memset('dve', self.onesb[:], 1.0, w=['onesb'])
        c = self.cft
        self.identf = c[:, 0:128]; self.iota128 = c[:, 128:256]
        self.triT = [c[0:64, 256:320], c[0:64, 320:384]]
        self.amT = [c[0:64, 384:448], c[0:64, 448:512]]
        self.maskT = [c[0:64, 512:576], c[0:64, 576:640]]
        self.iota16 = c[:, 640:656]
        self.maskP = c[:, 656:784]; self.maskN = c[:, 784:912]
        self.S.barrier()

    def psb(self, i):
        return self.ps[i][:].bitcast(BF16)

    def stage_prep(self):
        cfg = self.cfg; NK = cfg.NK
        for l in range(cfg.DEPTH):
            src = self.wf[l].rearrange("i p k e -> (i p) (k e)"); dst = self.WFB[l].rearrange("i p k e -> (i p) (k e)")
            for r0 in range(0, 37 * 128, 1024):
                r1 = min(37 * 128, r0 + 1024)
                self.dma('pool', dst[r0:r1, :], src[r0:r1, :])
        for l in range(cfg.DEPTH):
            src = self.ut[l].rearrange("i p k e -> (i p) (k e)"); dst = self.UTB[l].rearrange("i p k e -> (i p) (k e)")
            rows = NK * 128
            for r0 in range(0, rows, 1024):
                r1 = min(rows, r0 + 1024)
                self.dma('pool', dst[r0:r1, :], src[r0:r1, :])
            for r0 in range(0, cfg.NE, 1024):
                r1 = min(cfg.NE, r0 + 1024)
                self.dma('pool', self.VB[l][r0:r1, :], self.pv[l][r0:r1, :])
        self.S.barrier()

    def stage_adaln(self):
        cfg = self.cfg
        with ExitStack() as es:
            ccs = self.sb(es, "ccs", [128, 2, 16], F32); cs = self.sb(es, "cs", [128, 2, 16], F32)
            rep = self.sb(es, "rep", [128, 2, 16, 128], BF16)
            wat = [self.sb(es, "wat%d" % i, [128, 16, 512], BF16) for i in range(2)]
            bat = [self.sb(es, "bat%d" % i, [1, 512], BF16) for i in range(2)]
            mo = [self.sb(es, "mo%d" % i, [128, 512], F32) for i in range(4)]
            self.dma('sp', ccs[:], self.cc.rearrange("w p k -> p w k"), w=['ccs'])
            self.act(cs[:], ccs[:], AF.Silu, r=['ccs'], w=['cs'])
            for w in range(2):
                self.cp('dve', rep[:, w], cs[:, w, :].unsqueeze(2).broadcast_to([128, 16, 128]), r=['cs'], w=['rep'])
            n = 0
            for l in range(cfg.DEPTH):
                for cb in range(24):
                    b = (l * 24 + cb) % 2
                    self.dma('pool', wat[b][:], self.wa[l, cb], w=['wat%d' % b])
                    self.dma('pool', bat[b][:], self.ba[l, cb], w=['bat%d' % b])
                    for w in range(2):
                        pi = n % 8; m = n % 4; n += 1
                        for k in range(16):
                            self.mm(self.ps[pi][:], rep[:, w, k, :], wat[b][:, k, :], k == 0, False, r=['rep', 'wat%d' % b], w=['ps%d' % pi])
                        self.mm(self.ps[pi][:], self.onesb[0:1, :], bat[b][0:1, :], False, True, r=['onesb', 'bat%d' % b], w=['ps%d' % pi])
                        if cb // 4 in (1, 4):
                            self.ts('dve', mo[m][:], self.ps[pi][:], 1.0, None, ALU.add, r=['ps%d' % pi], w=['mo%d' % m])
                        else:
                            self.cp('act', mo[m][:], self.ps[pi][:], r=['ps%d' % pi], w=['mo%d' % m])
                        self.dma('sp', self.MOD[l, w, :, cb * 512:(cb + 1) * 512], mo[m][:], r=['mo%d' % m])
            self.S.barrier()

    def stage_modT(self, l, which, tiles):
        cfg = self.cfg
        with ExitStack() as es:
            xs = [self.sb(es, "mx%d" % i, [128, D], F32) for i in range(2)]
            tmp = self.sb(es, "mtmp", [128, D], F32)
            hb = [self.sb(es, "mhb%d" % i, [128, D], BF16) for i in range(2)]
            hts = [self.sb(es, "mhts%d" % i, [128, 16, 128], BF16) for i in range(2)]
            sct = [self.sb(es, "msc%d" % i, [128, D], F32) for i in range(2)]
            sht = [self.sb(es, "msh%d" % i, [128, D], F32) for i in range(2)]
            o_sh = 0 if which == 0 else 3 * D
            for w in range(2):
                self.dma('sp', sht[w][:], self.MOD[l, w, :, o_sh:o_sh + D], w=['msh%d' % w])
                self.dma('sp', sct[w][:], self.MOD[l, w, :, o_sh + D:o_sh + 2 * D], w=['msc%d' % w])
            src = self.xs_in if (l == 0 and which == 0) else self.XS
            for n, tt in enumerate(tiles):
                b = n % 2; w = 1 if tt < cfg.NC else 0
                self.dma('sp', xs[b][:], src[tt], w=['mx%d' % b])
                self.tt('dve', tmp[:], xs[b][:], sct[w][:], ALU.mult, r=['mx%d' % b, 'msc%d' % w], w=['mtmp'])
                self.tt('pool', hb[b][:], tmp[:], sht[w][:], ALU.add, r=['mtmp', 'msh%d' % w], w=['mhb%d' % b])
                for half in range(2):
                    pi = (n * 2 + half) % 8
                    pb = self.psb(pi)
                    for j in range(8):
                        k = half * 8 + j
                        self.tp(pb[:, j * 128:(j + 1) * 128], hb[b][:, k * 128:(k + 1) * 128], self.identb[:], r=['mhb%d' % b, 'identb'], w=['ps%d' % pi])
                    self.cp('act', hts[b][:, half * 8:(half + 1) * 8, :], pb.rearrange("p (k t) -> p k t", k=8), r=['ps%d' % pi], w=['mhts%d' % b])
                self.dma('pool', self.HTB[tt], hts[b][:], r=['mhts%d' % b])
            self.S.barrier()

    def stage_inproj(self, l):
        cfg = self.cfg
        with ExitStack() as es:
            hT = [self.sb(es, "ihT%d" % i, [128, 4, 16, 128], BF16) for i in range(2)]
            wtm = [self.sb(es, "iwtm%d" % i, [128, 16, 512], BF16) for i in range(3)]
            wst = [self.sb(es, "iwst%d" % i, [128, 16, 128], BF16) for i in range(4)]
            zo = [self.sb(es, "izo%d" % i, [128, 512], F32) for i in range(4)]
            for tb in range(3):
                self.dma('pool', wtm[tb][:], self.wt[l, tb], w=['iwtm%d' % tb])
            n = 0
            for mi, t0 in enumerate(range(0, cfg.TT, 4)):
                nt = min(4, cfg.TT - t0); hb = mi % 2
                for j in range(nt):
                    self.dma('sp', hT[hb][:, j], self.HTB[t0 + j], w=['ihT%d' % hb])
                for nb in range(37):
                    wb = nb % 4
                    self.dma('sp', wst[wb][:], self.WFB[l, nb], w=['iwst%d' % wb])
                    pi = n % 8; zb = n % 4; n += 1
                    for k in range(16):
                        self.mm(self.ps[pi][:, 0:nt * 128].rearrange("p (j t) -> p j t", j=nt), wst[wb][:, k, :], hT[hb][:, 0:nt, k, :],
                                k == 0, k == 15, r=['iwst%d' % wb, 'ihT%d' % hb], w=['ps%d' % pi])
                    self.cp('act' if n % 2 else 'dve', zo[zb][:, 0:nt * 128], self.ps[pi][:, 0:nt * 128], r=['ps%d' % pi], w=['izo%d' % zb])
                    self.dma('sp', self.ZT[nb * 128:(nb + 1) * 128, t0 * 128:(t0 + nt) * 128], zo[zb][:, 0:nt * 128], r=['izo%d' % zb])
                for j in range(nt):
                    for tb in range(3):
                        pi = n % 8; zb = n % 4; n += 1
                        for k in range(16):
                            self.mm(self.ps[pi][:], hT[hb][:, j, k, :], wtm[tb][:, k, :], k == 0, k == 15, r=['ihT%d' % hb, 'iwtm%d' % tb], w=['ps%d' % pi])
                        self.cp('act' if n % 2 else 'dve', zo[zb][:], self.ps[pi][:], r=['ps%d' % pi], w=['izo%d' % zb])
                        self.dma('sp', self.ZV[t0 + j][:, tb * 512:(tb + 1) * 512], zo[zb][:], r=['izo%d' % zb])
            self.S.barrier()

    def stage_rope(self, l):
        cfg = self.cfg; LC = cfg.LC
        rows = [(2048 + h * 128, 3072 + h * 128, h * 128) for h in range(8)] + [(4096 + g * 128, 4352 + g * 128, 1024 + g * 128) for g in range(2)]
        with ExitStack() as es:
            ta = [self.sb(es, "rta%d" % i, [128, 512], F32) for i in range(2)]
            tb = [self.sb(es, "rtb%d" % i, [128, 512], F32) for i in range(2)]
            ob = [self.sb(es, "rob%d" % i, [128, 512], BF16) for i in range(2)]
            ct = self.sb(es, "rct", [128, 512], F32); st = self.sb(es, "rst", [128, 512], F32)
            n = 0
            for c0 in range(0, LC, 512):
                cn = min(512, LC - c0)
                for (ra, rp, ro) in rows:
                    b = n % 2; n += 1
                    self.dma('sp', ta[b][:, 0:cn], self.ZT[ra:ra + 128, c0:c0 + cn], w=['rta%d' % b])
                    self.cp('act', ob[b][:, 0:cn], ta[b][:, 0:cn], r=['rta%d' % b], w=['rob%d' % b])
                    self.dma('pool', self.QKT[ro:ro + 128, c0:c0 + cn], ob[b][:, 0:cn], r=['rob%d' % b])
            for s0 in range(0, cfg.L, 512):
                self.dma('sp', ct[:], self.cos[:, s0:s0 + 512], w=['rct'])
                self.dma('sp', st[:], self.sins[:, s0:s0 + 512], w=['rst'])
                for (ra, rp, ro) in rows:
                    b = n % 2; n += 1
                    self.dma('sp', ta[b][:], self.ZT[ra:ra + 128, LC + s0:LC + s0 + 512], w=['rta%d' % b])
                    self.dma('sp', tb[b][:], self.ZT[rp:rp + 128, LC + s0:LC + s0 + 512], w=['rtb%d' % b])
                    self.tt('dve', ta[b][:], ta[b][:], ct[:], ALU.mult, r=['rta%d' % b, 'rct'], w=['rta%d' % b])
                    self.tt('pool', tb[b][:], tb[b][:], st[:], ALU.mult, r=['rtb%d' % b, 'rst'], w=['rtb%d' % b])
                    self.tt('dve', ob[b][:], ta[b][:], tb[b][:], ALU.add, r=['rta%d' % b, 'rtb%d' % b], w=['rob%d' % b])
                    self.dma('pool', self.QKT[ro:ro + 128, LC + s0:LC + s0 + 512], ob[b][:], r=['rob%d' % b])
            self.S.barrier()

    def stage_conv(self, l):
        cfg = self.cfg; SEG = 1024
        with ExitStack() as es:
            xi = [self.sb(es, "cxi%d" % i, [128, SEG + 2], F32) for i in range(2)]
            cg = [self.sb(es, "ccg%d" % i, [128, SEG + 2], F32) for i in range(2)]
            bb = [self.sb(es, "cbb%d" % i, [128, SEG], F32) for i in range(2)]
            u = [self.sb(es, "cu%d" % i, [128, SEG + 2], F32) for i in range(2)]
            acc = [self.sb(es, "cacc%d" % i, [128, SEG], F32) for i in range(2)]
            ob = [self.sb(es, "cob%d" % i, [128, SEG], BF16) for i in range(2)]
            cwt = self.sb(es, "ccw", [128, 4, 3], F32)
            self.dma('sp', cwt[:], self.cw[l], w=['ccw'])
            n = 0
            for (q0, q1) in ((0, cfg.LC), (cfg.LC, cfg.TOK)):
                for s0 in range(q0, q1, SEG):
                    sn = min(SEG, q1 - s0)
                    lo = max(q0, s0 - 1); hi = min(q1, s0 + sn + 1); off = lo - (s0 - 1); ln = hi - lo
                    for cc in range(4):
                        b = n % 2; n += 1
                        kx, kc, kb, ku, ka, ko = 'cxi%d' % b, 'ccg%d' % b, 'cbb%d' % b, 'cu%d' % b, 'cacc%d' % b, 'cob%d' % b
                        self.dma('sp', xi[b][:, off:off + ln], self.ZT[cc * 128:(cc + 1) * 128, lo:hi], w=[kx])
                        self.dma('sp', cg[b][:, off:off + ln], self.ZT[1024 + cc * 128:1024 + (cc + 1) * 128, lo:hi], w=[kc])
                        self.dma('sp', bb[b][:, 0:sn], self.ZT[512 + cc * 128:512 + (cc + 1) * 128, s0:s0 + sn], w=[kb])
                        if off > 0:
                            self.memset('pool', u[b][:, 0:1], 0.0, w=[ku])
                        if off + ln < sn + 2:
                            self.memset('pool', u[b][:, sn + 1:sn + 2], 0.0, w=[ku])
                        self.tt('pool', u[b][:, off:off + ln], cg[b][:, off:off + ln], xi[b][:, off:off + ln], ALU.mult, r=[kx, kc], w=[ku])
                        self.ts('dve', acc[b][:, 0:sn], u[b][:, 0:sn], cwt[:, cc, 0:1], None, ALU.mult, r=[ku, 'ccw'], w=[ka])
                        self.stt('dve', acc[b][:, 0:sn], u[b][:, 1:sn + 1], cwt[:, cc, 1:2], acc[b][:, 0:sn], ALU.mult, ALU.add, r=[ku, 'ccw', ka], w=[ka])
                        self.stt('dve', acc[b][:, 0:sn], u[b][:, 2:sn + 2], cwt[:, cc, 2:3], acc[b][:, 0:sn], ALU.mult, ALU.add, r=[ku, 'ccw', ka], w=[ka])
                        self.tt('pool', ob[b][:, 0:sn], acc[b][:, 0:sn], bb[b][:, 0:sn], ALU.mult, r=[ka, kb], w=[ko])
                        self.dma('pool', self.MIXT[cc * 128:(cc + 1) * 128, s0:s0 + sn], ob[b][:, 0:sn], r=[ko])
            self.S.barrier()

    def stage_gla(self, l):
        cfg = self.cfg
        NCHC = cfg.LC // 64; NCH = cfg.TOK // 64
        ps = self.ps
        with ExitStack() as es:
            sb = lambda n, s, d: self.sb(es, n, s, d)
            St = sb("gS", [64, 512], F32); Sbf = sb("gSbf", [64, 512], BF16)
            w2t = sb("gw2", [33, 512], F32); NW = sb("gNW", [64, 128], F32)
            B2 = range(2)
            qT = [sb("gqT%d" % i, [64, 4, 64], F32) for i in B2]; kT = [sb("gkT%d" % i, [64, 4, 64], F32) for i in B2]
            lrT = [sb("glr%d" % i, [33, 64], F32) for i in B2]
            kTok = [sb("gkk%d" % i, [64, 256], F32) for i in B2]; vTok = [sb("gvv%d" % i, [64, 512], F32) for i in B2]
            gTok = [sb("ggg%d" % i, [64, 512], F32) for i in B2]; ofl = [sb("gof%d" % i, [64, 512], F32) for i in B2]
            vbf = [sb("gvb%d" % i, [64, 512], BF16) for i in B2]
            lsb = [sb("gls%d" % i, [64, 256], F32) for i in B2]
            Eb = [sb("gEb%d" % i, [64, 256], F32) for i in B2]; Enb = [sb("gEn%d" % i, [64, 256], F32) for i in B2]
            Ek = [sb("gEk%d" % i, [64, 256], F32) for i in B2]
            qe = [sb("gqe%d" % i, [64, 4, 64], BF16) for i in B2]; ke = [sb("gke%d" % i, [64, 4, 64], BF16) for i in B2]
            kend = [sb("gkd%d" % i, [64, 256], BF16) for i in B2]; attT = [sb("gat%d" % i, [64, 4, 64], BF16) for i in B2]
            sq = sb("gsq", [64, 4, 128], F32); ss = sb("gss", [64, 4], F32); rstd = sb("grs", [64, 4], F32)
            on = sb("gon", [64, 512], F32); sg = sb("gsg", [64, 512], F32); res = sb("gres", [64, 512], BF16)
            mo = [sb("gmo%d" % i, [128, 4, 64], BF16) for i in B2]
            self.dma('sp', w2t[:], self.w2a[l], w=['gw2'])
            self.dma('sp', NW[:], self.gnorm[l][0:64, :], w=['gNW'])
            for i in B2:
                self.memset('pool', lrT[i][:], 1.0, w=['glr%d' % i])
            orders = [list(range(NCH)), list(range(NCHC - 1, -1, -1)) + list(range(NCH - 1, NCHC - 1, -1))]
            for d in range(2):
                self.memset('dve', St[:], 0.0, w=['gS'])
                self.memset('dve', Sbf[:], 0.0, w=['gSbf'])
                for n, ci in enumerate(orders[d]):
                    b = n % 2; c0 = ci * 64; tile = ci // 2; hf = ci % 2
                    K = lambda s: s + str(b)
                    self.dma('sp', qT[b][:], self.ZT[1536:1792, c0:c0 + 64].rearrange("(h k) t -> k h t", h=4), w=[K('gqT')])
                    self.dma('sp', kT[b][:], self.ZT[1792:2048, c0:c0 + 64].rearrange("(h k) t -> k h t", h=4), w=[K('gkT')])
                    self.dma('sp', lrT[b][0:32, :], self.ZT[4608:4640, c0:c0 + 64], w=[K('glr')])
                    zv = self.ZV[tile]
                    self.dma('sp', kTok[b][:], zv[hf * 64:(hf + 1) * 64, 0:256], w=[K('gkk')])
                    self.dma('sp', vTok[b][:], zv[hf * 64:(hf + 1) * 64, 256:768], w=[K('gvv')])
                    if d == 1:
                        self.dma('sp', gTok[b][:], zv[hf * 64:(hf + 1) * 64, 1024:1536], w=[K('ggg')])
                        self.dma('sp', ofl[b][:], self.OF[ci], w=[K('gof')])
                    self.mm(ps[0][0:64, 0:256], lrT[b][0:33, :], w2t[0:33, d * 256:(d + 1) * 256], True, True, r=[K('glr'), 'gw2'], w=['ps0'])
                    self.act(lsb[b][:], ps[0][0:64, 0:256], AF.Exp, scale=-1.0, r=['ps0'], w=[K('gls')])
                    self.act(lsb[b][:], lsb[b][:], AF.Ln, bias=1.0, r=[K('gls')], w=[K('gls')])
                    for h in range(4):
                        self.mm(ps[1][0:64, h * 64:(h + 1) * 64], lsb[b][:, h * 64:(h + 1) * 64], self.triT[d], True, True, r=[K('gls'), 'cft'], w=['ps1'])
                    self.mm(ps[2][0:64, 0:256], self.amT[d], lsb[b][:, 0:256], True, True, r=[K('gls'), 'cft'], w=['ps2'])
                    self.act(Eb[b][:], ps[1][0:64, 0:256], AF.Exp, r=['ps1'], w=[K('gEb')])
                    self.act(Enb[b][:], ps[1][0:64, 0:256], AF.Exp, scale=-1.0, r=['ps1'], w=[K('gEn')])
                    self.act(Ek[b][:], ps[2][0:64, 0:256], AF.Exp, r=['ps2'], w=[K('gEk')])
                    self.stt('dve', qe[b][:], qT[b][:], 0.125, Eb[b][:].rearrange("p (h t) -> p h t", h=4), ALU.mult, ALU.mult, r=[K('gqT'), K('gEb')], w=[K('gqe')])
                    self.tt('dve', ke[b][:], kT[b][:], Enb[b][:].rearrange("p (h t) -> p h t", h=4), ALU.mult, r=[K('gkT'), K('gEn')], w=[K('gke')])
                    self.tt('pool', kend[b][:], kTok[b][:], Ek[b][:], ALU.mult, r=[K('gkk'), K('gEk')], w=[K('gkd')])
                    self.cp('pool', vbf[b][:], vTok[b][:], r=[K('gvv')], w=[K('gvb')])
                    for h in range(4):
                        self.mm(ps[3][0:64, h * 64:(h + 1) * 64], ke[b][:, h, :], qe[b][:, h, :], True, True, r=[K('gke'), K('gqe')], w=['ps3'])
                    self.tt('dve', attT[b][:], ps[3][0:64, 0:256].rearrange("p (h t) -> p h t", h=4),
                            self.maskT[d].unsqueeze(1).broadcast_to([64, 4, 64]), ALU.mult, r=['ps3', 'cft'], w=[K('gat')])
                    for h in range(4):
                        self.mm(ps[4][0:64, h * 128:(h + 1) * 128], attT[b][:, h, :], vbf[b][:, h * 128:(h + 1) * 128], True, False, r=[K('gat'), K('gvb')], w=['ps4'])
                        self.mm(ps[4][0:64, h * 128:(h + 1) * 128], qe[b][:, h, :], Sbf[:, h * 128:(h + 1) * 128], False, True, r=[K('gqe'), 'gSbf'], w=['ps4'])
                    for h in range(4):
                        self.mm(ps[5][0:64, h * 128:(h + 1) * 128], kend[b][:, h * 64:(h + 1) * 64], vbf[b][:, h * 128:(h + 1) * 128], True, True, r=[K('gkd'), K('gvb')], w=['ps5'])
                    col = 63 if d == 0 else 0
                    for h in range(4):
                        self.stt('dve', St[:, h * 128:(h + 1) * 128], St[:, h * 128:(h + 1) * 128], Eb[b][:, h * 64 + col:h * 64 + col + 1],
                                 ps[5][0:64, h * 128:(h + 1) * 128], ALU.mult, ALU.add, r=['gS', K('gEb'), 'ps5'], w=['gS'])
                    self.cp('pool', Sbf[:], St[:], r=['gS'], w=['gSbf'])
                    if d == 0:
                        self.cp('act', ofl[b][:], ps[4][0:64, :], r=['ps4'], w=[K('gof')])
                        self.dma('pool', self.OF[ci], ofl[b][:], r=[K('gof')])
                    else:
                        self.tt('dve', ofl[b][:], ps[4][0:64, :], ofl[b][:], ALU.add, r=['ps4', K('gof')], w=[K('gof')])
                        o3 = ofl[b][:].rearrange("p (h v) -> p h v", h=4)
                        self.tt('dve', sq[:], o3, o3, ALU.mult, r=[K('gof')], w=['gsq'])
                        self.S.op('dve', lambda e, o=ss[:], i=sq[:]: e.tensor_reduce(out=o, in_=i, axis=AX.X, op=ALU.add), reads=['gsq'], writes=['gss'])
                        self.ts('dve', ss[:], ss[:], 1.0 / 128.0, LN_EPS, ALU.mult, ALU.add, r=['gss'], w=['gss'])
                        self.act(rstd[:], ss[:], AF.Ln, r=['gss'], w=['grs'])
                        self.act(rstd[:], rstd[:], AF.Exp, scale=-0.5, r=['grs'], w=['grs'])
                        for h in range(4):
                            self.stt('dve', on[:, h * 128:(h + 1) * 128], ofl[b][:, h * 128:(h + 1) * 128], rstd[:, h:h + 1], NW[:], ALU.mult, ALU.mult,
                                     r=[K('gof'), 'grs', 'gNW'], w=['gon'])
                        self.act(sg[:], gTok[b][:], AF.Silu, r=[K('ggg')], w=['gsg'])
                        self.tt('pool', res[:], on[:], sg[:], ALU.mult, r=['gon', 'gsg'], w=['gres'])
                        pb = self.psb(6)
                        for h in range(4):
                            self.tp(pb[:, h * 64:(h + 1) * 64], res[:, h * 128:(h + 1) * 128], self.identb[0:64, 0:64], r=['gres', 'identb'], w=['ps6'])
                        self.cp('act', mo[b][:], pb[:, 0:256].rearrange("p (h t) -> p h t", h=4), r=['ps6'], w=[K('gmo')])
                        self.dma('pool', self.MIXT[512:1024, c0:c0 + 64].rearrange("(h v) t -> v h t", h=4), mo[b][:], r=[K('gmo')])
                self.S.barrier()

    def stage_swa(self, l, last):
        cfg = self.cfg; NC, TT = cfg.NC, cfg.TT
        ps = self.ps
        scale = 128.0 ** -0.5
        with ExitStack() as es:
            sb = lambda n, s, d: self.sb(es, n, s, d)
            KT = sb("sKT", [128, cfg.TOK], BF16); VT = sb("sVT", [128, TT, 128], BF16)
            sk = sb("ssk", [128, 8], F32); sinkE = sb("ssinkE", [128, 8], F32)
            q4 = [sb("sq4%d" % i, [128, 4, 128], BF16) for i in range(2)]
            pT = [sb("spT%d" % i, [128, 4, 128], BF16) for i in range(3)]
            dn = sb("sdn", [128, 4, 128], F32); ob = [sb("sob%d" % i, [128, 4, 128], BF16) for i in range(2)]
            self.dma('sp', sk[:], self.sink[l], w=['ssk'])
            self.act(sinkE[:], sk[:], AF.Exp, r=['ssk'], w=['ssinkE'])
            n = 0; m = 0
            for g in range(2):
                self.dma('sp', KT[:], self.QKT[1024 + g * 128:1024 + (g + 1) * 128, :], w=['sKT'])
                for t0 in range(0, TT, 16):
                    t1 = min(TT, t0 + 16)
                    self.dma('pool', VT[:, t0:t1, :], self.ZV[t0:t1, :, 768 + g * 128:768 + (g + 1) * 128].rearrange("t p c -> p t c"), w=['sVT'])
                for qt in (range(NC, TT) if last else range(TT)):
                    if qt < NC:
                        keys = [(kt, None) for kt in range(NC)]
                    else:
                        keys = []
                        if qt - 1 >= NC: keys.append((qt - 1, self.maskP))
                        keys.append((qt, None))
                        if qt + 1 < TT: keys.append((qt + 1, self.maskN))
                        keys += [(kt, None) for kt in range(NC)]
                    b = n % 2; n += 1
                    po = 3 + 2 * b; pd = 4 + 2 * b
                    self.dma('sp', q4[b][:], self.QKT[g * 512:(g + 1) * 512, qt * 128:(qt + 1) * 128].rearrange("(j d) t -> d j t", j=4), w=['sq4%d' % b])
                    for ki, (kt, mask) in enumerate(keys):
                        pi = m % 3; m += 1
                        first = ki == 0; lastk = ki == len(keys) - 1
                        self.mm(ps[pi][:].rearrange("p (j t) -> p j t", j=4), KT[:, kt * 128:(kt + 1) * 128], q4[b][:], True, True, r=['sKT', 'sq4%d' % b], w=['ps%d' % pi])
                        self.act(pT[pi][:], ps[pi][:].rearrange("p (j t) -> p j t", j=4), AF.Exp, scale=scale, r=['ps%d' % pi], w=['spT%d' % pi])
                        if mask is not None:
                            self.tt('pool', pT[pi][:], pT[pi][:], mask.unsqueeze(1).broadcast_to([128, 4, 128]), ALU.mult, r=['spT%d' % pi, 'cft'], w=['spT%d' % pi])
                        self.mm(ps[po][:].rearrange("p (j t) -> p j t", j=4), VT[:, kt, :], pT[pi][:], first, lastk, r=['sVT', 'spT%d' % pi], w=['ps%d' % po])
                        self.mm(ps[pd][:].rearrange("p (j t) -> p j t", j=4), self.onesb[:], pT[pi][:], first, lastk, r=['onesb', 'spT%d' % pi], w=['ps%d' % pd])
                    self.tt('dve', dn[:], ps[pd][:].rearrange("p (j t) -> p j t", j=4), sinkE[:, g * 4:(g + 1) * 4].unsqueeze(2).broadcast_to([128, 4, 128]),
                            ALU.add, r=['ps%d' % pd, 'ssinkE'], w=['sdn'])
                    self.S.op('dve', lambda e, o=dn[:]: e.reciprocal(out=o, in_=o), reads=['sdn'], writes=['sdn'])
                    self.tt('dve', ob[b][:], ps[po][:].rearrange("p (j t) -> p j t", j=4), dn[:], ALU.mult, r=['ps%d' % po, 'sdn'], w=['sob%d' % b])
                    self.dma('pool', self.MIXT[1024 + g * 512:1024 + (g + 1) * 512, qt * 128:(qt + 1) * 128].rearrange("(j d) t -> d j t", j=4), ob[b][:], r=['sob%d' % b])
            self.S.barrier()

    def resid_ln(self, bufs, xsrc, banks, Gt, gkey, lngt, lnbt, dsts):
        xt, tq, st, mv, rs = bufs
        ps = self.ps
        self.dma('sp', xt[:], xsrc, w=['lx'])
        for nb in range(4):
            self.tt('dve', tq[:, nb * 512:(nb + 1) * 512], ps[banks[nb]][:], Gt[:, nb * 512:(nb + 1) * 512], ALU.mult, r=['ps%d' % banks[nb], gkey], w=['lt'])
        self.stt('dve', tq[:], xt[:], ALPHA, tq[:], ALU.mult, ALU.add, r=['lx', 'lt'], w=['lt'])
        for nb in range(4):
            self.S.op('dve', lambda e, o=st[:, nb, :], i=tq[:, nb * 512:(nb + 1) * 512]: e.bn_stats(out=o, in_=i), reads=['lt'], writes=['lst'])
        self.S.op('dve', lambda e, o=mv[:], i=st[:]: e.bn_aggr(out=o, in_=i), reads=['lst'], writes=['lmv'])
        self.act(rs[:], mv[:, 1:2], AF.Ln, bias=LN_EPS, r=['lmv'], w=['lrs'])
        self.act(rs[:], rs[:], AF.Exp, scale=-0.5, r=['lrs'], w=['lrs'])
        self.ts('dve', xt[:], tq[:], mv[:, 0:1], rs[:, 0:1], ALU.subtract, ALU.mult, r=['lt', 'lmv', 'lrs'], w=['lx'])
        self.tt('pool', xt[:], xt[:], lngt[:], ALU.mult, r=['lx', 'lng'], w=['lx'])
        self.tt('dve', xt[:], xt[:], lnbt[:], ALU.add, r=['lx', 'lnb'], w=['lx'])
        for dq, dst in zip(('sp', 'pool'), dsts):
            self.dma(dq, dst, xt[:], r=['lx'])

    def ln_bufs(self, es):
        sb = lambda n, s, d: self.sb(es, n, s, d)
        return (sb("lx", [128, D], F32), sb("lt", [128, D], F32), sb("lst", [128, 4, 6], F32), sb("lmv", [128, 2], F32), sb("lrs", [128, 1], F32))

    def stage_outproj(self, l, tiles):
        cfg = self.cfg
        with ExitStack() as es:
            sb = lambda n, s, d: self.sb(es, n, s, d)
            wot = sb("owo", [128, 16, D], BF16)
            lngt = sb("lng", [128, D], F32); lnbt = sb("lnb", [128, D], F32)
            Gt = [sb("oG%d" % i, [128, D], F32) for i in range(2)]
            mixT = [sb("omx%d" % i, [128, 16, 128], BF16) for i in range(2)]
            bufs = self.ln_bufs(es)
            for nb in range(4):
                self.dma('pool', wot[:, :, nb * 512:(nb + 1) * 512], self.wo[l][:, :, nb * 512:(nb + 1) * 512], w=['owo'])
            self.dma('sp', lngt[:], self.lng[l, 0], w=['lng']); self.dma('sp', lnbt[:], self.lnb[l, 0], w=['lnb'])
            for w in range(2):
                self.dma('sp', Gt[w][:], self.MOD[l, w, :, 2 * D:3 * D], w=['oG%d' % w])
            src = self.xs_in if l == 0 else self.XS
            for n, tt in enumerate(tiles):
                b = n % 2; w = 1 if tt < cfg.NC else 0
                self.dma('sp', mixT[b][:], self.MIXT[:, tt * 128:(tt + 1) * 128].rearrange("(k p) t -> p k t", p=128), w=['omx%d' % b])
                banks = [4 * b + i for i in range(4)]
                for nb in range(4):
                    for k in range(16):
                        self.mm(self.ps[banks[nb]][:], mixT[b][:, k, :], wot[:, k, nb * 512:(nb + 1) * 512], k == 0, k == 15, r=['omx%d' % b, 'owo'], w=['ps%d' % banks[nb]])
                self.resid_ln(bufs, src[tt], banks, Gt[w], 'oG%d' % w, lngt, lnbt, [self.XS[tt]])
            self.S.barrier()

    def stage_peer(self, l, tiles, last):
        cfg = self.cfg; NK = cfg.NK; NC = cfg.NC
        ps = self.ps
        tiles = list(tiles)
        pairs = [tiles[i:i + 2] for i in range(0, len(tiles), 2)]
        if not hasattr(self, 'RT'):
            self.RT = self.dscr("RT", [cfg.TT, 128, 3, 128], F32, dbg=True)
        with ExitStack() as es:
            sb = lambda n, s, d: self.sb(es, n, s, d)
            h2T = [sb("ph2T%d" % i, [128, 2, 16, 128], BF16) for i in range(2)]
            qTb = sb("pqTb", [128, 16, 256], BF16)
            wqr = sb("pwqr", [128, 16, D], BF16)
            sktt = sb("pskt", [128, 2, NK], BF16)
            ssb = sb("pssb", [128, 16, NK], F32); wk = sb("pwk", [128, 16, NK], F32)
            m8 = sb("pm8", [128, 16, 16], F32); i8 = sb("pi8", [128, 16, 16], U32); tif = sb("ptif", [128, 16, 16], F32)
            cand = sb("pcand", [128, 8, 256], F32); wk2 = sb("pwk2", [128, 8, 256], F32)
            b8 = sb("pb8", [128, 8, 16], F32); f8 = sb("pf8", [128, 8, 16], U32); ff = sb("pff", [128, 8, 16], F32)
            fa = sb("pfa", [128, 8, 16], F32); fb = sb("pfb", [128, 8, 16], F32)
            fau = sb("pfau", [128, 8, 16], U32); fbu = sb("pfbu", [128, 8, 16], U32)
            eq = sb("peq", [128, 8, 16, 16], F32)
            rt = [sb("prt%d" % i, [128, 3, 128], F32) for i in range(2)]
            ez = sb("pez", [128, 8, 16], F32); zz = sb("pzz", [128, 8], F32)
            self.dma('pool', sktt[:], self.skt[l], w=['pskt'])
            for blk in range(16):
                self.dma('pool', wqr[:, :, blk * 128:(blk + 1) * 128], self.wq[l, blk], w=['pwqr'])
            n = 0
            for pi_, pr in enumerate(pairs):
                hb = pi_ % 2
                for j, tt in enumerate(pr):
                    self.dma('sp', h2T[hb][:, j], self.HTB[tt], w=['ph2T%d' % hb])
                for blk in range(16):
                    pi = n % 8; n += 1
                    for k in range(16):
                        self.mm(ps[pi][:, 0:256].rearrange("p (j t) -> p j t", j=2), wqr[:, k, blk * 128:(blk + 1) * 128], h2T[hb][:, :, k, :], k == 0, k == 15,
                                r=['pwqr', 'ph2T%d' % hb], w=['ps%d' % pi])
                    self.cp('act' if blk % 2 else 'dve', qTb[:, blk, :], ps[pi][:, 0:256], r=['ps%d' % pi], w=['pqTb'])
                for j, tt in enumerate(pr):
                    rb = j
                    for hp in range(16):
                        bank = hp // 4
                        self.mm(ps[bank][:, (hp % 4) * NK:(hp % 4 + 1) * NK], qTb[:, hp, j * 128:(j + 1) * 128], sktt[:, hp % 2, :], True, True,
                                r=['pqTb', 'pskt'], w=['ps%d' % bank])
                    for bank in range(4):
                        self.cp('act', ssb[:, bank * 4:(bank + 1) * 4, :], ps[bank][:, 0:4 * NK].rearrange("p (a n) -> p a n", a=4), r=['ps%d' % bank], w=['pssb'])
                    V = self.S.op
                    for hp in range(16):
                        V('dve', lambda e, o=m8[:, hp, 0:8], i=ssb[:, hp, :]: e.max(out=o, in_=i), reads=['pssb'], writes=['pm8'])
                        V('dve', lambda e, o=i8[:, hp, 0:8], a=m8[:, hp, 0:8], i=ssb[:, hp, :]: e.max_index(out=o, in_max=a, in_values=i), reads=['pssb', 'pm8'], writes=['pi8'])
                        V('dve', lambda e, o=wk[:, hp, :], a=m8[:, hp, 0:8], i=ssb[:, hp, :]: e.match_replace(out=o, in_to_replace=a, in_values=i, imm_value=-1e30), reads=['pssb', 'pm8'], writes=['pwk'])
                        V('dve', lambda e, o=m8[:, hp, 8:16], i=wk[:, hp, :]: e.max(out=o, in_=i), reads=['pwk'], writes=['pm8'])
                        V('dve', lambda e, o=i8[:, hp, 8:16], a=m8[:, hp, 8:16], i=wk[:, hp, :]: e.max_index(out=o, in_max=a, in_values=i), reads=['pwk', 'pm8'], writes=['pi8'])
                    self.cp('dve', tif[:], i8[:], r=['pi8'], w=['ptif'])
                    tv4 = m8[:].rearrange("q (h p) k -> q h p k", p=2); ti4 = tif[:].rearrange("q (h p) k -> q h p k", p=2)
                    c4 = cand[:].rearrange("q h (a b) -> q h a b", a=16)
                    self.tt('dve', c4, tv4[:, :, 0, :].unsqueeze(3).broadcast_to([128, 8, 16, 16]), tv4[:, :, 1, :].unsqueeze(2).broadcast_to([128, 8, 16, 16]),
                            ALU.add, r=['pm8'], w=['pcand'])
                    for h in range(8):
                        V('dve', lambda e, o=b8[:, h, 0:8], i=cand[:, h, :]: e.max(out=o, in_=i), reads=['pcand'], writes=['pb8'])
                        V('dve', lambda e, o=f8[:, h, 0:8], a=b8[:, h, 0:8], i=cand[:, h, :]: e.max_index(out=o, in_max=a, in_values=i), reads=['pcand', 'pb8'], writes=['pf8'])
                        V('dve', lambda e, o=wk2[:, h, :], a=b8[:, h, 0:8], i=cand[:, h, :]: e.match_replace(out=o, in_to_replace=a, in_values=i, imm_value=-1e30), reads=['pcand', 'pb8'], writes=['pwk2'])
                        V('dve', lambda e, o=b8[:, h, 8:16], i=wk2[:, h, :]: e.max(out=o, in_=i), reads=['pwk2'], writes=['pb8'])
                        V('dve', lambda e, o=f8[:, h, 8:16], a=b8[:, h, 8:16], i=wk2[:, h, :]: e.max_index(out=o, in_max=a, in_values=i), reads=['pwk2', 'pb8'], writes=['pf8'])
                    self.ts('dve', fau[:], f8[:], 4, None, ALU.logical_shift_right, r=['pf8'], w=['pfau'])
                    self.ts('dve', fbu[:], f8[:], 15, None, ALU.bitwise_and, r=['pf8'], w=['pfbu'])
                    self.cp('dve', fa[:], fau[:], r=['pfau'], w=['pfa'])
                    self.cp('dve', fb[:], fbu[:], r=['pfbu'], w=['pfb'])
                    io4 = self.iota16.unsqueeze(1).unsqueeze(1).broadcast_to([128, 8, 16, 16])
                    for which, (fsel, pp) in enumerate(((fa, 0), (fb, 1))):
                        self.tt('dve', eq[:], fsel[:].unsqueeze(3).broadcast_to([128, 8, 16, 16]), io4, ALU.is_equal, r=['pfa', 'pfb', 'cft'], w=['peq'])
                        self.tt('pool', eq[:], eq[:], ti4[:, :, pp, :].unsqueeze(2).broadcast_to([128, 8, 16, 16]), ALU.mult, r=['peq', 'ptif'], w=['peq'])
                        V('dve', lambda e, o=rt[rb][:, which, :].rearrange("q (h k) -> q h k", h=8), i=eq[:]: e.tensor_reduce(out=o, in_=i, axis=AX.X, op=ALU.add),
                          reads=['peq'], writes=['prt%d' % rb])
                    self.tt('dve', ez[:], b8[:], b8[:, :, 0:1].broadcast_to([128, 8, 16]), ALU.subtract, r=['pb8'], w=['pez'])
                    self.act(ez[:], ez[:], AF.Exp, r=['pez'], w=['pez'])
                    V('dve', lambda e, o=zz[:], i=ez[:]: e.tensor_reduce(out=o, in_=i, axis=AX.X, op=ALU.add), reads=['pez'], writes=['pzz'])
                    V('dve', lambda e, o=zz[:]: e.reciprocal(out=o, in_=o), reads=['pzz'], writes=['pzz'])
                    self.tt('dve', rt[rb][:, 2, :].rearrange("q (h k) -> q h k", h=8), ez[:], zz[:].unsqueeze(2).broadcast_to([128, 8, 16]), ALU.mult,
                            r=['pez', 'pzz'], w=['prt%d' % rb])
                    self.dma('pool', self.RT[tt], rt[rb][:], r=['prt%d' % rb])
            self.S.barrier()
        G = 8
        with ExitStack() as es:
            sb = lambda n, s, d: self.sb(es, n, s, d)
            WT = sb("dWT", [128, NK, 256], BF16)
            h2T = sb("dh2T", [128, 2, 16, 128], BF16)
            utc = [sb("dut%d" % i, [128, 16, NK], BF16) for i in range(6)]
            vch = [sb("dvc%d" % i, [128, D], BF16) for i in range(6)]
            rtt = sb("drt", [128, 3, 128], F32); rT = sb("drT", [128, 3, 128], F32)
            oh2 = [sb("doh2%d" % i, [128, G, NK], BF16) for i in range(2)]
            oh1 = [sb("doh1%d" % i, [128, G, NK], BF16) for i in range(2)]
            ga = [sb("dga%d" % i, [128, 256], F32) for i in range(2)]
            lngt = sb("lng", [128, D], F32); lnbt = sb("lnb", [128, D], F32)
            Gt = sb("dG", [128, D], F32)
            bufs = self.ln_bufs(es)
            self.dma('sp', lngt[:], self.lng[l, 1], w=['lng']); self.dma('sp', lnbt[:], self.lnb[l, 1], w=['lnb'])
            io3 = self.iota128[:, 0:NK].unsqueeze(1).broadcast_to([128, G, NK])
            curw = None; n = 0; m = 0
            for pr in pairs:
                w = 1 if pr[0] < NC else 0
                if w != curw:
                    self.dma('sp', Gt[:], self.MOD[l, w, :, 5 * D:6 * D], w=['dG']); curw = w
                for j, tt in enumerate(pr):
                    self.dma('sp', h2T[:, j], self.HTB[tt], w=['dh2T'])
                for j, tt in enumerate(pr):
                    self.dma('sp', rtt[:], self.RT[tt], w=['drt'])
                    for a in range(3):
                        self.tp(ps[a][:, 0:128], rtt[:, a, :], self.identf, r=['drt', 'cft'], w=['ps%d' % a])
                        self.cp('act', rT[:, a, :], ps[a][:, 0:128], r=['ps%d' % a], w=['drT'])
                    for t0 in range(0, 128, G):
                        ob = m % 2; m += 1
                        self.tt('dve', oh2[ob][:], io3, rT[:, 1, t0:t0 + G].unsqueeze(2).broadcast_to([128, G, NK]), ALU.is_equal, r=['cft', 'drT'], w=['doh2%d' % ob])
                        self.tt('dve', oh1[ob][:], io3, rT[:, 0, t0:t0 + G].unsqueeze(2).broadcast_to([128, G, NK]), ALU.is_equal, r=['cft', 'drT'], w=['doh1%d' % ob])
                        self.tt('pool', oh1[ob][:], oh1[ob][:], rT[:, 2, t0:t0 + G].unsqueeze(2).broadcast_to([128, G, NK]), ALU.mult, r=['doh1%d' % ob, 'drT'], w=['doh1%d' % ob])
                        for q0 in range(0, G, 4):
                            pi = 3 + (n % 5); n += 1
                            for q in range(4):
                                self.mm(ps[pi][0:NK, q * NK:(q + 1) * NK], oh2[ob][:, q0 + q, :], oh1[ob][:, q0 + q, :], True, True,
                                        r=['doh2%d' % ob, 'doh1%d' % ob], w=['ps%d' % pi])
                            tb = j * 128 + t0 + q0
                            self.cp('act', WT[0:NK, :, tb:tb + 4].rearrange("p i t -> p t i"), ps[pi][0:NK, 0:4 * NK].rearrange("p (t i) -> p t i", t=4),
                                    r=['ps%d' % pi], w=['dWT'])
                for i1 in range(NK):
                    ub = i1 % 6; pi = i1 % 8; gb = i1 % 2
                    self.dma('sp', utc[ub][:], self.UTB[l, i1], w=['dut%d' % ub])
                    for k in range(16):
                        self.mm(ps[pi][0:NK, 0:256].rearrange("p (j t) -> p j t", j=2), utc[ub][:, k, :], h2T[:, :, k, :], k == 0, k == 15,
                                r=['dut%d' % ub, 'dh2T'], w=['ps%d' % pi])
                    self.act(ga[gb][0:NK, :], ps[pi][0:NK, 0:256], AF.Gelu, r=['ps%d' % pi], w=['dga%d' % gb])
                    self.tt('dve' if i1 % 2 else 'pool', WT[0:NK, i1, :], WT[0:NK, i1, :], ga[gb][0:NK, :], ALU.mult, r=['dWT', 'dga%d' % gb], w=['dWT'])
                for i1 in range(NK):
                    vb = i1 % 6
                    self.dma('act', vch[vb][0:NK, :], self.VB[l, i1 * NK:(i1 + 1) * NK, :], w=['dvc%d' % vb])
                    for j in range(len(pr)):
                        for nb in range(4):
                            self.mm(ps[j * 4 + nb][:], WT[0:NK, i1, j * 128:(j + 1) * 128], vch[vb][0:NK, nb * 512:(nb + 1) * 512], i1 == 0, i1 == NK - 1,
                                    r=['dWT', 'dvc%d' % vb], w=['ps%d' % (j * 4 + nb)])
                for j, tt in enumerate(pr):
                    dsts = [self.XS[tt]]
                    if last:
                        dsts = [self.out[tt - NC]]
                    self.resid_ln(bufs, self.XS[tt], [j * 4 + i for i in range(4)], Gt, 'dG', lngt, lnbt, dsts)
            self.S.barrier()


_CACHE = {}


def run(inputs, cfg, core_ids=None):
    maps = prep_inputs(inputs, cfg)
    key = (cfg.L, cfg.LC, cfg.DEPTH, cfg.NK, cfg.dbg)
    if key not in _CACHE:
        p = Prog(cfg); p.build(); _CACHE[key] = p
    p = _CACHE[key]
    res = run_bass_kernel_spmd(p.nc, maps, core_ids=list(range(len(maps))))
    outs = [r["out"].reshape(cfg.L, D) for r in res.results]
    return np.stack(outs).astype(np.float32), res, p


def kernel(**inputs):
    cfg = Cfg()
    out, _, _ = run(inputs, cfg)
    return out
```

```python
import numpy as np
import ml_dtypes
from contextlib import ExitStack
import concourse.bass as bass
import concourse.mybir as mybir
from concourse.bass_utils import run_bass_kernel_spmd

F32 = mybir.dt.float32; BF16 = mybir.dt.bfloat16; U32 = mybir.dt.uint32
ALU = mybir.AluOpType; AF = mybir.ActivationFunctionType; AX = mybir.AxisListType

D = 2048
ALPHA = (2.0 * 4) ** 0.25
LN_EPS = 1e-6
GRID_W = 64


class Cfg:
    def __init__(self, L=8192, LC=256, DEPTH=4, NK=128, dbg=False):
        self.L = L; self.LC = LC; self.DEPTH = DEPTH; self.NK = NK; self.dbg = dbg
        self.NC = LC // 128; self.NL = L // 128; self.TT = self.NC + self.NL
        self.TOK = L + LC
        self.NE = NK * NK


class Sched:
    ENG = ['pe', 'dve', 'act', 'pool', 'sp']
    CE = ['pe', 'dve', 'act', 'pool']
    EPOCH = 30000
    NEP = {'pe': 26, 'dve': 5, 'act': 4, 'pool': 3}

    def __init__(self, nc, es, ndma=48):
        self.nc = nc
        self.esem = {e: [es.enter_context(nc.semaphore('s_%s%d' % (e, i))) for i in range(self.NEP[e])] for e in self.CE}
        self.dsem = [es.enter_context(nc.semaphore('d%d' % i)) for i in range(ndma)]
        self.dcnt = [0] * ndma; self.dnext = 0
        self.cnt = {e: 0 for e in self.CE}; self.ep = {e: 0 for e in self.CE}
        self.prog = {e: [] for e in self.ENG}
        self.seen = {e: {} for e in self.ENG}
        self.res = {}
        self.ninst = 0

    def semh(self, sid):
        return self.esem[sid[1]][sid[2]] if sid[0] == 'e' else self.dsem[sid[1]]

    def _need(self, eng, sid, val, waits):
        s = self.seen[eng]
        if sid[0] == 'e':
            k = ('e', sid[1]); cur = s.get(k, (-1, 0)); new = (sid[2], val)
            if cur >= new:
                return
            s[k] = new
        else:
            if s.get(sid, 0) >= val:
                return
            s[sid] = val
        waits.append((sid, val))

    def _deps(self, eng, reads, writes):
        waits = []
        pe = eng == 'pe'
        for k in reads:
            st = self.res.get(k)
            if st:
                for sid, v in st[0].items():
                    if not (pe and sid[0] == 'e' and sid[1] == 'pe'):
                        self._need(eng, sid, v, waits)
        for k in writes:
            st = self.res.get(k)
            if st:
                for d in st:
                    for sid, v in d.items():
                        if not (pe and sid[0] == 'e' and sid[1] == 'pe'):
                            self._need(eng, sid, v, waits)
        return waits

    def _mark(self, ev, reads, writes):
        sid, v = ev
        for k in reads:
            self.res.setdefault(k, ({}, {}))[1][sid] = v
        for k in writes:
            self.res.setdefault(k, ({}, {}))[0][sid] = v

    def op(self, eng, fn, reads=(), writes=()):
        waits = self._deps(eng, reads, writes)
        if self.cnt[eng] >= self.EPOCH:
            self.ep[eng] += 1; self.cnt[eng] = 0
        self.cnt[eng] += 1; ev = (('e', eng, self.ep[eng]), self.cnt[eng])
        self.prog[eng].append((waits, fn, ev)); self._mark(ev, reads, writes); self.ninst += 1

    def dma(self, eng, fn, reads=(), writes=()):
        slot = self.dnext; self.dnext = (self.dnext + 1) % len(self.dsem)
        waits = self._deps(eng, reads, writes)
        if self.dcnt[slot] > 0:
            self._need(eng, ('d', slot), self.dcnt[slot], waits)
        self.dcnt[slot] += 16; ev = (('d', slot), self.dcnt[slot])
        self.prog[eng].append((waits, fn, ev)); self._mark(ev, reads, writes); self.ninst += 1

    def barrier(self):
        for e in self.ENG:
            waits = []
            for i, c in enumerate(self.dcnt):
                if c:
                    self._need(e, ('d', i), c, waits)
            for o in self.CE:
                if o != e and (self.cnt[o] or self.ep[o]):
                    self._need(e, ('e', o, self.ep[o]), self.cnt[o], waits)
            if waits:
                self.prog[e].append((waits, None, None))
        self.res = {}

    def emit(self, block):
        engmap = {'pe': 'tensor', 'dve': 'vector', 'act': 'scalar', 'pool': 'gpsimd', 'sp': 'sync'}
        for e in self.ENG:
            prog = self.prog[e]

            def body(engine, prog=prog):
                for waits, fn, ev in prog:
                    for sid, v in waits:
                        engine.wait_ge(self.semh(sid), v)
                    if fn is None:
                        continue
                    ins = fn(engine)
                    sid, v = ev
                    ins.then_inc(self.semh(sid), 16 if sid[0] == 'd' else 1)
            getattr(block, engmap[e])(body)


def _blk_stationary(w):
    n = w.shape[1] // 128
    return np.ascontiguousarray(w.reshape(16, 128, n, 128).transpose(2, 1, 0, 3))


def _blk_moving(w, width=512):
    n = w.shape[1] // width
    return np.ascontiguousarray(w.reshape(16, 128, n, width).transpose(2, 1, 0, 3))


def rope_perm():
    d = np.arange(128)
    partner = np.where((d % 64) < 32, d + 32, d - 32)
    sign = np.where((d % 64) < 32, -1.0, 1.0).astype(np.float32)
    return partner, sign


def const_tables(cfg):
    cf = np.zeros((128, 912), np.float32)
    cf[:, 0:128] = np.eye(128, dtype=np.float32)
    cf[:, 128:256] = np.arange(128, dtype=np.float32)[None, :]
    s = -1.0 / 16.0
    j = np.arange(64)[:, None]; i = np.arange(64)[None, :]
    cf[:64, 256:320] = np.where(j <= i, s, 0.0)
    cf[:64, 320:384] = np.where(j >= i, s, 0.0)
    cf[:64, 384:448] = np.where(j > i, s, 0.0)
    cf[:64, 448:512] = np.where(j < i, s, 0.0)
    cf[:64, 512:576] = np.where(j <= i, 1.0, 0.0)
    cf[:64, 576:640] = np.where(j >= i, 1.0, 0.0)
    cf[:, 640:656] = np.arange(16, dtype=np.float32)[None, :]
    jj = np.arange(128)[:, None]; ii = np.arange(128)[None, :]
    cf[:, 656:784] = np.where(jj >= ii, 1.0, 0.0)
    cf[:, 784:912] = np.where(jj <= ii, 1.0, 0.0)
    t = np.arange(cfg.L)
    r = (t // GRID_W).astype(np.float32); c = (t % GRID_W).astype(np.float32)
    inv = (10000.0 ** (-np.arange(0, 64, 2, dtype=np.float32) / 64.0)).astype(np.float32)
    ang_r = r[None, :] * inv[:, None]; ang_c = c[None, :] * inv[:, None]
    d = np.arange(128)
    ang = np.where((d < 64)[:, None], ang_r[d % 32], ang_c[d % 32]).astype(np.float32)
    _, sign = rope_perm()
    cos = np.cos(ang).astype(np.float32); sins = (np.sin(ang).astype(np.float32) * sign[:, None])
    return cf, np.ascontiguousarray(cos), np.ascontiguousarray(sins)


def prep_inputs(inp, cfg):
    f = lambda a: np.ascontiguousarray(np.asarray(a, dtype=np.float32))
    x = f(inp['x']); c = f(inp['c']); ctx = f(inp['ctx']); c_ctx = f(inp['c_ctx'])
    w_ada = f(inp['w_ada']); b_ada = f(inp['b_ada']); w_in = f(inp['w_in'])
    NL_ = cfg.DEPTH
    cf, cos, sins = const_tables(cfg)
    partner, _ = rope_perm()
    wa = np.stack([_blk_moving(w_ada[l]) for l in range(NL_)])
    ba = np.ascontiguousarray(b_ada[:NL_].reshape(NL_, 24, 1, 512))
    wf_l, wt_l = [], []
    for l in range(NL_):
        w = w_in[l]
        sq = w[:, 3104:4128].reshape(D, 8, 128); sk = w[:, 4128:4384].reshape(D, 2, 128)
        lr = np.zeros((D, 128), np.float32); lr[:, :32] = w[:, 3072:3104]
        fm = np.concatenate([w[:, 0:1536], w[:, 1536:2048], sq.reshape(D, 1024), sq[:, :, partner].reshape(D, 1024),
                             sk.reshape(D, 256), sk[:, :, partner].reshape(D, 256), lr], axis=1)
        wf_l.append(_blk_stationary(fm))
        tm = np.concatenate([w[:, 1792:2048], w[:, 2048:2560], w[:, 4384:4640], w[:, 2560:3072]], axis=1)
        wt_l.append(_blk_moving(tm))
    wf = np.stack(wf_l); wt = np.stack(wt_l)
    conv_w = f(inp['conv_w'])[:NL_]
    cw = np.ascontiguousarray(conv_w.reshape(NL_, 3, 4, 128).transpose(0, 3, 2, 1))
    w2 = f(inp['gla_w2'])[:NL_]; b2 = f(inp['gla_b2'])[:NL_]
    w2a = np.zeros((NL_, 33, 512), np.float32)
    w2a[:, 0:16, 0:256] = w2[:, 0]; w2a[:, 16:32, 256:512] = w2[:, 1]
    w2a[:, 32, 0:256] = b2[:, 0]; w2a[:, 32, 256:512] = b2[:, 1]
    gnorm = np.ascontiguousarray(np.broadcast_to(f(inp['gla_norm'])[:NL_, None, :], (NL_, 128, 128)))
    sink = np.ascontiguousarray(np.broadcast_to(f(inp['swa_sink'])[:NL_, None, :], (NL_, 128, 8)))
    w_out = f(inp['w_out'])[:NL_]
    wo = np.ascontiguousarray(w_out.reshape(NL_, 16, 128, D).transpose(0, 2, 1, 3))
    lng = np.ascontiguousarray(np.broadcast_to(f(inp['ln_g'])[:NL_, :, None, :], (NL_, 2, 128, D)))
    lnb = np.ascontiguousarray(np.broadcast_to(f(inp['ln_b'])[:NL_, :, None, :], (NL_, 2, 128, D)))
    wq = np.stack([_blk_stationary(f(inp['peer_wq'])[l]) for l in range(NL_)])
    sk_ = f(inp['peer_subkeys'])[:NL_]
    skt = np.ascontiguousarray(sk_.transpose(0, 3, 1, 2))
    NK = cfg.NK
    pu = f(inp['peer_u'])[:NL_]
    ut = np.ascontiguousarray(pu.reshape(NL_, NK, NK, 16, 128).transpose(0, 1, 4, 3, 2))
    pv = f(inp['peer_v'])[:NL_]
    maps = []
    for b in range(x.shape[0]):
        cc = np.stack([c[b].reshape(16, 128).T, c_ctx.reshape(16, 128).T])
        xs = np.concatenate([ctx[b], x[b]], axis=0).reshape(cfg.TT, 128, D)
        maps.append(dict(xs_in=np.ascontiguousarray(xs), cc=np.ascontiguousarray(cc), wa=wa, ba=ba, wf=wf, wt=wt,
                         cw=cw, w2a=w2a, gnorm=gnorm, sink=sink, wo=wo, lng=lng, lnb=lnb, wq=wq, skt=skt,
                         ut=ut, pv=pv, cf=cf, cos=cos, sins=sins))
    return maps


class Prog:
    def __init__(self, cfg):
        self.cfg = cfg
        self.nc = bass.Bass("TRN2", target_bir_lowering=False)
        self.dbg_outs = []

    def din(self, name, shape, dt=F32):
        return self.nc.dram_tensor(name, list(shape), dt, kind="ExternalInput").ap()

    def dscr(self, name, shape, dt=F32, dbg=False):
        if dbg and self.cfg.dbg:
            self.dbg_outs.append(name)
            return self.nc.dram_tensor(name, list(shape), dt, kind="ExternalOutput").ap()
        return self.nc.dram_tensor(name, list(shape), dt).ap()

    def sb(self, es, name, shape, dt):
        self.uid = getattr(self, 'uid', 0) + 1
        return es.enter_context(self.nc.sbuf_tensor("%s_u%d" % (name, self.uid), list(shape), dt))

    def dma(self, q, out, in_, r=(), w=()):
        self.S.dma(q, lambda e, o=out, i=in_: e.dma_start(out=o, in_=i), reads=r, writes=w)

    def mm(self, out, lhsT, rhs, start, stop, r=(), w=()):
        self.S.op('pe', lambda e, o=out, l=lhsT, rh=rhs, s=start, t=stop: e.matmul(o, lhsT=l, rhs=rh, start=s, stop=t), reads=r, writes=w)

    def tp(self, out, in_, ident, r=(), w=()):
        self.S.op('pe', lambda e, o=out, i=in_, d=ident: e.transpose(out=o, in_=i, identity=d), reads=r, writes=w)

    def tt(self, eng, out, in0, in1, op, r=(), w=()):
        self.S.op(eng, lambda e, o=out, a=in0, b=in1, p=op: e.tensor_tensor(out=o, in0=a, in1=b, op=p), reads=r, writes=w)

    def ts(self, eng, out, in0, s1, s2, op0, op1=None, r=(), w=()):
        if op1 is None:
            self.S.op(eng, lambda e, o=out, a=in0, x=s1, p=op0: e.tensor_scalar(out=o, in0=a, scalar1=x, scalar2=None, op0=p), reads=r, writes=w)
        else:
            self.S.op(eng, lambda e, o=out, a=in0, x=s1, y=s2, p=op0, q=op1: e.tensor_scalar(out=o, in0=a, scalar1=x, scalar2=y, op0=p, op1=q), reads=r, writes=w)

    def stt(self, eng, out, in0, scalar, in1, op0, op1, r=(), w=()):
        self.S.op(eng, lambda e, o=out, a=in0, s=scalar, b=in1, p=op0, q=op1: e.scalar_tensor_tensor(out=o, in0=a, scalar=s, in1=b, op0=p, op1=q), reads=r, writes=w)

    def act(self, out, in_, func, r=(), w=(), bias=None, scale=None, accum=None):
        kw = {}
        if bias is not None: kw['bias'] = bias
        if scale is not None: kw['scale'] = scale
        if accum is not None: kw['accum_out'] = accum
        self.S.op('act', lambda e, o=out, i=in_, f=func, kw=kw: e.activation(out=o, in_=i, func=f, **kw), reads=r, writes=w)

    def cp(self, eng, out, in_, r=(), w=()):
        if eng == 'act':
            self.S.op('act', lambda e, o=out, i=in_: e.copy(out=o, in_=i), reads=r, writes=w)
        else:
            self.S.op(eng, lambda e, o=out, i=in_: e.tensor_copy(out=o, in_=i), reads=r, writes=w)

    def memset(self, eng, ap, val, w=()):
        self.S.op(eng, lambda e, a=ap, v=val: e.memset(a, v), writes=w)

    def build(self):
        cfg = self.cfg; nc = self.nc
        TT, NLY, NK = cfg.TT, cfg.DEPTH, cfg.NK
        self.xs_in = self.din("xs_in", [TT, 128, D])
        self.cc = self.din("cc", [2, 128, 16])
        self.wa = self.din("wa", [NLY, 24, 128, 16, 512]); self.ba = self.din("ba", [NLY, 24, 1, 512])
        self.wf = self.din("wf", [NLY, 37, 128, 16, 128]); self.wt = self.din("wt", [NLY, 3, 128, 16, 512])
        self.cw = self.din("cw", [NLY, 128, 4, 3]); self.w2a = self.din("w2a", [NLY, 33, 512])
        self.gnorm = self.din("gnorm", [NLY, 128, 128]); self.sink = self.din("sink", [NLY, 128, 8])
        self.wo = self.din("wo", [NLY, 128, 16, D])
        self.lng = self.din("lng", [NLY, 2, 128, D]); self.lnb = self.din("lnb", [NLY, 2, 128, D])
        self.wq = self.din("wq", [NLY, 16, 128, 16, 128]); self.skt = self.din("skt", [NLY, 128, 2, NK])
        self.ut = self.din("ut", [NLY, NK, 128, 16, NK]); self.pv = self.din("pv", [NLY, cfg.NE, D])
        self.cf = self.din("cf", [128, 912]); self.cos = self.din("cos", [128, cfg.L]); self.sins = self.din("sins", [128, cfg.L])
        self.out = nc.dram_tensor("out", [cfg.NL, 128, D], F32, kind="ExternalOutput").ap()
        self.XS = self.dscr("XS", [TT, 128, D], F32, dbg=True)
        self.MOD = self.dscr("MOD", [NLY, 2, 128, 6 * D], F32, dbg=True)
        self.HTB = self.dscr("HTB", [TT, 128, 16, 128], BF16)
        self.ZT = self.dscr("ZT", [4736, cfg.TOK], F32, dbg=True)
        self.ZV = self.dscr("ZV", [TT, 128, 1536], F32, dbg=True)
        self.QKT = self.dscr("QKT", [1280, cfg.TOK], BF16)
        self.MIXT = self.dscr("MIXT", [D, cfg.TOK], BF16, dbg=True)
        self.OF = self.dscr("OF", [cfg.TOK // 64, 64, 512], F32)
        self.WFB = self.dscr("WFB", [NLY, 37, 128, 16, 128], BF16)
        self.UTB = self.dscr("UTB", [NLY, NK, 128, 16, NK], BF16)
        self.VB = self.dscr("VB", [NLY, cfg.NE, D], BF16)
        es = ExitStack()
        with es:
            self.S = Sched(nc, es)
            self.ps = [es.enter_context(nc.psum_tensor("ps%d" % i, [128, 512], F32)) for i in range(8)]
            self.consts(es)
            self.stage_prep()
            self.stage_adaln()
            for l in range(NLY):
                last = (l == NLY - 1)
                self.stage_modT(l, 0, range(TT))
                self.stage_inproj(l)
                self.stage_rope(l)
                self.stage_conv(l)
                self.stage_gla(l)
                self.stage_swa(l, last)
                tiles = range(cfg.NC, TT) if last else range(TT)
                self.stage_outproj(l, tiles)
                self.stage_modT(l, 1, tiles)
                self.stage_peer(l, tiles, last)
            self.S.barrier()
            with nc.Block() as block:
                self.S.emit(block)
        return nc

    def consts(self, es):
        self.cft = self.sb(es, "cft", [128, 912], F32)
        self.identb = self.sb(es, "identb", [128, 128], BF16)
        self.onesb = self.sb(es, "onesb", [128, 128], BF16)
        self.dma('sp', self.cft[:], self.cf, w=['cft'])
        self.cp('dve', self.identb[:], self.cft[:, 0:128], r=['cft'], w=['identb'])
        self.memset('dve', self.onesb[:], 1.0, w=['onesb'])
        c = self.cft
        self.identf = c[:, 0:128]; self.iota128 = c[:, 128:256]
        self.triT = [c[0:64, 256:320], c[0:64, 320:384]]
        self.amT = [c[0:64, 384:448], c[0:64, 448:512]]
        self.maskT = [c[0:64, 512:576], c[0:64, 576:640]]
        self.iota16 = c[:, 640:656]
        self.maskP = c[:, 656:784]; self.maskN = c[:, 784:912]
        self.S.barrier()

    def psb(self, i):
        return self.ps[i][:].bitcast(BF16)

    def stage_prep(self):
        cfg = self.cfg; NK = cfg.NK
        for l in range(cfg.DEPTH):
            src = self.wf[l].rearrange("i p k e -> (i p) (k e)"); dst = self.WFB[l].rearrange("i p k e -> (i p) (k e)")
            for r0 in range(0, 37 * 128, 1024):
                r1 = min(37 * 128, r0 + 1024)
                self.dma('pool', dst[r0:r1, :], src[r0:r1, :])
        for l in range(cfg.DEPTH):
            src = self.ut[l].rearrange("i p k e -> (i p) (k e)"); dst = self.UTB[l].rearrange("i p k e -> (i p) (k e)")
            rows = NK * 128
            for r0 in range(0, rows, 1024):
                r1 = min(rows, r0 + 1024)
                self.dma('pool', dst[r0:r1, :], src[r0:r1, :])
            for r0 in range(0, cfg.NE, 1024):
                r1 = min(cfg.NE, r0 + 1024)
                self.dma('pool', self.VB[l][r0:r1, :], self.pv[l][r0:r1, :])
        self.S.barrier()

    def stage_adaln(self):
        cfg = self.cfg
        with ExitStack() as es:
            ccs = self.sb(es, "ccs", [128, 2, 16], F32); cs = self.sb(es, "cs", [128, 2, 16], F32)
            rep = self.sb(es, "rep", [128, 2, 16, 128], BF16)
            wat = [self.sb(es, "wat%d" % i, [128, 16, 512], BF16) for i in range(2)]
            bat = [self.sb(es, "bat%d" % i, [1, 512], BF16) for i in range(2)]
            mo = [self.sb(es, "mo%d" % i, [128, 512], F32) for i in range(4)]
            self.dma('sp', ccs[:], self.cc.rearrange("w p k -> p w k"), w=['ccs'])
            self.act(cs[:], ccs[:], AF.Silu, r=['ccs'], w=['cs'])
            for w in range(2):
                self.cp('dve', rep[:, w], cs[:, w, :].unsqueeze(2).broadcast_to([128, 16, 128]), r=['cs'], w=['rep'])
            n = 0
            for l in range(cfg.DEPTH):
                for cb in range(24):
                    b = (l * 24 + cb) % 2
                    self.dma('pool', wat[b][:], self.wa[l, cb], w=['wat%d' % b])
                    self.dma('pool', bat[b][:], self.ba[l, cb], w=['bat%d' % b])
                    for w in range(2):
                        pi = n % 8; m = n % 4; n += 1
                        for k in range(16):
                            self.mm(self.ps[pi][:], rep[:, w, k, :], wat[b][:, k, :], k == 0, False, r=['rep', 'wat%d' % b], w=['ps%d' % pi])
                        self.mm(self.ps[pi][:], self.onesb[0:1, :], bat[b][0:1, :], False, True, r=['onesb', 'bat%d' % b], w=['ps%d' % pi])
                        if cb // 4 in (1, 4):
                            self.ts('dve', mo[m][:], self.ps[pi][:], 1.0, None, ALU.add, r=['ps%d' % pi], w=['mo%d' % m])
                        else:
                            self.cp('act', mo[m][:], self.ps[pi][:], r=['ps%d' % pi], w=['mo%d' % m])
                        self.dma('sp', self.MOD[l, w, :, cb * 512:(cb + 1) * 512], mo[m][:], r=['mo%d' % m])
            self.S.barrier()

    def stage_modT(self, l, which, tiles):
        cfg = self.cfg
        with ExitStack() as es:
            xs = [self.sb(es, "mx%d" % i, [128, D], F32) for i in range(2)]
            tmp = self.sb(es, "mtmp", [128, D], F32)
            hb = [self.sb(es, "mhb%d" % i, [128, D], BF16) for i in range(2)]
            hts = [self.sb(es, "mhts%d" % i, [128, 16, 128], BF16) for i in range(2)]
            sct = [self.sb(es, "msc%d" % i, [128, D], F32) for i in range(2)]
            sht = [self.sb(es, "msh%d" % i, [128, D], F32) for i in range(2)]
            o_sh = 0 if which == 0 else 3 * D
            for w in range(2):
                self.dma('sp', sht[w][:], self.MOD[l, w, :, o_sh:o_sh + D], w=['msh%d' % w])
                self.dma('sp', sct[w][:], self.MOD[l, w, :, o_sh + D:o_sh + 2 * D], w=['msc%d' % w])
            src = self.xs_in if (l == 0 and which == 0) else self.XS
            for n, tt in enumerate(tiles):
                b = n % 2; w = 1 if tt < cfg.NC else 0
                self.dma('sp', xs[b][:], src[tt], w=['mx%d' % b])
                self.tt('dve', tmp[:], xs[b][:], sct[w][:], ALU.mult, r=['mx%d' % b, 'msc%d' % w], w=['mtmp'])
                self.tt('pool', hb[b][:], tmp[:], sht[w][:], ALU.add, r=['mtmp', 'msh%d' % w], w=['mhb%d' % b])
                for half in range(2):
                    pi = (n * 2 + half) % 8
                    pb = self.psb(pi)
                    for j in range(8):
                        k = half * 8 + j
                        self.tp(pb[:, j * 128:(j + 1) * 128], hb[b][:, k * 128:(k + 1) * 128], self.identb[:], r=['mhb%d' % b, 'identb'], w=['ps%d' % pi])
                    self.cp('act', hts[b][:, half * 8:(half + 1) * 8, :], pb.rearrange("p (k t) -> p k t", k=8), r=['ps%d' % pi], w=['mhts%d' % b])
                self.dma('pool', self.HTB[tt], hts[b][:], r=['mhts%d' % b])
            self.S.barrier()

    def stage_inproj(self, l):
        cfg = self.cfg
        with ExitStack() as es:
            hT = [self.sb(es, "ihT%d" % i, [128, 4, 16, 128], BF16) for i in range(2)]
            wtm = [self.sb(es, "iwtm%d" % i, [128, 16, 512], BF16) for i in range(3)]
            wst = [self.sb(es, "iwst%d" % i, [128, 16, 128], BF16) for i in range(4)]
            zo = [self.sb(es, "izo%d" % i, [128, 512], F32) for i in range(4)]
            for tb in range(3):
                self.dma('pool', wtm[tb][:], self.wt[l, tb], w=['iwtm%d' % tb])
            n = 0
            for mi, t0 in enumerate(range(0, cfg.TT, 4)):
                nt = min(4, cfg.TT - t0); hb = mi % 2
                for j in range(nt):
                    self.dma('sp', hT[hb][:, j], self.HTB[t0 + j], w=['ihT%d' % hb])
                for nb in range(37):
                    wb = nb % 4
                    self.dma('sp', wst[wb][:], self.WFB[l, nb], w=['iwst%d' % wb])
                    pi = n % 8; zb = n % 4; n += 1
                    for k in range(16):
                        self.mm(self.ps[pi][:, 0:nt * 128].rearrange("p (j t) -> p j t", j=nt), wst[wb][:, k, :], hT[hb][:, 0:nt, k, :],
                                k == 0, k == 15, r=['iwst%d' % wb, 'ihT%d' % hb], w=['ps%d' % pi])
                    self.cp('act' if n % 2 else 'dve', zo[zb][:, 0:nt * 128], self.ps[pi][:, 0:nt * 128], r=['ps%d' % pi], w=['izo%d' % zb])
                    self.dma('sp', self.ZT[nb * 128:(nb + 1) * 128, t0 * 128:(t0 + nt) * 128], zo[zb][:, 0:nt * 128], r=['izo%d' % zb])
                for j in range(nt):
                    for tb in range(3):
                        pi = n % 8; zb = n % 4; n += 1
                        for k in range(16):
                            self.mm(self.ps[pi][:], hT[hb][:, j, k, :], wtm[tb][:, k, :], k == 0, k == 15, r=['ihT%d' % hb, 'iwtm%d' % tb], w=['ps%d' % pi])
                        self.cp('act' if n % 2 else 'dve', zo[zb][:], self.ps[pi][:], r=['ps%d' % pi], w=['izo%d' % zb])
                        self.dma('sp', self.ZV[t0 + j][:, tb * 512:(tb + 1) * 512], zo[zb][:], r=['izo%d' % zb])
            self.S.barrier()

    def stage_rope(self, l):
        cfg = self.cfg; LC = cfg.LC
        rows = [(2048 + h * 128, 3072 + h * 128, h * 128) for h in range(8)] + [(4096 + g * 128, 4352 + g * 128, 1024 + g * 128) for g in range(2)]
        with ExitStack() as es:
            ta = [self.sb(es, "rta%d" % i, [128, 512], F32) for i in range(2)]
            tb = [self.sb(es, "rtb%d" % i, [128, 512], F32) for i in range(2)]
            ob = [self.sb(es, "rob%d" % i, [128, 512], BF16) for i in range(2)]
            ct = self.sb(es, "rct", [128, 512], F32); st = self.sb(es, "rst", [128, 512], F32)
            n = 0
            for c0 in range(0, LC, 512):
                cn = min(512, LC - c0)
                for (ra, rp, ro) in rows:
                    b = n % 2; n += 1
                    self.dma('sp', ta[b][:, 0:cn], self.ZT[ra:ra + 128, c0:c0 + cn], w=['rta%d' % b])
                    self.cp('act', ob[b][:, 0:cn], ta[b][:, 0:cn], r=['rta%d' % b], w=['rob%d' % b])
                    self.dma('pool', self.QKT[ro:ro + 128, c0:c0 + cn], ob[b][:, 0:cn], r=['rob%d' % b])
            for s0 in range(0, cfg.L, 512):
                self.dma('sp', ct[:], self.cos[:, s0:s0 + 512], w=['rct'])
                self.dma('sp', st[:], self.sins[:, s0:s0 + 512], w=['rst'])
                for (ra, rp, ro) in rows:
                    b = n % 2; n += 1
                    self.dma('sp', ta[b][:], self.ZT[ra:ra + 128, LC + s0:LC + s0 + 512], w=['rta%d' % b])
                    self.dma('sp', tb[b][:], self.ZT[rp:rp + 128, LC + s0:LC + s0 + 512], w=['rtb%d' % b])
                    self.tt('dve', ta[b][:], ta[b][:], ct[:], ALU.mult, r=['rta%d' % b, 'rct'], w=['rta%d' % b])
                    self.tt('pool', tb[b][:], tb[b][:], st[:], ALU.mult, r=['rtb%d' % b, 'rst'], w=['rtb%d' % b])
                    self.tt('dve', ob[b][:], ta[b][:], tb[b][:], ALU.add, r=['rta%d' % b, 'rtb%d' % b], w=['rob%d' % b])
                    self.dma('pool', self.QKT[ro:ro + 128, LC + s0:LC + s0 + 512], ob[b][:], r=['rob%d' % b])
            self.S.barrier()

    def stage_conv(self, l):
        cfg = self.cfg; SEG = 1024
        with ExitStack() as es:
            xi = [self.sb(es, "cxi%d" % i, [128, SEG + 2], F32) for i in range(2)]
            cg = [self.sb(es, "ccg%d" % i, [128, SEG + 2], F32) for i in range(2)]
            bb = [self.sb(es, "cbb%d" % i, [128, SEG], F32) for i in range(2)]
            u = [self.sb(es, "cu%d" % i, [128, SEG + 2], F32) for i in range(2)]
            acc = [self.sb(es, "cacc%d" % i, [128, SEG], F32) for i in range(2)]
            ob = [self.sb(es, "cob%d" % i, [128, SEG], BF16) for i in range(2)]
            cwt = self.sb(es, "ccw", [128, 4, 3], F32)
            self.dma('sp', cwt[:], self.cw[l], w=['ccw'])
            n = 0
            for (q0, q1) in ((0, cfg.LC), (cfg.LC, cfg.TOK)):
                for s0 in range(q0, q1, SEG):
                    sn = min(SEG, q1 - s0)
                    lo = max(q0, s0 - 1); hi = min(q1, s0 + sn + 1); off = lo - (s0 - 1); ln = hi - lo
                    for cc in range(4):
                        b = n % 2; n += 1
                        kx, kc, kb, ku, ka, ko = 'cxi%d' % b, 'ccg%d' % b, 'cbb%d' % b, 'cu%d' % b, 'cacc%d' % b, 'cob%d' % b
                        self.dma('sp', xi[b][:, off:off + ln], self.ZT[cc * 128:(cc + 1) * 128, lo:hi], w=[kx])
                        self.dma('sp', cg[b][:, off:off + ln], self.ZT[1024 + cc * 128:1024 + (cc + 1) * 128, lo:hi], w=[kc])
                        self.dma('sp', bb[b][:, 0:sn], self.ZT[512 + cc * 128:512 + (cc + 1) * 128, s0:s0 + sn], w=[kb])
                        if off > 0:
                            self.memset('pool', u[b][:, 0:1], 0.0, w=[ku])
                        if off + ln < sn + 2:
                            self.memset('pool', u[b][:, sn + 1:sn + 2], 0.0, w=[ku])
                        self.tt('pool', u[b][:, off:off + ln], cg[b][:, off:off + ln], xi[b][:, off:off + ln], ALU.mult, r=[kx, kc], w=[ku])
                        self.ts('dve', acc[b][:, 0:sn], u[b][:, 0:sn], cwt[:, cc, 0:1], None, ALU.mult, r=[ku, 'ccw'], w=[ka])
                        self.stt('dve', acc[b][:, 0:sn], u[b][:, 1:sn + 1], cwt[:, cc, 1:2], acc[b][:, 0:sn], ALU.mult, ALU.add, r=[ku, 'ccw', ka], w=[ka])
                        self.stt('dve', acc[b][:, 0:sn], u[b][:, 2:sn + 2], cwt[:, cc, 2:3], acc[b][:, 0:sn], ALU.mult, ALU.add, r=[ku, 'ccw', ka], w=[ka])
                        self.tt('pool', ob[b][:, 0:sn], acc[b][:, 0:sn], bb[b][:, 0:sn], ALU.mult, r=[ka, kb], w=[ko])
                        self.dma('pool', self.MIXT[cc * 128:(cc + 1) * 128, s0:s0 + sn], ob[b][:, 0:sn], r=[ko])
            self.S.barrier()

    def stage_gla(self, l):
        cfg = self.cfg
        NCHC = cfg.LC // 64; NCH = cfg.TOK // 64
        ps = self.ps
        with ExitStack() as es:
            sb = lambda n, s, d: self.sb(es, n, s, d)
            St = sb("gS", [64, 512], F32); Sbf = sb("gSbf", [64, 512], BF16)
            w2t = sb("gw2", [33, 512], F32); NW = sb("gNW", [64, 128], F32)
            B2 = range(2)
            qT = [sb("gqT%d" % i, [64, 4, 64], F32) for i in B2]; kT = [sb("gkT%d" % i, [64, 4, 64], F32) for i in B2]
            lrT = [sb("glr%d" % i, [33, 64], F32) for i in B2]
            kTok = [sb("gkk%d" % i, [64, 256], F32) for i in B2]; vTok = [sb("gvv%d" % i, [64, 512], F32) for i in B2]
            gTok = [sb("ggg%d" % i, [64, 512], F32) for i in B2]; ofl = [sb("gof%d" % i, [64, 512], F32) for i in B2]
            vbf = [sb("gvb%d" % i, [64, 512], BF16) for i in B2]
            lsb = [sb("gls%d" % i, [64, 256], F32) for i in B2]
            Eb = [sb("gEb%d" % i, [64, 256], F32) for i in B2]; Enb = [sb("gEn%d" % i, [64, 256], F32) for i in B2]
            Ek = [sb("gEk%d" % i, [64, 256], F32) for i in B2]
            qe = [sb("gqe%d" % i, [64, 4, 64], BF16) for i in B2]; ke = [sb("gke%d" % i, [64, 4, 64], BF16) for i in B2]
            kend = [sb("gkd%d" % i, [64, 256], BF16) for i in B2]; attT = [sb("gat%d" % i, [64, 4, 64], BF16) for i in B2]
            sq = sb("gsq", [64, 4, 128], F32); ss = sb("gss", [64, 4], F32); rstd = sb("grs", [64, 4], F32)
            on = sb("gon", [64, 512], F32); sg = sb("gsg", [64, 512], F32); res = sb("gres", [64, 512], BF16)
            mo = [sb("gmo%d" % i, [128, 4, 64], BF16) for i in B2]
            self.dma('sp', w2t[:], self.w2a[l], w=['gw2'])
            self.dma('sp', NW[:], self.gnorm[l][0:64, :], w=['gNW'])
            for i in B2:
                self.memset('pool', lrT[i][:], 1.0, w=['glr%d' % i])
            orders = [list(range(NCH)), list(range(NCHC - 1, -1, -1)) + list(range(NCH - 1, NCHC - 1, -1))]
            for d in range(2):
                self.memset('dve', St[:], 0.0, w=['gS'])
                self.memset('dve', Sbf[:], 0.0, w=['gSbf'])
                for n, ci in enumerate(orders[d]):
                    b = n % 2; c0 = ci * 64; tile = ci // 2; hf = ci % 2
                    K = lambda s: s + str(b)
                    self.dma('sp', qT[b][:], self.ZT[1536:1792, c0:c0 + 64].rearrange("(h k) t -> k h t", h=4), w=[K('gqT')])
                    self.dma('sp', kT[b][:], self.ZT[1792:2048, c0:c0 + 64].rearrange("(h k) t -> k h t", h=4), w=[K('gkT')])
                    self.dma('sp', lrT[b][0:32, :], self.ZT[4608:4640, c0:c0 + 64], w=[K('glr')])
                    zv = self.ZV[tile]
                    self.dma('sp', kTok[b][:], zv[hf * 64:(hf + 1) * 64, 0:256], w=[K('gkk')])
                    self.dma('sp', vTok[b][:], zv[hf * 64:(hf + 1) * 64, 256:768], w=[K('gvv')])
                    if d == 1:
                        self.dma('sp', gTok[b][:], zv[hf * 64:(hf + 1) * 64, 1024:1536], w=[K('ggg')])
                        self.dma('sp', ofl[b][:], self.OF[ci], w=[K('gof')])
                    self.mm(ps[0][0:64, 0:256], lrT[b][0:33, :], w2t[0:33, d * 256:(d + 1) * 256], True, True, r=[K('glr'), 'gw2'], w=['ps0'])
                    self.act(lsb[b][:], ps[0][0:64, 0:256], AF.Exp, scale=-1.0, r=['ps0'], w=[K('gls')])
                    self.act(lsb[b][:], lsb[b][:], AF.Ln, bias=1.0, r=[K('gls')], w=[K('gls')])
                    for h in range(4):
                        self.mm(ps[1][0:64, h * 64:(h + 1) * 64], lsb[b][:, h * 64:(h + 1) * 64], self.triT[d], True, True, r=[K('gls'), 'cft'], w=['ps1'])
                    self.mm(ps[2][0:64, 0:256], self.amT[d], lsb[b][:, 0:256], True, True, r=[K('gls'), 'cft'], w=['ps2'])
                    self.act(Eb[b][:], ps[1][0:64, 0:256], AF.Exp, r=['ps1'], w=[K('gEb')])
                    self.act(Enb[b][:], ps[1][0:64, 0:256], AF.Exp, scale=-1.0, r=['ps1'], w=[K('gEn')])
                    self.act(Ek[b][:], ps[2][0:64, 0:256], AF.Exp, r=['ps2'], w=[K('gEk')])
                    self.stt('dve', qe[b][:], qT[b][:], 0.125, Eb[b][:].rearrange("p (h t) -> p h t", h=4), ALU.mult, ALU.mult, r=[K('gqT'), K('gEb')], w=[K('gqe')])
                    self.tt('dve', ke[b][:], kT[b][:], Enb[b][:].rearrange("p (h t) -> p h t", h=4), ALU.mult, r=[K('gkT'), K('gEn')], w=[K('gke')])
                    self.tt('pool', kend[b][:], kTok[b][:], Ek[b][:], ALU.mult, r=[K('gkk'), K('gEk')], w=[K('gkd')])
                    self.cp('pool', vbf[b][:], vTok[b][:], r=[K('gvv')], w=[K('gvb')])
                    for h in range(4):
                        self.mm(ps[3][0:64, h * 64:(h + 1) * 64], ke[b][:, h, :], qe[b][:, h, :], True, True, r=[K('gke'), K('gqe')], w=['ps3'])
                    self.tt('dve', attT[b][:], ps[3][0:64, 0:256].rearrange("p (h t) -> p h t", h=4),
                            self.maskT[d].unsqueeze(1).broadcast_to([64, 4, 64]), ALU.mult, r=['ps3', 'cft'], w=[K('gat')])
                    for h in range(4):
                        self.mm(ps[4][0:64, h * 128:(h + 1) * 128], attT[b][:, h, :], vbf[b][:, h * 128:(h + 1) * 128], True, False, r=[K('gat'), K('gvb')], w=['ps4'])
                        self.mm(ps[4][0:64, h * 128:(h + 1) * 128], qe[b][:, h, :], Sbf[:, h * 128:(h + 1) * 128], False, True, r=[K('gqe'), 'gSbf'], w=['ps4'])
                    for h in range(4):
                        self.mm(ps[5][0:64, h * 128:(h + 1) * 128], kend[b][:, h * 64:(h + 1) * 64], vbf[b][:, h * 128:(h + 1) * 128], True, True, r=[K('gkd'), K('gvb')], w=['ps5'])
                    col = 63 if d == 0 else 0
                    for h in range(4):
                        self.stt('dve', St[:, h * 128:(h + 1) * 128], St[:, h * 128:(h + 1) * 128], Eb[b][:, h * 64 + col:h * 64 + col + 1],
                                 ps[5][0:64, h * 128:(h + 1) * 128], ALU.mult, ALU.add, r=['gS', K('gEb'), 'ps5'], w=['gS'])
                    self.cp('pool', Sbf[:], St[:], r=['gS'], w=['gSbf'])
                    if d == 0:
                        self.cp('act', ofl[b][:], ps[4][0:64, :], r=['ps4'], w=[K('gof')])
                        self.dma('pool', self.OF[ci], ofl[b][:], r=[K('gof')])
                    else:
                        self.tt('dve', ofl[b][:], ps[4][0:64, :], ofl[b][:], ALU.add, r=['ps4', K('gof')], w=[K('gof')])
                        o3 = ofl[b][:].rearrange("p (h v) -> p h v", h=4)
                        self.tt('dve', sq[:], o3, o3, ALU.mult, r=[K('gof')], w=['gsq'])
                        self.S.op('dve', lambda e, o=ss[:], i=sq[:]: e.tensor_reduce(out=o, in_=i, axis=AX.X, op=ALU.add), reads=['gsq'], writes=['gss'])
                        self.ts('dve', ss[:], ss[:], 1.0 / 128.0, LN_EPS, ALU.mult, ALU.add, r=['gss'], w=['gss'])
                        self.act(rstd[:], ss[:], AF.Ln, r=['gss'], w=['grs'])
                        self.act(rstd[:], rstd[:], AF.Exp, scale=-0.5, r=['grs'], w=['grs'])
                        for h in range(4):
                            self.stt('dve', on[:, h * 128:(h + 1) * 128], ofl[b][:, h * 128:(h + 1) * 128], rstd[:, h:h + 1], NW[:], ALU.mult, ALU.mult,
                                     r=[K('gof'), 'grs', 'gNW'], w=['gon'])
                        self.act(sg[:], gTok[b][:], AF.Silu, r=[K('ggg')], w=['gsg'])
                        self.tt('pool', res[:], on[:], sg[:], ALU.mult, r=['gon', 'gsg'], w=['gres'])
                        pb = self.psb(6)
                        for h in range(4):
                            self.tp(pb[:, h * 64:(h + 1) * 64], res[:, h * 128:(h + 1) * 128], self.identb[0:64, 0:64], r=['gres', 'identb'], w=['ps6'])
                        self.cp('act', mo[b][:], pb[:, 0:256].rearrange("p (h t) -> p h t", h=4), r=['ps6'], w=[K('gmo')])
                        self.dma('pool', self.MIXT[512:1024, c0:c0 + 64].rearrange("(h v) t -> v h t", h=4), mo[b][:], r=[K('gmo')])
                self.S.barrier()

    def stage_swa(self, l, last):
        cfg = self.cfg; NC, TT = cfg.NC, cfg.TT
        ps = self.ps
        scale = 128.0 ** -0.5
        with ExitStack() as es:
            sb = lambda n, s, d: self.sb(es, n, s, d)
            KT = sb("sKT", [128, cfg.TOK], BF16); VT = sb("sVT", [128, TT, 128], BF16)
            sk = sb("ssk", [128, 8], F32); sinkE = sb("ssinkE", [128, 8], F32)
            q4 = [sb("sq4%d" % i, [128, 4, 128], BF16) for i in range(2)]
            pT = [sb("spT%d" % i, [128, 4, 128], BF16) for i in range(3)]
            dn = sb("sdn", [128, 4, 128], F32); ob = [sb("sob%d" % i, [128, 4, 128], BF16) for i in range(2)]
            self.dma('sp', sk[:], self.sink[l], w=['ssk'])
            self.act(sinkE[:], sk[:], AF.Exp, r=['ssk'], w=['ssinkE'])
            n = 0; m = 0
            for g in range(2):
                self.dma('sp', KT[:], self.QKT[1024 + g * 128:1024 + (g + 1) * 128, :], w=['sKT'])
                for t0 in range(0, TT, 16):
                    t1 = min(TT, t0 + 16)
                    self.dma('pool', VT[:, t0:t1, :], self.ZV[t0:t1, :, 768 + g * 128:768 + (g + 1) * 128].rearrange("t p c -> p t c"), w=['sVT'])
                for qt in (range(NC, TT) if last else range(TT)):
                    if qt < NC:
                        keys = [(kt, None) for kt in range(NC)]
                    else:
                        keys = []
                        if qt - 1 >= NC: keys.append((qt - 1, self.maskP))
                        keys.append((qt, None))
                        if qt + 1 < TT: keys.append((qt + 1, self.maskN))
                        keys += [(kt, None) for kt in range(NC)]
                    b = n % 2; n += 1
                    po = 3 + 2 * b; pd = 4 + 2 * b
                    self.dma('sp', q4[b][:], self.QKT[g * 512:(g + 1) * 512, qt * 128:(qt + 1) * 128].rearrange("(j d) t -> d j t", j=4), w=['sq4%d' % b])
                    for ki, (kt, mask) in enumerate(keys):
                        pi = m % 3; m += 1
                        first = ki == 0; lastk = ki == len(keys) - 1
                        self.mm(ps[pi][:].rearrange("p (j t) -> p j t", j=4), KT[:, kt * 128:(kt + 1) * 128], q4[b][:], True, True, r=['sKT', 'sq4%d' % b], w=['ps%d' % pi])
                        self.act(pT[pi][:], ps[pi][:].rearrange("p (j t) -> p j t", j=4), AF.Exp, scale=scale, r=['ps%d' % pi], w=['spT%d' % pi])
                        if mask is not None:
                            self.tt('pool', pT[pi][:], pT[pi][:], mask.unsqueeze(1).broadcast_to([128, 4, 128]), ALU.mult, r=['spT%d' % pi, 'cft'], w=['spT%d' % pi])
                        self.mm(ps[po][:].rearrange("p (j t) -> p j t", j=4), VT[:, kt, :], pT[pi][:], first, lastk, r=['sVT', 'spT%d' % pi], w=['ps%d' % po])
                        self.mm(ps[pd][:].rearrange("p (j t) -> p j t", j=4), self.onesb[:], pT[pi][:], first, lastk, r=['onesb', 'spT%d' % pi], w=['ps%d' % pd])
                    self.tt('dve', dn[:], ps[pd][:].rearrange("p (j t) -> p j t", j=4), sinkE[:, g * 4:(g + 1) * 4].unsqueeze(2).broadcast_to([128, 4, 128]),
                            ALU.add, r=['ps%d' % pd, 'ssinkE'], w=['sdn'])
                    self.S.op('dve', lambda e, o=dn[:]: e.reciprocal(out=o, in_=o), reads=['sdn'], writes=['sdn'])
                    self.tt('dve', ob[b][:], ps[po][:].rearrange("p (j t) -> p j t", j=4), dn[:], ALU.mult, r=['ps%d' % po, 'sdn'], w=['sob%d' % b])
                    self.dma('pool', self.MIXT[1024 + g * 512:1024 + (g + 1) * 512, qt * 128:(qt + 1) * 128].rearrange("(j d) t -> d j t", j=4), ob[b][:], r=['sob%d' % b])
            self.S.barrier()

    def resid_ln(self, bufs, xsrc, banks, Gt, gkey, lngt, lnbt, dsts):
        xt, tq, st, mv, rs = bufs
        ps = self.ps
        self.dma('sp', xt[:], xsrc, w=['lx'])
        for nb in range(4):
            self.tt('dve', tq[:, nb * 512:(nb + 1) * 512], ps[banks[nb]][:], Gt[:, nb * 512:(nb + 1) * 512], ALU.mult, r=['ps%d' % banks[nb], gkey], w=['lt'])
        self.stt('dve', tq[:], xt[:], ALPHA, tq[:], ALU.mult, ALU.add, r=['lx', 'lt'], w=['lt'])
        for nb in range(4):
            self.S.op('dve', lambda e, o=st[:, nb, :], i=tq[:, nb * 512:(nb + 1) * 512]: e.bn_stats(out=o, in_=i), reads=['lt'], writes=['lst'])
        self.S.op('dve', lambda e, o=mv[:], i=st[:]: e.bn_aggr(out=o, in_=i), reads=['lst'], writes=['lmv'])
        self.act(rs[:], mv[:, 1:2], AF.Ln, bias=LN_EPS, r=['lmv'], w=['lrs'])
        self.act(rs[:], rs[:], AF.Exp, scale=-0.5, r=['lrs'], w=['lrs'])
        self.ts('dve', xt[:], tq[:], mv[:, 0:1], rs[:, 0:1], ALU.subtract, ALU.mult, r=['lt', 'lmv', 'lrs'], w=['lx'])
        self.tt('pool', xt[:], xt[:], lngt[:], ALU.mult, r=['lx', 'lng'], w=['lx'])
        self.tt('dve', xt[:], xt[:], lnbt[:], ALU.add, r=['lx', 'lnb'], w=['lx'])
        for dq, dst in zip(('sp', 'pool'), dsts):
            self.dma(dq, dst, xt[:], r=['lx'])

    def ln_bufs(self, es):
        sb = lambda n, s, d: self.sb(es, n, s, d)
        return (sb("lx", [128, D], F32), sb("lt", [128, D], F32), sb("lst", [128, 4, 6], F32), sb("lmv", [128, 2], F32), sb("lrs", [128, 1], F32))

    def stage_outproj(self, l, tiles):
        cfg = self.cfg
        with ExitStack() as es:
            sb = lambda n, s, d: self.sb(es, n, s, d)
            wot = sb("owo", [128, 16, D], BF16)
            lngt = sb("lng", [128, D], F32); lnbt = sb("lnb", [128, D], F32)
            Gt = [sb("oG%d" % i, [128, D], F32) for i in range(2)]
            mixT = [sb("omx%d" % i, [128, 16, 128], BF16) for i in range(2)]
            bufs = self.ln_bufs(es)
            for nb in range(4):
                self.dma('pool', wot[:, :, nb * 512:(nb + 1) * 512], self.wo[l][:, :, nb * 512:(nb + 1) * 512], w=['owo'])
            self.dma('sp', lngt[:], self.lng[l, 0], w=['lng']); self.dma('sp', lnbt[:], self.lnb[l, 0], w=['lnb'])
            for w in range(2):
                self.dma('sp', Gt[w][:], self.MOD[l, w, :, 2 * D:3 * D], w=['oG%d' % w])
            src = self.xs_in if l == 0 else self.XS
            for n, tt in enumerate(tiles):
                b = n % 2; w = 1 if tt < cfg.NC else 0
                self.dma('sp', mixT[b][:], self.MIXT[:, tt * 128:(tt + 1) * 128].rearrange("(k p) t -> p k t", p=128), w=['omx%d' % b])
                banks = [4 * b + i for i in range(4)]
                for nb in range(4):
                    for k in range(16):
                        self.mm(self.ps[banks[nb]][:], mixT[b][:, k, :], wot[:, k, nb * 512:(nb + 1) * 512], k == 0, k == 15, r=['omx%d' % b, 'owo'], w=['ps%d' % banks[nb]])
                self.resid_ln(bufs, src[tt], banks, Gt[w], 'oG%d' % w, lngt, lnbt, [self.XS[tt]])
            self.S.barrier()

    def stage_peer(self, l, tiles, last):
        cfg = self.cfg; NK = cfg.NK; NC = cfg.NC
        ps = self.ps
        tiles = list(tiles)
        pairs = [tiles[i:i + 2] for i in range(0, len(tiles), 2)]
        if not hasattr(self, 'RT'):
            self.RT = self.dscr("RT", [cfg.TT, 128, 3, 128], F32, dbg=True)
        with ExitStack() as es:
            sb = lambda n, s, d: self.sb(es, n, s, d)
            h2T = [sb("ph2T%d" % i, [128, 2, 16, 128], BF16) for i in range(2)]
            qTb = sb("pqTb", [128, 16, 256], BF16)
            wqr = sb("pwqr", [128, 16, D], BF16)
            sktt = sb("pskt", [128, 2, NK], BF16)
            ssb = sb("pssb", [128, 16, NK], F32); wk = sb("pwk", [128, 16, NK], F32)
            m8 = sb("pm8", [128, 16, 16], F32); i8 = sb("pi8", [128, 16, 16], U32); tif = sb("ptif", [128, 16, 16], F32)
            cand = sb("pcand", [128, 8, 256], F32); wk2 = sb("pwk2", [128, 8, 256], F32)
            b8 = sb("pb8", [128, 8, 16], F32); f8 = sb("pf8", [128, 8, 16], U32); ff = sb("pff", [128, 8, 16], F32)
            fa = sb("pfa", [128, 8, 16], F32); fb = sb("pfb", [128, 8, 16], F32)
            fau = sb("pfau", [128, 8, 16], U32); fbu = sb("pfbu", [128, 8, 16], U32)
            eq = sb("peq", [128, 8, 16, 16], F32)
            rt = [sb("prt%d" % i, [128, 3, 128], F32) for i in range(2)]
            ez = sb("pez", [128, 8, 16], F32); zz = sb("pzz", [128, 8], F32)
            self.dma('pool', sktt[:], self.skt[l], w=['pskt'])
            for blk in range(16):
                self.dma('pool', wqr[:, :, blk * 128:(blk + 1) * 128], self.wq[l, blk], w=['pwqr'])
            n = 0
            for pi_, pr in enumerate(pairs):
                hb = pi_ % 2
                for j, tt in enumerate(pr):
                    self.dma('sp', h2T[hb][:, j], self.HTB[tt], w=['ph2T%d' % hb])
                for blk in range(16):
                    pi = n % 8; n += 1
                    for k in range(16):
                        self.mm(ps[pi][:, 0:256].rearrange("p (j t) -> p j t", j=2), wqr[:, k, blk * 128:(blk + 1) * 128], h2T[hb][:, :, k, :], k == 0, k == 15,
                                r=['pwqr', 'ph2T%d' % hb], w=['ps%d' % pi])
                    self.cp('act' if blk % 2 else 'dve', qTb[:, blk, :], ps[pi][:, 0:256], r=['ps%d' % pi], w=['pqTb'])
                for j, tt in enumerate(pr):
                    rb = j
                    for hp in range(16):
                        bank = hp // 4
                        self.mm(ps[bank][:, (hp % 4) * NK:(hp % 4 + 1) * NK], qTb[:, hp, j * 128:(j + 1) * 128], sktt[:, hp % 2, :], True, True,
                                r=['pqTb', 'pskt'], w=['ps%d' % bank])
                    for bank in range(4):
                        self.cp('act', ssb[:, bank * 4:(bank + 1) * 4, :], ps[bank][:, 0:4 * NK].rearrange("p (a n) -> p a n", a=4), r=['ps%d' % bank], w=['pssb'])
                    V = self.S.op
                    for hp in range(16):
                        V('dve', lambda e, o=m8[:, hp, 0:8], i=ssb[:, hp, :]: e.max(out=o, in_=i), reads=['pssb'], writes=['pm8'])
                        V('dve', lambda e, o=i8[:, hp, 0:8], a=m8[:, hp, 0:8], i=ssb[:, hp, :]: e.max_index(out=o, in_max=a, in_values=i), reads=['pssb', 'pm8'], writes=['pi8'])
                        V('dve', lambda e, o=wk[:, hp, :], a=m8[:, hp, 0:8], i=ssb[:, hp, :]: e.match_replace(out=o, in_to_replace=a, in_values=i, imm_value=-1e30), reads=['pssb', 'pm8'], writes=['pwk'])
                        V('dve', lambda e, o=m8[:, hp, 8:16], i=wk[:, hp, :]: e.max(out=o, in_=i), reads=['pwk'], writes=['pm8'])
                        V('dve', lambda e, o=i8[:, hp, 8:16], a=m8[:, hp, 8:16], i=wk[:, hp, :]: e.max_index(out=o, in_max=a, in_values=i), reads=['pwk', 'pm8'], writes=['pi8'])
                    self.cp('dve', tif[:], i8[:], r=['pi8'], w=['ptif'])
                    tv4 = m8[:].rearrange("q (h p) k -> q h p k", p=2); ti4 = tif[:].rearrange("q (h p) k -> q h p k", p=2)
                    c4 = cand[:].rearrange("q h (a b) -> q h a b", a=16)
                    self.tt('dve', c4, tv4[:, :, 0, :].unsqueeze(3).broadcast_to([128, 8, 16, 16]), tv4[:, :, 1, :].unsqueeze(2).broadcast_to([128, 8, 16, 16]),
                            ALU.add, r=['pm8'], w=['pcand'])
                    for h in range(8):
                        V('dve', lambda e, o=b8[:, h, 0:8], i=cand[:, h, :]: e.max(out=o, in_=i), reads=['pcand'], writes=['pb8'])
                        V('dve', lambda e, o=f8[:, h, 0:8], a=b8[:, h, 0:8], i=cand[:, h, :]: e.max_index(out=o, in_max=a, in_values=i), reads=['pcand', 'pb8'], writes=['pf8'])
                        V('dve', lambda e, o=wk2[:, h, :], a=b8[:, h, 0:8], i=cand[:, h, :]: e.match_replace(out=o, in_to_replace=a, in_values=i, imm_value=-1e30), reads=['pcand', 'pb8'], writes=['pwk2'])
                        V('dve', lambda e, o=b8[:, h, 8:16], i=wk2[:, h, :]: e.max(out=o, in_=i), reads=['pwk2'], writes=['pb8'])
                        V('dve', lambda e, o=f8[:, h, 8:16], a=b8[:, h, 8:16], i=wk2[:, h, :]: e.max_index(out=o, in_max=a, in_values=i), reads=['pwk2', 'pb8'], writes=['pf8'])
                    self.ts('dve', fau[:], f8[:], 4, None, ALU.logical_shift_right, r=['pf8'], w=['pfau'])
                    self.ts('dve', fbu[:], f8[:], 15, None, ALU.bitwise_and, r=['pf8'], w=['pfbu'])
                    self.cp('dve', fa[:], fau[:], r=['pfau'], w=['pfa'])
                    self.cp('dve', fb[:], fbu[:], r=['pfbu'], w=['pfb'])
                    io4 = self.iota16.unsqueeze(1).unsqueeze(1).broadcast_to([128, 8, 16, 16])
                    for which, (fsel, pp) in enumerate(((fa, 0), (fb, 1))):
                        self.tt('dve', eq[:], fsel[:].unsqueeze(3).broadcast_to([128, 8, 16, 16]), io4, ALU.is_equal, r=['pfa', 'pfb', 'cft'], w=['peq'])
                        self.tt('pool', eq[:], eq[:], ti4[:, :, pp, :].unsqueeze(2).broadcast_to([128, 8, 16, 16]), ALU.mult, r=['peq', 'ptif'], w=['peq'])
                        V('dve', lambda e, o=rt[rb][:, which, :].rearrange("q (h k) -> q h k", h=8), i=eq[:]: e.tensor_reduce(out=o, in_=i, axis=AX.X, op=ALU.add),
                          reads=['peq'], writes=['prt%d' % rb])
                    self.tt('dve', ez[:], b8[:], b8[:, :, 0:1].broadcast_to([128, 8, 16]), ALU.subtract, r=['pb8'], w=['pez'])
                    self.act(ez[:], ez[:], AF.Exp, r=['pez'], w=['pez'])
                    V('dve', lambda e, o=zz[:], i=ez[:]: e.tensor_reduce(out=o, in_=i, axis=AX.X, op=ALU.add), reads=['pez'], writes=['pzz'])
                    V('dve', lambda e, o=zz[:]: e.reciprocal(out=o, in_=o), reads=['pzz'], writes=['pzz'])
                    self.tt('dve', rt[rb][:, 2, :].rearrange("q (h k) -> q h k", h=8), ez[:], zz[:].unsqueeze(2).broadcast_to([128, 8, 16]), ALU.mult,
                            r=['pez', 'pzz'], w=['prt%d' % rb])
                    self.dma('pool', self.RT[tt], rt[rb][:], r=['prt%d' % rb])
            self.S.barrier()
        G = 8
        with ExitStack() as es:
            sb = lambda n, s, d: self.sb(es, n, s, d)
            WT = sb("dWT", [128, NK, 256], BF16)
            h2T = sb("dh2T", [128, 2, 16, 128], BF16)
            NUB = 5; NI = min(NK, 24)
            utc = [sb("dut%d" % i, [128, 16, NK], BF16) for i in range(NUB)]
            vch = [sb("dvc%d" % i, [128, D], BF16) for i in range(NUB)]
            gaI = [sb("dgI%d" % i, [128, 256], BF16) for i in range(NI)]
            rtt = sb("drt", [128, 3, 128], F32); rT = sb("drT", [128, 3, 128], F32)
            oh2 = [sb("doh2%d" % i, [128, G, NK], BF16) for i in range(2)]
            oh1 = [sb("doh1%d" % i, [128, G, NK], BF16) for i in range(2)]
            ga = [sb("dga%d" % i, [128, 256], F32) for i in range(2)]
            lngt = sb("lng", [128, D], F32); lnbt = sb("lnb", [128, D], F32)
            Gt = sb("dG", [128, D], F32)
            bufs = self.ln_bufs(es)
            self.dma('sp', lngt[:], self.lng[l, 1], w=['lng']); self.dma('sp', lnbt[:], self.lnb[l, 1], w=['lnb'])
            io3 = self.iota128[:, 0:NK].unsqueeze(1).broadcast_to([128, G, NK])
            curw = None; n = 0; m = 0
            for pr in pairs:
                w = 1 if pr[0] < NC else 0
                if w != curw:
                    self.dma('sp', Gt[:], self.MOD[l, w, :, 5 * D:6 * D], w=['dG']); curw = w
                for j, tt in enumerate(pr):
                    self.dma('sp', h2T[:, j], self.HTB[tt], w=['dh2T'])
                started = 0

                def phase1_mm(i1, dest, dkey, banks):
                    ub = i1 % NUB; pi = banks[i1 % len(banks)]
                    self.dma('sp', utc[ub][:], self.UTB[l, i1], w=['dut%d' % ub])
                    for k in range(16):
                        self.mm(ps[pi][0:NK, 0:256].rearrange("p (j t) -> p j t", j=2), utc[ub][:, k, :], h2T[:, :, k, :], k == 0, k == 15,
                                r=['dut%d' % ub, 'dh2T'], w=['ps%d' % pi])
                    self.act(dest, ps[pi][0:NK, 0:256], AF.Gelu, r=['ps%d' % pi], w=[dkey])

                for j, tt in enumerate(pr):
                    self.dma('sp', rtt[:], self.RT[tt], w=['drt'])
                    for a in range(3):
                        self.tp(ps[5 + a][:, 0:128], rtt[:, a, :], self.identf, r=['drt', 'cft'], w=['ps%d' % (5 + a)])
                        self.cp('act', rT[:, a, :], ps[5 + a][:, 0:128], r=['ps%d' % (5 + a)], w=['drT'])
                    for t0 in range(0, 128, G):
                        ob = m % 2; m += 1
                        self.tt('dve', oh2[ob][:], io3, rT[:, 1, t0:t0 + G].unsqueeze(2).broadcast_to([128, G, NK]), ALU.is_equal, r=['cft', 'drT'], w=['doh2%d' % ob])
                        self.tt('dve', oh1[ob][:], io3, rT[:, 0, t0:t0 + G].unsqueeze(2).broadcast_to([128, G, NK]), ALU.is_equal, r=['cft', 'drT'], w=['doh1%d' % ob])
                        self.tt('pool', oh1[ob][:], oh1[ob][:], rT[:, 2, t0:t0 + G].unsqueeze(2).broadcast_to([128, G, NK]), ALU.mult, r=['doh1%d' % ob, 'drT'], w=['doh1%d' % ob])
                        for q0 in range(0, G, 4):
                            pi = 5 + (n % 3); n += 1
                            for q in range(4):
                                self.mm(ps[pi][0:NK, q * NK:(q + 1) * NK], oh2[ob][:, q0 + q, :], oh1[ob][:, q0 + q, :], True, True,
                                        r=['doh2%d' % ob, 'doh1%d' % ob], w=['ps%d' % pi])
                            tb = j * 128 + t0 + q0
                            self.cp('act', WT[0:NK, :, tb:tb + 4].rearrange("p i t -> p t i"), ps[pi][0:NK, 0:4 * NK].rearrange("p (t i) -> p t i", t=4),
                                    r=['ps%d' % pi], w=['dWTb'])
                        if started < NI:
                            phase1_mm(started, gaI[started][0:NK, :], 'dgI%d' % started, [0, 1, 2, 3, 4]); started += 1
                while started < NI:
                    phase1_mm(started, gaI[started][0:NK, :], 'dgI%d' % started, [0, 1, 2, 3, 4]); started += 1
                for i1 in range(NI):
                    self.tt('dve' if i1 % 2 else 'pool', WT[0:NK, i1, :], WT[0:NK, i1, :], gaI[i1][0:NK, :], ALU.mult, r=['dWTb', 'dgI%d' % i1], w=[('dWT', i1)])
                for i1 in range(NI, NK):
                    gb = i1 % 2
                    phase1_mm(i1, ga[gb][0:NK, :], 'dga%d' % gb, list(range(8)))
                    self.tt('dve' if i1 % 2 else 'pool', WT[0:NK, i1, :], WT[0:NK, i1, :], ga[gb][0:NK, :], ALU.mult, r=['dWTb', 'dga%d' % gb], w=[('dWT', i1)])
                for i1 in range(NK):
                    vb = i1 % NUB
                    self.dma('act', vch[vb][0:NK, :], self.VB[l, i1 * NK:(i1 + 1) * NK, :], w=['dvc%d' % vb])
                    for j in range(len(pr)):
                        for nb in range(4):
                            self.mm(ps[j * 4 + nb][:], WT[0:NK, i1, j * 128:(j + 1) * 128], vch[vb][0:NK, nb * 512:(nb + 1) * 512], i1 == 0, i1 == NK - 1,
                                    r=[('dWT', i1), 'dWTb', 'dvc%d' % vb], w=['ps%d' % (j * 4 + nb)])
                for j, tt in enumerate(pr):
                    dsts = [self.XS[tt]]
                    if last:
                        dsts = [self.out[tt - NC]]
                    self.resid_ln(bufs, self.XS[tt], [j * 4 + i for i in range(4)], Gt, 'dG', lngt, lnbt, dsts)
            self.S.barrier()


_CACHE = {}


def run(inputs, cfg, core_ids=None):
    maps = prep_inputs(inputs, cfg)
    key = (cfg.L, cfg.LC, cfg.DEPTH, cfg.NK, cfg.dbg)
    if key not in _CACHE:
        p = Prog(cfg); p.build(); _CACHE[key] = p
    p = _CACHE[key]
    res = run_bass_kernel_spmd(p.nc, maps, core_ids=list(range(len(maps))))
    outs = [r["out"].reshape(cfg.L, D) for r in res.results]
    return np.stack(outs).astype(np.float32), res, p


def kernel(**inputs):
    cfg = Cfg()
    out, _, _ = run(inputs, cfg)
    return out
```

```python
import numpy as np
import ml_dtypes
from contextlib import ExitStack
import concourse.bass as bass
import concourse.mybir as mybir
from concourse.bass_utils import run_bass_kernel_spmd

F32 = mybir.dt.float32; BF16 = mybir.dt.bfloat16; U32 = mybir.dt.uint32
ALU = mybir.AluOpType; AF = mybir.ActivationFunctionType; AX = mybir.AxisListType

D = 2048
ALPHA = (2.0 * 4) ** 0.25
LN_EPS = 1e-6
GRID_W = 64


class Cfg:
    def __init__(self, L=8192, LC=256, DEPTH=4, NK=128, dbg=False):
        self.L = L; self.LC = LC; self.DEPTH = DEPTH; self.NK = NK; self.dbg = dbg
        self.NC = LC // 128; self.NL = L // 128; self.TT = self.NC + self.NL
        self.TOK = L + LC
        self.NE = NK * NK


class Sched:
    ENG = ['pe', 'dve', 'act', 'pool', 'sp']
    CE = ['pe', 'dve', 'act', 'pool']
    EPOCH = 30000
    NEP = {'pe': 26, 'dve': 5, 'act': 4, 'pool': 3}

    def __init__(self, nc, es, ndma=48):
        self.nc = nc
        self.esem = {e: [es.enter_context(nc.semaphore('s_%s%d' % (e, i))) for i in range(self.NEP[e])] for e in self.CE}
        self.dsem = [es.enter_context(nc.semaphore('d%d' % i)) for i in range(ndma)]
        self.dcnt = [0] * ndma; self.dnext = 0
        self.cnt = {e: 0 for e in self.CE}; self.ep = {e: 0 for e in self.CE}
        self.prog = {e: [] for e in self.ENG}
        self.seen = {e: {} for e in self.ENG}
        self.res = {}
        self.ninst = 0

    def semh(self, sid):
        return self.esem[sid[1]][sid[2]] if sid[0] == 'e' else self.dsem[sid[1]]

    def _need(self, eng, sid, val, waits):
        s = self.seen[eng]
        if sid[0] == 'e':
            k = ('e', sid[1]); cur = s.get(k, (-1, 0)); new = (sid[2], val)
            if cur >= new:
                return
            s[k] = new
        else:
            if s.get(sid, 0) >= val:
                return
            s[sid] = val
        waits.append((sid, val))

    def _deps(self, eng, reads, writes):
        waits = []
        pe = eng == 'pe'
        for k in reads:
            st = self.res.get(k)
            if st:
                for sid, v in st[0].items():
                    if not (pe and sid[0] == 'e' and sid[1] == 'pe'):
                        self._need(eng, sid, v, waits)
        for k in writes:
            st = self.res.get(k)
            if st:
                for d in st:
                    for sid, v in d.items():
                        if not (pe and sid[0] == 'e' and sid[1] == 'pe'):
                            self._need(eng, sid, v, waits)
        return waits

    def _mark(self, ev, reads, writes):
        sid, v = ev
        for k in reads:
            self.res.setdefault(k, ({}, {}))[1][sid] = v
        for k in writes:
            self.res.setdefault(k, ({}, {}))[0][sid] = v

    def op(self, eng, fn, reads=(), writes=()):
        waits = self._deps(eng, reads, writes)
        if self.cnt[eng] >= self.EPOCH:
            self.ep[eng] += 1; self.cnt[eng] = 0
        self.cnt[eng] += 1; ev = (('e', eng, self.ep[eng]), self.cnt[eng])
        self.prog[eng].append((waits, fn, ev)); self._mark(ev, reads, writes); self.ninst += 1

    def dma(self, eng, fn, reads=(), writes=()):
        slot = self.dnext; self.dnext = (self.dnext + 1) % len(self.dsem)
        waits = self._deps(eng, reads, writes)
        if self.dcnt[slot] > 0:
            self._need(eng, ('d', slot), self.dcnt[slot], waits)
        self.dcnt[slot] += 16; ev = (('d', slot), self.dcnt[slot])
        self.prog[eng].append((waits, fn, ev)); self._mark(ev, reads, writes); self.ninst += 1

    def barrier(self):
        for e in self.ENG:
            waits = []
            for i, c in enumerate(self.dcnt):
                if c:
                    self._need(e, ('d', i), c, waits)
            for o in self.CE:
                if o != e and (self.cnt[o] or self.ep[o]):
                    self._need(e, ('e', o, self.ep[o]), self.cnt[o], waits)
            if waits:
                self.prog[e].append((waits, None, None))
        self.res = {}

    def emit(self, block):
        engmap = {'pe': 'tensor', 'dve': 'vector', 'act': 'scalar', 'pool': 'gpsimd', 'sp': 'sync'}
        for e in self.ENG:
            prog = self.prog[e]

            def body(engine, prog=prog):
                for waits, fn, ev in prog:
                    for sid, v in waits:
                        engine.wait_ge(self.semh(sid), v)
                    if fn is None:
                        continue
                    ins = fn(engine)
                    sid, v = ev
                    ins.then_inc(self.semh(sid), 16 if sid[0] == 'd' else 1)
            getattr(block, engmap[e])(body)


def _blk_stationary(w):
    n = w.shape[1] // 128
    return np.ascontiguousarray(w.reshape(16, 128, n, 128).transpose(2, 1, 0, 3))


def _blk_moving(w, width=512):
    n = w.shape[1] // width
    return np.ascontiguousarray(w.reshape(16, 128, n, width).transpose(2, 1, 0, 3))


def rope_perm():
    d = np.arange(128)
    partner = np.where((d % 64) < 32, d + 32, d - 32)
    sign = np.where((d % 64) < 32, -1.0, 1.0).astype(np.float32)
    return partner, sign


def const_tables(cfg):
    cf = np.zeros((128, 912), np.float32)
    cf[:, 0:128] = np.eye(128, dtype=np.float32)
    cf[:, 128:256] = np.arange(128, dtype=np.float32)[None, :]
    s = -1.0 / 16.0
    j = np.arange(64)[:, None]; i = np.arange(64)[None, :]
    cf[:64, 256:320] = np.where(j <= i, s, 0.0)
    cf[:64, 320:384] = np.where(j >= i, s, 0.0)
    cf[:64, 384:448] = np.where(j > i, s, 0.0)
    cf[:64, 448:512] = np.where(j < i, s, 0.0)
    cf[:64, 512:576] = np.where(j <= i, 1.0, 0.0)
    cf[:64, 576:640] = np.where(j >= i, 1.0, 0.0)
    cf[:, 640:656] = np.arange(16, dtype=np.float32)[None, :]
    jj = np.arange(128)[:, None]; ii = np.arange(128)[None, :]
    cf[:, 656:784] = np.where(jj >= ii, 1.0, 0.0)
    cf[:, 784:912] = np.where(jj <= ii, 1.0, 0.0)
    t = np.arange(cfg.L)
    r = (t // GRID_W).astype(np.float32); c = (t % GRID_W).astype(np.float32)
    inv = (10000.0 ** (-np.arange(0, 64, 2, dtype=np.float32) / 64.0)).astype(np.float32)
    ang_r = r[None, :] * inv[:, None]; ang_c = c[None, :] * inv[:, None]
    d = np.arange(128)
    ang = np.where((d < 64)[:, None], ang_r[d % 32], ang_c[d % 32]).astype(np.float32)
    _, sign = rope_perm()
    cos = np.cos(ang).astype(np.float32); sins = (np.sin(ang).astype(np.float32) * sign[:, None])
    return cf, np.ascontiguousarray(cos), np.ascontiguousarray(sins)


def prep_inputs(inp, cfg):
    f = lambda a: np.ascontiguousarray(np.asarray(a, dtype=np.float32))
    x = f(inp['x']); c = f(inp['c']); ctx = f(inp['ctx']); c_ctx = f(inp['c_ctx'])
    w_ada = f(inp['w_ada']); b_ada = f(inp['b_ada']); w_in = f(inp['w_in'])
    NL_ = cfg.DEPTH
    cf, cos, sins = const_tables(cfg)
    partner, _ = rope_perm()
    wa = np.stack([_blk_moving(w_ada[l]) for l in range(NL_)])
    ba = np.ascontiguousarray(b_ada[:NL_].reshape(NL_, 24, 1, 512))
    wf_l, wt_l = [], []
    for l in range(NL_):
        w = w_in[l]
        sq = w[:, 3104:4128].reshape(D, 8, 128); sk = w[:, 4128:4384].reshape(D, 2, 128)
        lr = np.zeros((D, 128), np.float32); lr[:, :32] = w[:, 3072:3104]
        fm = np.concatenate([w[:, 0:1536], w[:, 1536:2048], sq.reshape(D, 1024), sq[:, :, partner].reshape(D, 1024),
                             sk.reshape(D, 256), sk[:, :, partner].reshape(D, 256), lr], axis=1)
        wf_l.append(_blk_stationary(fm))
        tm = np.concatenate([w[:, 1792:2048], w[:, 2048:2560], w[:, 4384:4640], w[:, 2560:3072]], axis=1)
        wt_l.append(_blk_moving(tm))
    wf = np.stack(wf_l); wt = np.stack(wt_l)
    conv_w = f(inp['conv_w'])[:NL_]
    cw = np.ascontiguousarray(conv_w.reshape(NL_, 3, 4, 128).transpose(0, 3, 2, 1))
    w2 = f(inp['gla_w2'])[:NL_]; b2 = f(inp['gla_b2'])[:NL_]
    w2a = np.zeros((NL_, 33, 512), np.float32)
    w2a[:, 0:16, 0:256] = w2[:, 0]; w2a[:, 16:32, 256:512] = w2[:, 1]
    w2a[:, 32, 0:256] = b2[:, 0]; w2a[:, 32, 256:512] = b2[:, 1]
    gnorm = np.ascontiguousarray(np.broadcast_to(f(inp['gla_norm'])[:NL_, None, :], (NL_, 128, 128)))
    sink = np.ascontiguousarray(np.broadcast_to(f(inp['swa_sink'])[:NL_, None, :], (NL_, 128, 8)))
    w_out = f(inp['w_out'])[:NL_]
    wo = np.ascontiguousarray(w_out.reshape(NL_, 16, 128, D).transpose(0, 2, 1, 3))
    lng = np.ascontiguousarray(np.broadcast_to(f(inp['ln_g'])[:NL_, :, None, :], (NL_, 2, 128, D)))
    lnb = np.ascontiguousarray(np.broadcast_to(f(inp['ln_b'])[:NL_, :, None, :], (NL_, 2, 128, D)))
    wq = np.stack([_blk_stationary(f(inp['peer_wq'])[l]) for l in range(NL_)])
    sk_ = f(inp['peer_subkeys'])[:NL_]
    skt = np.ascontiguousarray(sk_.transpose(0, 3, 1, 2))
    NK = cfg.NK
    pu = f(inp['peer_u'])[:NL_]
    ut = np.ascontiguousarray(pu.reshape(NL_, NK, NK, 16, 128).transpose(0, 1, 4, 3, 2))
    pv = f(inp['peer_v'])[:NL_]
    maps = []
    for b in range(x.shape[0]):
        cc = np.stack([c[b].reshape(16, 128).T, c_ctx.reshape(16, 128).T])
        xs = np.concatenate([ctx[b], x[b]], axis=0).reshape(cfg.TT, 128, D)
        maps.append(dict(xs_in=np.ascontiguousarray(xs), cc=np.ascontiguousarray(cc), wa=wa, ba=ba, wf=wf, wt=wt,
                         cw=cw, w2a=w2a, gnorm=gnorm, sink=sink, wo=wo, lng=lng, lnb=lnb, wq=wq, skt=skt,
                         ut=ut, pv=pv, cf=cf, cos=cos, sins=sins))
    return maps


class Prog:
    def __init__(self, cfg):
        self.cfg = cfg
        self.nc = bass.Bass("TRN2", target_bir_lowering=False)
        self.dbg_outs = []

    def din(self, name, shape, dt=F32):
        return self.nc.dram_tensor(name, list(shape), dt, kind="ExternalInput").ap()

    def dscr(self, name, shape, dt=F32, dbg=False):
        if dbg and self.cfg.dbg:
            self.dbg_outs.append(name)
            return self.nc.dram_tensor(name, list(shape), dt, kind="ExternalOutput").ap()
        return self.nc.dram_tensor(name, list(shape), dt).ap()

    def sb(self, es, name, shape, dt):
        self.uid = getattr(self, 'uid', 0) + 1
        return es.enter_context(self.nc.sbuf_tensor("%s_u%d" % (name, self.uid), list(shape), dt))

    def dma(self, q, out, in_, r=(), w=()):
        self.S.dma(q, lambda e, o=out, i=in_: e.dma_start(out=o, in_=i), reads=r, writes=w)

    def mm(self, out, lhsT, rhs, start, stop, r=(), w=()):
        self.S.op('pe', lambda e, o=out, l=lhsT, rh=rhs, s=start, t=stop: e.matmul(o, lhsT=l, rhs=rh, start=s, stop=t), reads=r, writes=w)

    def tp(self, out, in_, ident, r=(), w=()):
        self.S.op('pe', lambda e, o=out, i=in_, d=ident: e.transpose(out=o, in_=i, identity=d), reads=r, writes=w)

    def tt(self, eng, out, in0, in1, op, r=(), w=()):
        self.S.op(eng, lambda e, o=out, a=in0, b=in1, p=op: e.tensor_tensor(out=o, in0=a, in1=b, op=p), reads=r, writes=w)

    def ts(self, eng, out, in0, s1, s2, op0, op1=None, r=(), w=()):
        if op1 is None:
            self.S.op(eng, lambda e, o=out, a=in0, x=s1, p=op0: e.tensor_scalar(out=o, in0=a, scalar1=x, scalar2=None, op0=p), reads=r, writes=w)
        else:
            self.S.op(eng, lambda e, o=out, a=in0, x=s1, y=s2, p=op0, q=op1: e.tensor_scalar(out=o, in0=a, scalar1=x, scalar2=y, op0=p, op1=q), reads=r, writes=w)

    def stt(self, eng, out, in0, scalar, in1, op0, op1, r=(), w=()):
        self.S.op(eng, lambda e, o=out, a=in0, s=scalar, b=in1, p=op0, q=op1: e.scalar_tensor_tensor(out=o, in0=a, scalar=s, in1=b, op0=p, op1=q), reads=r, writes=w)

    def act(self, out, in_, func, r=(), w=(), bias=None, scale=None, accum=None):
        kw = {}
        if bias is not None: kw['bias'] = bias
        if scale is not None: kw['scale'] = scale
        if accum is not None: kw['accum_out'] = accum
        self.S.op('act', lambda e, o=out, i=in_, f=func, kw=kw: e.activation(out=o, in_=i, func=f, **kw), reads=r, writes=w)

    def cp(self, eng, out, in_, r=(), w=()):
        if eng == 'act':
            self.S.op('act', lambda e, o=out, i=in_: e.copy(out=o, in_=i), reads=r, writes=w)
        else:
            self.S.op(eng, lambda e, o=out, i=in_: e.tensor_copy(out=o, in_=i), reads=r, writes=w)

    def memset(self, eng, ap, val, w=()):
        self.S.op(eng, lambda e, a=ap, v=val: e.memset(a, v), writes=w)

    def build(self):
        cfg = self.cfg; nc = self.nc
        TT, NLY, NK = cfg.TT, cfg.DEPTH, cfg.NK
        self.xs_in = self.din("xs_in", [TT, 128, D])
        self.cc = self.din("cc", [2, 128, 16])
        self.wa = self.din("wa", [NLY, 24, 128, 16, 512]); self.ba = self.din("ba", [NLY, 24, 1, 512])
        self.wf = self.din("wf", [NLY, 37, 128, 16, 128]); self.wt = self.din("wt", [NLY, 3, 128, 16, 512])
        self.cw = self.din("cw", [NLY, 128, 4, 3]); self.w2a = self.din("w2a", [NLY, 33, 512])
        self.gnorm = self.din("gnorm", [NLY, 128, 128]); self.sink = self.din("sink", [NLY, 128, 8])
        self.wo = self.din("wo", [NLY, 128, 16, D])
        self.lng = self.din("lng", [NLY, 2, 128, D]); self.lnb = self.din("lnb", [NLY, 2, 128, D])
        self.wq = self.din("wq", [NLY, 16, 128, 16, 128]); self.skt = self.din("skt", [NLY, 128, 2, NK])
        self.ut = self.din("ut", [NLY, NK, 128, 16, NK]); self.pv = self.din("pv", [NLY, cfg.NE, D])
        self.cf = self.din("cf", [128, 912]); self.cos = self.din("cos", [128, cfg.L]); self.sins = self.din("sins", [128, cfg.L])
        self.out = nc.dram_tensor("out", [cfg.NL, 128, D], F32, kind="ExternalOutput").ap()
        self.XS = self.dscr("XS", [TT, 128, D], F32, dbg=True)
        self.MOD = self.dscr("MOD", [NLY, 2, 128, 6 * D], F32, dbg=True)
        self.HTB = self.dscr("HTB", [TT, 128, 16, 128], BF16)
        self.ZT = self.dscr("ZT", [4736, cfg.TOK], F32, dbg=True)
        self.ZV = self.dscr("ZV", [TT, 128, 1536], F32, dbg=True)
        self.QKT = self.dscr("QKT", [1280, cfg.TOK], BF16)
        self.MIXT = self.dscr("MIXT", [D, cfg.TOK], BF16, dbg=True)
        self.OF = self.dscr("OF", [cfg.TOK // 64, 64, 512], F32)
        self.WFB = self.dscr("WFB", [NLY, 37, 128, 16, 128], BF16)
        self.UTB = self.dscr("UTB", [NLY, NK, 128, 16, NK], BF16)
        self.VB = self.dscr("VB", [NLY, cfg.NE, D], BF16)
        es = ExitStack()
        with es:
            self.S = Sched(nc, es)
            self.ps = [es.enter_context(nc.psum_tensor("ps%d" % i, [128, 512], F32)) for i in range(8)]
            self.consts(es)
            self.stage_prep()
            self.stage_adaln()
            for l in range(NLY):
                last = (l == NLY - 1)
                self.stage_modT(l, 0, range(TT))
                self.stage_inproj(l)
                self.stage_rope(l)
                self.stage_conv(l)
                self.stage_gla(l)
                self.stage_swa(l, last)
                tiles = range(cfg.NC, TT) if last else range(TT)
                self.stage_outproj(l, tiles)
                self.stage_modT(l, 1, tiles)
                self.stage_peer(l, tiles, last)
            self.S.barrier()
            with nc.Block() as block:
                self.S.emit(block)
        return nc

    def consts(self, es):
        self.cft = self.sb(es, "cft", [128, 912], F32)
        self.identb = self.sb(es, "identb", [128, 128], BF16)
        self.onesb = self.sb(es, "onesb", [128, 128], BF16)
        self.dma('sp', self.cft[:], self.cf, w=['cft'])
        self.cp('dve', self.identb[:], self.cft[:, 0:128], r=['cft'], w=['identb'])
        self.memset('dve', self.onesb[:], 1.0, w=['onesb'])
        c = self.cft
        self.identf = c[:, 0:128]; self.iota128 = c[:, 128:256]
        self.triT = [c[0:64, 256:320], c[0:64, 320:384]]
        self.amT = [c[0:64, 384:448], c[0:64, 448:512]]
        self.maskT = [c[0:64, 512:576], c[0:64, 576:640]]
        self.iota16 = c[:, 640:656]
        self.maskP = c[:, 656:784]; self.maskN = c[:, 784:912]
        self.S.barrier()

    def psb(self, i):
        return self.ps[i][:].bitcast(BF16)

    def stage_prep(self):
        cfg = self.cfg; NK = cfg.NK
        for l in range(cfg.DEPTH):
            src = self.wf[l].rearrange("i p k e -> (i p) (k e)"); dst = self.WFB[l].rearrange("i p k e -> (i p) (k e)")
            for r0 in range(0, 37 * 128, 1024):
                r1 = min(37 * 128, r0 + 1024)
                self.dma('pool', dst[r0:r1, :], src[r0:r1, :])
        for l in range(cfg.DEPTH):
            src = self.ut[l].rearrange("i p k e -> (i p) (k e)"); dst = self.UTB[l].rearrange("i p k e -> (i p) (k e)")
            rows = NK * 128
            for r0 in range(0, rows, 1024):
                r1 = min(rows, r0 + 1024)
                self.dma('pool', dst[r0:r1, :], src[r0:r1, :])
            for r0 in range(0, cfg.NE, 1024):
                r1 = min(cfg.NE, r0 + 1024)
                self.dma('pool', self.VB[l][r0:r1, :], self.pv[l][r0:r1, :])
        self.S.barrier()

    def stage_adaln(self):
        cfg = self.cfg
        with ExitStack() as es:
            ccs = self.sb(es, "ccs", [128, 2, 16], F32); cs = self.sb(es, "cs", [128, 2, 16], F32)
            rep = self.sb(es, "rep", [128, 2, 16, 128], BF16)
            wat = [self.sb(es, "wat%d" % i, [128, 16, 512], BF16) for i in range(2)]
            bat = [self.sb(es, "bat%d" % i, [1, 512], BF16) for i in range(2)]
            mo = [self.sb(es, "mo%d" % i, [128, 512], F32) for i in range(4)]
            self.dma('sp', ccs[:], self.cc.rearrange("w p k -> p w k"), w=['ccs'])
            self.act(cs[:], ccs[:], AF.Silu, r=['ccs'], w=['cs'])
            for w in range(2):
                self.cp('dve', rep[:, w], cs[:, w, :].unsqueeze(2).broadcast_to([128, 16, 128]), r=['cs'], w=['rep'])
            n = 0
            for l in range(cfg.DEPTH):
                for cb in range(24):
                    b = (l * 24 + cb) % 2
                    self.dma('pool', wat[b][:], self.wa[l, cb], w=['wat%d' % b])
                    self.dma('pool', bat[b][:], self.ba[l, cb], w=['bat%d' % b])
                    for w in range(2):
                        pi = n % 8; m = n % 4; n += 1
                        for k in range(16):
                            self.mm(self.ps[pi][:], rep[:, w, k, :], wat[b][:, k, :], k == 0, False, r=['rep', 'wat%d' % b], w=['ps%d' % pi])
                        self.mm(self.ps[pi][:], self.onesb[0:1, :], bat[b][0:1, :], False, True, r=['onesb', 'bat%d' % b], w=['ps%d' % pi])
                        if cb // 4 in (1, 4):
                            self.ts('dve', mo[m][:], self.ps[pi][:], 1.0, None, ALU.add, r=['ps%d' % pi], w=['mo%d' % m])
                        else:
                            self.cp('act', mo[m][:], self.ps[pi][:], r=['ps%d' % pi], w=['mo%d' % m])
                        self.dma('sp', self.MOD[l, w, :, cb * 512:(cb + 1) * 512], mo[m][:], r=['mo%d' % m])
            self.S.barrier()

    def stage_modT(self, l, which, tiles):
        cfg = self.cfg
        with ExitStack() as es:
            xs = [self.sb(es, "mx%d" % i, [128, D], F32) for i in range(2)]
            tmp = self.sb(es, "mtmp", [128, D], F32)
            hb = [self.sb(es, "mhb%d" % i, [128, D], BF16) for i in range(2)]
            hts = [self.sb(es, "mhts%d" % i, [128, 16, 128], BF16) for i in range(2)]
            sct = [self.sb(es, "msc%d" % i, [128, D], F32) for i in range(2)]
            sht = [self.sb(es, "msh%d" % i, [128, D], F32) for i in range(2)]
            o_sh = 0 if which == 0 else 3 * D
            for w in range(2):
                self.dma('sp', sht[w][:], self.MOD[l, w, :, o_sh:o_sh + D], w=['msh%d' % w])
                self.dma('sp', sct[w][:], self.MOD[l, w, :, o_sh + D:o_sh + 2 * D], w=['msc%d' % w])
            src = self.xs_in if (l == 0 and which == 0) else self.XS
            for n, tt in enumerate(tiles):
                b = n % 2; w = 1 if tt < cfg.NC else 0
                self.dma('sp', xs[b][:], src[tt], w=['mx%d' % b])
                self.tt('dve', tmp[:], xs[b][:], sct[w][:], ALU.mult, r=['mx%d' % b, 'msc%d' % w], w=['mtmp'])
                self.tt('pool', hb[b][:], tmp[:], sht[w][:], ALU.add, r=['mtmp', 'msh%d' % w], w=['mhb%d' % b])
                for half in range(2):
                    pi = (n * 2 + half) % 8
                    pb = self.psb(pi)
                    for j in range(8):
                        k = half * 8 + j
                        self.tp(pb[:, j * 128:(j + 1) * 128], hb[b][:, k * 128:(k + 1) * 128], self.identb[:], r=['mhb%d' % b, 'identb'], w=['ps%d' % pi])
                    self.cp('act', hts[b][:, half * 8:(half + 1) * 8, :], pb.rearrange("p (k t) -> p k t", k=8), r=['ps%d' % pi], w=['mhts%d' % b])
                self.dma('pool', self.HTB[tt], hts[b][:], r=['mhts%d' % b])
            self.S.barrier()

    def stage_inproj(self, l):
        cfg = self.cfg
        with ExitStack() as es:
            hT = [self.sb(es, "ihT%d" % i, [128, 4, 16, 128], BF16) for i in range(2)]
            wtm = [self.sb(es, "iwtm%d" % i, [128, 16, 512], BF16) for i in range(3)]
            wst = [self.sb(es, "iwst%d" % i, [128, 16, 128], BF16) for i in range(4)]
            zo = [self.sb(es, "izo%d" % i, [128, 512], F32) for i in range(4)]
            for tb in range(3):
                self.dma('pool', wtm[tb][:], self.wt[l, tb], w=['iwtm%d' % tb])
            n = 0
            for mi, t0 in enumerate(range(0, cfg.TT, 4)):
                nt = min(4, cfg.TT - t0); hb = mi % 2
                for j in range(nt):
                    self.dma('sp', hT[hb][:, j], self.HTB[t0 + j], w=['ihT%d' % hb])
                for nb in range(37):
                    wb = nb % 4
                    self.dma('sp', wst[wb][:], self.WFB[l, nb], w=['iwst%d' % wb])
                    pi = n % 8; zb = n % 4; n += 1
                    for k in range(16):
                        self.mm(self.ps[pi][:, 0:nt * 128].rearrange("p (j t) -> p j t", j=nt), wst[wb][:, k, :], hT[hb][:, 0:nt, k, :],
                                k == 0, k == 15, r=['iwst%d' % wb, 'ihT%d' % hb], w=['ps%d' % pi])
                    self.cp('act' if n % 2 else 'dve', zo[zb][:, 0:nt * 128], self.ps[pi][:, 0:nt * 128], r=['ps%d' % pi], w=['izo%d' % zb])
                    self.dma('sp', self.ZT[nb * 128:(nb + 1) * 128, t0 * 128:(t0 + nt) * 128], zo[zb][:, 0:nt * 128], r=['izo%d' % zb])
                for j in range(nt):
                    for tb in range(3):
                        pi = n % 8; zb = n % 4; n += 1
                        for k in range(16):
                            self.mm(self.ps[pi][:], hT[hb][:, j, k, :], wtm[tb][:, k, :], k == 0, k == 15, r=['ihT%d' % hb, 'iwtm%d' % tb], w=['ps%d' % pi])
                        self.cp('act' if n % 2 else 'dve', zo[zb][:], self.ps[pi][:], r=['ps%d' % pi], w=['izo%d' % zb])
                        self.dma('sp', self.ZV[t0 + j][:, tb * 512:(tb + 1) * 512], zo[zb][:], r=['izo%d' % zb])
            self.S.barrier()

    def stage_rope(self, l):
        cfg = self.cfg; LC = cfg.LC
        rows = [(2048 + h * 128, 3072 + h * 128, h * 128) for h in range(8)] + [(4096 + g * 128, 4352 + g * 128, 1024 + g * 128) for g in range(2)]
        with ExitStack() as es:
            ta = [self.sb(es, "rta%d" % i, [128, 512], F32) for i in range(2)]
            tb = [self.sb(es, "rtb%d" % i, [128, 512], F32) for i in range(2)]
            ob = [self.sb(es, "rob%d" % i, [128, 512], BF16) for i in range(2)]
            ct = self.sb(es, "rct", [128, 512], F32); st = self.sb(es, "rst", [128, 512], F32)
            n = 0
            for c0 in range(0, LC, 512):
                cn = min(512, LC - c0)
                for (ra, rp, ro) in rows:
                    b = n % 2; n += 1
                    self.dma('sp', ta[b][:, 0:cn], self.ZT[ra:ra + 128, c0:c0 + cn], w=['rta%d' % b])
                    self.cp('act', ob[b][:, 0:cn], ta[b][:, 0:cn], r=['rta%d' % b], w=['rob%d' % b])
                    self.dma('pool', self.QKT[ro:ro + 128, c0:c0 + cn], ob[b][:, 0:cn], r=['rob%d' % b])
            for s0 in range(0, cfg.L, 512):
                self.dma('sp', ct[:], self.cos[:, s0:s0 + 512], w=['rct'])
                self.dma('sp', st[:], self.sins[:, s0:s0 + 512], w=['rst'])
                for (ra, rp, ro) in rows:
                    b = n % 2; n += 1
                    self.dma('sp', ta[b][:], self.ZT[ra:ra + 128, LC + s0:LC + s0 + 512], w=['rta%d' % b])
                    self.dma('sp', tb[b][:], self.ZT[rp:rp + 128, LC + s0:LC + s0 + 512], w=['rtb%d' % b])
                    self.tt('dve', ta[b][:], ta[b][:], ct[:], ALU.mult, r=['rta%d' % b, 'rct'], w=['rta%d' % b])
                    self.tt('pool', tb[b][:], tb[b][:], st[:], ALU.mult, r=['rtb%d' % b, 'rst'], w=['rtb%d' % b])
                    self.tt('dve', ob[b][:], ta[b][:], tb[b][:], ALU.add, r=['rta%d' % b, 'rtb%d' % b], w=['rob%d' % b])
                    self.dma('pool', self.QKT[ro:ro + 128, LC + s0:LC + s0 + 512], ob[b][:], r=['rob%d' % b])
            self.S.barrier()

    def stage_conv(self, l):
        cfg = self.cfg; SEG = 1024
        with ExitStack() as es:
            xi = [self.sb(es, "cxi%d" % i, [128, SEG + 2], F32) for i in range(2)]
            cg = [self.sb(es, "ccg%d" % i, [128, SEG + 2], F32) for i in range(2)]
            bb = [self.sb(es, "cbb%d" % i, [128, SEG], F32) for i in range(2)]
            u = [self.sb(es, "cu%d" % i, [128, SEG + 2], F32) for i in range(2)]
            acc = [self.sb(es, "cacc%d" % i, [128, SEG], F32) for i in range(2)]
            ob = [self.sb(es, "cob%d" % i, [128, SEG], BF16) for i in range(2)]
            cwt = self.sb(es, "ccw", [128, 4, 3], F32)
            self.dma('sp', cwt[:], self.cw[l], w=['ccw'])
            n = 0
            for (q0, q1) in ((0, cfg.LC), (cfg.LC, cfg.TOK)):
                for s0 in range(q0, q1, SEG):
                    sn = min(SEG, q1 - s0)
                    lo = max(q0, s0 - 1); hi = min(q1, s0 + sn + 1); off = lo - (s0 - 1); ln = hi - lo
                    for cc in range(4):
                        b = n % 2; n += 1
                        kx, kc, kb, ku, ka, ko = 'cxi%d' % b, 'ccg%d' % b, 'cbb%d' % b, 'cu%d' % b, 'cacc%d' % b, 'cob%d' % b
                        self.dma('sp', xi[b][:, off:off + ln], self.ZT[cc * 128:(cc + 1) * 128, lo:hi], w=[kx])
                        self.dma('sp', cg[b][:, off:off + ln], self.ZT[1024 + cc * 128:1024 + (cc + 1) * 128, lo:hi], w=[kc])
                        self.dma('sp', bb[b][:, 0:sn], self.ZT[512 + cc * 128:512 + (cc + 1) * 128, s0:s0 + sn], w=[kb])
                        if off > 0:
                            self.memset('pool', u[b][:, 0:1], 0.0, w=[ku])
                        if off + ln < sn + 2:
                            self.memset('pool', u[b][:, sn + 1:sn + 2], 0.0, w=[ku])
                        self.tt('pool', u[b][:, off:off + ln], cg[b][:, off:off + ln], xi[b][:, off:off + ln], ALU.mult, r=[kx, kc], w=[ku])
                        self.ts('dve', acc[b][:, 0:sn], u[b][:, 0:sn], cwt[:, cc, 0:1], None, ALU.mult, r=[ku, 'ccw'], w=[ka])
                        self.stt('dve', acc[b][:, 0:sn], u[b][:, 1:sn + 1], cwt[:, cc, 1:2], acc[b][:, 0:sn], ALU.mult, ALU.add, r=[ku, 'ccw', ka], w=[ka])
                        self.stt('dve', acc[b][:, 0:sn], u[b][:, 2:sn + 2], cwt[:, cc, 2:3], acc[b][:, 0:sn], ALU.mult, ALU.add, r=[ku, 'ccw', ka], w=[ka])
                        self.tt('pool', ob[b][:, 0:sn], acc[b][:, 0:sn], bb[b][:, 0:sn], ALU.mult, r=[ka, kb], w=[ko])
                        self.dma('pool', self.MIXT[cc * 128:(cc + 1) * 128, s0:s0 + sn], ob[b][:, 0:sn], r=[ko])
            self.S.barrier()

    def stage_gla(self, l):
        cfg = self.cfg
        NCHC = cfg.LC // 64; NCH = cfg.TOK // 64
        ps = self.ps
        with ExitStack() as es:
            sb = lambda n, s, d: self.sb(es, n, s, d)
            St = sb("gS", [64, 512], F32); Sbf = sb("gSbf", [64, 512], BF16)
            w2t = sb("gw2", [33, 512], F32); NW = sb("gNW", [64, 128], F32)
            B2 = range(2)
            qT = [sb("gqT%d" % i, [64, 4, 64], F32) for i in B2]; kT = [sb("gkT%d" % i, [64, 4, 64], F32) for i in B2]
            lrT = [sb("glr%d" % i, [33, 64], F32) for i in B2]
            kTok = [sb("gkk%d" % i, [64, 256], F32) for i in B2]; vTok = [sb("gvv%d" % i, [64, 512], F32) for i in B2]
            gTok = [sb("ggg%d" % i, [64, 512], F32) for i in B2]; ofl = [sb("gof%d" % i, [64, 512], F32) for i in B2]
            vbf = [sb("gvb%d" % i, [64, 512], BF16) for i in B2]
            lsb = [sb("gls%d" % i, [64, 256], F32) for i in B2]
            Eb = [sb("gEb%d" % i, [64, 256], F32) for i in B2]; Enb = [sb("gEn%d" % i, [64, 256], F32) for i in B2]
            Ek = [sb("gEk%d" % i, [64, 256], F32) for i in B2]
            qe = [sb("gqe%d" % i, [64, 4, 64], BF16) for i in B2]; ke = [sb("gke%d" % i, [64, 4, 64], BF16) for i in B2]
            kend = [sb("gkd%d" % i, [64, 256], BF16) for i in B2]; attT = [sb("gat%d" % i, [64, 4, 64], BF16) for i in B2]
            sq = sb("gsq", [64, 4, 128], F32); ss = sb("gss", [64, 4], F32); rstd = sb("grs", [64, 4], F32)
            on = sb("gon", [64, 512], F32); sg = sb("gsg", [64, 512], F32); res = sb("gres", [64, 512], BF16)
            mo = [sb("gmo%d" % i, [128, 4, 64], BF16) for i in B2]
            self.dma('sp', w2t[:], self.w2a[l], w=['gw2'])
            self.dma('sp', NW[:], self.gnorm[l][0:64, :], w=['gNW'])
            for i in B2:
                self.memset('pool', lrT[i][:], 1.0, w=['glr%d' % i])
            orders = [list(range(NCH)), list(range(NCHC - 1, -1, -1)) + list(range(NCH - 1, NCHC - 1, -1))]
            for d in range(2):
                self.memset('dve', St[:], 0.0, w=['gS'])
                self.memset('dve', Sbf[:], 0.0, w=['gSbf'])
                for n, ci in enumerate(orders[d]):
                    b = n % 2; c0 = ci * 64; tile = ci // 2; hf = ci % 2
                    K = lambda s: s + str(b)
                    self.dma('sp', qT[b][:], self.ZT[1536:1792, c0:c0 + 64].rearrange("(h k) t -> k h t", h=4), w=[K('gqT')])
                    self.dma('sp', kT[b][:], self.ZT[1792:2048, c0:c0 + 64].rearrange("(h k) t -> k h t", h=4), w=[K('gkT')])
                    self.dma('sp', lrT[b][0:32, :], self.ZT[4608:4640, c0:c0 + 64], w=[K('glr')])
                    zv = self.ZV[tile]
                    self.dma('sp', kTok[b][:], zv[hf * 64:(hf + 1) * 64, 0:256], w=[K('gkk')])
                    self.dma('sp', vTok[b][:], zv[hf * 64:(hf + 1) * 64, 256:768], w=[K('gvv')])
                    if d == 1:
                        self.dma('sp', gTok[b][:], zv[hf * 64:(hf + 1) * 64, 1024:1536], w=[K('ggg')])
                        self.dma('sp', ofl[b][:], self.OF[ci], w=[K('gof')])
                    self.mm(ps[0][0:64, 0:256], lrT[b][0:33, :], w2t[0:33, d * 256:(d + 1) * 256], True, True, r=[K('glr'), 'gw2'], w=['ps0'])
                    self.act(lsb[b][:], ps[0][0:64, 0:256], AF.Exp, scale=-1.0, r=['ps0'], w=[K('gls')])
                    self.act(lsb[b][:], lsb[b][:], AF.Ln, bias=1.0, r=[K('gls')], w=[K('gls')])
                    for h in range(4):
                        self.mm(ps[1][0:64, h * 64:(h + 1) * 64], lsb[b][:, h * 64:(h + 1) * 64], self.triT[d], True, True, r=[K('gls'), 'cft'], w=['ps1'])
                    self.mm(ps[2][0:64, 0:256], self.amT[d], lsb[b][:, 0:256], True, True, r=[K('gls'), 'cft'], w=['ps2'])
                    self.act(Eb[b][:], ps[1][0:64, 0:256], AF.Exp, r=['ps1'], w=[K('gEb')])
                    self.act(Enb[b][:], ps[1][0:64, 0:256], AF.Exp, scale=-1.0, r=['ps1'], w=[K('gEn')])
                    self.act(Ek[b][:], ps[2][0:64, 0:256], AF.Exp, r=['ps2'], w=[K('gEk')])
                    self.stt('dve', qe[b][:], qT[b][:], 0.125, Eb[b][:].rearrange("p (h t) -> p h t", h=4), ALU.mult, ALU.mult, r=[K('gqT'), K('gEb')], w=[K('gqe')])
                    self.tt('dve', ke[b][:], kT[b][:], Enb[b][:].rearrange("p (h t) -> p h t", h=4), ALU.mult, r=[K('gkT'), K('gEn')], w=[K('gke')])
                    self.tt('pool', kend[b][:], kTok[b][:], Ek[b][:], ALU.mult, r=[K('gkk'), K('gEk')], w=[K('gkd')])
                    self.cp('pool', vbf[b][:], vTok[b][:], r=[K('gvv')], w=[K('gvb')])
                    for h in range(4):
                        self.mm(ps[3][0:64, h * 64:(h + 1) * 64], ke[b][:, h, :], qe[b][:, h, :], True, True, r=[K('gke'), K('gqe')], w=['ps3'])
                    self.tt('dve', attT[b][:], ps[3][0:64, 0:256].rearrange("p (h t) -> p h t", h=4),
                            self.maskT[d].unsqueeze(1).broadcast_to([64, 4, 64]), ALU.mult, r=['ps3', 'cft'], w=[K('gat')])
                    for h in range(4):
                        self.mm(ps[4][0:64, h * 128:(h + 1) * 128], attT[b][:, h, :], vbf[b][:, h * 128:(h + 1) * 128], True, False, r=[K('gat'), K('gvb')], w=['ps4'])
                        self.mm(ps[4][0:64, h * 128:(h + 1) * 128], qe[b][:, h, :], Sbf[:, h * 128:(h + 1) * 128], False, True, r=[K('gqe'), 'gSbf'], w=['ps4'])
                    for h in range(4):
                        self.mm(ps[5][0:64, h * 128:(h + 1) * 128], kend[b][:, h * 64:(h + 1) * 64], vbf[b][:, h * 128:(h + 1) * 128], True, True, r=[K('gkd'), K('gvb')], w=['ps5'])
                    col = 63 if d == 0 else 0
                    for h in range(4):
                        self.stt('dve', St[:, h * 128:(h + 1) * 128], St[:, h * 128:(h + 1) * 128], Eb[b][:, h * 64 + col:h * 64 + col + 1],
                                 ps[5][0:64, h * 128:(h + 1) * 128], ALU.mult, ALU.add, r=['gS', K('gEb'), 'ps5'], w=['gS'])
                    self.cp('pool', Sbf[:], St[:], r=['gS'], w=['gSbf'])
                    if d == 0:
                        self.cp('act', ofl[b][:], ps[4][0:64, :], r=['ps4'], w=[K('gof')])
                        self.dma('pool', self.OF[ci], ofl[b][:], r=[K('gof')])
                    else:
                        self.tt('dve', ofl[b][:], ps[4][0:64, :], ofl[b][:], ALU.add, r=['ps4', K('gof')], w=[K('gof')])
                        o3 = ofl[b][:].rearrange("p (h v) -> p h v", h=4)
                        self.tt('dve', sq[:], o3, o3, ALU.mult, r=[K('gof')], w=['gsq'])
                        self.S.op('dve', lambda e, o=ss[:], i=sq[:]: e.tensor_reduce(out=o, in_=i, axis=AX.X, op=ALU.add), reads=['gsq'], writes=['gss'])
                        self.ts('dve', ss[:], ss[:], 1.0 / 128.0, LN_EPS, ALU.mult, ALU.add, r=['gss'], w=['gss'])
                        self.act(rstd[:], ss[:], AF.Ln, r=['gss'], w=['grs'])
                        self.act(rstd[:], rstd[:], AF.Exp, scale=-0.5, r=['grs'], w=['grs'])
                        for h in range(4):
                            self.stt('dve', on[:, h * 128:(h + 1) * 128], ofl[b][:, h * 128:(h + 1) * 128], rstd[:, h:h + 1], NW[:], ALU.mult, ALU.mult,
                                     r=[K('gof'), 'grs', 'gNW'], w=['gon'])
                        self.act(sg[:], gTok[b][:], AF.Silu, r=[K('ggg')], w=['gsg'])
                        self.tt('pool', res[:], on[:], sg[:], ALU.mult, r=['gon', 'gsg'], w=['gres'])
                        pb = self.psb(6)
                        for h in range(4):
                            self.tp(pb[:, h * 64:(h + 1) * 64], res[:, h * 128:(h + 1) * 128], self.identb[0:64, 0:64], r=['gres', 'identb'], w=['ps6'])
                        self.cp('act', mo[b][:], pb[:, 0:256].rearrange("p (h t) -> p h t", h=4), r=['ps6'], w=[K('gmo')])
                        self.dma('pool', self.MIXT[512:1024, c0:c0 + 64].rearrange("(h v) t -> v h t", h=4), mo[b][:], r=[K('gmo')])
                self.S.barrier()

    def stage_swa(self, l, last):
        cfg = self.cfg; NC, TT = cfg.NC, cfg.TT
        ps = self.ps
        scale = 128.0 ** -0.5
        with ExitStack() as es:
            sb = lambda n, s, d: self.sb(es, n, s, d)
            KT = sb("sKT", [128, cfg.TOK], BF16); VT = sb("sVT", [128, TT, 128], BF16)
            sk = sb("ssk", [128, 8], F32); sinkE = sb("ssinkE", [128, 8], F32)
            q4 = [sb("sq4%d" % i, [128, 4, 128], BF16) for i in range(2)]
            pT = [sb("spT%d" % i, [128, 4, 128], BF16) for i in range(3)]
            dn = sb("sdn", [128, 4, 128], F32); ob = [sb("sob%d" % i, [128, 4, 128], BF16) for i in range(2)]
            self.dma('sp', sk[:], self.sink[l], w=['ssk'])
            self.act(sinkE[:], sk[:], AF.Exp, r=['ssk'], w=['ssinkE'])
            n = 0; m = 0
            for g in range(2):
                self.dma('sp', KT[:], self.QKT[1024 + g * 128:1024 + (g + 1) * 128, :], w=['sKT'])
                for t0 in range(0, TT, 16):
                    t1 = min(TT, t0 + 16)
                    self.dma('pool', VT[:, t0:t1, :], self.ZV[t0:t1, :, 768 + g * 128:768 + (g + 1) * 128].rearrange("t p c -> p t c"), w=['sVT'])
                for qt in (range(NC, TT) if last else range(TT)):
                    if qt < NC:
                        keys = [(kt, None) for kt in range(NC)]
                    else:
                        keys = []
                        if qt - 1 >= NC: keys.append((qt - 1, self.maskP))
                        keys.append((qt, None))
                        if qt + 1 < TT: keys.append((qt + 1, self.maskN))
                        keys += [(kt, None) for kt in range(NC)]
                    b = n % 2; n += 1
                    po = 3 + 2 * b; pd = 4 + 2 * b
                    self.dma('sp', q4[b][:], self.QKT[g * 512:(g + 1) * 512, qt * 128:(qt + 1) * 128].rearrange("(j d) t -> d j t", j=4), w=['sq4%d' % b])
                    for ki, (kt, mask) in enumerate(keys):
                        pi = m % 3; m += 1
                        first = ki == 0; lastk = ki == len(keys) - 1
                        self.mm(ps[pi][:].rearrange("p (j t) -> p j t", j=4), KT[:, kt * 128:(kt + 1) * 128], q4[b][:], True, True, r=['sKT', 'sq4%d' % b], w=['ps%d' % pi])
                        self.act(pT[pi][:], ps[pi][:].rearrange("p (j t) -> p j t", j=4), AF.Exp, scale=scale, r=['ps%d' % pi], w=['spT%d' % pi])
                        if mask is not None:
                            self.tt('pool', pT[pi][:], pT[pi][:], mask.unsqueeze(1).broadcast_to([128, 4, 128]), ALU.mult, r=['spT%d' % pi, 'cft'], w=['spT%d' % pi])
                        self.mm(ps[po][:].rearrange("p (j t) -> p j t", j=4), VT[:, kt, :], pT[pi][:], first, lastk, r=['sVT', 'spT%d' % pi], w=['ps%d' % po])
                        self.mm(ps[pd][:].rearrange("p (j t) -> p j t", j=4), self.onesb[:], pT[pi][:], first, lastk, r=['onesb', 'spT%d' % pi], w=['ps%d' % pd])
                    self.tt('dve', dn[:], ps[pd][:].rearrange("p (j t) -> p j t", j=4), sinkE[:, g * 4:(g + 1) * 4].unsqueeze(2).broadcast_to([128, 4, 128]),
                            ALU.add, r=['ps%d' % pd, 'ssinkE'], w=['sdn'])
                    self.S.op('dve', lambda e, o=dn[:]: e.reciprocal(out=o, in_=o), reads=['sdn'], writes=['sdn'])
                    self.tt('dve', ob[b][:], ps[po][:].rearrange("p (j t) -> p j t", j=4), dn[:], ALU.mult, r=['ps%d' % po, 'sdn'], w=['sob%d' % b])
                    self.dma('pool', self.MIXT[1024 + g * 512:1024 + (g + 1) * 512, qt * 128:(qt + 1) * 128].rearrange("(j d) t -> d j t", j=4), ob[b][:], r=['sob%d' % b])
            self.S.barrier()

    def resid_ln(self, bufs, xsrc, banks, Gt, gkey, lngt, lnbt, dsts):
        xt, tq, st, mv, rs = bufs
        ps = self.ps
        self.dma('sp', xt[:], xsrc, w=['lx'])
        for nb in range(4):
            self.tt('dve', tq[:, nb * 512:(nb + 1) * 512], ps[banks[nb]][:], Gt[:, nb * 512:(nb + 1) * 512], ALU.mult, r=['ps%d' % banks[nb], gkey], w=['lt'])
        self.stt('dve', tq[:], xt[:], ALPHA, tq[:], ALU.mult, ALU.add, r=['lx', 'lt'], w=['lt'])
        for nb in range(4):
            self.S.op('dve', lambda e, o=st[:, nb, :], i=tq[:, nb * 512:(nb + 1) * 512]: e.bn_stats(out=o, in_=i), reads=['lt'], writes=['lst'])
        self.S.op('dve', lambda e, o=mv[:], i=st[:]: e.bn_aggr(out=o, in_=i), reads=['lst'], writes=['lmv'])
        self.act(rs[:], mv[:, 1:2], AF.Ln, bias=LN_EPS, r=['lmv'], w=['lrs'])
        self.act(rs[:], rs[:], AF.Exp, scale=-0.5, r=['lrs'], w=['lrs'])
        self.ts('dve', xt[:], tq[:], mv[:, 0:1], rs[:, 0:1], ALU.subtract, ALU.mult, r=['lt', 'lmv', 'lrs'], w=['lx'])
        self.tt('pool', xt[:], xt[:], lngt[:], ALU.mult, r=['lx', 'lng'], w=['lx'])
        self.tt('dve', xt[:], xt[:], lnbt[:], ALU.add, r=['lx', 'lnb'], w=['lx'])
        for dq, dst in zip(('sp', 'pool'), dsts):
            self.dma(dq, dst, xt[:], r=['lx'])

    def ln_bufs(self, es):
        sb = lambda n, s, d: self.sb(es, n, s, d)
        return (sb("lx", [128, D], F32), sb("lt", [128, D], F32), sb("lst", [128, 4, 6], F32), sb("lmv", [128, 2], F32), sb("lrs", [128, 1], F32))

    def stage_outproj(self, l, tiles):
        cfg = self.cfg
        with ExitStack() as es:
            sb = lambda n, s, d: self.sb(es, n, s, d)
            wot = sb("owo", [128, 16, D], BF16)
            lngt = sb("lng", [128, D], F32); lnbt = sb("lnb", [128, D], F32)
            Gt = [sb("oG%d" % i, [128, D], F32) for i in range(2)]
            mixT = [sb("omx%d" % i, [128, 16, 128], BF16) for i in range(2)]
            bufs = self.ln_bufs(es)
            for nb in range(4):
                self.dma('pool', wot[:, :, nb * 512:(nb + 1) * 512], self.wo[l][:, :, nb * 512:(nb + 1) * 512], w=['owo'])
            self.dma('sp', lngt[:], self.lng[l, 0], w=['lng']); self.dma('sp', lnbt[:], self.lnb[l, 0], w=['lnb'])
            for w in range(2):
                self.dma('sp', Gt[w][:], self.MOD[l, w, :, 2 * D:3 * D], w=['oG%d' % w])
            src = self.xs_in if l == 0 else self.XS
            for n, tt in enumerate(tiles):
                b = n % 2; w = 1 if tt < cfg.NC else 0
                self.dma('sp', mixT[b][:], self.MIXT[:, tt * 128:(tt + 1) * 128].rearrange("(k p) t -> p k t", p=128), w=['omx%d' % b])
                banks = [4 * b + i for i in range(4)]
                for nb in range(4):
                    for k in range(16):
                        self.mm(self.ps[banks[nb]][:], mixT[b][:, k, :], wot[:, k, nb * 512:(nb + 1) * 512], k == 0, k == 15, r=['omx%d' % b, 'owo'], w=['ps%d' % banks[nb]])
                self.resid_ln(bufs, src[tt], banks, Gt[w], 'oG%d' % w, lngt, lnbt, [self.XS[tt]])
            self.S.barrier()

    def stage_peer(self, l, tiles, last):
        cfg = self.cfg; NK = cfg.NK; NC = cfg.NC
        ps = self.ps
        tiles = list(tiles)
        pairs = [tiles[i:i + 2] for i in range(0, len(tiles), 2)]
        if not hasattr(self, 'RT'):
            self.RT = self.dscr("RT", [cfg.TT, 128, 3, 128], F32, dbg=True)
        with ExitStack() as es:
            sb = lambda n, s, d: self.sb(es, n, s, d)
            h2T = [sb("ph2T%d" % i, [128, 2, 16, 128], BF16) for i in range(2)]
            qTb = sb("pqTb", [128, 16, 256], BF16)
            wqr = sb("pwqr", [128, 16, D], BF16)
            sktt = sb("pskt", [128, 2, NK], BF16)
            ssb = sb("pssb", [128, 16, NK], F32); wk = sb("pwk", [128, 16, NK], F32)
            m8 = sb("pm8", [128, 16, 16], F32); i8 = sb("pi8", [128, 16, 16], U32); tif = sb("ptif", [128, 16, 16], F32)
            cand = sb("pcand", [128, 8, 256], F32); wk2 = sb("pwk2", [128, 8, 256], F32)
            b8 = sb("pb8", [128, 8, 16], F32); f8 = sb("pf8", [128, 8, 16], U32); ff = sb("pff", [128, 8, 16], F32)
            fa = sb("pfa", [128, 8, 16], F32); fb = sb("pfb", [128, 8, 16], F32)
            fau = sb("pfau", [128, 8, 16], U32); fbu = sb("pfbu", [128, 8, 16], U32)
            eq = sb("peq", [128, 8, 16, 16], F32)
            rt = [sb("prt%d" % i, [128, 3, 128], F32) for i in range(2)]
            ez = sb("pez", [128, 8, 16], F32); zz = sb("pzz", [128, 8], F32)
            self.dma('pool', sktt[:], self.skt[l], w=['pskt'])
            for blk in range(16):
                self.dma('pool', wqr[:, :, blk * 128:(blk + 1) * 128], self.wq[l, blk], w=['pwqr'])
            n = 0
            for pi_, pr in enumerate(pairs):
                hb = pi_ % 2
                for j, tt in enumerate(pr):
                    self.dma('sp', h2T[hb][:, j], self.HTB[tt], w=['ph2T%d' % hb])
                for blk in range(16):
                    pi = n % 8; n += 1
                    for k in range(16):
                        self.mm(ps[pi][:, 0:256].rearrange("p (j t) -> p j t", j=2), wqr[:, k, blk * 128:(blk + 1) * 128], h2T[hb][:, :, k, :], k == 0, k == 15,
                                r=['pwqr', 'ph2T%d' % hb], w=['ps%d' % pi])
                    self.cp('act' if blk % 2 else 'dve', qTb[:, blk, :], ps[pi][:, 0:256], r=['ps%d' % pi], w=['pqTb'])
                for j, tt in enumerate(pr):
                    rb = j
                    for hp in range(16):
                        bank = hp // 4
                        self.mm(ps[bank][:, (hp % 4) * NK:(hp % 4 + 1) * NK], qTb[:, hp, j * 128:(j + 1) * 128], sktt[:, hp % 2, :], True, True,
                                r=['pqTb', 'pskt'], w=['ps%d' % bank])
                    for bank in range(4):
                        self.cp('act', ssb[:, bank * 4:(bank + 1) * 4, :], ps[bank][:, 0:4 * NK].rearrange("p (a n) -> p a n", a=4), r=['ps%d' % bank], w=['pssb'])
                    V = self.S.op
                    for hp in range(16):
                        V('dve', lambda e, o=m8[:, hp, 0:8], i=ssb[:, hp, :]: e.max(out=o, in_=i), reads=['pssb'], writes=['pm8'])
                        V('dve', lambda e, o=i8[:, hp, 0:8], a=m8[:, hp, 0:8], i=ssb[:, hp, :]: e.max_index(out=o, in_max=a, in_values=i), reads=['pssb', 'pm8'], writes=['pi8'])
                        V('dve', lambda e, o=wk[:, hp, :], a=m8[:, hp, 0:8], i=ssb[:, hp, :]: e.match_replace(out=o, in_to_replace=a, in_values=i, imm_value=-1e30), reads=['pssb', 'pm8'], writes=['pwk'])
                        V('dve', lambda e, o=m8[:, hp, 8:16], i=wk[:, hp, :]: e.max(out=o, in_=i), reads=['pwk'], writes=['pm8'])
                        V('dve', lambda e, o=i8[:, hp, 8:16], a=m8[:, hp, 8:16], i=wk[:, hp, :]: e.max_index(out=o, in_max=a, in_values=i), reads=['pwk', 'pm8'], writes=['pi8'])
                    self.cp('dve', tif[:], i8[:], r=['pi8'], w=['ptif'])
                    tv4 = m8[:].rearrange("q (h p) k -> q h p k", p=2); ti4 = tif[:].rearrange("q (h p) k -> q h p k", p=2)
                    c4 = cand[:].rearrange("q h (a b) -> q h a b", a=16)
                    self.tt('dve', c4, tv4[:, :, 0, :].unsqueeze(3).broadcast_to([128, 8, 16, 16]), tv4[:, :, 1, :].unsqueeze(2).broadcast_to([128, 8, 16, 16]),
                            ALU.add, r=['pm8'], w=['pcand'])
                    for h in range(8):
                        V('dve', lambda e, o=b8[:, h, 0:8], i=cand[:, h, :]: e.max(out=o, in_=i), reads=['pcand'], writes=['pb8'])
                        V('dve', lambda e, o=f8[:, h, 0:8], a=b8[:, h, 0:8], i=cand[:, h, :]: e.max_index(out=o, in_max=a, in_values=i), reads=['pcand', 'pb8'], writes=['pf8'])
                        V('dve', lambda e, o=wk2[:, h, :], a=b8[:, h, 0:8], i=cand[:, h, :]: e.match_replace(out=o, in_to_replace=a, in_values=i, imm_value=-1e30), reads=['pcand', 'pb8'], writes=['pwk2'])
                        V('dve', lambda e, o=b8[:, h, 8:16], i=wk2[:, h, :]: e.max(out=o, in_=i), reads=['pwk2'], writes=['pb8'])
                        V('dve', lambda e, o=f8[:, h, 8:16], a=b8[:, h, 8:16], i=wk2[:, h, :]: e.max_index(out=o, in_max=a, in_values=i), reads=['pwk2', 'pb8'], writes=['pf8'])
                    self.ts('dve', fau[:], f8[:], 4, None, ALU.logical_shift_right, r=['pf8'], w=['pfau'])
                    self.ts('dve', fbu[:], f8[:], 15, None, ALU.bitwise_and, r=['pf8'], w=['pfbu'])
                    self.cp('dve', fa[:], fau[:], r=['pfau'], w=['pfa'])
                    self.cp('dve', fb[:], fbu[:], r=['pfbu'], w=['pfb'])
                    io4 = self.iota16.unsqueeze(1).unsqueeze(1).broadcast_to([128, 8, 16, 16])
                    for which, (fsel, pp) in enumerate(((fa, 0), (fb, 1))):
                        self.tt('dve', eq[:], fsel[:].unsqueeze(3).broadcast_to([128, 8, 16, 16]), io4, ALU.is_equal, r=['pfa', 'pfb', 'cft'], w=['peq'])
                        self.tt('pool', eq[:], eq[:], ti4[:, :, pp, :].unsqueeze(2).broadcast_to([128, 8, 16, 16]), ALU.mult, r=['peq', 'ptif'], w=['peq'])
                        V('dve', lambda e, o=rt[rb][:, which, :].rearrange("q (h k) -> q h k", h=8), i=eq[:]: e.tensor_reduce(out=o, in_=i, axis=AX.X, op=ALU.add),
                          reads=['peq'], writes=['prt%d' % rb])
                    self.tt('dve', ez[:], b8[:], b8[:, :, 0:1].broadcast_to([128, 8, 16]), ALU.subtract, r=['pb8'], w=['pez'])
                    self.act(ez[:], ez[:], AF.Exp, r=['pez'], w=['pez'])
                    V('dve', lambda e, o=zz[:], i=ez[:]: e.tensor_reduce(out=o, in_=i, axis=AX.X, op=ALU.add), reads=['pez'], writes=['pzz'])
                    V('dve', lambda e, o=zz[:]: e.reciprocal(out=o, in_=o), reads=['pzz'], writes=['pzz'])
                    self.tt('dve', rt[rb][:, 2, :].rearrange("q (h k) -> q h k", h=8), ez[:], zz[:].unsqueeze(2).broadcast_to([128, 8, 16]), ALU.mult,
                            r=['pez', 'pzz'], w=['prt%d' % rb])
                    self.dma('pool', self.RT[tt], rt[rb][:], r=['prt%d' % rb])
            self.S.barrier()
        G = 8
        with ExitStack() as es:
            sb = lambda n, s, d: self.sb(es, n, s, d)
            WT = sb("dWT", [128, NK, 256], BF16)
            h2T = sb("dh2T", [128, 2, 16, 128], BF16)
            NUB = 3; NI = min(NK, 24)
            utc = [sb("dut%d" % i, [128, 2, 16, NK], BF16) for i in range(NUB)]
            vch = [sb("dvc%d" % i, [128, 2, D], BF16) for i in range(NUB)]
            gaI = [sb("dgI%d" % i, [128, 256], BF16) for i in range(NI)]
            rtt = sb("drt", [128, 3, 128], F32); rT = sb("drT", [128, 3, 128], F32)
            oh2 = [sb("doh2%d" % i, [128, G, NK], BF16) for i in range(2)]
            oh1 = [sb("doh1%d" % i, [128, G, NK], BF16) for i in range(2)]
            ga = [sb("dga%d" % i, [128, 256], F32) for i in range(2)]
            lngt = sb("lng", [128, D], F32); lnbt = sb("lnb", [128, D], F32)
            Gt = sb("dG", [128, D], F32)
            bufs = self.ln_bufs(es)
            self.dma('sp', lngt[:], self.lng[l, 1], w=['lng']); self.dma('sp', lnbt[:], self.lnb[l, 1], w=['lnb'])
            io3 = self.iota128[:, 0:NK].unsqueeze(1).broadcast_to([128, G, NK])
            curw = None; n = 0; m = 0
            for pr in pairs:
                w = 1 if pr[0] < NC else 0
                if w != curw:
                    self.dma('sp', Gt[:], self.MOD[l, w, :, 5 * D:6 * D], w=['dG']); curw = w
                for j, tt in enumerate(pr):
                    self.dma('sp', h2T[:, j], self.HTB[tt], w=['dh2T'])
                started = 0

                def phase1_mm(i1, dest, dkey, banks):
                    ub = (i1 // 2) % NUB; pi = banks[i1 % len(banks)]
                    if i1 % 2 == 0:
                        self.dma('sp', utc[ub][:], self.UTB[l, i1:i1 + 2].rearrange("c p k e -> p c k e"), w=['dut%d' % ub])
                    for k in range(16):
                        self.mm(ps[pi][0:NK, 0:256].rearrange("p (j t) -> p j t", j=2), utc[ub][:, i1 % 2, k, :], h2T[:, :, k, :], k == 0, k == 15,
                                r=['dut%d' % ub, 'dh2T'], w=['ps%d' % pi])
                    self.act(dest, ps[pi][0:NK, 0:256], AF.Gelu, r=['ps%d' % pi], w=[dkey])

                for j, tt in enumerate(pr):
                    self.dma('sp', rtt[:], self.RT[tt], w=['drt'])
                    for a in range(3):
                        self.tp(ps[5 + a][:, 0:128], rtt[:, a, :], self.identf, r=['drt', 'cft'], w=['ps%d' % (5 + a)])
                        self.cp('act', rT[:, a, :], ps[5 + a][:, 0:128], r=['ps%d' % (5 + a)], w=['drT'])
                    for t0 in range(0, 128, G):
                        ob = m % 2; m += 1
                        self.tt('dve', oh2[ob][:], io3, rT[:, 1, t0:t0 + G].unsqueeze(2).broadcast_to([128, G, NK]), ALU.is_equal, r=['cft', 'drT'], w=['doh2%d' % ob])
                        self.tt('dve', oh1[ob][:], io3, rT[:, 0, t0:t0 + G].unsqueeze(2).broadcast_to([128, G, NK]), ALU.is_equal, r=['cft', 'drT'], w=['doh1%d' % ob])
                        self.tt('pool', oh1[ob][:], oh1[ob][:], rT[:, 2, t0:t0 + G].unsqueeze(2).broadcast_to([128, G, NK]), ALU.mult, r=['doh1%d' % ob, 'drT'], w=['doh1%d' % ob])
                        for q0 in range(0, G, 4):
                            pi = 5 + (n % 3); n += 1
                            for q in range(4):
                                self.mm(ps[pi][0:NK, q * NK:(q + 1) * NK], oh2[ob][:, q0 + q, :], oh1[ob][:, q0 + q, :], True, True,
                                        r=['doh2%d' % ob, 'doh1%d' % ob], w=['ps%d' % pi])
                            tb = j * 128 + t0 + q0
                            self.cp('act', WT[0:NK, :, tb:tb + 4].rearrange("p i t -> p t i"), ps[pi][0:NK, 0:4 * NK].rearrange("p (t i) -> p t i", t=4),
                                    r=['ps%d' % pi], w=['dWTb'])
                        if started < NI:
                            phase1_mm(started, gaI[started][0:NK, :], 'dgI%d' % started, [0, 1, 2, 3, 4]); started += 1
                while started < NI:
                    phase1_mm(started, gaI[started][0:NK, :], 'dgI%d' % started, [0, 1, 2, 3, 4]); started += 1
                for i1 in range(NI):
                    self.tt('dve' if i1 % 2 else 'pool', WT[0:NK, i1, :], WT[0:NK, i1, :], gaI[i1][0:NK, :], ALU.mult, r=['dWTb', 'dgI%d' % i1], w=[('dWT', i1)])
                for i1 in range(NI, NK):
                    gb = i1 % 2
                    phase1_mm(i1, ga[gb][0:NK, :], 'dga%d' % gb, list(range(8)))
                    self.tt('dve' if i1 % 2 else 'pool', WT[0:NK, i1, :], WT[0:NK, i1, :], ga[gb][0:NK, :], ALU.mult, r=['dWTb', 'dga%d' % gb], w=[('dWT', i1)])
                for i1 in range(NK):
                    vb = (i1 // 2) % NUB
                    if i1 % 2 == 0:
                        self.dma('act' if (i1 // 2) % 2 else 'sp', vch[vb][0:NK], self.VB[l, i1 * NK:(i1 + 2) * NK, :].rearrange("(c p) d -> p c d", p=NK), w=['dvc%d' % vb])
                    for j in range(len(pr)):
                        for nb in range(4):
                            self.mm(ps[j * 4 + nb][:], WT[0:NK, i1, j * 128:(j + 1) * 128], vch[vb][0:NK, i1 % 2, nb * 512:(nb + 1) * 512], i1 == 0, i1 == NK - 1,
                                    r=[('dWT', i1), 'dWTb', 'dvc%d' % vb], w=['ps%d' % (j * 4 + nb)])
                for j, tt in enumerate(pr):
                    dsts = [self.XS[tt]]
                    if last:
                        dsts = [self.out[tt - NC]]
                    self.resid_ln(bufs, self.XS[tt], [j * 4 + i for i in range(4)], Gt, 'dG', lngt, lnbt, dsts)
            self.S.barrier()


_CACHE = {}


def run(inputs, cfg, core_ids=None):
    maps = prep_inputs(inputs, cfg)
    key = (cfg.L, cfg.LC, cfg.DEPTH, cfg.NK, cfg.dbg)
    if key not in _CACHE:
        p = Prog(cfg); p.build(); _CACHE[key] = p
    p = _CACHE[key]
    res = run_bass_kernel_spmd(p.nc, maps, core_ids=list(range(len(maps))))
    outs = [r["out"].reshape(cfg.L, D) for r in res.results]
    return np.stack(outs).astype(np.float32), res, p


def kernel(**inputs):
    cfg = Cfg()
    out, _, _ = run(inputs, cfg)
    return out
```

```python
import numpy as np
import ml_dtypes
from contextlib import ExitStack
import concourse.bass as bass
import concourse.mybir as mybir
from concourse.bass_utils import run_bass_kernel_spmd

F32 = mybir.dt.float32; BF16 = mybir.dt.bfloat16; U32 = mybir.dt.uint32
ALU = mybir.AluOpType; AF = mybir.ActivationFunctionType; AX = mybir.AxisListType

D = 2048
ALPHA = (2.0 * 4) ** 0.25
LN_EPS = 1e-6
GRID_W = 64


class Cfg:
    def __init__(self, L=8192, LC=256, DEPTH=4, NK=128, dbg=False):
        self.L = L; self.LC = LC; self.DEPTH = DEPTH; self.NK = NK; self.dbg = dbg
        self.NC = LC // 128; self.NL = L // 128; self.TT = self.NC + self.NL
        self.TOK = L + LC
        self.NE = NK * NK


class Sched:
    ENG = ['pe', 'dve', 'act', 'pool', 'sp']
    CE = ['pe', 'dve', 'act', 'pool']
    EPOCH = 30000
    NEP = {'pe': 26, 'dve': 5, 'act': 4, 'pool': 3}

    def __init__(self, nc, es, ndma=48):
        self.nc = nc
        self.esem = {e: [es.enter_context(nc.semaphore('s_%s%d' % (e, i))) for i in range(self.NEP[e])] for e in self.CE}
        self.dsem = [es.enter_context(nc.semaphore('d%d' % i)) for i in range(ndma)]
        self.nreg = ndma
        self.dsem += [es.enter_context(nc.semaphore('db%d' % i)) for i in range(8)]
        self.dcnt = [0] * (ndma + 8); self.dnext = 0; self.bnext = 0
        self.cnt = {e: 0 for e in self.CE}; self.ep = {e: 0 for e in self.CE}
        self.prog = {e: [] for e in self.ENG}
        self.seen = {e: {} for e in self.ENG}
        self.res = {}
        self.ninst = 0

    def semh(self, sid):
        return self.esem[sid[1]][sid[2]] if sid[0] == 'e' else self.dsem[sid[1]]

    def _need(self, eng, sid, val, waits):
        s = self.seen[eng]
        if sid[0] == 'e':
            k = ('e', sid[1]); cur = s.get(k, (-1, 0)); new = (sid[2], val)
            if cur >= new:
                return
            s[k] = new
        else:
            if s.get(sid, 0) >= val:
                return
            s[sid] = val
        waits.append((sid, val))

    def _deps(self, eng, reads, writes):
        waits = []
        pe = eng == 'pe'
        for k in reads:
            st = self.res.get(k)
            if st:
                for sid, v in st[0].items():
                    if not (pe and sid[0] == 'e' and sid[1] == 'pe'):
                        self._need(eng, sid, v, waits)
        for k in writes:
            st = self.res.get(k)
            if st:
                for d in st:
                    for sid, v in d.items():
                        if not (pe and sid[0] == 'e' and sid[1] == 'pe'):
                            self._need(eng, sid, v, waits)
        return waits

    def _mark(self, ev, reads, writes):
        sid, v = ev
        for k in reads:
            self.res.setdefault(k, ({}, {}))[1][sid] = v
        for k in writes:
            self.res.setdefault(k, ({}, {}))[0][sid] = v

    def op(self, eng, fn, reads=(), writes=()):
        waits = self._deps(eng, reads, writes)
        if self.cnt[eng] >= self.EPOCH:
            self.ep[eng] += 1; self.cnt[eng] = 0
        self.cnt[eng] += 1; ev = (('e', eng, self.ep[eng]), self.cnt[eng])
        self.prog[eng].append((waits, fn, ev)); self._mark(ev, reads, writes); self.ninst += 1

    def dma(self, eng, fn, reads=(), writes=(), bulk=False):
        if bulk:
            slot = self.nreg + self.bnext; self.bnext = (self.bnext + 1) % 8
        else:
            slot = self.dnext; self.dnext = (self.dnext + 1) % self.nreg
        waits = self._deps(eng, reads, writes)
        if self.dcnt[slot] > 0:
            self._need(eng, ('d', slot), self.dcnt[slot], waits)
        self.dcnt[slot] += 16; ev = (('d', slot), self.dcnt[slot])
        self.prog[eng].append((waits, fn, ev)); self._mark(ev, reads, writes); self.ninst += 1

    def barrier(self):
        for e in self.ENG:
            waits = []
            for i, c in enumerate(self.dcnt):
                if c:
                    self._need(e, ('d', i), c, waits)
            for o in self.CE:
                if o != e and (self.cnt[o] or self.ep[o]):
                    self._need(e, ('e', o, self.ep[o]), self.cnt[o], waits)
            if waits:
                self.prog[e].append((waits, None, None))
        self.res = {}

    def emit(self, block):
        engmap = {'pe': 'tensor', 'dve': 'vector', 'act': 'scalar', 'pool': 'gpsimd', 'sp': 'sync'}
        for e in self.ENG:
            prog = self.prog[e]

            def body(engine, prog=prog):
                for waits, fn, ev in prog:
                    for sid, v in waits:
                        engine.wait_ge(self.semh(sid), v)
                    if fn is None:
                        continue
                    ins = fn(engine)
                    sid, v = ev
                    ins.then_inc(self.semh(sid), 16 if sid[0] == 'd' else 1)
            getattr(block, engmap[e])(body)


def _blk_stationary(w):
    n = w.shape[1] // 128
    return np.ascontiguousarray(w.reshape(16, 128, n, 128).transpose(2, 1, 0, 3))


def _blk_moving(w, width=512):
    n = w.shape[1] // width
    return np.ascontiguousarray(w.reshape(16, 128, n, width).transpose(2, 1, 0, 3))


def rope_perm():
    d = np.arange(128)
    partner = np.where((d % 64) < 32, d + 32, d - 32)
    sign = np.where((d % 64) < 32, -1.0, 1.0).astype(np.float32)
    return partner, sign


def const_tables(cfg):
    cf = np.zeros((128, 912), np.float32)
    cf[:, 0:128] = np.eye(128, dtype=np.float32)
    cf[:, 128:256] = np.arange(128, dtype=np.float32)[None, :]
    s = -1.0 / 16.0
    j = np.arange(64)[:, None]; i = np.arange(64)[None, :]
    cf[:64, 256:320] = np.where(j <= i, s, 0.0)
    cf[:64, 320:384] = np.where(j >= i, s, 0.0)
    cf[:64, 384:448] = np.where(j > i, s, 0.0)
    cf[:64, 448:512] = np.where(j < i, s, 0.0)
    cf[:64, 512:576] = np.where(j <= i, 1.0, 0.0)
    cf[:64, 576:640] = np.where(j >= i, 1.0, 0.0)
    cf[:, 640:656] = np.arange(16, dtype=np.float32)[None, :]
    jj = np.arange(128)[:, None]; ii = np.arange(128)[None, :]
    cf[:, 656:784] = np.where(jj >= ii, 1.0, 0.0)
    cf[:, 784:912] = np.where(jj <= ii, 1.0, 0.0)
    t = np.arange(cfg.L)
    r = (t // GRID_W).astype(np.float32); c = (t % GRID_W).astype(np.float32)
    inv = (10000.0 ** (-np.arange(0, 64, 2, dtype=np.float32) / 64.0)).astype(np.float32)
    ang_r = r[None, :] * inv[:, None]; ang_c = c[None, :] * inv[:, None]
    d = np.arange(128)
    ang = np.where((d < 64)[:, None], ang_r[d % 32], ang_c[d % 32]).astype(np.float32)
    _, sign = rope_perm()
    cos = np.cos(ang).astype(np.float32); sins = (np.sin(ang).astype(np.float32) * sign[:, None])
    return cf, np.ascontiguousarray(cos), np.ascontiguousarray(sins)


def prep_inputs(inp, cfg):
    f = lambda a: np.ascontiguousarray(np.asarray(a, dtype=np.float32))
    x = f(inp['x']); c = f(inp['c']); ctx = f(inp['ctx']); c_ctx = f(inp['c_ctx'])
    w_ada = f(inp['w_ada']); b_ada = f(inp['b_ada']); w_in = f(inp['w_in'])
    NL_ = cfg.DEPTH
    cf, cos, sins = const_tables(cfg)
    partner, _ = rope_perm()
    wa = np.stack([_blk_moving(w_ada[l]) for l in range(NL_)])
    ba = np.ascontiguousarray(b_ada[:NL_].reshape(NL_, 24, 1, 512))
    wf_l, wt_l = [], []
    for l in range(NL_):
        w = w_in[l]
        sq = w[:, 3104:4128].reshape(D, 8, 128); sk = w[:, 4128:4384].reshape(D, 2, 128)
        lr = np.zeros((D, 128), np.float32); lr[:, :32] = w[:, 3072:3104]
        fm = np.concatenate([w[:, 0:1536], w[:, 1536:2048], sq.reshape(D, 1024), sq[:, :, partner].reshape(D, 1024),
                             sk.reshape(D, 256), sk[:, :, partner].reshape(D, 256), lr], axis=1)
        wf_l.append(_blk_stationary(fm))
        tm = np.concatenate([w[:, 1792:2048], w[:, 2048:2560], w[:, 4384:4640], w[:, 2560:3072]], axis=1)
        wt_l.append(_blk_moving(tm))
    wf = np.stack(wf_l); wt = np.stack(wt_l)
    conv_w = f(inp['conv_w'])[:NL_]
    cw = np.ascontiguousarray(conv_w.reshape(NL_, 3, 4, 128).transpose(0, 3, 2, 1))
    w2 = f(inp['gla_w2'])[:NL_]; b2 = f(inp['gla_b2'])[:NL_]
    w2a = np.zeros((NL_, 33, 512), np.float32)
    w2a[:, 0:16, 0:256] = w2[:, 0]; w2a[:, 16:32, 256:512] = w2[:, 1]
    w2a[:, 32, 0:256] = b2[:, 0]; w2a[:, 32, 256:512] = b2[:, 1]
    gnorm = np.ascontiguousarray(np.broadcast_to(f(inp['gla_norm'])[:NL_, None, :], (NL_, 128, 128)))
    sink = np.ascontiguousarray(np.broadcast_to(f(inp['swa_sink'])[:NL_, None, :], (NL_, 128, 8)))
    w_out = f(inp['w_out'])[:NL_]
    wo = np.ascontiguousarray(w_out.reshape(NL_, 16, 128, D).transpose(0, 2, 1, 3))
    lng = np.ascontiguousarray(np.broadcast_to(f(inp['ln_g'])[:NL_, :, None, :], (NL_, 2, 128, D)))
    lnb = np.ascontiguousarray(np.broadcast_to(f(inp['ln_b'])[:NL_, :, None, :], (NL_, 2, 128, D)))
    wq = np.stack([_blk_stationary(f(inp['peer_wq'])[l]) for l in range(NL_)])
    sk_ = f(inp['peer_subkeys'])[:NL_]
    skt = np.ascontiguousarray(sk_.transpose(0, 3, 1, 2))
    NK = cfg.NK
    pu = f(inp['peer_u'])[:NL_]
    ut = np.ascontiguousarray(pu.reshape(NL_, NK, NK, 16, 128).transpose(0, 1, 4, 3, 2))
    pv = f(inp['peer_v'])[:NL_]
    maps = []
    for b in range(x.shape[0]):
        cc = np.stack([c[b].reshape(16, 128).T, c_ctx.reshape(16, 128).T])
        xs = np.concatenate([ctx[b], x[b]], axis=0).reshape(cfg.TT, 128, D)
        maps.append(dict(xs_in=np.ascontiguousarray(xs), cc=np.ascontiguousarray(cc), wa=wa, ba=ba, wf=wf, wt=wt,
                         cw=cw, w2a=w2a, gnorm=gnorm, sink=sink, wo=wo, lng=lng, lnb=lnb, wq=wq, skt=skt,
                         ut=ut, pv=pv, cf=cf, cos=cos, sins=sins))
    return maps


class Prog:
    def __init__(self, cfg):
        self.cfg = cfg
        self.nc = bass.Bass("TRN2", target_bir_lowering=False)
        self.dbg_outs = []

    def din(self, name, shape, dt=F32):
        return self.nc.dram_tensor(name, list(shape), dt, kind="ExternalInput").ap()

    def dscr(self, name, shape, dt=F32, dbg=False):
        if dbg and self.cfg.dbg:
            self.dbg_outs.append(name)
            return self.nc.dram_tensor(name, list(shape), dt, kind="ExternalOutput").ap()
        return self.nc.dram_tensor(name, list(shape), dt).ap()

    def sb(self, es, name, shape, dt):
        self.uid = getattr(self, 'uid', 0) + 1
        return es.enter_context(self.nc.sbuf_tensor("%s_u%d" % (name, self.uid), list(shape), dt))

    def dma(self, q, out, in_, r=(), w=(), bulk=False):
        self.S.dma(q, lambda e, o=out, i=in_: e.dma_start(out=o, in_=i), reads=r, writes=w, bulk=bulk)

    def mm(self, out, lhsT, rhs, start, stop, r=(), w=()):
        self.S.op('pe', lambda e, o=out, l=lhsT, rh=rhs, s=start, t=stop: e.matmul(o, lhsT=l, rhs=rh, start=s, stop=t), reads=r, writes=w)

    def tp(self, out, in_, ident, r=(), w=()):
        self.S.op('pe', lambda e, o=out, i=in_, d=ident: e.transpose(out=o, in_=i, identity=d), reads=r, writes=w)

    def tt(self, eng, out, in0, in1, op, r=(), w=()):
        self.S.op(eng, lambda e, o=out, a=in0, b=in1, p=op: e.tensor_tensor(out=o, in0=a, in1=b, op=p), reads=r, writes=w)

    def ts(self, eng, out, in0, s1, s2, op0, op1=None, r=(), w=()):
        if op1 is None:
            self.S.op(eng, lambda e, o=out, a=in0, x=s1, p=op0: e.tensor_scalar(out=o, in0=a, scalar1=x, scalar2=None, op0=p), reads=r, writes=w)
        else:
            self.S.op(eng, lambda e, o=out, a=in0, x=s1, y=s2, p=op0, q=op1: e.tensor_scalar(out=o, in0=a, scalar1=x, scalar2=y, op0=p, op1=q), reads=r, writes=w)

    def stt(self, eng, out, in0, scalar, in1, op0, op1, r=(), w=()):
        self.S.op(eng, lambda e, o=out, a=in0, s=scalar, b=in1, p=op0, q=op1: e.scalar_tensor_tensor(out=o, in0=a, scalar=s, in1=b, op0=p, op1=q), reads=r, writes=w)

    def act(self, out, in_, func, r=(), w=(), bias=None, scale=None, accum=None):
        kw = {}
        if bias is not None: kw['bias'] = bias
        if scale is not None: kw['scale'] = scale
        if accum is not None: kw['accum_out'] = accum
        self.S.op('act', lambda e, o=out, i=in_, f=func, kw=kw: e.activation(out=o, in_=i, func=f, **kw), reads=r, writes=w)

    def cp(self, eng, out, in_, r=(), w=()):
        if eng == 'act':
            self.S.op('act', lambda e, o=out, i=in_: e.copy(out=o, in_=i), reads=r, writes=w)
        else:
            self.S.op(eng, lambda e, o=out, i=in_: e.tensor_copy(out=o, in_=i), reads=r, writes=w)

    def memset(self, eng, ap, val, w=()):
        self.S.op(eng, lambda e, a=ap, v=val: e.memset(a, v), writes=w)

    def build(self):
        cfg = self.cfg; nc = self.nc
        TT, NLY, NK = cfg.TT, cfg.DEPTH, cfg.NK
        self.xs_in = self.din("xs_in", [TT, 128, D])
        self.cc = self.din("cc", [2, 128, 16])
        self.wa = self.din("wa", [NLY, 24, 128, 16, 512]); self.ba = self.din("ba", [NLY, 24, 1, 512])
        self.wf = self.din("wf", [NLY, 37, 128, 16, 128]); self.wt = self.din("wt", [NLY, 3, 128, 16, 512])
        self.cw = self.din("cw", [NLY, 128, 4, 3]); self.w2a = self.din("w2a", [NLY, 33, 512])
        self.gnorm = self.din("gnorm", [NLY, 128, 128]); self.sink = self.din("sink", [NLY, 128, 8])
        self.wo = self.din("wo", [NLY, 128, 16, D])
        self.lng = self.din("lng", [NLY, 2, 128, D]); self.lnb = self.din("lnb", [NLY, 2, 128, D])
        self.wq = self.din("wq", [NLY, 16, 128, 16, 128]); self.skt = self.din("skt", [NLY, 128, 2, NK])
        self.ut = self.din("ut", [NLY, NK, 128, 16, NK]); self.pv = self.din("pv", [NLY, cfg.NE, D])
        self.cf = self.din("cf", [128, 912]); self.cos = self.din("cos", [128, cfg.L]); self.sins = self.din("sins", [128, cfg.L])
        self.out = nc.dram_tensor("out", [cfg.NL, 128, D], F32, kind="ExternalOutput").ap()
        self.XS = self.dscr("XS", [TT, 128, D], F32, dbg=True)
        self.MOD = self.dscr("MOD", [NLY, 2, 128, 6 * D], F32, dbg=True)
        self.HTB = self.dscr("HTB", [TT, 128, 16, 128], BF16)
        self.ZT = self.dscr("ZT", [4736, cfg.TOK], F32, dbg=True)
        self.ZV = self.dscr("ZV", [TT, 128, 1536], F32, dbg=True)
        self.QKT = self.dscr("QKT", [1280, cfg.TOK], BF16)
        self.MIXT = self.dscr("MIXT", [D, cfg.TOK], BF16, dbg=True)
        self.OF = self.dscr("OF", [cfg.TOK // 64, 64, 512], F32)
        self.WFB = self.dscr("WFB", [NLY, 37, 128, 16, 128], BF16)
        self.UTB = self.dscr("UTB", [NLY, NK, 128, 16, NK], BF16)
        self.VB = self.dscr("VB", [NLY, cfg.NE, D], BF16)
        es = ExitStack()
        with es:
            self.S = Sched(nc, es)
            self.ps = [es.enter_context(nc.psum_tensor("ps%d" % i, [128, 512], F32)) for i in range(8)]
            self.consts(es)
            self.stage_prep()
            self.stage_adaln()
            for l in range(NLY):
                last = (l == NLY - 1)
                self.stage_modT(l, 0, range(TT))
                self.stage_inproj(l)
                self.stage_rope(l)
                self.stage_conv(l)
                self.stage_gla(l)
                self.stage_swa(l, last)
                tiles = range(cfg.NC, TT) if last else range(TT)
                self.stage_outproj(l, tiles)
                self.stage_modT(l, 1, tiles)
                self.stage_peer(l, tiles, last)
            self.S.barrier()
            with nc.Block() as block:
                self.S.emit(block)
        return nc

    def consts(self, es):
        self.cft = self.sb(es, "cft", [128, 912], F32)
        self.identb = self.sb(es, "identb", [128, 128], BF16)
        self.onesb = self.sb(es, "onesb", [128, 128], BF16)
        self.dma('sp', self.cft[:], self.cf, w=['cft'])
        self.cp('dve', self.identb[:], self.cft[:, 0:128], r=['cft'], w=['identb'])
        self.memset('dve', self.onesb[:], 1.0, w=['onesb'])
        c = self.cft
        self.identf = c[:, 0:128]; self.iota128 = c[:, 128:256]
        self.triT = [c[0:64, 256:320], c[0:64, 320:384]]
        self.amT = [c[0:64, 384:448], c[0:64, 448:512]]
        self.maskT = [c[0:64, 512:576], c[0:64, 576:640]]
        self.iota16 = c[:, 640:656]
        self.maskP = c[:, 656:784]; self.maskN = c[:, 784:912]
        self.S.barrier()

    def psb(self, i):
        return self.ps[i][:].bitcast(BF16)

    def stage_prep(self):
        cfg = self.cfg; NK = cfg.NK
        for l in range(cfg.DEPTH):
            src = self.wf[l].rearrange("i p k e -> (i p) (k e)"); dst = self.WFB[l].rearrange("i p k e -> (i p) (k e)")
            for r0 in range(0, 37 * 128, 1024):
                r1 = min(37 * 128, r0 + 1024)
                self.dma('pool', dst[r0:r1, :], src[r0:r1, :])
        self.convert_uv(0)
        self.S.barrier()

    def convert_uv(self, l, parts=None):
        cfg = self.cfg; NK = cfg.NK
        src = self.ut[l].rearrange("i p k e -> (i p) (k e)"); dst = self.UTB[l].rearrange("i p k e -> (i p) (k e)")
        jobs = []
        rows = NK * 128
        for r0 in range(0, rows, 1024):
            r1 = min(rows, r0 + 1024)
            jobs.append((dst[r0:r1, :], src[r0:r1, :]))
        for r0 in range(0, cfg.NE, 1024):
            r1 = min(cfg.NE, r0 + 1024)
            jobs.append((self.VB[l][r0:r1, :], self.pv[l][r0:r1, :]))
        per = (len(jobs) + 3) // 4
        for p in (range(4) if parts is None else parts):
            for (o, i) in jobs[p * per:(p + 1) * per]:
                self.dma('pool', o, i, bulk=True)

    def stage_adaln(self):
        cfg = self.cfg
        with ExitStack() as es:
            ccs = self.sb(es, "ccs", [128, 2, 16], F32); cs = self.sb(es, "cs", [128, 2, 16], F32)
            rep = self.sb(es, "rep", [128, 2, 16, 128], BF16)
            wat = [self.sb(es, "wat%d" % i, [128, 16, 512], BF16) for i in range(2)]
            bat = [self.sb(es, "bat%d" % i, [1, 512], BF16) for i in range(2)]
            mo = [self.sb(es, "mo%d" % i, [128, 512], F32) for i in range(4)]
            self.dma('sp', ccs[:], self.cc.rearrange("w p k -> p w k"), w=['ccs'])
            self.act(cs[:], ccs[:], AF.Silu, r=['ccs'], w=['cs'])
            for w in range(2):
                self.cp('dve', rep[:, w], cs[:, w, :].unsqueeze(2).broadcast_to([128, 16, 128]), r=['cs'], w=['rep'])
            n = 0
            for l in range(cfg.DEPTH):
                for cb in range(24):
                    b = (l * 24 + cb) % 2
                    self.dma('pool', wat[b][:], self.wa[l, cb], w=['wat%d' % b])
                    self.dma('pool', bat[b][:], self.ba[l, cb], w=['bat%d' % b])
                    for w in range(2):
                        pi = n % 8; m = n % 4; n += 1
                        for k in range(16):
                            self.mm(self.ps[pi][:], rep[:, w, k, :], wat[b][:, k, :], k == 0, False, r=['rep', 'wat%d' % b], w=['ps%d' % pi])
                        self.mm(self.ps[pi][:], self.onesb[0:1, :], bat[b][0:1, :], False, True, r=['onesb', 'bat%d' % b], w=['ps%d' % pi])
                        if cb // 4 in (1, 4):
                            self.ts('dve', mo[m][:], self.ps[pi][:], 1.0, None, ALU.add, r=['ps%d' % pi], w=['mo%d' % m])
                        else:
                            self.cp('act', mo[m][:], self.ps[pi][:], r=['ps%d' % pi], w=['mo%d' % m])
                        self.dma('sp', self.MOD[l, w, :, cb * 512:(cb + 1) * 512], mo[m][:], r=['mo%d' % m])
            self.S.barrier()

    def stage_modT(self, l, which, tiles):
        cfg = self.cfg
        with ExitStack() as es:
            xs = [self.sb(es, "mx%d" % i, [128, D], F32) for i in range(2)]
            tmp = self.sb(es, "mtmp", [128, D], F32)
            hb = [self.sb(es, "mhb%d" % i, [128, D], BF16) for i in range(2)]
            hts = [self.sb(es, "mhts%d" % i, [128, 16, 128], BF16) for i in range(2)]
            sct = [self.sb(es, "msc%d" % i, [128, D], F32) for i in range(2)]
            sht = [self.sb(es, "msh%d" % i, [128, D], F32) for i in range(2)]
            o_sh = 0 if which == 0 else 3 * D
            for w in range(2):
                self.dma('sp', sht[w][:], self.MOD[l, w, :, o_sh:o_sh + D], w=['msh%d' % w])
                self.dma('sp', sct[w][:], self.MOD[l, w, :, o_sh + D:o_sh + 2 * D], w=['msc%d' % w])
            src = self.xs_in if (l == 0 and which == 0) else self.XS
            for n, tt in enumerate(tiles):
                b = n % 2; w = 1 if tt < cfg.NC else 0
                self.dma('sp', xs[b][:], src[tt], w=['mx%d' % b])
                self.tt('dve', tmp[:], xs[b][:], sct[w][:], ALU.mult, r=['mx%d' % b, 'msc%d' % w], w=['mtmp'])
                self.tt('pool', hb[b][:], tmp[:], sht[w][:], ALU.add, r=['mtmp', 'msh%d' % w], w=['mhb%d' % b])
                for half in range(2):
                    pi = (n * 2 + half) % 8
                    pb = self.psb(pi)
                    for j in range(8):
                        k = half * 8 + j
                        self.tp(pb[:, j * 128:(j + 1) * 128], hb[b][:, k * 128:(k + 1) * 128], self.identb[:], r=['mhb%d' % b, 'identb'], w=['ps%d' % pi])
                    self.cp('act', hts[b][:, half * 8:(half + 1) * 8, :], pb.rearrange("p (k t) -> p k t", k=8), r=['ps%d' % pi], w=['mhts%d' % b])
                self.dma('pool', self.HTB[tt], hts[b][:], r=['mhts%d' % b])
            self.S.barrier()

    def stage_inproj(self, l):
        cfg = self.cfg
        with ExitStack() as es:
            hT = [self.sb(es, "ihT%d" % i, [128, 4, 16, 128], BF16) for i in range(2)]
            wtm = [self.sb(es, "iwtm%d" % i, [128, 16, 512], BF16) for i in range(3)]
            wst = [self.sb(es, "iwst%d" % i, [128, 16, 128], BF16) for i in range(4)]
            zo = [self.sb(es, "izo%d" % i, [128, 512], F32) for i in range(4)]
            for tb in range(3):
                self.dma('pool', wtm[tb][:], self.wt[l, tb], w=['iwtm%d' % tb])
            n = 0
            for mi, t0 in enumerate(range(0, cfg.TT, 4)):
                nt = min(4, cfg.TT - t0); hb = mi % 2
                for j in range(nt):
                    self.dma('sp', hT[hb][:, j], self.HTB[t0 + j], w=['ihT%d' % hb])
                for nb in range(37):
                    wb = nb % 4
                    self.dma('sp', wst[wb][:], self.WFB[l, nb], w=['iwst%d' % wb])
                    pi = n % 8; zb = n % 4; n += 1
                    for k in range(16):
                        self.mm(self.ps[pi][:, 0:nt * 128].rearrange("p (j t) -> p j t", j=nt), wst[wb][:, k, :], hT[hb][:, 0:nt, k, :],
                                k == 0, k == 15, r=['iwst%d' % wb, 'ihT%d' % hb], w=['ps%d' % pi])
                    self.cp('act' if n % 2 else 'dve', zo[zb][:, 0:nt * 128], self.ps[pi][:, 0:nt * 128], r=['ps%d' % pi], w=['izo%d' % zb])
                    self.dma('sp', self.ZT[nb * 128:(nb + 1) * 128, t0 * 128:(t0 + nt) * 128], zo[zb][:, 0:nt * 128], r=['izo%d' % zb])
                for j in range(nt):
                    for tb in range(3):
                        pi = n % 8; zb = n % 4; n += 1
                        for k in range(16):
                            self.mm(self.ps[pi][:], hT[hb][:, j, k, :], wtm[tb][:, k, :], k == 0, k == 15, r=['ihT%d' % hb, 'iwtm%d' % tb], w=['ps%d' % pi])
                        self.cp('act' if n % 2 else 'dve', zo[zb][:], self.ps[pi][:], r=['ps%d' % pi], w=['izo%d' % zb])
                        self.dma('sp', self.ZV[t0 + j][:, tb * 512:(tb + 1) * 512], zo[zb][:], r=['izo%d' % zb])
            self.S.barrier()

    def stage_rope(self, l):
        cfg = self.cfg; LC = cfg.LC
        rows = [(2048 + h * 128, 3072 + h * 128, h * 128) for h in range(8)] + [(4096 + g * 128, 4352 + g * 128, 1024 + g * 128) for g in range(2)]
        with ExitStack() as es:
            ta = [self.sb(es, "rta%d" % i, [128, 512], F32) for i in range(2)]
            tb = [self.sb(es, "rtb%d" % i, [128, 512], F32) for i in range(2)]
            ob = [self.sb(es, "rob%d" % i, [128, 512], BF16) for i in range(2)]
            ct = self.sb(es, "rct", [128, 512], F32); st = self.sb(es, "rst", [128, 512], F32)
            n = 0
            for c0 in range(0, LC, 512):
                cn = min(512, LC - c0)
                for (ra, rp, ro) in rows:
                    b = n % 2; n += 1
                    self.dma('sp', ta[b][:, 0:cn], self.ZT[ra:ra + 128, c0:c0 + cn], w=['rta%d' % b])
                    self.cp('act', ob[b][:, 0:cn], ta[b][:, 0:cn], r=['rta%d' % b], w=['rob%d' % b])
                    self.dma('pool', self.QKT[ro:ro + 128, c0:c0 + cn], ob[b][:, 0:cn], r=['rob%d' % b])
            for s0 in range(0, cfg.L, 512):
                self.dma('sp', ct[:], self.cos[:, s0:s0 + 512], w=['rct'])
                self.dma('sp', st[:], self.sins[:, s0:s0 + 512], w=['rst'])
                for (ra, rp, ro) in rows:
                    b = n % 2; n += 1
                    self.dma('sp', ta[b][:], self.ZT[ra:ra + 128, LC + s0:LC + s0 + 512], w=['rta%d' % b])
                    self.dma('sp', tb[b][:], self.ZT[rp:rp + 128, LC + s0:LC + s0 + 512], w=['rtb%d' % b])
                    self.tt('dve', ta[b][:], ta[b][:], ct[:], ALU.mult, r=['rta%d' % b, 'rct'], w=['rta%d' % b])
                    self.tt('pool', tb[b][:], tb[b][:], st[:], ALU.mult, r=['rtb%d' % b, 'rst'], w=['rtb%d' % b])
                    self.tt('dve', ob[b][:], ta[b][:], tb[b][:], ALU.add, r=['rta%d' % b, 'rtb%d' % b], w=['rob%d' % b])
                    self.dma('pool', self.QKT[ro:ro + 128, LC + s0:LC + s0 + 512], ob[b][:], r=['rob%d' % b])
            self.S.barrier()

    def stage_conv(self, l):
        cfg = self.cfg; SEG = 1024
        with ExitStack() as es:
            xi = [self.sb(es, "cxi%d" % i, [128, SEG + 2], F32) for i in range(2)]
            cg = [self.sb(es, "ccg%d" % i, [128, SEG + 2], F32) for i in range(2)]
            bb = [self.sb(es, "cbb%d" % i, [128, SEG], F32) for i in range(2)]
            u = [self.sb(es, "cu%d" % i, [128, SEG + 2], F32) for i in range(2)]
            acc = [self.sb(es, "cacc%d" % i, [128, SEG], F32) for i in range(2)]
            ob = [self.sb(es, "cob%d" % i, [128, SEG], BF16) for i in range(2)]
            cwt = self.sb(es, "ccw", [128, 4, 3], F32)
            self.dma('sp', cwt[:], self.cw[l], w=['ccw'])
            n = 0
            for (q0, q1) in ((0, cfg.LC), (cfg.LC, cfg.TOK)):
                for s0 in range(q0, q1, SEG):
                    sn = min(SEG, q1 - s0)
                    lo = max(q0, s0 - 1); hi = min(q1, s0 + sn + 1); off = lo - (s0 - 1); ln = hi - lo
                    for cc in range(4):
                        b = n % 2; n += 1
                        kx, kc, kb, ku, ka, ko = 'cxi%d' % b, 'ccg%d' % b, 'cbb%d' % b, 'cu%d' % b, 'cacc%d' % b, 'cob%d' % b
                        self.dma('sp', xi[b][:, off:off + ln], self.ZT[cc * 128:(cc + 1) * 128, lo:hi], w=[kx])
                        self.dma('sp', cg[b][:, off:off + ln], self.ZT[1024 + cc * 128:1024 + (cc + 1) * 128, lo:hi], w=[kc])
                        self.dma('sp', bb[b][:, 0:sn], self.ZT[512 + cc * 128:512 + (cc + 1) * 128, s0:s0 + sn], w=[kb])
                        if off > 0:
                            self.memset('pool', u[b][:, 0:1], 0.0, w=[ku])
                        if off + ln < sn + 2:
                            self.memset('pool', u[b][:, sn + 1:sn + 2], 0.0, w=[ku])
                        self.tt('pool', u[b][:, off:off + ln], cg[b][:, off:off + ln], xi[b][:, off:off + ln], ALU.mult, r=[kx, kc], w=[ku])
                        self.ts('dve', acc[b][:, 0:sn], u[b][:, 0:sn], cwt[:, cc, 0:1], None, ALU.mult, r=[ku, 'ccw'], w=[ka])
                        self.stt('dve', acc[b][:, 0:sn], u[b][:, 1:sn + 1], cwt[:, cc, 1:2], acc[b][:, 0:sn], ALU.mult, ALU.add, r=[ku, 'ccw', ka], w=[ka])
                        self.stt('dve', acc[b][:, 0:sn], u[b][:, 2:sn + 2], cwt[:, cc, 2:3], acc[b][:, 0:sn], ALU.mult, ALU.add, r=[ku, 'ccw', ka], w=[ka])
                        self.tt('pool', ob[b][:, 0:sn], acc[b][:, 0:sn], bb[b][:, 0:sn], ALU.mult, r=[ka, kb], w=[ko])
                        self.dma('pool', self.MIXT[cc * 128:(cc + 1) * 128, s0:s0 + sn], ob[b][:, 0:sn], r=[ko])
            self.S.barrier()

    def stage_gla(self, l):
        cfg = self.cfg
        NCHC = cfg.LC // 64; NCH = cfg.TOK // 64
        ps = self.ps
        with ExitStack() as es:
            sb = lambda n, s, d: self.sb(es, n, s, d)
            St = sb("gS", [64, 512], F32); Sbf = sb("gSbf", [64, 512], BF16)
            w2t = sb("gw2", [33, 512], F32); NW = sb("gNW", [64, 128], F32)
            B2 = range(2)
            qT = [sb("gqT%d" % i, [64, 4, 64], F32) for i in B2]; kT = [sb("gkT%d" % i, [64, 4, 64], F32) for i in B2]
            lrT = [sb("glr%d" % i, [33, 64], F32) for i in B2]
            kTok = [sb("gkk%d" % i, [64, 256], F32) for i in B2]; vTok = [sb("gvv%d" % i, [64, 512], F32) for i in B2]
            gTok = [sb("ggg%d" % i, [64, 512], F32) for i in B2]; ofl = [sb("gof%d" % i, [64, 512], F32) for i in B2]
            vbf = [sb("gvb%d" % i, [64, 512], BF16) for i in B2]
            lsb = [sb("gls%d" % i, [64, 256], F32) for i in B2]
            Eb = [sb("gEb%d" % i, [64, 256], F32) for i in B2]; Enb = [sb("gEn%d" % i, [64, 256], F32) for i in B2]
            Ek = [sb("gEk%d" % i, [64, 256], F32) for i in B2]
            qe = [sb("gqe%d" % i, [64, 4, 64], BF16) for i in B2]; ke = [sb("gke%d" % i, [64, 4, 64], BF16) for i in B2]
            kend = [sb("gkd%d" % i, [64, 256], BF16) for i in B2]; attT = [sb("gat%d" % i, [64, 4, 64], BF16) for i in B2]
            sq = sb("gsq", [64, 4, 128], F32); ss = sb("gss", [64, 4], F32); rstd = sb("grs", [64, 4], F32)
            on = sb("gon", [64, 512], F32); sg = sb("gsg", [64, 512], F32); res = sb("gres", [64, 512], BF16)
            mo = [sb("gmo%d" % i, [128, 4, 64], BF16) for i in B2]
            self.dma('sp', w2t[:], self.w2a[l], w=['gw2'])
            self.dma('sp', NW[:], self.gnorm[l][0:64, :], w=['gNW'])
            for i in B2:
                self.memset('pool', lrT[i][:], 1.0, w=['glr%d' % i])
            orders = [list(range(NCH)), list(range(NCHC - 1, -1, -1)) + list(range(NCH - 1, NCHC - 1, -1))]
            for d in range(2):
                self.memset('dve', St[:], 0.0, w=['gS'])
                self.memset('dve', Sbf[:], 0.0, w=['gSbf'])
                for n, ci in enumerate(orders[d]):
                    b = n % 2; c0 = ci * 64; tile = ci // 2; hf = ci % 2
                    K = lambda s: s + str(b)
                    self.dma('sp', qT[b][:], self.ZT[1536:1792, c0:c0 + 64].rearrange("(h k) t -> k h t", h=4), w=[K('gqT')])
                    self.dma('sp', kT[b][:], self.ZT[1792:2048, c0:c0 + 64].rearrange("(h k) t -> k h t", h=4), w=[K('gkT')])
                    self.dma('sp', lrT[b][0:32, :], self.ZT[4608:4640, c0:c0 + 64], w=[K('glr')])
                    zv = self.ZV[tile]
                    self.dma('sp', kTok[b][:], zv[hf * 64:(hf + 1) * 64, 0:256], w=[K('gkk')])
                    self.dma('sp', vTok[b][:], zv[hf * 64:(hf + 1) * 64, 256:768], w=[K('gvv')])
                    if d == 1:
                        self.dma('sp', gTok[b][:], zv[hf * 64:(hf + 1) * 64, 1024:1536], w=[K('ggg')])
                        self.dma('sp', ofl[b][:], self.OF[ci], w=[K('gof')])
                    self.mm(ps[0][0:64, 0:256], lrT[b][0:33, :], w2t[0:33, d * 256:(d + 1) * 256], True, True, r=[K('glr'), 'gw2'], w=['ps0'])
                    self.act(lsb[b][:], ps[0][0:64, 0:256], AF.Exp, scale=-1.0, r=['ps0'], w=[K('gls')])
                    self.act(lsb[b][:], lsb[b][:], AF.Ln, bias=1.0, r=[K('gls')], w=[K('gls')])
                    for h in range(4):
                        self.mm(ps[1][0:64, h * 64:(h + 1) * 64], lsb[b][:, h * 64:(h + 1) * 64], self.triT[d], True, True, r=[K('gls'), 'cft'], w=['ps1'])
                    self.mm(ps[2][0:64, 0:256], self.amT[d], lsb[b][:, 0:256], True, True, r=[K('gls'), 'cft'], w=['ps2'])
                    self.act(Eb[b][:], ps[1][0:64, 0:256], AF.Exp, r=['ps1'], w=[K('gEb')])
                    self.act(Enb[b][:], ps[1][0:64, 0:256], AF.Exp, scale=-1.0, r=['ps1'], w=[K('gEn')])
                    self.act(Ek[b][:], ps[2][0:64, 0:256], AF.Exp, r=['ps2'], w=[K('gEk')])
                    self.stt('dve', qe[b][:], qT[b][:], 0.125, Eb[b][:].rearrange("p (h t) -> p h t", h=4), ALU.mult, ALU.mult, r=[K('gqT'), K('gEb')], w=[K('gqe')])
                    self.tt('dve', ke[b][:], kT[b][:], Enb[b][:].rearrange("p (h t) -> p h t", h=4), ALU.mult, r=[K('gkT'), K('gEn')], w=[K('gke')])
                    self.tt('pool', kend[b][:], kTok[b][:], Ek[b][:], ALU.mult, r=[K('gkk'), K('gEk')], w=[K('gkd')])
                    self.cp('pool', vbf[b][:], vTok[b][:], r=[K('gvv')], w=[K('gvb')])
                    for h in range(4):
                        self.mm(ps[3][0:64, h * 64:(h + 1) * 64], ke[b][:, h, :], qe[b][:, h, :], True, True, r=[K('gke'), K('gqe')], w=['ps3'])
                    self.tt('dve', attT[b][:], ps[3][0:64, 0:256].rearrange("p (h t) -> p h t", h=4),
                            self.maskT[d].unsqueeze(1).broadcast_to([64, 4, 64]), ALU.mult, r=['ps3', 'cft'], w=[K('gat')])
                    for h in range(4):
                        self.mm(ps[4][0:64, h * 128:(h + 1) * 128], attT[b][:, h, :], vbf[b][:, h * 128:(h + 1) * 128], True, False, r=[K('gat'), K('gvb')], w=['ps4'])
                        self.mm(ps[4][0:64, h * 128:(h + 1) * 128], qe[b][:, h, :], Sbf[:, h * 128:(h + 1) * 128], False, True, r=[K('gqe'), 'gSbf'], w=['ps4'])
                    for h in range(4):
                        self.mm(ps[5][0:64, h * 128:(h + 1) * 128], kend[b][:, h * 64:(h + 1) * 64], vbf[b][:, h * 128:(h + 1) * 128], True, True, r=[K('gkd'), K('gvb')], w=['ps5'])
                    col = 63 if d == 0 else 0
                    for h in range(4):
                        self.stt('dve', St[:, h * 128:(h + 1) * 128], St[:, h * 128:(h + 1) * 128], Eb[b][:, h * 64 + col:h * 64 + col + 1],
                                 ps[5][0:64, h * 128:(h + 1) * 128], ALU.mult, ALU.add, r=['gS', K('gEb'), 'ps5'], w=['gS'])
                    self.cp('pool', Sbf[:], St[:], r=['gS'], w=['gSbf'])
                    if d == 0:
                        self.cp('act', ofl[b][:], ps[4][0:64, :], r=['ps4'], w=[K('gof')])
                        self.dma('pool', self.OF[ci], ofl[b][:], r=[K('gof')])
                    else:
                        self.tt('dve', ofl[b][:], ps[4][0:64, :], ofl[b][:], ALU.add, r=['ps4', K('gof')], w=[K('gof')])
                        o3 = ofl[b][:].rearrange("p (h v) -> p h v", h=4)
                        self.tt('dve', sq[:], o3, o3, ALU.mult, r=[K('gof')], w=['gsq'])
                        self.S.op('dve', lambda e, o=ss[:], i=sq[:]: e.tensor_reduce(out=o, in_=i, axis=AX.X, op=ALU.add), reads=['gsq'], writes=['gss'])
                        self.ts('dve', ss[:], ss[:], 1.0 / 128.0, LN_EPS, ALU.mult, ALU.add, r=['gss'], w=['gss'])
                        self.act(rstd[:], ss[:], AF.Ln, r=['gss'], w=['grs'])
                        self.act(rstd[:], rstd[:], AF.Exp, scale=-0.5, r=['grs'], w=['grs'])
                        for h in range(4):
                            self.stt('dve', on[:, h * 128:(h + 1) * 128], ofl[b][:, h * 128:(h + 1) * 128], rstd[:, h:h + 1], NW[:], ALU.mult, ALU.mult,
                                     r=[K('gof'), 'grs', 'gNW'], w=['gon'])
                        self.act(sg[:], gTok[b][:], AF.Silu, r=[K('ggg')], w=['gsg'])
                        self.tt('pool', res[:], on[:], sg[:], ALU.mult, r=['gon', 'gsg'], w=['gres'])
                        pb = self.psb(6)
                        for h in range(4):
                            self.tp(pb[:, h * 64:(h + 1) * 64], res[:, h * 128:(h + 1) * 128], self.identb[0:64, 0:64], r=['gres', 'identb'], w=['ps6'])
                        self.cp('act', mo[b][:], pb[:, 0:256].rearrange("p (h t) -> p h t", h=4), r=['ps6'], w=[K('gmo')])
                        self.dma('pool', self.MIXT[512:1024, c0:c0 + 64].rearrange("(h v) t -> v h t", h=4), mo[b][:], r=[K('gmo')])
                self.S.barrier()

    def stage_swa(self, l, last):
        cfg = self.cfg; NC, TT = cfg.NC, cfg.TT
        ps = self.ps
        scale = 128.0 ** -0.5
        with ExitStack() as es:
            sb = lambda n, s, d: self.sb(es, n, s, d)
            KT = sb("sKT", [128, cfg.TOK], BF16); VT = sb("sVT", [128, TT, 128], BF16)
            sk = sb("ssk", [128, 8], F32); sinkE = sb("ssinkE", [128, 8], F32)
            q4 = [sb("sq4%d" % i, [128, 4, 128], BF16) for i in range(2)]
            pT = [sb("spT%d" % i, [128, 4, 128], BF16) for i in range(3)]
            dn = sb("sdn", [128, 4, 128], F32); ob = [sb("sob%d" % i, [128, 4, 128], BF16) for i in range(2)]
            self.dma('sp', sk[:], self.sink[l], w=['ssk'])
            self.act(sinkE[:], sk[:], AF.Exp, r=['ssk'], w=['ssinkE'])
            n = 0; m = 0
            for g in range(2):
                self.dma('sp', KT[:], self.QKT[1024 + g * 128:1024 + (g + 1) * 128, :], w=['sKT'])
                for t0 in range(0, TT, 16):
                    t1 = min(TT, t0 + 16)
                    self.dma('pool', VT[:, t0:t1, :], self.ZV[t0:t1, :, 768 + g * 128:768 + (g + 1) * 128].rearrange("t p c -> p t c"), w=['sVT'])
                for qt in (range(NC, TT) if last else range(TT)):
                    if qt < NC:
                        keys = [(kt, None) for kt in range(NC)]
                    else:
                        keys = []
                        if qt - 1 >= NC: keys.append((qt - 1, self.maskP))
                        keys.append((qt, None))
                        if qt + 1 < TT: keys.append((qt + 1, self.maskN))
                        keys += [(kt, None) for kt in range(NC)]
                    b = n % 2; n += 1
                    po = 3 + 2 * b; pd = 4 + 2 * b
                    self.dma('sp', q4[b][:], self.QKT[g * 512:(g + 1) * 512, qt * 128:(qt + 1) * 128].rearrange("(j d) t -> d j t", j=4), w=['sq4%d' % b])
                    for ki, (kt, mask) in enumerate(keys):
                        pi = m % 3; m += 1
                        first = ki == 0; lastk = ki == len(keys) - 1
                        self.mm(ps[pi][:].rearrange("p (j t) -> p j t", j=4), KT[:, kt * 128:(kt + 1) * 128], q4[b][:], True, True, r=['sKT', 'sq4%d' % b], w=['ps%d' % pi])
                        self.act(pT[pi][:], ps[pi][:].rearrange("p (j t) -> p j t", j=4), AF.Exp, scale=scale, r=['ps%d' % pi], w=['spT%d' % pi])
                        if mask is not None:
                            self.tt('pool', pT[pi][:], pT[pi][:], mask.unsqueeze(1).broadcast_to([128, 4, 128]), ALU.mult, r=['spT%d' % pi, 'cft'], w=['spT%d' % pi])
                        self.mm(ps[po][:].rearrange("p (j t) -> p j t", j=4), VT[:, kt, :], pT[pi][:], first, lastk, r=['sVT', 'spT%d' % pi], w=['ps%d' % po])
                        self.mm(ps[pd][:].rearrange("p (j t) -> p j t", j=4), self.onesb[:], pT[pi][:], first, lastk, r=['onesb', 'spT%d' % pi], w=['ps%d' % pd])
                    self.tt('dve', dn[:], ps[pd][:].rearrange("p (j t) -> p j t", j=4), sinkE[:, g * 4:(g + 1) * 4].unsqueeze(2).broadcast_to([128, 4, 128]),
                            ALU.add, r=['ps%d' % pd, 'ssinkE'], w=['sdn'])
                    self.S.op('dve', lambda e, o=dn[:]: e.reciprocal(out=o, in_=o), reads=['sdn'], writes=['sdn'])
                    self.tt('dve', ob[b][:], ps[po][:].rearrange("p (j t) -> p j t", j=4), dn[:], ALU.mult, r=['ps%d' % po, 'sdn'], w=['sob%d' % b])
                    self.dma('pool', self.MIXT[1024 + g * 512:1024 + (g + 1) * 512, qt * 128:(qt + 1) * 128].rearrange("(j d) t -> d j t", j=4), ob[b][:], r=['sob%d' % b])
            self.S.barrier()

    def resid_ln(self, bufs, xsrc, banks, Gt, gkey, lngt, lnbt, dsts):
        xt, tq, st, mv, rs = bufs
        ps = self.ps
        self.dma('sp', xt[:], xsrc, w=['lx'])
        for nb in range(4):
            self.tt('dve', tq[:, nb * 512:(nb + 1) * 512], ps[banks[nb]][:], Gt[:, nb * 512:(nb + 1) * 512], ALU.mult, r=['ps%d' % banks[nb], gkey], w=['lt'])
        self.stt('dve', tq[:], xt[:], ALPHA, tq[:], ALU.mult, ALU.add, r=['lx', 'lt'], w=['lt'])
        for nb in range(4):
            self.S.op('dve', lambda e, o=st[:, nb, :], i=tq[:, nb * 512:(nb + 1) * 512]: e.bn_stats(out=o, in_=i), reads=['lt'], writes=['lst'])
        self.S.op('dve', lambda e, o=mv[:], i=st[:]: e.bn_aggr(out=o, in_=i), reads=['lst'], writes=['lmv'])
        self.act(rs[:], mv[:, 1:2], AF.Ln, bias=LN_EPS, r=['lmv'], w=['lrs'])
        self.act(rs[:], rs[:], AF.Exp, scale=-0.5, r=['lrs'], w=['lrs'])
        self.ts('dve', xt[:], tq[:], mv[:, 0:1], rs[:, 0:1], ALU.subtract, ALU.mult, r=['lt', 'lmv', 'lrs'], w=['lx'])
        self.tt('pool', xt[:], xt[:], lngt[:], ALU.mult, r=['lx', 'lng'], w=['lx'])
        self.tt('dve', xt[:], xt[:], lnbt[:], ALU.add, r=['lx', 'lnb'], w=['lx'])
        for dq, dst in zip(('sp', 'pool'), dsts):
            self.dma(dq, dst, xt[:], r=['lx'])

    def ln_bufs(self, es):
        sb = lambda n, s, d: self.sb(es, n, s, d)
        return (sb("lx", [128, D], F32), sb("lt", [128, D], F32), sb("lst", [128, 4, 6], F32), sb("lmv", [128, 2], F32), sb("lrs", [128, 1], F32))

    def stage_outproj(self, l, tiles):
        cfg = self.cfg
        with ExitStack() as es:
            sb = lambda n, s, d: self.sb(es, n, s, d)
            wot = sb("owo", [128, 16, D], BF16)
            lngt = sb("lng", [128, D], F32); lnbt = sb("lnb", [128, D], F32)
            Gt = [sb("oG%d" % i, [128, D], F32) for i in range(2)]
            mixT = [sb("omx%d" % i, [128, 16, 128], BF16) for i in range(2)]
            bufs = self.ln_bufs(es)
            for nb in range(4):
                self.dma('pool', wot[:, :, nb * 512:(nb + 1) * 512], self.wo[l][:, :, nb * 512:(nb + 1) * 512], w=['owo'])
            self.dma('sp', lngt[:], self.lng[l, 0], w=['lng']); self.dma('sp', lnbt[:], self.lnb[l, 0], w=['lnb'])
            for w in range(2):
                self.dma('sp', Gt[w][:], self.MOD[l, w, :, 2 * D:3 * D], w=['oG%d' % w])
            src = self.xs_in if l == 0 else self.XS
            for n, tt in enumerate(tiles):
                b = n % 2; w = 1 if tt < cfg.NC else 0
                self.dma('sp', mixT[b][:], self.MIXT[:, tt * 128:(tt + 1) * 128].rearrange("(k p) t -> p k t", p=128), w=['omx%d' % b])
                banks = [4 * b + i for i in range(4)]
                for nb in range(4):
                    for k in range(16):
                        self.mm(self.ps[banks[nb]][:], mixT[b][:, k, :], wot[:, k, nb * 512:(nb + 1) * 512], k == 0, k == 15, r=['omx%d' % b, 'owo'], w=['ps%d' % banks[nb]])
                self.resid_ln(bufs, src[tt], banks, Gt[w], 'oG%d' % w, lngt, lnbt, [self.XS[tt]])
            self.S.barrier()

    def stage_peer(self, l, tiles, last):
        cfg = self.cfg; NK = cfg.NK; NC = cfg.NC
        ps = self.ps
        tiles = list(tiles)
        pairs = [tiles[i:i + 2] for i in range(0, len(tiles), 2)]
        if not hasattr(self, 'RT'):
            self.RT = self.dscr("RT", [cfg.TT, 128, 3, 128], F32, dbg=True)
        with ExitStack() as es:
            sb = lambda n, s, d: self.sb(es, n, s, d)
            h2T = [sb("ph2T%d" % i, [128, 2, 16, 128], BF16) for i in range(2)]
            qTb = sb("pqTb", [128, 16, 256], BF16)
            wqr = sb("pwqr", [128, 16, D], BF16)
            sktt = sb("pskt", [128, 2, NK], BF16)
            ssb = sb("pssb", [128, 16, NK], F32); wk = sb("pwk", [128, 16, NK], F32)
            m8 = sb("pm8", [128, 16, 16], F32); i8 = sb("pi8", [128, 16, 16], U32); tif = sb("ptif", [128, 16, 16], F32)
            cand = sb("pcand", [128, 8, 256], F32); wk2 = sb("pwk2", [128, 8, 256], F32)
            b8 = sb("pb8", [128, 8, 16], F32); f8 = sb("pf8", [128, 8, 16], U32); ff = sb("pff", [128, 8, 16], F32)
            fa = sb("pfa", [128, 8, 16], F32); fb = sb("pfb", [128, 8, 16], F32)
            fau = sb("pfau", [128, 8, 16], U32); fbu = sb("pfbu", [128, 8, 16], U32)
            eq = sb("peq", [128, 8, 16, 16], F32)
            rt = [sb("prt%d" % i, [128, 3, 128], F32) for i in range(2)]
            ez = sb("pez", [128, 8, 16], F32); zz = sb("pzz", [128, 8], F32)
            self.dma('pool', sktt[:], self.skt[l], w=['pskt'])
            for blk in range(16):
                self.dma('pool', wqr[:, :, blk * 128:(blk + 1) * 128], self.wq[l, blk], w=['pwqr'])
            n = 0
            for pi_, pr in enumerate(pairs):
                hb = pi_ % 2
                for j, tt in enumerate(pr):
                    self.dma('sp', h2T[hb][:, j], self.HTB[tt], w=['ph2T%d' % hb])
                for blk in range(16):
                    pi = n % 8; n += 1
                    for k in range(16):
                        self.mm(ps[pi][:, 0:256].rearrange("p (j t) -> p j t", j=2), wqr[:, k, blk * 128:(blk + 1) * 128], h2T[hb][:, :, k, :], k == 0, k == 15,
                                r=['pwqr', 'ph2T%d' % hb], w=['ps%d' % pi])
                    self.cp('act' if blk % 2 else 'dve', qTb[:, blk, :], ps[pi][:, 0:256], r=['ps%d' % pi], w=['pqTb'])
                for j, tt in enumerate(pr):
                    rb = j
                    for hp in range(16):
                        bank = hp // 4
                        self.mm(ps[bank][:, (hp % 4) * NK:(hp % 4 + 1) * NK], qTb[:, hp, j * 128:(j + 1) * 128], sktt[:, hp % 2, :], True, True,
                                r=['pqTb', 'pskt'], w=['ps%d' % bank])
                    for bank in range(4):
                        self.cp('act', ssb[:, bank * 4:(bank + 1) * 4, :], ps[bank][:, 0:4 * NK].rearrange("p (a n) -> p a n", a=4), r=['ps%d' % bank], w=['pssb'])
                    V = self.S.op
                    for hp in range(16):
                        V('dve', lambda e, o=m8[:, hp, 0:8], i=ssb[:, hp, :]: e.max(out=o, in_=i), reads=['pssb'], writes=['pm8'])
                        V('dve', lambda e, o=i8[:, hp, 0:8], a=m8[:, hp, 0:8], i=ssb[:, hp, :]: e.max_index(out=o, in_max=a, in_values=i), reads=['pssb', 'pm8'], writes=['pi8'])
                        V('dve', lambda e, o=wk[:, hp, :], a=m8[:, hp, 0:8], i=ssb[:, hp, :]: e.match_replace(out=o, in_to_replace=a, in_values=i, imm_value=-1e30), reads=['pssb', 'pm8'], writes=['pwk'])
                        V('dve', lambda e, o=m8[:, hp, 8:16], i=wk[:, hp, :]: e.max(out=o, in_=i), reads=['pwk'], writes=['pm8'])
                        V('dve', lambda e, o=i8[:, hp, 8:16], a=m8[:, hp, 8:16], i=wk[:, hp, :]: e.max_index(out=o, in_max=a, in_values=i), reads=['pwk', 'pm8'], writes=['pi8'])
                    self.cp('dve', tif[:], i8[:], r=['pi8'], w=['ptif'])
                    tv4 = m8[:].rearrange("q (h p) k -> q h p k", p=2); ti4 = tif[:].rearrange("q (h p) k -> q h p k", p=2)
                    c4 = cand[:].rearrange("q h (a b) -> q h a b", a=16)
                    self.tt('dve', c4, tv4[:, :, 0, :].unsqueeze(3).broadcast_to([128, 8, 16, 16]), tv4[:, :, 1, :].unsqueeze(2).broadcast_to([128, 8, 16, 16]),
                            ALU.add, r=['pm8'], w=['pcand'])
                    for h in range(8):
                        V('dve', lambda e, o=b8[:, h, 0:8], i=cand[:, h, :]: e.max(out=o, in_=i), reads=['pcand'], writes=['pb8'])
                        V('dve', lambda e, o=f8[:, h, 0:8], a=b8[:, h, 0:8], i=cand[:, h, :]: e.max_index(out=o, in_max=a, in_values=i), reads=['pcand', 'pb8'], writes=['pf8'])
                        V('dve', lambda e, o=wk2[:, h, :], a=b8[:, h, 0:8], i=cand[:, h, :]: e.match_replace(out=o, in_to_replace=a, in_values=i, imm_value=-1e30), reads=['pcand', 'pb8'], writes=['pwk2'])
                        V('dve', lambda e, o=b8[:, h, 8:16], i=wk2[:, h, :]: e.max(out=o, in_=i), reads=['pwk2'], writes=['pb8'])
                        V('dve', lambda e, o=f8[:, h, 8:16], a=b8[:, h, 8:16], i=wk2[:, h, :]: e.max_index(out=o, in_max=a, in_values=i), reads=['pwk2', 'pb8'], writes=['pf8'])
                    self.ts('dve', fau[:], f8[:], 4, None, ALU.logical_shift_right, r=['pf8'], w=['pfau'])
                    self.ts('dve', fbu[:], f8[:], 15, None, ALU.bitwise_and, r=['pf8'], w=['pfbu'])
                    self.cp('dve', fa[:], fau[:], r=['pfau'], w=['pfa'])
                    self.cp('dve', fb[:], fbu[:], r=['pfbu'], w=['pfb'])
                    io4 = self.iota16.unsqueeze(1).unsqueeze(1).broadcast_to([128, 8, 16, 16])
                    for which, (fsel, pp) in enumerate(((fa, 0), (fb, 1))):
                        self.tt('dve', eq[:], fsel[:].unsqueeze(3).broadcast_to([128, 8, 16, 16]), io4, ALU.is_equal, r=['pfa', 'pfb', 'cft'], w=['peq'])
                        self.tt('pool', eq[:], eq[:], ti4[:, :, pp, :].unsqueeze(2).broadcast_to([128, 8, 16, 16]), ALU.mult, r=['peq', 'ptif'], w=['peq'])
                        V('dve', lambda e, o=rt[rb][:, which, :].rearrange("q (h k) -> q h k", h=8), i=eq[:]: e.tensor_reduce(out=o, in_=i, axis=AX.X, op=ALU.add),
                          reads=['peq'], writes=['prt%d' % rb])
                    self.tt('dve', ez[:], b8[:], b8[:, :, 0:1].broadcast_to([128, 8, 16]), ALU.subtract, r=['pb8'], w=['pez'])
                    self.act(ez[:], ez[:], AF.Exp, r=['pez'], w=['pez'])
                    V('dve', lambda e, o=zz[:], i=ez[:]: e.tensor_reduce(out=o, in_=i, axis=AX.X, op=ALU.add), reads=['pez'], writes=['pzz'])
                    V('dve', lambda e, o=zz[:]: e.reciprocal(out=o, in_=o), reads=['pzz'], writes=['pzz'])
                    self.tt('dve', rt[rb][:, 2, :].rearrange("q (h k) -> q h k", h=8), ez[:], zz[:].unsqueeze(2).broadcast_to([128, 8, 16]), ALU.mult,
                            r=['pez', 'pzz'], w=['prt%d' % rb])
                    self.dma('pool', self.RT[tt], rt[rb][:], r=['prt%d' % rb])
            self.S.barrier()
        G = 8
        with ExitStack() as es:
            sb = lambda n, s, d: self.sb(es, n, s, d)
            WT = sb("dWT", [128, NK, 256], BF16)
            h2T = sb("dh2T", [128, 2, 16, 128], BF16)
            NUB = 3; NI = min(NK, 24)
            utc = [sb("dut%d" % i, [128, 2, 16, NK], BF16) for i in range(NUB)]
            vch = [sb("dvc%d" % i, [128, 2, D], BF16) for i in range(NUB)]
            gaI = [sb("dgI%d" % i, [128, 256], BF16) for i in range(NI)]
            rtt = sb("drt", [128, 3, 128], F32); rT = sb("drT", [128, 3, 128], F32)
            oh2 = [sb("doh2%d" % i, [128, G, NK], BF16) for i in range(2)]
            oh1 = [sb("doh1%d" % i, [128, G, NK], BF16) for i in range(2)]
            ga = [sb("dga%d" % i, [128, 256], F32) for i in range(2)]
            lngt = sb("lng", [128, D], F32); lnbt = sb("lnb", [128, D], F32)
            Gt = sb("dG", [128, D], F32)
            bufs = self.ln_bufs(es)
            self.dma('sp', lngt[:], self.lng[l, 1], w=['lng']); self.dma('sp', lnbt[:], self.lnb[l, 1], w=['lnb'])
            io3 = self.iota128[:, 0:NK].unsqueeze(1).broadcast_to([128, G, NK])
            curw = None; n = 0; m = 0
            for pidx, pr in enumerate(pairs):
                if l + 1 < cfg.DEPTH:
                    if pidx < 3:
                        self.convert_uv(l + 1, [pidx])
                    if pidx == min(3, len(pairs) - 1):
                        self.convert_uv(l + 1, list(range(pidx if pidx == 3 else pidx + 1, 4)))
                w = 1 if pr[0] < NC else 0
                if w != curw:
                    self.dma('sp', Gt[:], self.MOD[l, w, :, 5 * D:6 * D], w=['dG']); curw = w
                for j, tt in enumerate(pr):
                    self.dma('sp', h2T[:, j], self.HTB[tt], w=['dh2T'])
                started = 0

                def phase1_mm(i1, dest, dkey, banks):
                    ub = (i1 // 2) % NUB; pi = banks[i1 % len(banks)]
                    if i1 % 2 == 0:
                        self.dma('sp', utc[ub][:], self.UTB[l, i1:i1 + 2].rearrange("c p k e -> p c k e"), w=['dut%d' % ub])
                    for k in range(16):
                        self.mm(ps[pi][0:NK, 0:256].rearrange("p (j t) -> p j t", j=2), utc[ub][:, i1 % 2, k, :], h2T[:, :, k, :], k == 0, k == 15,
                                r=['dut%d' % ub, 'dh2T'], w=['ps%d' % pi])
                    self.act(dest, ps[pi][0:NK, 0:256], AF.Gelu, r=['ps%d' % pi], w=[dkey])

                for j, tt in enumerate(pr):
                    self.dma('sp', rtt[:], self.RT[tt], w=['drt'])
                    for a in range(3):
                        self.tp(ps[5 + a][:, 0:128], rtt[:, a, :], self.identf, r=['drt', 'cft'], w=['ps%d' % (5 + a)])
                        self.cp('act', rT[:, a, :], ps[5 + a][:, 0:128], r=['ps%d' % (5 + a)], w=['drT'])
                    for t0 in range(0, 128, G):
                        ob = m % 2; m += 1
                        self.tt('dve', oh2[ob][:], io3, rT[:, 1, t0:t0 + G].unsqueeze(2).broadcast_to([128, G, NK]), ALU.is_equal, r=['cft', 'drT'], w=['doh2%d' % ob])
                        self.tt('dve', oh1[ob][:], io3, rT[:, 0, t0:t0 + G].unsqueeze(2).broadcast_to([128, G, NK]), ALU.is_equal, r=['cft', 'drT'], w=['doh1%d' % ob])
                        self.tt('pool', oh1[ob][:], oh1[ob][:], rT[:, 2, t0:t0 + G].unsqueeze(2).broadcast_to([128, G, NK]), ALU.mult, r=['doh1%d' % ob, 'drT'], w=['doh1%d' % ob])
                        for q0 in range(0, G, 4):
                            pi = 5 + (n % 3); n += 1
                            for q in range(4):
                                self.mm(ps[pi][0:NK, q * NK:(q + 1) * NK], oh2[ob][:, q0 + q, :], oh1[ob][:, q0 + q, :], True, True,
                                        r=['doh2%d' % ob, 'doh1%d' % ob], w=['ps%d' % pi])
                            tb = j * 128 + t0 + q0
                            self.cp('act', WT[0:NK, :, tb:tb + 4].rearrange("p i t -> p t i"), ps[pi][0:NK, 0:4 * NK].rearrange("p (t i) -> p t i", t=4),
                                    r=['ps%d' % pi], w=['dWTb'])
                        if started < NI:
                            phase1_mm(started, gaI[started][0:NK, :], 'dgI%d' % started, [0, 1, 2, 3, 4]); started += 1
                while started < NI:
                    phase1_mm(started, gaI[started][0:NK, :], 'dgI%d' % started, [0, 1, 2, 3, 4]); started += 1
                for i1 in range(NI):
                    self.tt('dve' if i1 % 2 else 'pool', WT[0:NK, i1, :], WT[0:NK, i1, :], gaI[i1][0:NK, :], ALU.mult, r=['dWTb', 'dgI%d' % i1], w=[('dWT', i1)])
                for i1 in range(NI, NK):
                    gb = i1 % 2
                    phase1_mm(i1, ga[gb][0:NK, :], 'dga%d' % gb, list(range(8)))
                    self.tt('dve' if i1 % 2 else 'pool', WT[0:NK, i1, :], WT[0:NK, i1, :], ga[gb][0:NK, :], ALU.mult, r=['dWTb', 'dga%d' % gb], w=[('dWT', i1)])
                for i1 in range(NK):
                    vb = (i1 // 2) % NUB
                    if i1 % 2 == 0:
                        self.dma('act' if (i1 // 2) % 2 else 'sp', vch[vb][0:NK], self.VB[l, i1 * NK:(i1 + 2) * NK, :].rearrange("(c p) d -> p c d", p=NK), w=['dvc%d' % vb])
                    for j in range(len(pr)):
                        for nb in range(4):
                            self.mm(ps[j * 4 + nb][:], WT[0:NK, i1, j * 128:(j + 1) * 128], vch[vb][0:NK, i1 % 2, nb * 512:(nb + 1) * 512], i1 == 0, i1 == NK - 1,
                                    r=[('dWT', i1), 'dWTb', 'dvc%d' % vb], w=['ps%d' % (j * 4 + nb)])
                for j, tt in enumerate(pr):
                    dsts = [self.XS[tt]]
                    if last:
                        dsts = [self.out[tt - NC]]
                    self.resid_ln(bufs, self.XS[tt], [j * 4 + i for i in range(4)], Gt, 'dG', lngt, lnbt, dsts)
            self.S.barrier()


_CACHE = {}


def run(inputs, cfg, core_ids=None):
    maps = prep_inputs(inputs, cfg)
    key = (cfg.L, cfg.LC, cfg.DEPTH, cfg.NK, cfg.dbg)
    if key not in _CACHE:
        p = Prog(cfg); p.build(); _CACHE[key] = p
    p = _CACHE[key]
    res = run_bass_kernel_spmd(p.nc, maps, core_ids=list(range(len(maps))))
    outs = [r["out"].reshape(cfg.L, D) for r in res.results]
    return np.stack(outs).astype(np.float32), res, p


def kernel(**inputs):
    cfg = Cfg()
    out, _, _ = run(inputs, cfg)
    return out
```

```python
import numpy as np
import ml_dtypes
from contextlib import ExitStack
import concourse.bass as bass
import concourse.mybir as mybir
from concourse.bass_utils import run_bass_kernel_spmd

F32 = mybir.dt.float32; BF16 = mybir.dt.bfloat16; U32 = mybir.dt.uint32
ALU = mybir.AluOpType; AF = mybir.ActivationFunctionType; AX = mybir.AxisListType

D = 2048
ALPHA = (2.0 * 4) ** 0.25
LN_EPS = 1e-6
GRID_W = 64


class Cfg:
    def __init__(self, L=8192, LC=256, DEPTH=4, NK=128, dbg=False):
        self.L = L; self.LC = LC; self.DEPTH = DEPTH; self.NK = NK; self.dbg = dbg
        self.NC = LC // 128; self.NL = L // 128; self.TT = self.NC + self.NL
        self.TOK = L + LC
        self.NE = NK * NK


class Sched:
    ENG = ['pe', 'dve', 'act', 'pool', 'sp']
    CE = ['pe', 'dve', 'act', 'pool']
    EPOCH = 30000
    NEP = {'pe': 26, 'dve': 5, 'act': 4, 'pool': 3}

    def __init__(self, nc, es, ndma=48):
        self.nc = nc
        self.esem = {e: [es.enter_context(nc.semaphore('s_%s%d' % (e, i))) for i in range(self.NEP[e])] for e in self.CE}
        self.dsem = [es.enter_context(nc.semaphore('d%d' % i)) for i in range(ndma)]
        self.nreg = ndma
        self.dsem += [es.enter_context(nc.semaphore('db%d' % i)) for i in range(8)]
        self.dcnt = [0] * (ndma + 8); self.dnext = 0; self.bnext = 0
        self.cnt = {e: 0 for e in self.CE}; self.ep = {e: 0 for e in self.CE}
        self.prog = {e: [] for e in self.ENG}
        self.seen = {e: {} for e in self.ENG}
        self.res = {}
        self.ninst = 0

    def semh(self, sid):
        return self.esem[sid[1]][sid[2]] if sid[0] == 'e' else self.dsem[sid[1]]

    def _need(self, eng, sid, val, waits):
        s = self.seen[eng]
        if sid[0] == 'e':
            k = ('e', sid[1]); cur = s.get(k, (-1, 0)); new = (sid[2], val)
            if cur >= new:
                return
            s[k] = new
        else:
            if s.get(sid, 0) >= val:
                return
            s[sid] = val
        waits.append((sid, val))

    def _deps(self, eng, reads, writes):
        waits = []
        pe = eng == 'pe'
        for k in reads:
            st = self.res.get(k)
            if st:
                for sid, v in st[0].items():
                    if not (pe and sid[0] == 'e' and sid[1] == 'pe'):
                        self._need(eng, sid, v, waits)
        for k in writes:
            st = self.res.get(k)
            if st:
                for d in st:
                    for sid, v in d.items():
                        if not (pe and sid[0] == 'e' and sid[1] == 'pe'):
                            self._need(eng, sid, v, waits)
        return waits

    def _mark(self, ev, reads, writes):
        sid, v = ev
        for k in reads:
            self.res.setdefault(k, ({}, {}))[1][sid] = v
        for k in writes:
            self.res.setdefault(k, ({}, {}))[0][sid] = v

    def op(self, eng, fn, reads=(), writes=()):
        waits = self._deps(eng, reads, writes)
        if self.cnt[eng] >= self.EPOCH:
            self.ep[eng] += 1; self.cnt[eng] = 0
        self.cnt[eng] += 1; ev = (('e', eng, self.ep[eng]), self.cnt[eng])
        self.prog[eng].append((waits, fn, ev)); self._mark(ev, reads, writes); self.ninst += 1

    def dma(self, eng, fn, reads=(), writes=(), bulk=False):
        if bulk:
            slot = self.nreg + self.bnext; self.bnext = (self.bnext + 1) % 8
        else:
            slot = self.dnext; self.dnext = (self.dnext + 1) % self.nreg
        waits = self._deps(eng, reads, writes)
        if self.dcnt[slot] > 0:
            self._need(eng, ('d', slot), self.dcnt[slot], waits)
        self.dcnt[slot] += 16; ev = (('d', slot), self.dcnt[slot])
        self.prog[eng].append((waits, fn, ev)); self._mark(ev, reads, writes); self.ninst += 1

    def barrier(self):
        for e in self.ENG:
            waits = []
            for i, c in enumerate(self.dcnt):
                if c:
                    self._need(e, ('d', i), c, waits)
            for o in self.CE:
                if o != e and (self.cnt[o] or self.ep[o]):
                    self._need(e, ('e', o, self.ep[o]), self.cnt[o], waits)
            if waits:
                self.prog[e].append((waits, None, None))
        self.res = {}

    def emit(self, block):
        engmap = {'pe': 'tensor', 'dve': 'vector', 'act': 'scalar', 'pool': 'gpsimd', 'sp': 'sync'}
        for e in self.ENG:
            prog = self.prog[e]

            def body(engine, prog=prog):
                for waits, fn, ev in prog:
                    for sid, v in waits:
                        engine.wait_ge(self.semh(sid), v)
                    if fn is None:
                        continue
                    ins = fn(engine)
                    sid, v = ev
                    ins.then_inc(self.semh(sid), 16 if sid[0] == 'd' else 1)
            getattr(block, engmap[e])(body)


def _blk_stationary(w):
    n = w.shape[1] // 128
    return np.ascontiguousarray(w.reshape(16, 128, n, 128).transpose(2, 1, 0, 3))


def _blk_moving(w, width=512):
    n = w.shape[1] // width
    return np.ascontiguousarray(w.reshape(16, 128, n, width).transpose(2, 1, 0, 3))


def rope_perm():
    d = np.arange(128)
    partner = np.where((d % 64) < 32, d + 32, d - 32)
    sign = np.where((d % 64) < 32, -1.0, 1.0).astype(np.float32)
    return partner, sign


def const_tables(cfg):
    cf = np.zeros((128, 912), np.float32)
    cf[:, 0:128] = np.eye(128, dtype=np.float32)
    cf[:, 128:256] = np.arange(128, dtype=np.float32)[None, :]
    s = -1.0 / 16.0
    j = np.arange(64)[:, None]; i = np.arange(64)[None, :]
    cf[:64, 256:320] = np.where(j <= i, s, 0.0)
    cf[:64, 320:384] = np.where(j >= i, s, 0.0)
    cf[:64, 384:448] = np.where(j > i, s, 0.0)
    cf[:64, 448:512] = np.where(j < i, s, 0.0)
    cf[:64, 512:576] = np.where(j <= i, 1.0, 0.0)
    cf[:64, 576:640] = np.where(j >= i, 1.0, 0.0)
    cf[:, 640:656] = np.arange(16, dtype=np.float32)[None, :]
    jj = np.arange(128)[:, None]; ii = np.arange(128)[None, :]
    cf[:, 656:784] = np.where(jj >= ii, 1.0, 0.0)
    cf[:, 784:912] = np.where(jj <= ii, 1.0, 0.0)
    t = np.arange(cfg.L)
    r = (t // GRID_W).astype(np.float32); c = (t % GRID_W).astype(np.float32)
    inv = (10000.0 ** (-np.arange(0, 64, 2, dtype=np.float32) / 64.0)).astype(np.float32)
    ang_r = r[None, :] * inv[:, None]; ang_c = c[None, :] * inv[:, None]
    d = np.arange(128)
    ang = np.where((d < 64)[:, None], ang_r[d % 32], ang_c[d % 32]).astype(np.float32)
    _, sign = rope_perm()
    cos = np.cos(ang).astype(np.float32); sins = (np.sin(ang).astype(np.float32) * sign[:, None])
    return cf, np.ascontiguousarray(cos), np.ascontiguousarray(sins)


def prep_inputs(inp, cfg):
    f = lambda a: np.ascontiguousarray(np.asarray(a, dtype=np.float32))
    x = f(inp['x']); c = f(inp['c']); ctx = f(inp['ctx']); c_ctx = f(inp['c_ctx'])
    w_ada = f(inp['w_ada']); b_ada = f(inp['b_ada']); w_in = f(inp['w_in'])
    NL_ = cfg.DEPTH
    cf, cos, sins = const_tables(cfg)
    partner, _ = rope_perm()
    wa = np.stack([_blk_moving(w_ada[l]) for l in range(NL_)])
    ba = np.ascontiguousarray(b_ada[:NL_].reshape(NL_, 24, 1, 512))
    wf_l, wt_l = [], []
    for l in range(NL_):
        w = w_in[l]
        sq = w[:, 3104:4128].reshape(D, 8, 128); sk = w[:, 4128:4384].reshape(D, 2, 128)
        lr = np.zeros((D, 128), np.float32); lr[:, :32] = w[:, 3072:3104]
        fm = np.concatenate([w[:, 0:1536], w[:, 1536:2048], sq.reshape(D, 1024), sq[:, :, partner].reshape(D, 1024),
                             sk.reshape(D, 256), sk[:, :, partner].reshape(D, 256), lr], axis=1)
        wf_l.append(_blk_stationary(fm))
        tm = np.concatenate([w[:, 1792:2048], w[:, 2048:2560], w[:, 4384:4640], w[:, 2560:3072]], axis=1)
        wt_l.append(_blk_moving(tm))
    wf = np.stack(wf_l); wt = np.stack(wt_l)
    conv_w = f(inp['conv_w'])[:NL_]
    cw = np.ascontiguousarray(conv_w.reshape(NL_, 3, 4, 128).transpose(0, 3, 2, 1))
    w2 = f(inp['gla_w2'])[:NL_]; b2 = f(inp['gla_b2'])[:NL_]
    w2a = np.zeros((NL_, 33, 512), np.float32)
    w2a[:, 0:16, 0:256] = w2[:, 0]; w2a[:, 16:32, 256:512] = w2[:, 1]
    w2a[:, 32, 0:256] = b2[:, 0]; w2a[:, 32, 256:512] = b2[:, 1]
    gnorm = np.ascontiguousarray(np.broadcast_to(f(inp['gla_norm'])[:NL_, None, :], (NL_, 128, 128)))
    sink = np.ascontiguousarray(np.broadcast_to(f(inp['swa_sink'])[:NL_, None, :], (NL_, 128, 8)))
    w_out = f(inp['w_out'])[:NL_]
    wo = np.ascontiguousarray(w_out.reshape(NL_, 16, 128, D).transpose(0, 2, 1, 3))
    lng = np.ascontiguousarray(np.broadcast_to(f(inp['ln_g'])[:NL_, :, None, :], (NL_, 2, 128, D)))
    lnb = np.ascontiguousarray(np.broadcast_to(f(inp['ln_b'])[:NL_, :, None, :], (NL_, 2, 128, D)))
    wq = np.stack([_blk_stationary(f(inp['peer_wq'])[l]) for l in range(NL_)])
    sk_ = f(inp['peer_subkeys'])[:NL_]
    skt = np.ascontiguousarray(sk_.transpose(0, 3, 1, 2))
    NK = cfg.NK
    pu = f(inp['peer_u'])[:NL_]
    ut = np.ascontiguousarray(pu.reshape(NL_, NK, NK, 16, 128).transpose(0, 1, 4, 3, 2))
    pv = f(inp['peer_v'])[:NL_]
    maps = []
    for b in range(x.shape[0]):
        cc = np.stack([c[b].reshape(16, 128).T, c_ctx.reshape(16, 128).T])
        xs = np.concatenate([ctx[b], x[b]], axis=0).reshape(cfg.TT, 128, D)
        maps.append(dict(xs_in=np.ascontiguousarray(xs), cc=np.ascontiguousarray(cc), wa=wa, ba=ba, wf=wf, wt=wt,
                         cw=cw, w2a=w2a, gnorm=gnorm, sink=sink, wo=wo, lng=lng, lnb=lnb, wq=wq, skt=skt,
                         ut=ut, pv=pv, cf=cf, cos=cos, sins=sins))
    return maps


class Prog:
    def __init__(self, cfg):
        self.cfg = cfg
        self.nc = bass.Bass("TRN2", target_bir_lowering=False)
        self.dbg_outs = []

    def din(self, name, shape, dt=F32):
        return self.nc.dram_tensor(name, list(shape), dt, kind="ExternalInput").ap()

    def dscr(self, name, shape, dt=F32, dbg=False):
        if dbg and self.cfg.dbg:
            self.dbg_outs.append(name)
            return self.nc.dram_tensor(name, list(shape), dt, kind="ExternalOutput").ap()
        return self.nc.dram_tensor(name, list(shape), dt).ap()

    def sb(self, es, name, shape, dt):
        self.uid = getattr(self, 'uid', 0) + 1
        return es.enter_context(self.nc.sbuf_tensor("%s_u%d" % (name, self.uid), list(shape), dt))

    def dma(self, q, out, in_, r=(), w=(), bulk=False):
        self.S.dma(q, lambda e, o=out, i=in_: e.dma_start(out=o, in_=i), reads=r, writes=w, bulk=bulk)

    def mm(self, out, lhsT, rhs, start, stop, r=(), w=()):
        self.S.op('pe', lambda e, o=out, l=lhsT, rh=rhs, s=start, t=stop: e.matmul(o, lhsT=l, rhs=rh, start=s, stop=t), reads=r, writes=w)

    def tp(self, out, in_, ident, r=(), w=()):
        self.S.op('pe', lambda e, o=out, i=in_, d=ident: e.transpose(out=o, in_=i, identity=d), reads=r, writes=w)

    def tt(self, eng, out, in0, in1, op, r=(), w=()):
        self.S.op(eng, lambda e, o=out, a=in0, b=in1, p=op: e.tensor_tensor(out=o, in0=a, in1=b, op=p), reads=r, writes=w)

    def ts(self, eng, out, in0, s1, s2, op0, op1=None, r=(), w=()):
        if op1 is None:
            self.S.op(eng, lambda e, o=out, a=in0, x=s1, p=op0: e.tensor_scalar(out=o, in0=a, scalar1=x, scalar2=None, op0=p), reads=r, writes=w)
        else:
            self.S.op(eng, lambda e, o=out, a=in0, x=s1, y=s2, p=op0, q=op1: e.tensor_scalar(out=o, in0=a, scalar1=x, scalar2=y, op0=p, op1=q), reads=r, writes=w)

    def stt(self, eng, out, in0, scalar, in1, op0, op1, r=(), w=()):
        self.S.op(eng, lambda e, o=out, a=in0, s=scalar, b=in1, p=op0, q=op1: e.scalar_tensor_tensor(out=o, in0=a, scalar=s, in1=b, op0=p, op1=q), reads=r, writes=w)

    def act(self, out, in_, func, r=(), w=(), bias=None, scale=None, accum=None):
        kw = {}
        if bias is not None: kw['bias'] = bias
        if scale is not None: kw['scale'] = scale
        if accum is not None: kw['accum_out'] = accum
        self.S.op('act', lambda e, o=out, i=in_, f=func, kw=kw: e.activation(out=o, in_=i, func=f, **kw), reads=r, writes=w)

    def cp(self, eng, out, in_, r=(), w=()):
        if eng == 'act':
            self.S.op('act', lambda e, o=out, i=in_: e.copy(out=o, in_=i), reads=r, writes=w)
        else:
            self.S.op(eng, lambda e, o=out, i=in_: e.tensor_copy(out=o, in_=i), reads=r, writes=w)

    def memset(self, eng, ap, val, w=()):
        self.S.op(eng, lambda e, a=ap, v=val: e.memset(a, v), writes=w)

    def build(self):
        cfg = self.cfg; nc = self.nc
        TT, NLY, NK = cfg.TT, cfg.DEPTH, cfg.NK
        self.xs_in = self.din("xs_in", [TT, 128, D])
        self.cc = self.din("cc", [2, 128, 16])
        self.wa = self.din("wa", [NLY, 24, 128, 16, 512]); self.ba = self.din("ba", [NLY, 24, 1, 512])
        self.wf = self.din("wf", [NLY, 37, 128, 16, 128]); self.wt = self.din("wt", [NLY, 3, 128, 16, 512])
        self.cw = self.din("cw", [NLY, 128, 4, 3]); self.w2a = self.din("w2a", [NLY, 33, 512])
        self.gnorm = self.din("gnorm", [NLY, 128, 128]); self.sink = self.din("sink", [NLY, 128, 8])
        self.wo = self.din("wo", [NLY, 128, 16, D])
        self.lng = self.din("lng", [NLY, 2, 128, D]); self.lnb = self.din("lnb", [NLY, 2, 128, D])
        self.wq = self.din("wq", [NLY, 16, 128, 16, 128]); self.skt = self.din("skt", [NLY, 128, 2, NK])
        self.ut = self.din("ut", [NLY, NK, 128, 16, NK]); self.pv = self.din("pv", [NLY, cfg.NE, D])
        self.cf = self.din("cf", [128, 912]); self.cos = self.din("cos", [128, cfg.L]); self.sins = self.din("sins", [128, cfg.L])
        self.out = nc.dram_tensor("out", [cfg.NL, 128, D], F32, kind="ExternalOutput").ap()
        self.XS = self.dscr("XS", [TT, 128, D], F32, dbg=True)
        self.MOD = self.dscr("MOD", [NLY, 2, 128, 6 * D], F32, dbg=True)
        self.HTB = self.dscr("HTB", [TT, 128, 16, 128], BF16)
        self.ZT = self.dscr("ZT", [4736, cfg.TOK], F32, dbg=True)
        self.ZV = self.dscr("ZV", [TT, 128, 1536], F32, dbg=True)
        self.QKT = self.dscr("QKT", [1280, cfg.TOK], BF16)
        self.MIXT = self.dscr("MIXT", [D, cfg.TOK], BF16, dbg=True)
        self.OF = self.dscr("OF", [cfg.TOK // 64, 64, 512], F32)
        self.WFB = self.dscr("WFB", [NLY, 37, 128, 16, 128], BF16)
        self.UTB = self.dscr("UTB", [NLY, NK, 128, 16, NK], BF16)
        self.VB = self.dscr("VB", [NLY, cfg.NE, D], BF16)
        es = ExitStack()
        with es:
            self.S = Sched(nc, es)
            self.ps = [es.enter_context(nc.psum_tensor("ps%d" % i, [128, 512], F32)) for i in range(8)]
            self.consts(es)
            self.stage_prep()
            self.stage_adaln()
            for l in range(NLY):
                last = (l == NLY - 1)
                self.stage_modT(l, 0, range(TT))
                self.stage_inproj(l)
                self.stage_rope(l)
                self.stage_conv(l)
                self.stage_gla(l)
                self.stage_swa(l, last)
                tiles = range(cfg.NC, TT) if last else range(TT)
                self.stage_outproj(l, tiles)
                self.stage_modT(l, 1, tiles)
                self.stage_peer(l, tiles, last)
            self.S.barrier()
            with nc.Block() as block:
                self.S.emit(block)
        return nc

    def consts(self, es):
        self.cft = self.sb(es, "cft", [128, 912], F32)
        self.identb = self.sb(es, "identb", [128, 128], BF16)
        self.onesb = self.sb(es, "onesb", [128, 128], BF16)
        self.dma('sp', self.cft[:], self.cf, w=['cft'])
        self.cp('dve', self.identb[:], self.cft[:, 0:128], r=['cft'], w=['identb'])
        self.memset('dve', self.onesb[:], 1.0, w=['onesb'])
        c = self.cft
        self.identf = c[:, 0:128]; self.iota128 = c[:, 128:256]
        self.triT = [c[0:64, 256:320], c[0:64, 320:384]]
        self.amT = [c[0:64, 384:448], c[0:64, 448:512]]
        self.maskT = [c[0:64, 512:576], c[0:64, 576:640]]
        self.iota16 = c[:, 640:656]
        self.maskP = c[:, 656:784]; self.maskN = c[:, 784:912]
        self.S.barrier()

    def psb(self, i):
        return self.ps[i][:].bitcast(BF16)

    def stage_prep(self):
        cfg = self.cfg; NK = cfg.NK
        for l in range(cfg.DEPTH):
            src = self.wf[l].rearrange("i p k e -> (i p) (k e)"); dst = self.WFB[l].rearrange("i p k e -> (i p) (k e)")
            for r0 in range(0, 37 * 128, 1024):
                r1 = min(37 * 128, r0 + 1024)
                self.dma('pool', dst[r0:r1, :], src[r0:r1, :])

    def convert_uv(self, l, parts=None):
        cfg = self.cfg; NK = cfg.NK
        src = self.ut[l].rearrange("i p k e -> (i p) (k e)"); dst = self.UTB[l].rearrange("i p k e -> (i p) (k e)")
        jobs = []
        rows = NK * 128
        for r0 in range(0, rows, 1024):
            r1 = min(rows, r0 + 1024)
            jobs.append((dst[r0:r1, :], src[r0:r1, :]))
        for r0 in range(0, cfg.NE, 1024):
            r1 = min(cfg.NE, r0 + 1024)
            jobs.append((self.VB[l][r0:r1, :], self.pv[l][r0:r1, :]))
        per = (len(jobs) + 3) // 4
        for p in (range(4) if parts is None else parts):
            for (o, i) in jobs[p * per:(p + 1) * per]:
                self.dma('pool', o, i, bulk=True)

    def stage_adaln(self):
        cfg = self.cfg
        with ExitStack() as es:
            ccs = self.sb(es, "ccs", [128, 2, 16], F32); cs = self.sb(es, "cs", [128, 2, 16], F32)
            rep = self.sb(es, "rep", [128, 2, 16, 128], BF16)
            wat = [self.sb(es, "wat%d" % i, [128, 16, 512], BF16) for i in range(2)]
            bat = [self.sb(es, "bat%d" % i, [1, 512], BF16) for i in range(2)]
            mo = [self.sb(es, "mo%d" % i, [128, 512], F32) for i in range(4)]
            self.dma('sp', ccs[:], self.cc.rearrange("w p k -> p w k"), w=['ccs'])
            self.act(cs[:], ccs[:], AF.Silu, r=['ccs'], w=['cs'])
            for w in range(2):
                self.cp('dve', rep[:, w], cs[:, w, :].unsqueeze(2).broadcast_to([128, 16, 128]), r=['cs'], w=['rep'])
            n = 0
            for l in range(cfg.DEPTH):
                for cb in range(24):
                    b = (l * 24 + cb) % 2
                    self.dma('pool', wat[b][:], self.wa[l, cb], w=['wat%d' % b])
                    self.dma('pool', bat[b][:], self.ba[l, cb], w=['bat%d' % b])
                    for w in range(2):
                        pi = n % 8; m = n % 4; n += 1
                        for k in range(16):
                            self.mm(self.ps[pi][:], rep[:, w, k, :], wat[b][:, k, :], k == 0, False, r=['rep', 'wat%d' % b], w=['ps%d' % pi])
                        self.mm(self.ps[pi][:], self.onesb[0:1, :], bat[b][0:1, :], False, True, r=['onesb', 'bat%d' % b], w=['ps%d' % pi])
                        if cb // 4 in (1, 4):
                            self.ts('dve', mo[m][:], self.ps[pi][:], 1.0, None, ALU.add, r=['ps%d' % pi], w=['mo%d' % m])
                        else:
                            self.cp('act', mo[m][:], self.ps[pi][:], r=['ps%d' % pi], w=['mo%d' % m])
                        self.dma('sp', self.MOD[l, w, :, cb * 512:(cb + 1) * 512], mo[m][:], r=['mo%d' % m])
            self.S.barrier()

    def stage_modT(self, l, which, tiles):
        cfg = self.cfg
        with ExitStack() as es:
            xs = [self.sb(es, "mx%d" % i, [128, D], F32) for i in range(2)]
            tmp = self.sb(es, "mtmp", [128, D], F32)
            hb = [self.sb(es, "mhb%d" % i, [128, D], BF16) for i in range(2)]
            hts = [self.sb(es, "mhts%d" % i, [128, 16, 128], BF16) for i in range(2)]
            sct = [self.sb(es, "msc%d" % i, [128, D], F32) for i in range(2)]
            sht = [self.sb(es, "msh%d" % i, [128, D], F32) for i in range(2)]
            o_sh = 0 if which == 0 else 3 * D
            for w in range(2):
                self.dma('sp', sht[w][:], self.MOD[l, w, :, o_sh:o_sh + D], w=['msh%d' % w])
                self.dma('sp', sct[w][:], self.MOD[l, w, :, o_sh + D:o_sh + 2 * D], w=['msc%d' % w])
            src = self.xs_in if (l == 0 and which == 0) else self.XS
            for n, tt in enumerate(tiles):
                b = n % 2; w = 1 if tt < cfg.NC else 0
                self.dma('sp', xs[b][:], src[tt], w=['mx%d' % b])
                self.tt('dve', tmp[:], xs[b][:], sct[w][:], ALU.mult, r=['mx%d' % b, 'msc%d' % w], w=['mtmp'])
                self.tt('pool', hb[b][:], tmp[:], sht[w][:], ALU.add, r=['mtmp', 'msh%d' % w], w=['mhb%d' % b])
                for half in range(2):
                    pi = (n * 2 + half) % 8
                    pb = self.psb(pi)
                    for j in range(8):
                        k = half * 8 + j
                        self.tp(pb[:, j * 128:(j + 1) * 128], hb[b][:, k * 128:(k + 1) * 128], self.identb[:], r=['mhb%d' % b, 'identb'], w=['ps%d' % pi])
                    self.cp('act', hts[b][:, half * 8:(half + 1) * 8, :], pb.rearrange("p (k t) -> p k t", k=8), r=['ps%d' % pi], w=['mhts%d' % b])
                self.dma('pool', self.HTB[tt], hts[b][:], r=['mhts%d' % b])
            self.S.barrier()

    def stage_inproj(self, l):
        cfg = self.cfg
        with ExitStack() as es:
            hT = [self.sb(es, "ihT%d" % i, [128, 4, 16, 128], BF16) for i in range(2)]
            wtm = [self.sb(es, "iwtm%d" % i, [128, 16, 512], BF16) for i in range(3)]
            wst = [self.sb(es, "iwst%d" % i, [128, 16, 128], BF16) for i in range(4)]
            zo = [self.sb(es, "izo%d" % i, [128, 512], F32) for i in range(4)]
            for tb in range(3):
                self.dma('pool', wtm[tb][:], self.wt[l, tb], w=['iwtm%d' % tb])
            if l == 0:
                self.convert_uv(0)
            n = 0
            for mi, t0 in enumerate(range(0, cfg.TT, 4)):
                nt = min(4, cfg.TT - t0); hb = mi % 2
                for j in range(nt):
                    self.dma('sp', hT[hb][:, j], self.HTB[t0 + j], w=['ihT%d' % hb])
                for nb in range(37):
                    wb = nb % 4
                    self.dma('sp', wst[wb][:], self.WFB[l, nb], w=['iwst%d' % wb])
                    pi = n % 8; zb = n % 4; n += 1
                    for k in range(16):
                        self.mm(self.ps[pi][:, 0:nt * 128].rearrange("p (j t) -> p j t", j=nt), wst[wb][:, k, :], hT[hb][:, 0:nt, k, :],
                                k == 0, k == 15, r=['iwst%d' % wb, 'ihT%d' % hb], w=['ps%d' % pi])
                    self.cp('act' if n % 2 else 'dve', zo[zb][:, 0:nt * 128], self.ps[pi][:, 0:nt * 128], r=['ps%d' % pi], w=['izo%d' % zb])
                    self.dma('sp', self.ZT[nb * 128:(nb + 1) * 128, t0 * 128:(t0 + nt) * 128], zo[zb][:, 0:nt * 128], r=['izo%d' % zb])
                for j in range(nt):
                    for tb in range(3):
                        pi = n % 8; zb = n % 4; n += 1
                        for k in range(16):
                            self.mm(self.ps[pi][:], hT[hb][:, j, k, :], wtm[tb][:, k, :], k == 0, k == 15, r=['ihT%d' % hb, 'iwtm%d' % tb], w=['ps%d' % pi])
                        self.cp('act' if n % 2 else 'dve', zo[zb][:], self.ps[pi][:], r=['ps%d' % pi], w=['izo%d' % zb])
                        self.dma('sp', self.ZV[t0 + j][:, tb * 512:(tb + 1) * 512], zo[zb][:], r=['izo%d' % zb])
            self.S.barrier()

    def stage_rope(self, l):
        cfg = self.cfg; LC = cfg.LC
        rows = [(2048 + h * 128, 3072 + h * 128, h * 128) for h in range(8)] + [(4096 + g * 128, 4352 + g * 128, 1024 + g * 128) for g in range(2)]
        with ExitStack() as es:
            ta = [self.sb(es, "rta%d" % i, [128, 512], F32) for i in range(2)]
            tb = [self.sb(es, "rtb%d" % i, [128, 512], F32) for i in range(2)]
            ob = [self.sb(es, "rob%d" % i, [128, 512], BF16) for i in range(2)]
            ct = self.sb(es, "rct", [128, 512], F32); st = self.sb(es, "rst", [128, 512], F32)
            n = 0
            for c0 in range(0, LC, 512):
                cn = min(512, LC - c0)
                for (ra, rp, ro) in rows:
                    b = n % 2; n += 1
                    self.dma('sp', ta[b][:, 0:cn], self.ZT[ra:ra + 128, c0:c0 + cn], w=['rta%d' % b])
                    self.cp('act', ob[b][:, 0:cn], ta[b][:, 0:cn], r=['rta%d' % b], w=['rob%d' % b])
                    self.dma('pool', self.QKT[ro:ro + 128, c0:c0 + cn], ob[b][:, 0:cn], r=['rob%d' % b])
            for s0 in range(0, cfg.L, 512):
                self.dma('sp', ct[:], self.cos[:, s0:s0 + 512], w=['rct'])
                self.dma('sp', st[:], self.sins[:, s0:s0 + 512], w=['rst'])
                for (ra, rp, ro) in rows:
                    b = n % 2; n += 1
                    self.dma('sp', ta[b][:], self.ZT[ra:ra + 128, LC + s0:LC + s0 + 512], w=['rta%d' % b])
                    self.dma('sp', tb[b][:], self.ZT[rp:rp + 128, LC + s0:LC + s0 + 512], w=['rtb%d' % b])
                    self.tt('dve', ta[b][:], ta[b][:], ct[:], ALU.mult, r=['rta%d' % b, 'rct'], w=['rta%d' % b])
                    self.tt('pool', tb[b][:], tb[b][:], st[:], ALU.mult, r=['rtb%d' % b, 'rst'], w=['rtb%d' % b])
                    self.tt('dve', ob[b][:], ta[b][:], tb[b][:], ALU.add, r=['rta%d' % b, 'rtb%d' % b], w=['rob%d' % b])
                    self.dma('pool', self.QKT[ro:ro + 128, LC + s0:LC + s0 + 512], ob[b][:], r=['rob%d' % b])
            self.S.barrier()

    def stage_conv(self, l):
        cfg = self.cfg; SEG = 1024
        with ExitStack() as es:
            xi = [self.sb(es, "cxi%d" % i, [128, SEG + 2], F32) for i in range(2)]
            cg = [self.sb(es, "ccg%d" % i, [128, SEG + 2], F32) for i in range(2)]
            bb = [self.sb(es, "cbb%d" % i, [128, SEG], F32) for i in range(2)]
            u = [self.sb(es, "cu%d" % i, [128, SEG + 2], F32) for i in range(2)]
            acc = [self.sb(es, "cacc%d" % i, [128, SEG], F32) for i in range(2)]
            ob = [self.sb(es, "cob%d" % i, [128, SEG], BF16) for i in range(2)]
            cwt = self.sb(es, "ccw", [128, 4, 3], F32)
            self.dma('sp', cwt[:], self.cw[l], w=['ccw'])
            n = 0
            for (q0, q1) in ((0, cfg.LC), (cfg.LC, cfg.TOK)):
                for s0 in range(q0, q1, SEG):
                    sn = min(SEG, q1 - s0)
                    lo = max(q0, s0 - 1); hi = min(q1, s0 + sn + 1); off = lo - (s0 - 1); ln = hi - lo
                    for cc in range(4):
                        b = n % 2; n += 1
                        kx, kc, kb, ku, ka, ko = 'cxi%d' % b, 'ccg%d' % b, 'cbb%d' % b, 'cu%d' % b, 'cacc%d' % b, 'cob%d' % b
                        self.dma('sp', xi[b][:, off:off + ln], self.ZT[cc * 128:(cc + 1) * 128, lo:hi], w=[kx])
                        self.dma('sp', cg[b][:, off:off + ln], self.ZT[1024 + cc * 128:1024 + (cc + 1) * 128, lo:hi], w=[kc])
                        self.dma('sp', bb[b][:, 0:sn], self.ZT[512 + cc * 128:512 + (cc + 1) * 128, s0:s0 + sn], w=[kb])
                        if off > 0:
                            self.memset('pool', u[b][:, 0:1], 0.0, w=[ku])
                        if off + ln < sn + 2:
                            self.memset('pool', u[b][:, sn + 1:sn + 2], 0.0, w=[ku])
                        self.tt('pool', u[b][:, off:off + ln], cg[b][:, off:off + ln], xi[b][:, off:off + ln], ALU.mult, r=[kx, kc], w=[ku])
                        self.ts('dve', acc[b][:, 0:sn], u[b][:, 0:sn], cwt[:, cc, 0:1], None, ALU.mult, r=[ku, 'ccw'], w=[ka])
                        self.stt('dve', acc[b][:, 0:sn], u[b][:, 1:sn + 1], cwt[:, cc, 1:2], acc[b][:, 0:sn], ALU.mult, ALU.add, r=[ku, 'ccw', ka], w=[ka])
                        self.stt('dve', acc[b][:, 0:sn], u[b][:, 2:sn + 2], cwt[:, cc, 2:3], acc[b][:, 0:sn], ALU.mult, ALU.add, r=[ku, 'ccw', ka], w=[ka])
                        self.tt('pool', ob[b][:, 0:sn], acc[b][:, 0:sn], bb[b][:, 0:sn], ALU.mult, r=[ka, kb], w=[ko])
                        self.dma('pool', self.MIXT[cc * 128:(cc + 1) * 128, s0:s0 + sn], ob[b][:, 0:sn], r=[ko])
            self.S.barrier()

    def stage_gla(self, l):
        cfg = self.cfg
        NCHC = cfg.LC // 64; NCH = cfg.TOK // 64
        ps = self.ps
        with ExitStack() as es:
            sb = lambda n, s, d: self.sb(es, n, s, d)
            St = sb("gS", [64, 512], F32); Sbf = sb("gSbf", [64, 512], BF16)
            w2t = sb("gw2", [33, 512], F32); NW = sb("gNW", [64, 128], F32)
            B2 = range(2)
            qT = [sb("gqT%d" % i, [64, 4, 64], F32) for i in B2]; kT = [sb("gkT%d" % i, [64, 4, 64], F32) for i in B2]
            lrT = [sb("glr%d" % i, [33, 64], F32) for i in B2]
            kTok = [sb("gkk%d" % i, [64, 256], F32) for i in B2]; vTok = [sb("gvv%d" % i, [64, 512], F32) for i in B2]
            gTok = [sb("ggg%d" % i, [64, 512], F32) for i in B2]; ofl = [sb("gof%d" % i, [64, 512], F32) for i in B2]
            vbf = [sb("gvb%d" % i, [64, 512], BF16) for i in B2]
            lsb = [sb("gls%d" % i, [64, 256], F32) for i in B2]
            Eb = [sb("gEb%d" % i, [64, 256], F32) for i in B2]; Enb = [sb("gEn%d" % i, [64, 256], F32) for i in B2]
            Ek = [sb("gEk%d" % i, [64, 256], F32) for i in B2]
            qe = [sb("gqe%d" % i, [64, 4, 64], BF16) for i in B2]; ke = [sb("gke%d" % i, [64, 4, 64], BF16) for i in B2]
            kend = [sb("gkd%d" % i, [64, 256], BF16) for i in B2]; attT = [sb("gat%d" % i, [64, 4, 64], BF16) for i in B2]
            sq = sb("gsq", [64, 4, 128], F32); ss = sb("gss", [64, 4], F32); rstd = sb("grs", [64, 4], F32)
            on = sb("gon", [64, 512], F32); sg = sb("gsg", [64, 512], F32); res = sb("gres", [64, 512], BF16)
            mo = [sb("gmo%d" % i, [128, 4, 64], BF16) for i in B2]
            self.dma('sp', w2t[:], self.w2a[l], w=['gw2'])
            self.dma('sp', NW[:], self.gnorm[l][0:64, :], w=['gNW'])
            for i in B2:
                self.memset('pool', lrT[i][:], 1.0, w=['glr%d' % i])
            orders = [list(range(NCH)), list(range(NCHC - 1, -1, -1)) + list(range(NCH - 1, NCHC - 1, -1))]
            for d in range(2):
                self.memset('dve', St[:], 0.0, w=['gS'])
                self.memset('dve', Sbf[:], 0.0, w=['gSbf'])
                for n, ci in enumerate(orders[d]):
                    b = n % 2; c0 = ci * 64; tile = ci // 2; hf = ci % 2
                    K = lambda s: s + str(b)
                    self.dma('sp', qT[b][:], self.ZT[1536:1792, c0:c0 + 64].rearrange("(h k) t -> k h t", h=4), w=[K('gqT')])
                    self.dma('sp', kT[b][:], self.ZT[1792:2048, c0:c0 + 64].rearrange("(h k) t -> k h t", h=4), w=[K('gkT')])
                    self.dma('sp', lrT[b][0:32, :], self.ZT[4608:4640, c0:c0 + 64], w=[K('glr')])
                    zv = self.ZV[tile]
                    self.dma('sp', kTok[b][:], zv[hf * 64:(hf + 1) * 64, 0:256], w=[K('gkk')])
                    self.dma('sp', vTok[b][:], zv[hf * 64:(hf + 1) * 64, 256:768], w=[K('gvv')])
                    if d == 1:
                        self.dma('sp', gTok[b][:], zv[hf * 64:(hf + 1) * 64, 1024:1536], w=[K('ggg')])
                        self.dma('sp', ofl[b][:], self.OF[ci], w=[K('gof')])
                    self.mm(ps[0][0:64, 0:256], lrT[b][0:33, :], w2t[0:33, d * 256:(d + 1) * 256], True, True, r=[K('glr'), 'gw2'], w=['ps0'])
                    self.act(lsb[b][:], ps[0][0:64, 0:256], AF.Exp, scale=-1.0, r=['ps0'], w=[K('gls')])
                    self.act(lsb[b][:], lsb[b][:], AF.Ln, bias=1.0, r=[K('gls')], w=[K('gls')])
                    for h in range(4):
                        self.mm(ps[1][0:64, h * 64:(h + 1) * 64], lsb[b][:, h * 64:(h + 1) * 64], self.triT[d], True, True, r=[K('gls'), 'cft'], w=['ps1'])
                    self.mm(ps[2][0:64, 0:256], self.amT[d], lsb[b][:, 0:256], True, True, r=[K('gls'), 'cft'], w=['ps2'])
                    self.act(Eb[b][:], ps[1][0:64, 0:256], AF.Exp, r=['ps1'], w=[K('gEb')])
                    self.act(Enb[b][:], ps[1][0:64, 0:256], AF.Exp, scale=-1.0, r=['ps1'], w=[K('gEn')])
                    self.act(Ek[b][:], ps[2][0:64, 0:256], AF.Exp, r=['ps2'], w=[K('gEk')])
                    self.stt('dve', qe[b][:], qT[b][:], 0.125, Eb[b][:].rearrange("p (h t) -> p h t", h=4), ALU.mult, ALU.mult, r=[K('gqT'), K('gEb')], w=[K('gqe')])
                    self.tt('dve', ke[b][:], kT[b][:], Enb[b][:].rearrange("p (h t) -> p h t", h=4), ALU.mult, r=[K('gkT'), K('gEn')], w=[K('gke')])
                    self.tt('pool', kend[b][:], kTok[b][:], Ek[b][:], ALU.mult, r=[K('gkk'), K('gEk')], w=[K('gkd')])
                    self.cp('pool', vbf[b][:], vTok[b][:], r=[K('gvv')], w=[K('gvb')])
                    for h in range(4):
                        self.mm(ps[3][0:64, h * 64:(h + 1) * 64], ke[b][:, h, :], qe[b][:, h, :], True, True, r=[K('gke'), K('gqe')], w=['ps3'])
                    self.tt('dve', attT[b][:], ps[3][0:64, 0:256].rearrange("p (h t) -> p h t", h=4),
                            self.maskT[d].unsqueeze(1).broadcast_to([64, 4, 64]), ALU.mult, r=['ps3', 'cft'], w=[K('gat')])
                    for h in range(4):
                        self.mm(ps[4][0:64, h * 128:(h + 1) * 128], attT[b][:, h, :], vbf[b][:, h * 128:(h + 1) * 128], True, False, r=[K('gat'), K('gvb')], w=['ps4'])
                        self.mm(ps[4][0:64, h * 128:(h + 1) * 128], qe[b][:, h, :], Sbf[:, h * 128:(h + 1) * 128], False, True, r=[K('gqe'), 'gSbf'], w=['ps4'])
                    for h in range(4):
                        self.mm(ps[5][0:64, h * 128:(h + 1) * 128], kend[b][:, h * 64:(h + 1) * 64], vbf[b][:, h * 128:(h + 1) * 128], True, True, r=[K('gkd'), K('gvb')], w=['ps5'])
                    col = 63 if d == 0 else 0
                    for h in range(4):
                        self.stt('dve', St[:, h * 128:(h + 1) * 128], St[:, h * 128:(h + 1) * 128], Eb[b][:, h * 64 + col:h * 64 + col + 1],
                                 ps[5][0:64, h * 128:(h + 1) * 128], ALU.mult, ALU.add, r=['gS', K('gEb'), 'ps5'], w=['gS'])
                    self.cp('pool', Sbf[:], St[:], r=['gS'], w=['gSbf'])
                    if d == 0:
                        self.cp('act', ofl[b][:], ps[4][0:64, :], r=['ps4'], w=[K('gof')])
                        self.dma('pool', self.OF[ci], ofl[b][:], r=[K('gof')])
                    else:
                        self.tt('dve', ofl[b][:], ps[4][0:64, :], ofl[b][:], ALU.add, r=['ps4', K('gof')], w=[K('gof')])
                        o3 = ofl[b][:].rearrange("p (h v) -> p h v", h=4)
                        self.tt('dve', sq[:], o3, o3, ALU.mult, r=[K('gof')], w=['gsq'])
                        self.S.op('dve', lambda e, o=ss[:], i=sq[:]: e.tensor_reduce(out=o, in_=i, axis=AX.X, op=ALU.add), reads=['gsq'], writes=['gss'])
                        self.ts('dve', ss[:], ss[:], 1.0 / 128.0, LN_EPS, ALU.mult, ALU.add, r=['gss'], w=['gss'])
                        self.act(rstd[:], ss[:], AF.Ln, r=['gss'], w=['grs'])
                        self.act(rstd[:], rstd[:], AF.Exp, scale=-0.5, r=['grs'], w=['grs'])
                        for h in range(4):
                            self.stt('dve', on[:, h * 128:(h + 1) * 128], ofl[b][:, h * 128:(h + 1) * 128], rstd[:, h:h + 1], NW[:], ALU.mult, ALU.mult,
                                     r=[K('gof'), 'grs', 'gNW'], w=['gon'])
                        self.act(sg[:], gTok[b][:], AF.Silu, r=[K('ggg')], w=['gsg'])
                        self.tt('pool', res[:], on[:], sg[:], ALU.mult, r=['gon', 'gsg'], w=['gres'])
                        pb = self.psb(6)
                        for h in range(4):
                            self.tp(pb[:, h * 64:(h + 1) * 64], res[:, h * 128:(h + 1) * 128], self.identb[0:64, 0:64], r=['gres', 'identb'], w=['ps6'])
                        self.cp('act', mo[b][:], pb[:, 0:256].rearrange("p (h t) -> p h t", h=4), r=['ps6'], w=[K('gmo')])
                        self.dma('pool', self.MIXT[512:1024, c0:c0 + 64].rearrange("(h v) t -> v h t", h=4), mo[b][:], r=[K('gmo')])
                self.S.barrier()

    def stage_swa(self, l, last):
        cfg = self.cfg; NC, TT = cfg.NC, cfg.TT
        ps = self.ps
        scale = 128.0 ** -0.5
        with ExitStack() as es:
            sb = lambda n, s, d: self.sb(es, n, s, d)
            KT = sb("sKT", [128, cfg.TOK], BF16); VT = sb("sVT", [128, TT, 128], BF16)
            sk = sb("ssk", [128, 8], F32); sinkE = sb("ssinkE", [128, 8], F32)
            q4 = [sb("sq4%d" % i, [128, 4, 128], BF16) for i in range(2)]
            pT = [sb("spT%d" % i, [128, 4, 128], BF16) for i in range(3)]
            dn = sb("sdn", [128, 4, 128], F32); ob = [sb("sob%d" % i, [128, 4, 128], BF16) for i in range(2)]
            self.dma('sp', sk[:], self.sink[l], w=['ssk'])
            self.act(sinkE[:], sk[:], AF.Exp, r=['ssk'], w=['ssinkE'])
            n = 0; m = 0
            for g in range(2):
                self.dma('sp', KT[:], self.QKT[1024 + g * 128:1024 + (g + 1) * 128, :], w=['sKT'])
                for t0 in range(0, TT, 16):
                    t1 = min(TT, t0 + 16)
                    self.dma('pool', VT[:, t0:t1, :], self.ZV[t0:t1, :, 768 + g * 128:768 + (g + 1) * 128].rearrange("t p c -> p t c"), w=['sVT'])
                for qt in (range(NC, TT) if last else range(TT)):
                    if qt < NC:
                        keys = [(kt, None) for kt in range(NC)]
                    else:
                        keys = []
                        if qt - 1 >= NC: keys.append((qt - 1, self.maskP))
                        keys.append((qt, None))
                        if qt + 1 < TT: keys.append((qt + 1, self.maskN))
                        keys += [(kt, None) for kt in range(NC)]
                    b = n % 2; n += 1
                    po = 3 + 2 * b; pd = 4 + 2 * b
                    self.dma('sp', q4[b][:], self.QKT[g * 512:(g + 1) * 512, qt * 128:(qt + 1) * 128].rearrange("(j d) t -> d j t", j=4), w=['sq4%d' % b])
                    for ki, (kt, mask) in enumerate(keys):
                        pi = m % 3; m += 1
                        first = ki == 0; lastk = ki == len(keys) - 1
                        self.mm(ps[pi][:].rearrange("p (j t) -> p j t", j=4), KT[:, kt * 128:(kt + 1) * 128], q4[b][:], True, True, r=['sKT', 'sq4%d' % b], w=['ps%d' % pi])
                        self.act(pT[pi][:], ps[pi][:].rearrange("p (j t) -> p j t", j=4), AF.Exp, scale=scale, r=['ps%d' % pi], w=['spT%d' % pi])
                        if mask is not None:
                            self.tt('pool', pT[pi][:], pT[pi][:], mask.unsqueeze(1).broadcast_to([128, 4, 128]), ALU.mult, r=['spT%d' % pi, 'cft'], w=['spT%d' % pi])
                        self.mm(ps[po][:].rearrange("p (j t) -> p j t", j=4), VT[:, kt, :], pT[pi][:], first, lastk, r=['sVT', 'spT%d' % pi], w=['ps%d' % po])
                        self.mm(ps[pd][:].rearrange("p (j t) -> p j t", j=4), self.onesb[:], pT[pi][:], first, lastk, r=['onesb', 'spT%d' % pi], w=['ps%d' % pd])
                    self.tt('dve', dn[:], ps[pd][:].rearrange("p (j t) -> p j t", j=4), sinkE[:, g * 4:(g + 1) * 4].unsqueeze(2).broadcast_to([128, 4, 128]),
                            ALU.add, r=['ps%d' % pd, 'ssinkE'], w=['sdn'])
                    self.S.op('dve', lambda e, o=dn[:]: e.reciprocal(out=o, in_=o), reads=['sdn'], writes=['sdn'])
                    self.tt('dve', ob[b][:], ps[po][:].rearrange("p (j t) -> p j t", j=4), dn[:], ALU.mult, r=['ps%d' % po, 'sdn'], w=['sob%d' % b])
                    self.dma('pool', self.MIXT[1024 + g * 512:1024 + (g + 1) * 512, qt * 128:(qt + 1) * 128].rearrange("(j d) t -> d j t", j=4), ob[b][:], r=['sob%d' % b])
            self.S.barrier()

    def resid_ln(self, bufs, xsrc, banks, Gt, gkey, lngt, lnbt, dsts):
        xt, tq, st, mv, rs = bufs
        ps = self.ps
        self.dma('sp', xt[:], xsrc, w=['lx'])
        for nb in range(4):
            self.tt('dve', tq[:, nb * 512:(nb + 1) * 512], ps[banks[nb]][:], Gt[:, nb * 512:(nb + 1) * 512], ALU.mult, r=['ps%d' % banks[nb], gkey], w=['lt'])
        self.stt('dve', tq[:], xt[:], ALPHA, tq[:], ALU.mult, ALU.add, r=['lx', 'lt'], w=['lt'])
        for nb in range(4):
            self.S.op('dve', lambda e, o=st[:, nb, :], i=tq[:, nb * 512:(nb + 1) * 512]: e.bn_stats(out=o, in_=i), reads=['lt'], writes=['lst'])
        self.S.op('dve', lambda e, o=mv[:], i=st[:]: e.bn_aggr(out=o, in_=i), reads=['lst'], writes=['lmv'])
        self.act(rs[:], mv[:, 1:2], AF.Ln, bias=LN_EPS, r=['lmv'], w=['lrs'])
        self.act(rs[:], rs[:], AF.Exp, scale=-0.5, r=['lrs'], w=['lrs'])
        self.ts('dve', xt[:], tq[:], mv[:, 0:1], rs[:, 0:1], ALU.subtract, ALU.mult, r=['lt', 'lmv', 'lrs'], w=['lx'])
        self.tt('pool', xt[:], xt[:], lngt[:], ALU.mult, r=['lx', 'lng'], w=['lx'])
        self.tt('dve', xt[:], xt[:], lnbt[:], ALU.add, r=['lx', 'lnb'], w=['lx'])
        for dq, dst in zip(('sp', 'pool'), dsts):
            self.dma(dq, dst, xt[:], r=['lx'])

    def ln_bufs(self, es):
        sb = lambda n, s, d: self.sb(es, n, s, d)
        return (sb("lx", [128, D], F32), sb("lt", [128, D], F32), sb("lst", [128, 4, 6], F32), sb("lmv", [128, 2], F32), sb("lrs", [128, 1], F32))

    def stage_outproj(self, l, tiles):
        cfg = self.cfg
        with ExitStack() as es:
            sb = lambda n, s, d: self.sb(es, n, s, d)
            wot = sb("owo", [128, 16, D], BF16)
            lngt = sb("lng", [128, D], F32); lnbt = sb("lnb", [128, D], F32)
            Gt = [sb("oG%d" % i, [128, D], F32) for i in range(2)]
            mixT = [sb("omx%d" % i, [128, 16, 128], BF16) for i in range(2)]
            bufs = self.ln_bufs(es)
            for nb in range(4):
                self.dma('pool', wot[:, :, nb * 512:(nb + 1) * 512], self.wo[l][:, :, nb * 512:(nb + 1) * 512], w=['owo'])
            self.dma('sp', lngt[:], self.lng[l, 0], w=['lng']); self.dma('sp', lnbt[:], self.lnb[l, 0], w=['lnb'])
            for w in range(2):
                self.dma('sp', Gt[w][:], self.MOD[l, w, :, 2 * D:3 * D], w=['oG%d' % w])
            src = self.xs_in if l == 0 else self.XS
            for n, tt in enumerate(tiles):
                b = n % 2; w = 1 if tt < cfg.NC else 0
                self.dma('sp', mixT[b][:], self.MIXT[:, tt * 128:(tt + 1) * 128].rearrange("(k p) t -> p k t", p=128), w=['omx%d' % b])
                banks = [4 * b + i for i in range(4)]
                for nb in range(4):
                    for k in range(16):
                        self.mm(self.ps[banks[nb]][:], mixT[b][:, k, :], wot[:, k, nb * 512:(nb + 1) * 512], k == 0, k == 15, r=['omx%d' % b, 'owo'], w=['ps%d' % banks[nb]])
                self.resid_ln(bufs, src[tt], banks, Gt[w], 'oG%d' % w, lngt, lnbt, [self.XS[tt]])
            self.S.barrier()

    def stage_peer(self, l, tiles, last):
        cfg = self.cfg; NK = cfg.NK; NC = cfg.NC
        ps = self.ps
        tiles = list(tiles)
        pairs = [tiles[i:i + 2] for i in range(0, len(tiles), 2)]
        if not hasattr(self, 'RT'):
            self.RT = self.dscr("RT", [cfg.TT, 128, 3, 128], F32, dbg=True)
        with ExitStack() as es:
            sb = lambda n, s, d: self.sb(es, n, s, d)
            h2T = [sb("ph2T%d" % i, [128, 2, 16, 128], BF16) for i in range(2)]
            qTb = sb("pqTb", [128, 16, 256], BF16)
            wqr = sb("pwqr", [128, 16, D], BF16)
            sktt = sb("pskt", [128, 2, NK], BF16)
            ssb = sb("pssb", [128, 16, NK], F32); wk = sb("pwk", [128, 16, NK], F32)
            m8 = sb("pm8", [128, 16, 16], F32); i8 = sb("pi8", [128, 16, 16], U32); tif = sb("ptif", [128, 16, 16], F32)
            cand = sb("pcand", [128, 8, 256], F32); wk2 = sb("pwk2", [128, 8, 256], F32)
            b8 = sb("pb8", [128, 8, 16], F32); f8 = sb("pf8", [128, 8, 16], U32); ff = sb("pff", [128, 8, 16], F32)
            fa = sb("pfa", [128, 8, 16], F32); fb = sb("pfb", [128, 8, 16], F32)
            fau = sb("pfau", [128, 8, 16], U32); fbu = sb("pfbu", [128, 8, 16], U32)
            eq = sb("peq", [128, 8, 16, 16], F32)
            rt = [sb("prt%d" % i, [128, 3, 128], F32) for i in range(2)]
            ez = sb("pez", [128, 8, 16], F32); zz = sb("pzz", [128, 8], F32)
            self.dma('pool', sktt[:], self.skt[l], w=['pskt'])
            for blk in range(16):
                self.dma('pool', wqr[:, :, blk * 128:(blk + 1) * 128], self.wq[l, blk], w=['pwqr'])
            n = 0
            for pi_, pr in enumerate(pairs):
                hb = pi_ % 2
                for j, tt in enumerate(pr):
                    self.dma('sp', h2T[hb][:, j], self.HTB[tt], w=['ph2T%d' % hb])
                for blk in range(16):
                    pi = n % 8; n += 1
                    for k in range(16):
                        self.mm(ps[pi][:, 0:256].rearrange("p (j t) -> p j t", j=2), wqr[:, k, blk * 128:(blk + 1) * 128], h2T[hb][:, :, k, :], k == 0, k == 15,
                                r=['pwqr', 'ph2T%d' % hb], w=['ps%d' % pi])
                    self.cp('act' if blk % 2 else 'dve', qTb[:, blk, :], ps[pi][:, 0:256], r=['ps%d' % pi], w=['pqTb'])
                for j, tt in enumerate(pr):
                    rb = j
                    for hp in range(16):
                        bank = hp // 4
                        self.mm(ps[bank][:, (hp % 4) * NK:(hp % 4 + 1) * NK], qTb[:, hp, j * 128:(j + 1) * 128], sktt[:, hp % 2, :], True, True,
                                r=['pqTb', 'pskt'], w=['ps%d' % bank])
                    for bank in range(4):
                        self.cp('act', ssb[:, bank * 4:(bank + 1) * 4, :], ps[bank][:, 0:4 * NK].rearrange("p (a n) -> p a n", a=4), r=['ps%d' % bank], w=['pssb'])
                    V = self.S.op
                    for hp in range(16):
                        V('dve', lambda e, o=m8[:, hp, 0:8], i=ssb[:, hp, :]: e.max(out=o, in_=i), reads=['pssb'], writes=['pm8'])
                        V('dve', lambda e, o=i8[:, hp, 0:8], a=m8[:, hp, 0:8], i=ssb[:, hp, :]: e.max_index(out=o, in_max=a, in_values=i), reads=['pssb', 'pm8'], writes=['pi8'])
                        V('dve', lambda e, o=wk[:, hp, :], a=m8[:, hp, 0:8], i=ssb[:, hp, :]: e.match_replace(out=o, in_to_replace=a, in_values=i, imm_value=-1e30), reads=['pssb', 'pm8'], writes=['pwk'])
                        V('dve', lambda e, o=m8[:, hp, 8:16], i=wk[:, hp, :]: e.max(out=o, in_=i), reads=['pwk'], writes=['pm8'])
                        V('dve', lambda e, o=i8[:, hp, 8:16], a=m8[:, hp, 8:16], i=wk[:, hp, :]: e.max_index(out=o, in_max=a, in_values=i), reads=['pwk', 'pm8'], writes=['pi8'])
                    self.cp('dve', tif[:], i8[:], r=['pi8'], w=['ptif'])
                    tv4 = m8[:].rearrange("q (h p) k -> q h p k", p=2); ti4 = tif[:].rearrange("q (h p) k -> q h p k", p=2)
                    c4 = cand[:].rearrange("q h (a b) -> q h a b", a=16)
                    self.tt('dve', c4, tv4[:, :, 0, :].unsqueeze(3).broadcast_to([128, 8, 16, 16]), tv4[:, :, 1, :].unsqueeze(2).broadcast_to([128, 8, 16, 16]),
                            ALU.add, r=['pm8'], w=['pcand'])
                    for h in range(8):
                        V('dve', lambda e, o=b8[:, h, 0:8], i=cand[:, h, :]: e.max(out=o, in_=i), reads=['pcand'], writes=['pb8'])
                        V('dve', lambda e, o=f8[:, h, 0:8], a=b8[:, h, 0:8], i=cand[:, h, :]: e.max_index(out=o, in_max=a, in_values=i), reads=['pcand', 'pb8'], writes=['pf8'])
                        V('dve', lambda e, o=wk2[:, h, :], a=b8[:, h, 0:8], i=cand[:, h, :]: e.match_replace(out=o, in_to_replace=a, in_values=i, imm_value=-1e30), reads=['pcand', 'pb8'], writes=['pwk2'])
                        V('dve', lambda e, o=b8[:, h, 8:16], i=wk2[:, h, :]: e.max(out=o, in_=i), reads=['pwk2'], writes=['pb8'])
                        V('dve', lambda e, o=f8[:, h, 8:16], a=b8[:, h, 8:16], i=wk2[:, h, :]: e.max_index(out=o, in_max=a, in_values=i), reads=['pwk2', 'pb8'], writes=['pf8'])
                    self.ts('dve', fau[:], f8[:], 4, None, ALU.logical_shift_right, r=['pf8'], w=['pfau'])
                    self.ts('dve', fbu[:], f8[:], 15, None, ALU.bitwise_and, r=['pf8'], w=['pfbu'])
                    self.cp('dve', fa[:], fau[:], r=['pfau'], w=['pfa'])
                    self.cp('dve', fb[:], fbu[:], r=['pfbu'], w=['pfb'])
                    io4 = self.iota16.unsqueeze(1).unsqueeze(1).broadcast_to([128, 8, 16, 16])
                    for which, (fsel, pp) in enumerate(((fa, 0), (fb, 1))):
                        self.tt('dve', eq[:], fsel[:].unsqueeze(3).broadcast_to([128, 8, 16, 16]), io4, ALU.is_equal, r=['pfa', 'pfb', 'cft'], w=['peq'])
                        self.tt('pool', eq[:], eq[:], ti4[:, :, pp, :].unsqueeze(2).broadcast_to([128, 8, 16, 16]), ALU.mult, r=['peq', 'ptif'], w=['peq'])
                        V('dve', lambda e, o=rt[rb][:, which, :].rearrange("q (h k) -> q h k", h=8), i=eq[:]: e.tensor_reduce(out=o, in_=i, axis=AX.X, op=ALU.add),
                          reads=['peq'], writes=['prt%d' % rb])
                    self.tt('dve', ez[:], b8[:], b8[:, :, 0:1].broadcast_to([128, 8, 16]), ALU.subtract, r=['pb8'], w=['pez'])
                    self.act(ez[:], ez[:], AF.Exp, r=['pez'], w=['pez'])
                    V('dve', lambda e, o=zz[:], i=ez[:]: e.tensor_reduce(out=o, in_=i, axis=AX.X, op=ALU.add), reads=['pez'], writes=['pzz'])
                    V('dve', lambda e, o=zz[:]: e.reciprocal(out=o, in_=o), reads=['pzz'], writes=['pzz'])
                    self.tt('dve', rt[rb][:, 2, :].rearrange("q (h k) -> q h k", h=8), ez[:], zz[:].unsqueeze(2).broadcast_to([128, 8, 16]), ALU.mult,
                            r=['pez', 'pzz'], w=['prt%d' % rb])
                    self.dma('pool', self.RT[tt], rt[rb][:], r=['prt%d' % rb])
            self.S.barrier()
        G = 8
        with ExitStack() as es:
            sb = lambda n, s, d: self.sb(es, n, s, d)
            WT = sb("dWT", [128, NK, 256], BF16)
            h2T = sb("dh2T", [128, 2, 16, 128], BF16)
            NUB = 3; NI = min(NK, 24)
            utc = [sb("dut%d" % i, [128, 2, 16, NK], BF16) for i in range(NUB)]
            vch = [sb("dvc%d" % i, [128, 2, D], BF16) for i in range(NUB)]
            gaI = [sb("dgI%d" % i, [128, 256], BF16) for i in range(NI)]
            rtt = sb("drt", [128, 3, 128], F32); rT = sb("drT", [128, 3, 128], F32)
            oh2 = [sb("doh2%d" % i, [128, G, NK], BF16) for i in range(2)]
            oh1 = [sb("doh1%d" % i, [128, G, NK], BF16) for i in range(2)]
            ga = [sb("dga%d" % i, [128, 256], F32) for i in range(2)]
            lngt = sb("lng", [128, D], F32); lnbt = sb("lnb", [128, D], F32)
            Gt = sb("dG", [128, D], F32)
            bufs = self.ln_bufs(es)
            self.dma('sp', lngt[:], self.lng[l, 1], w=['lng']); self.dma('sp', lnbt[:], self.lnb[l, 1], w=['lnb'])
            io3 = self.iota128[:, 0:NK].unsqueeze(1).broadcast_to([128, G, NK])
            curw = None; n = 0; m = 0
            for pidx, pr in enumerate(pairs):
                if l + 1 < cfg.DEPTH:
                    if pidx < 3:
                        self.convert_uv(l + 1, [pidx])
                    if pidx == min(3, len(pairs) - 1):
                        self.convert_uv(l + 1, list(range(pidx if pidx == 3 else pidx + 1, 4)))
                w = 1 if pr[0] < NC else 0
                if w != curw:
                    self.dma('sp', Gt[:], self.MOD[l, w, :, 5 * D:6 * D], w=['dG']); curw = w
                for j, tt in enumerate(pr):
                    self.dma('sp', h2T[:, j], self.HTB[tt], w=['dh2T'])
                started = 0

                def phase1_mm(i1, dest, dkey, banks):
                    ub = (i1 // 2) % NUB; pi = banks[i1 % len(banks)]
                    if i1 % 2 == 0:
                        self.dma('sp', utc[ub][:], self.UTB[l, i1:i1 + 2].rearrange("c p k e -> p c k e"), w=['dut%d' % ub])
                    for k in range(16):
                        self.mm(ps[pi][0:NK, 0:256].rearrange("p (j t) -> p j t", j=2), utc[ub][:, i1 % 2, k, :], h2T[:, :, k, :], k == 0, k == 15,
                                r=['dut%d' % ub, 'dh2T'], w=['ps%d' % pi])
                    self.act(dest, ps[pi][0:NK, 0:256], AF.Gelu, r=['ps%d' % pi], w=[dkey])

                for j, tt in enumerate(pr):
                    self.dma('sp', rtt[:], self.RT[tt], w=['drt'])
                    for a in range(3):
                        self.tp(ps[5 + a][:, 0:128], rtt[:, a, :], self.identf, r=['drt', 'cft'], w=['ps%d' % (5 + a)])
                        self.cp('act', rT[:, a, :], ps[5 + a][:, 0:128], r=['ps%d' % (5 + a)], w=['drT'])
                    for t0 in range(0, 128, G):
                        ob = m % 2; m += 1
                        self.tt('dve', oh2[ob][:], io3, rT[:, 1, t0:t0 + G].unsqueeze(2).broadcast_to([128, G, NK]), ALU.is_equal, r=['cft', 'drT'], w=['doh2%d' % ob])
                        self.tt('dve', oh1[ob][:], io3, rT[:, 0, t0:t0 + G].unsqueeze(2).broadcast_to([128, G, NK]), ALU.is_equal, r=['cft', 'drT'], w=['doh1%d' % ob])
                        self.tt('pool', oh1[ob][:], oh1[ob][:], rT[:, 2, t0:t0 + G].unsqueeze(2).broadcast_to([128, G, NK]), ALU.mult, r=['doh1%d' % ob, 'drT'], w=['doh1%d' % ob])
                        for q0 in range(0, G, 4):
                            pi = 5 + (n % 3); n += 1
                            for q in range(4):
                                self.mm(ps[pi][0:NK, q * NK:(q + 1) * NK], oh2[ob][:, q0 + q, :], oh1[ob][:, q0 + q, :], True, True,
                                        r=['doh2%d' % ob, 'doh1%d' % ob], w=['ps%d' % pi])
                            tb = j * 128 + t0 + q0
                            self.cp('act', WT[0:NK, :, tb:tb + 4].rearrange("p i t -> p t i"), ps[pi][0:NK, 0:4 * NK].rearrange("p (t i) -> p t i", t=4),
                                    r=['ps%d' % pi], w=['dWTb'])
                        if started < NI:
                            phase1_mm(started, gaI[started][0:NK, :], 'dgI%d' % started, [0, 1, 2, 3, 4]); started += 1
                while started < NI:
                    phase1_mm(started, gaI[started][0:NK, :], 'dgI%d' % started, [0, 1, 2, 3, 4]); started += 1
                for i1 in range(NI):
                    self.tt('dve' if i1 % 2 else 'pool', WT[0:NK, i1, :], WT[0:NK, i1, :], gaI[i1][0:NK, :], ALU.mult, r=['dWTb', 'dgI%d' % i1], w=[('dWT', i1)])
                for i1 in range(NI, NK):
                    gb = i1 % 2
                    phase1_mm(i1, ga[gb][0:NK, :], 'dga%d' % gb, list(range(8)))
                    self.tt('dve' if i1 % 2 else 'pool', WT[0:NK, i1, :], WT[0:NK, i1, :], ga[gb][0:NK, :], ALU.mult, r=['dWTb', 'dga%d' % gb], w=[('dWT', i1)])
                for i1 in range(NK):
                    vb = (i1 // 2) % NUB
                    if i1 % 2 == 0:
                        self.dma('act' if (i1 // 2) % 2 else 'sp', vch[vb][0:NK], self.VB[l, i1 * NK:(i1 + 2) * NK, :].rearrange("(c p) d -> p c d", p=NK), w=['dvc%d' % vb])
                    for j in range(len(pr)):
                        for nb in range(4):
                            self.mm(ps[j * 4 + nb][:], WT[0:NK, i1, j * 128:(j + 1) * 128], vch[vb][0:NK, i1 % 2, nb * 512:(nb + 1) * 512], i1 == 0, i1 == NK - 1,
                                    r=[('dWT', i1), 'dWTb', 'dvc%d' % vb], w=['ps%d' % (j * 4 + nb)])
                for j, tt in enumerate(pr):
                    dsts = [self.XS[tt]]
                    if last:
                        dsts = [self.out[tt - NC]]
                    self.resid_ln(bufs, self.XS[tt], [j * 4 + i for i in range(4)], Gt, 'dG', lngt, lnbt, dsts)
            self.S.barrier()


_CACHE = {}


def run(inputs, cfg, core_ids=None):
    maps = prep_inputs(inputs, cfg)
    key = (cfg.L, cfg.LC, cfg.DEPTH, cfg.NK, cfg.dbg)
    if key not in _CACHE:
        p = Prog(cfg); p.build(); _CACHE[key] = p
    p = _CACHE[key]
    res = run_bass_kernel_spmd(p.nc, maps, core_ids=list(range(len(maps))))
    outs = [r["out"].reshape(cfg.L, D) for r in res.results]
    return np.stack(outs).astype(np.float32), res, p


def kernel(**inputs):
    cfg = Cfg()
    out, _, _ = run(inputs, cfg)
    return out
```
